# Optimizing a Trainium2 kernel written in Bass

```python
import math
import jax
import jax.numpy as jnp
from jax import lax
import numpy as np

D_MODEL = 1024
BATCH = 16
SEQ = 4096
DEPTH = 2

GRID_W = 64
CTX_LEN = 256
HD = 64
D_A = D_MODEL // 2
D_B = D_MODEL - D_A
H_A = D_A // (2 * HD)
H_B = D_B // HD
QBLK = 128
WIN_R = 8
WIN_C = 16
ROPE_THETA = 10000.0
D_RNN = D_MODEL
N_RG_BLOCKS = 8
RG_BW = D_RNN // N_RG_BLOCKS
RG_C = 8.0
CONV_RNN = 4
D_FF = ((8 * D_MODEL // 3 + 127) // 128) * 128
CONV_FFN = 3
N_EVEN = (DEPTH + 1) // 2
N_ODD = DEPTH // 2
DEEPNORM_ALPHA = (2 * DEPTH) ** 0.25
DEEPNORM_BETA = (8 * DEPTH) ** -0.25
NORM_EPS = 1e-5

kernel_name = 'hybrid_diffattn_natten_rglru_dit'


def layer_norm(x, g, b):
    xf = x.astype(jnp.float32)
    mu = jnp.mean(xf, -1, keepdims=True)
    var = jnp.mean(jnp.square(xf - mu), -1, keepdims=True)
    return ((xf - mu) * lax.rsqrt(var + NORM_EPS)).astype(x.dtype) * g + b


def head_rms_norm(o, g):
    of = o.astype(jnp.float32)
    of = of * lax.rsqrt(jnp.mean(jnp.square(of), -1, keepdims=True) + NORM_EPS)
    return of.astype(o.dtype) * g


def dwconv_centred(x, w, b):
    k = w.shape[0]
    left = k // 2
    out = lax.conv_general_dilated(
        x, w[:, None, :].astype(x.dtype), window_strides=(1,),
        padding=[(left, k - 1 - left)], dimension_numbers=('NWC', 'WIO', 'NWC'),
        feature_group_count=x.shape[-1])
    return out + b


def axial_rope(n_tok):
    t = jnp.arange(n_tok)
    row = (t // GRID_W).astype(jnp.float32)[:, None]
    col = (t % GRID_W).astype(jnp.float32)[:, None]
    nf = HD // 4
    inv = 1.0 / (ROPE_THETA ** (jnp.arange(nf, dtype=jnp.float32) / nf))
    ang = jnp.concatenate([row * inv, row * inv, col * inv, col * inv], -1)
    return jnp.cos(ang), jnp.sin(ang)


def apply_rope(x, cos, sin):
    xr = x.reshape(x.shape[:-1] + (2, 2, HD // 4))
    rot = jnp.stack([-xr[..., 1, :], xr[..., 0, :]], -2).reshape(x.shape)
    cos = cos[None, :, None, None, :].astype(x.dtype)
    sin = sin[None, :, None, None, :].astype(x.dtype)
    return x * cos + rot * sin


def blockwise(fn, q):
    b, t = q.shape[:2]
    nb = t // QBLK
    qb = jnp.moveaxis(q.reshape((b, nb, QBLK) + q.shape[2:]), 1, 0)
    out = lax.map(fn, qb)
    return jnp.moveaxis(out, 0, 1).reshape((b, t) + out.shape[3:])


def diff_attend(q, k, v, lam):
    s = jnp.einsum('bqhmd,bkhmd->bhmqk', q, k).astype(jnp.float32) * (HD ** -0.5)
    p = jax.nn.softmax(s, axis=-1)
    w = (p[:, :, 0] - lam * p[:, :, 1]).astype(v.dtype)
    return jnp.einsum('bhqk,bkhe->bqhe', w, v)


def softmax_attend(q, k, v):
    s = jnp.einsum('bqhd,bkhd->bhqk', q, k).astype(jnp.float32) * (q.shape[-1] ** -0.5)
    p = jax.nn.softmax(s, axis=-1).astype(v.dtype)
    return jnp.einsum('bhqk,bkhd->bqhd', p, v)


def neighbourhood_attend(q, k, v, k_ctx, v_ctx, rpb):
    b, t, h, d = q.shape
    rows = t // GRID_W
    wr = min(WIN_R, rows)
    kg = k.reshape(b, rows, GRID_W, h, d)
    vg = v.reshape(b, rows, GRID_W, h, d)
    qg = jnp.moveaxis(q.reshape(b, rows, GRID_W, h, d), 1, 0)
    cols = jnp.arange(GRID_W)
    col_idx = jnp.clip(cols - WIN_C // 2, 0, GRID_W - WIN_C)[:, None] + jnp.arange(WIN_C)[None]
    rel_c = col_idx - cols[:, None] + (WIN_C - 1)
    scale = d ** -0.5
    n_loc = wr * WIN_C

    def row_fn(args):
        r, q_row = args
        rs = jnp.clip(r - wr // 2, 0, rows - wr)
        kb = lax.dynamic_slice_in_dim(kg, rs, wr, axis=1)[:, :, col_idx]
        vb = lax.dynamic_slice_in_dim(vg, rs, wr, axis=1)[:, :, col_idx]
        rel_r = rs + jnp.arange(wr) - r + (WIN_R - 1)
        bias = jnp.transpose(rpb[:, rel_r[:, None, None], rel_c[None]], (0, 2, 1, 3))
        s_loc = jnp.einsum('bqhd,brqchd->bhqrc', q_row, kb).astype(jnp.float32) * scale
        s_loc = s_loc + bias.astype(jnp.float32)[None]
        s_ctx = jnp.einsum('bqhd,bkhd->bhqk', q_row, k_ctx).astype(jnp.float32) * scale
        s = jnp.concatenate([s_loc.reshape(b, h, GRID_W, n_loc), s_ctx], -1)
        p = jax.nn.softmax(s, axis=-1).astype(v.dtype)
        p_loc = p[..., :n_loc].reshape(b, h, GRID_W, wr, WIN_C)
        return (jnp.einsum('bhqrc,brqchd->bqhd', p_loc, vb)
                + jnp.einsum('bhqk,bkhd->bqhd', p[..., n_loc:], v_ctx))

    out = lax.map(row_fn, (jnp.arange(rows), qg))
    return jnp.moveaxis(out, 0, 1).reshape(b, t, h, d)


def even_mixer(h_lat, h_ctx, w_in, w_out, lq1, lk1, lq2, lk2, subln_g, rpb, lambda_init, need_ctx):
    b, t, _ = h_lat.shape
    n_ctx = h_ctx.shape[1]
    cos, sin = axial_rope(t)
    lam = (jnp.exp(jnp.sum(lq1 * lk1).astype(jnp.float32))
           - jnp.exp(jnp.sum(lq2 * lk2).astype(jnp.float32)) + lambda_init)

    def split(p, n):
        qa, ka, va, qb, kb, vb = jnp.split(
            p, [D_A, 2 * D_A, 3 * D_A, 3 * D_A + D_B, 3 * D_A + 2 * D_B], axis=-1)
        return (qa.reshape(b, n, H_A, 2, HD), ka.reshape(b, n, H_A, 2, HD),
                va.reshape(b, n, H_A, 2 * HD), qb.reshape(b, n, H_B, HD),
                kb.reshape(b, n, H_B, HD), vb.reshape(b, n, H_B, HD))

    qa, ka, va, qb, kb, vb = split(h_lat @ w_in, t)
    qa_c, ka_c, va_c, qb_c, kb_c, vb_c = split(h_ctx @ w_in, n_ctx)
    qa = apply_rope(qa, cos, sin)
    ka = apply_rope(ka, cos, sin)
    k_all = jnp.concatenate([ka_c, ka], axis=1)
    v_all = jnp.concatenate([va_c, va], axis=1)
    oa = blockwise(lambda qblk: diff_attend(qblk, k_all, v_all, lam), qa)
    oa = head_rms_norm(oa, subln_g) * (1.0 - lambda_init)
    ob = neighbourhood_attend(qb, kb, vb, kb_c, vb_c, rpb)
    out_lat = jnp.concatenate([oa.reshape(b, t, D_A), ob.reshape(b, t, D_B)], -1) @ w_out
    if not need_ctx:
        return out_lat, None
    oa_c = head_rms_norm(diff_attend(qa_c, ka_c, va_c, lam), subln_g) * (1.0 - lambda_init)
    ob_c = softmax_attend(qb_c, kb_c, vb_c)
    out_ctx = jnp.concatenate([oa_c.reshape(b, n_ctx, D_A), ob_c.reshape(b, n_ctx, D_B)], -1) @ w_out
    return out_lat, out_ctx


def rg_lru_coeffs(x, a_param, wa, ba, wx, bx):
    b, t, _ = x.shape
    xb = x.reshape(b, t, N_RG_BLOCKS, RG_BW).astype(jnp.float32)
    gate_x = jax.nn.sigmoid(jnp.einsum('btni,nij->btnj', xb, wx.astype(jnp.float32)) + bx)
    gate_a = jax.nn.sigmoid(jnp.einsum('btni,nij->btnj', xb, wa.astype(jnp.float32)) + ba)
    log_a = -RG_C * gate_a * jax.nn.softplus(-a_param.astype(jnp.float32)).reshape(N_RG_BLOCKS, RG_BW)
    a = jnp.exp(log_a)
    bterm = gate_x * xb * jnp.sqrt(-jnp.expm1(2.0 * log_a))
    return a.reshape(b, t, D_RNN), bterm.reshape(b, t, D_RNN)


def linear_scan(a, bterm, h0, reverse):
    def combine(e1, e2):
        a1, b1 = e1
        a2, b2 = e2
        return a1 * a2, a2 * b1 + b2
    a_cum, b_cum = lax.associative_scan(combine, (a, bterm), reverse=reverse, axis=1)
    if h0 is None:
        return b_cum
    return a_cum * h0[:, None, :] + b_cum


def odd_mixer(h_lat, h_ctx, w_in, conv_w, conv_b, a_param, wa, ba, wx, bx, w_out, need_ctx):
    y_l, x_l = jnp.split(h_lat @ w_in, 2, axis=-1)
    y_c, x_c = jnp.split(h_ctx @ w_in, 2, axis=-1)
    x_l = dwconv_centred(x_l, conv_w, conv_b)
    x_c = dwconv_centred(x_c, conv_w, conv_b)
    r_lat = []
    r_ctx = []
    for d, reverse in enumerate((False, True)):
        a_c, b_c = rg_lru_coeffs(x_c, a_param[d], wa[d], ba[d], wx[d], bx[d])
        h_c = linear_scan(a_c, b_c, None, reverse)
        h_final = h_c[:, 0] if reverse else h_c[:, -1]
        a_l, b_l = rg_lru_coeffs(x_l, a_param[d], wa[d], ba[d], wx[d], bx[d])
        r_lat.append(linear_scan(a_l, b_l, h_final, reverse))
        r_ctx.append(h_c)
    out_lat = ((r_lat[0] + r_lat[1]).astype(h_lat.dtype) * jax.nn.gelu(y_l, approximate=True)) @ w_out
    if not need_ctx:
        return out_lat, None
    out_ctx = ((r_ctx[0] + r_ctx[1]).astype(h_ctx.dtype) * jax.nn.gelu(y_c, approximate=True)) @ w_out
    return out_lat, out_ctx


def conv_ffn(h, w_up, conv_w, conv_b, w_down):
    g, v = jnp.split(dwconv_centred(h @ w_up, conv_w, conv_b), 2, axis=-1)
    return (jax.nn.gelu(g, approximate=True) * v) @ w_down


def setup_inputs(seed: int = 0) -> dict:
    key = jax.random.key(seed)
    keys = list(jax.random.split(key, 40))
    f32 = jnp.float32

    def nrm(shape, scale):
        return jax.random.normal(keys.pop(), shape, f32) * scale

    u = jax.random.uniform(keys.pop(), (N_ODD, 2, D_RNN), f32, 0.81, 0.998)
    s = u ** (1.0 / RG_C)
    beta = DEEPNORM_BETA
    return {
        'x': nrm((BATCH, SEQ, D_MODEL), 1.0),
        'c': nrm((BATCH, D_MODEL), 1.0),
        'ctx': nrm((BATCH, CTX_LEN, D_MODEL), 1.0),
        'c_ctx': nrm((D_MODEL,), 1.0),
        'ada_w': nrm((DEPTH, D_MODEL, 6 * D_MODEL), 0.5 * D_MODEL ** -0.5),
        'ada_b': nrm((DEPTH, 6 * D_MODEL), 0.01),
        'ln1_g': 1.0 + nrm((DEPTH, D_MODEL), 0.02),
        'ln1_b': nrm((DEPTH, D_MODEL), 0.02),
        'ln2_g': 1.0 + nrm((DEPTH, D_MODEL), 0.02),
        'ln2_b': nrm((DEPTH, D_MODEL), 0.02),
        'ffn_w_up': nrm((DEPTH, D_MODEL, 2 * D_FF), D_MODEL ** -0.5),
        'ffn_conv_w': nrm((DEPTH, CONV_FFN, 2 * D_FF), CONV_FFN ** -0.5),
        'ffn_conv_b': nrm((DEPTH, 2 * D_FF), 0.01),
        'ffn_w_down': nrm((DEPTH, D_FF, D_MODEL), beta * D_FF ** -0.5),
        'att_w_in': nrm((N_EVEN, D_MODEL, 3 * (D_A + D_B)), D_MODEL ** -0.5),
        'att_w_out': nrm((N_EVEN, D_A + D_B, D_MODEL), beta * (D_A + D_B) ** -0.5),
        'diff_lq1': nrm((N_EVEN, HD), 0.1),
        'diff_lk1': nrm((N_EVEN, HD), 0.1),
        'diff_lq2': nrm((N_EVEN, HD), 0.1),
        'diff_lk2': nrm((N_EVEN, HD), 0.1),
        'diff_subln_g': 1.0 + nrm((N_EVEN, 2 * HD), 0.02),
        'na_rpb': nrm((N_EVEN, H_B, 2 * WIN_R - 1, 2 * WIN_C - 1), 0.02),
        'rnn_w_in': nrm((N_ODD, D_MODEL, 2 * D_RNN), D_MODEL ** -0.5),
        'rnn_conv_w': nrm((N_ODD, CONV_RNN, D_RNN), CONV_RNN ** -0.5),
        'rnn_conv_b': nrm((N_ODD, D_RNN), 0.01),
        'rg_a_param': jnp.log(s) - jnp.log1p(-s),
        'rg_wa': nrm((N_ODD, 2, N_RG_BLOCKS, RG_BW, RG_BW), RG_BW ** -0.5),
        'rg_ba': nrm((N_ODD, 2, N_RG_BLOCKS, RG_BW), 0.01),
        'rg_wx': nrm((N_ODD, 2, N_RG_BLOCKS, RG_BW, RG_BW), RG_BW ** -0.5),
        'rg_bx': nrm((N_ODD, 2, N_RG_BLOCKS, RG_BW), 0.01),
        'rnn_w_out': nrm((N_ODD, D_RNN, D_MODEL), beta * D_RNN ** -0.5),
    }


def reference(x, c, ctx, c_ctx, ada_w, ada_b, ln1_g, ln1_b, ln2_g, ln2_b,
              ffn_w_up, ffn_conv_w, ffn_conv_b, ffn_w_down,
              att_w_in, att_w_out, diff_lq1, diff_lk1, diff_lq2, diff_lk2, diff_subln_g, na_rpb,
              rnn_w_in, rnn_conv_w, rnn_conv_b, rg_a_param, rg_wa, rg_ba, rg_wx, rg_bx, rnn_w_out):
    alpha = DEEPNORM_ALPHA
    for l in range(DEPTH):
        last = l == DEPTH - 1
        mod_lat = (jax.nn.silu(c) @ ada_w[l] + ada_b[l])[:, None, :]
        mod_ctx = jax.nn.silu(c_ctx) @ ada_w[l] + ada_b[l]
        sh1, sc1, g1, sh2, sc2, g2 = jnp.split(mod_lat, 6, axis=-1)
        csh1, csc1, cg1, csh2, csc2, cg2 = jnp.split(mod_ctx, 6, axis=-1)
        h_lat = x * (1.0 + sc1) + sh1
        h_ctx = ctx * (1.0 + csc1) + csh1
        i = l // 2
        if l % 2 == 0:
            lambda_init = 0.8 - 0.6 * math.exp(-0.3 * l)
            o_lat, o_ctx = even_mixer(h_lat, h_ctx, att_w_in[i], att_w_out[i], diff_lq1[i], diff_lk1[i],
                                      diff_lq2[i], diff_lk2[i], diff_subln_g[i], na_rpb[i],
                                      lambda_init, not last)
        else:
            o_lat, o_ctx = odd_mixer(h_lat, h_ctx, rnn_w_in[i], rnn_conv_w[i], rnn_conv_b[i],
                                     rg_a_param[i], rg_wa[i], rg_ba[i], rg_wx[i], rg_bx[i],
                                     rnn_w_out[i], not last)
        x = layer_norm(alpha * x + g1 * o_lat, ln1_g[l], ln1_b[l])
        f_lat = conv_ffn(x * (1.0 + sc2) + sh2, ffn_w_up[l], ffn_conv_w[l], ffn_conv_b[l], ffn_w_down[l])
        x = layer_norm(alpha * x + g2 * f_lat, ln2_g[l], ln2_b[l])
        if not last:
            ctx = layer_norm(alpha * ctx + cg1 * o_ctx, ln1_g[l], ln1_b[l])
            f_ctx = conv_ffn(ctx * (1.0 + csc2) + csh2, ffn_w_up[l], ffn_conv_w[l], ffn_conv_b[l], ffn_w_down[l])
            ctx = layer_norm(alpha * ctx + cg2 * f_ctx, ln2_g[l], ln2_b[l])
    return x
```

```python
import math
import contextlib
import numpy as np
import concourse.bass as bass
import concourse.mybir as mybir
from concourse.bass_utils import run_bass_kernel_spmd

F32 = mybir.dt.float32
BF16 = mybir.dt.bfloat16
AF = mybir.ActivationFunctionType
ALU = mybir.AluOpType

ENGS = ["pe", "act", "dve", "pool", "sp"]

D = 1024
T = 4096
TC = 256
TK = T + TC
DFF = 2816
ALPHA = 4.0 ** 0.25
EPS = 1e-5
LAMBDA_INIT0 = 0.8 - 0.6 * math.exp(0.0)
NEG = -30000.0


class MK:
    def __init__(self, nc):
        self.nc = nc
        self.ops = {e: [] for e in ENGS}
        self.cnt = {e: 0 for e in ENGS}
        self.dcnt = {}
        self.seen = {e: {} for e in ENGS}
        self.lastw = {}
        self.readers = {}
        self.sb_off = 16640
        self.sb_names = 0
        self.sb_max = 0

    def sb(self, shape, dtype, name=None):
        nbytes = int(np.prod(shape[1:])) * (4 if dtype == F32 else 2)
        nbytes = (nbytes + 63) // 64 * 64
        self.sb_names += 1
        nm = "%s_%d" % (name or "t", self.sb_names)
        t = self.nc.alloc_sbuf_tensor_at(nm, list(shape), dtype, offset=self.sb_off)
        self.sb_off += nbytes
        self.sb_max = max(self.sb_max, self.sb_off)
        assert self.sb_off <= 229376, ("sbuf overflow", nm, self.sb_off)
        return t

    def mark(self):
        return self.sb_off

    def release(self, m):
        self.barrier()
        self.sb_off = m

    def _deps(self, eng, reads, writes, is_dma):
        deps = {}

        def add(tok, raw):
            if tok is None:
                return
            sk, val, teng = tok
            if not is_dma and teng == eng and sk[0] == "e":
                if eng == "pe":
                    return
            if deps.get(sk, 0) < val:
                deps[sk] = val

        for k in reads:
            add(self.lastw.get(k), True)
        for k in writes:
            add(self.lastw.get(k), False)
            for sk, (val, teng) in self.readers.get(k, {}).items():
                add((sk, val, teng), False)
        waits = []
        seen = self.seen[eng]
        for sk, val in deps.items():
            if seen.get(sk, 0) >= val:
                continue
            seen[sk] = val
            waits.append((sk, val))
        return waits

    def _commit(self, tok, reads, writes):
        sk, val, teng = tok
        for k in writes:
            self.lastw[k] = tok
            self.readers[k] = {}
        for k in reads:
            self.readers.setdefault(k, {})[sk] = (val, teng)

    def op(self, eng, fn, reads=(), writes=()):
        if eng != "pe":
            pr = [k for k in reads if isinstance(k, tuple) and k and k[0] == "p"]
            if pr:
                writes = list(writes) + [k for k in pr if k not in writes]
        waits = self._deps(eng, reads, writes, False)
        self.cnt[eng] += 1
        tok = (("e", eng), self.cnt[eng], eng)
        self.ops[eng].append((waits, fn, tok, 1))
        self._commit(tok, reads, writes)
        return tok

    def dma(self, eng, pairs, reads, writes, slot, slow=False):
        sk = ("d", slot)
        prev = self.dcnt.get(slot, 0)
        waits = self._deps(eng, reads, writes, True)
        if prev and self.seen[eng].get(sk, 0) < prev:
            self.seen[eng][sk] = prev
            waits.append((sk, prev))
        n = len(pairs)
        self.dcnt[slot] = prev + 16 * n
        tok = (sk, prev + 16 * n, None)

        def fn(e, pairs=pairs, slow=slow):
            if slow:
                return [e.dma_start(out=o, in_=i, allow_slow_non_contiguous=True) for (o, i) in pairs]
            return [e.dma_start(out=o, in_=i) for (o, i) in pairs]

        self.ops[eng].append((waits, fn, tok, 16))
        self._commit(tok, reads, writes)
        return tok

    def barrier(self):
        final = {}
        for e in ENGS:
            if self.cnt[e]:
                final[("e", e)] = self.cnt[e]
        for s, v in self.dcnt.items():
            final[("d", s)] = v
        for e in ENGS:
            waits = []
            for sk, val in final.items():
                if sk == ("e", e):
                    continue
                if self.seen[e].get(sk, 0) < val:
                    self.seen[e][sk] = val
                    waits.append((sk, val))
            if waits:
                self.ops[e].append((waits, None, None, 0))
        self.lastw = {}
        self.readers = {}

    def emit(self):
        nc = self.nc
        self.barrier()
        waited = {e: set() for e in ENGS}
        for e in ENGS:
            for waits, fn, tok, inc in self.ops[e]:
                for sk, val in waits:
                    if sk[0] == "e":
                        waited[sk[1]].add(val)
        remap = {}
        for e in ENGS:
            vals = sorted(waited[e])
            remap[e] = {v: i + 1 for i, v in enumerate(vals)}
        sems = {}
        with contextlib.ExitStack() as st:
            for e in ENGS:
                sems[("e", e)] = st.enter_context(nc.semaphore("s_" + e))
            for s in self.dcnt:
                sems[("d", s)] = st.enter_context(nc.semaphore("d_" + str(s)))
            block = st.enter_context(nc.Block())

            def run(engname):
                def body(eng):
                    for waits, fn, tok, inc in self.ops[engname]:
                        for sk, val in waits:
                            v = remap[sk[1]][val] if sk[0] == "e" else val
                            eng.wait_ge(sems[sk], v)
                        if fn is None:
                            continue
                        r = fn(eng)
                        if inc == 16:
                            for ins in r:
                                ins.then_inc(sems[tok[0]], 16)
                        elif tok[1] in remap[engname]:
                            r.then_inc(sems[tok[0]], 1)

                return body

            block.tensor(run("pe"))
            block.scalar(run("act"))
            block.vector(run("dve"))
            block.gpsimd(run("pool"))
            block.sync(run("sp"))


def _vec_map():
    m = {}
    o = 0
    for l in range(2):
        for nm in ("ln1_g", "ln1_b", "ln2_g", "ln2_b"):
            m[(nm, l)] = o
            o += 8
    for l in range(2):
        for k in range(3):
            m[("fcw", l, k)] = o
            o += 44
    for l in range(2):
        m[("fcb", l)] = o
        o += 44
    for k in range(4):
        m[("rcw", k)] = o
        o += 8
    m[("rcb",)] = o
    o += 8
    for d in range(2):
        m[("apar", d)] = o
        o += 8
    for d in range(2):
        m[("rba", d)] = o
        o += 8
    for d in range(2):
        m[("rbx", d)] = o
        o += 8
    return m, o


VMAP, NV = _vec_map()


def _pl(v):
    v = np.asarray(v, np.float32).reshape(-1, 128)
    return np.ascontiguousarray(v.T)


def build(stop=None, dbg=()):
    nc = bass.Bass("TRN2", target_bir_lowering=False)
    mk = MK(nc)

    def din(name, shape, dt=F32):
        return nc.dram_tensor(name, list(shape), dt, kind="ExternalInput").ap()

    def dscr(name, shape, dt=F32):
        return nc.dram_tensor(name, list(shape), dt, kind="Internal").ap()

    x_d = din("x", [2, T, D])
    ctx_d = din("ctx", [2, TC, D])
    cct_d = din("cct", [128, 8, 3])
    vec_d = din("vec", [128, NV])
    rope_d = din("rope", [T, 2, 64])
    rpbt_d = din("rpbt", [15, 64, 8, 64])
    mask_d = din("mask", [128, 64])
    lam_d = din("lamv", [4, 64])
    subg_d = din("subg", [128])
    ada_w_d = din("ada_w", [2, D, 6 * D])
    ada_b_d = din("ada_b", [2, 6 * D])
    ln_d = {nm: din(nm, [2, D]) for nm in ("ln1_g", "ln1_b", "ln2_g", "ln2_b")}
    wup_d = din("ffn_w_up", [2, D, 2 * DFF])
    wdn_d = din("ffn_w_down", [2, DFF, D])
    awin_d = din("att_w_in", [1, D, 3 * D])
    awout_d = din("att_w_out", [1, D, D])
    rwin_d = din("rnn_w_in", [1, D, 2 * D])
    rga_d = din("rg_wa", [1, 2, 8, 128, 128])
    rgx_d = din("rg_wx", [1, 2, 8, 128, 128])
    rwout_d = din("rnn_w_out", [1, D, D])
    out_d = nc.dram_tensor("out", [2, T, D], F32, kind="ExternalOutput").ap()

    modd = dscr("modd", [2, 3, 6 * D])
    NBB = CFG.get("nb", 2)
    SEQ = [("lat", 0), ("ctx", 0), ("lat", 1), ("ctx", 1)][:2 * NBB]
    SEQ_ALL = [("lat", 0), ("ctx", 0), ("lat", 1), ("ctx", 1)]
    xs = {s: dscr("xs_%s%d" % s, [T if s[0] == "lat" else TC, D]) for s in SEQ_ALL}
    hts = {(s, a): dscr("hts%d_%s%d" % ((a,) + s), [8, 128, (T if s[0] == "lat" else TC) + 2], BF16)
           for s in SEQ_ALL for a in (0, 1)}
    qat = [dscr("qat%d" % b, [4, 128, TK], BF16) for b in range(2)]
    kat = [dscr("kat%d" % b, [4, 128, TK], BF16) for b in range(2)]
    qbt = [dscr("qbt%d" % b, [4, 128, TK], BF16) for b in range(2)]
    kbt = [dscr("kbt%d" % b, [4, 128, TK], BF16) for b in range(2)]
    vas = [dscr("va%d" % b, [TK, 512], BF16) for b in range(2)]
    vbs = [dscr("vb%d" % b, [TK, 512], BF16) for b in range(2)]
    aos = [dscr("ao%d" % b, [TK, D]) for b in range(2)]
    mts = [dscr("mt%d" % b, [8, 128, T], BF16) for b in range(2)]

    def seq_src(s):
        return x_d[s[1]] if s[0] == "lat" else ctx_d[s[1]]

    def seq_len(s):
        return T if s[0] == "lat" else TC

    def seq_var(s):
        return s[1] if s[0] == "lat" else 2

    PS = nc.alloc_psum_tensor("PS", [128, 4096], F32)

    def bank(i, n=512, off=0):
        return PS[:, 512 * i + off:512 * i + off + n]

    def pk(i):
        return [("p", i)]

    def mm(out, lhsT, rhs, start, stop, r, w, skip=False):
        mk.op("pe", lambda e: e.matmul(out, lhsT=lhsT, rhs=rhs, start=start, stop=stop, skip_group_check=skip), r, w)

    def tr(out, in_, idn, r, w):
        mk.op("pe", lambda e: e.transpose(out, in_, idn), r, w)

    def act(out, in_, func, r, w, bias=None, scale=None, accum=None):
        kw = {}
        if bias is not None:
            kw["bias"] = bias
        if scale is not None:
            kw["scale"] = scale
        if accum is not None:
            kw["accum_out"] = accum
        mk.op("act", lambda e: e.activation(out=out, in_=in_, func=func, **kw), r, w)

    def ts(eng, out, in0, s1, s2, op0, op1, r, w):
        if s2 is None:
            if op0 == ALU.mult:
                mk.op(eng, lambda e: e.tensor_scalar_mul(out=out, in0=in0, scalar1=s1), r, w)
            else:
                assert op0 == ALU.add
                mk.op(eng, lambda e: e.tensor_scalar_add(out=out, in0=in0, scalar1=s1), r, w)
        else:
            mk.op(eng, lambda e: e.tensor_scalar(out=out, in0=in0, scalar1=s1, scalar2=s2, op0=op0, op1=op1), r, w)

    def stt(eng, out, in0, scalar, in1, op0, op1, r, w):
        mk.op(eng, lambda e: e.scalar_tensor_tensor(out=out, in0=in0, scalar=scalar, in1=in1, op0=op0, op1=op1), r, w)

    def tt(eng, out, in0, in1, op, r, w):
        mk.op(eng, lambda e: e.tensor_tensor(out=out, in0=in0, in1=in1, op=op), r, w)

    def cp(eng, out, in_, r, w):
        if eng == "act":
            mk.op("act", lambda e: e.copy(out=out, in_=in_), r, w)
        else:
            mk.op(eng, lambda e: e.tensor_copy(out=out, in_=in_), r, w)

    def memset(eng, ap, val, w):
        mk.op(eng, lambda e: e.memset(ap, val), [], w)

    def ld(out, in_, r, w, slot):
        mk.dma("sp", [(out, in_)], r, w, slot)

    def ldc(pairs, r, w, slot):
        mk.dma("pool", pairs, r, w, slot)

    ident = mk.sb([128, 128], F32, "ident")
    memset("pool", ident[:], 0.0, ["ident"])
    mk.op("pool", lambda e: e.affine_select(out=ident[:], in_=ident[:], compare_op=ALU.not_equal, fill=1.0,
                                             base=0, pattern=[[-1, 128]], channel_multiplier=1), ["ident"], ["ident"])
    epsT = mk.sb([128, 1], F32, "eps")
    memset("dve", epsT[:], EPS, ["eps"])
    VEC = mk.sb([128, NV], F32, "VEC")
    ld(VEC[:], vec_d, [], ["VEC"], "c0")
    MODP = [mk.sb([128, 48, 3], F32, "MODP%d" % l) for l in range(2)]
    SCAL = mk.sb([128, 4, 3, 2, 8], F32, "SCAL")
    CST = mk.sb([128, 16], F32, "CST")
    NLAM = mk.sb([128, 1], F32, "NLAM")
    GSUB = mk.sb([128, 128], F32, "GSUB")

    def vcol(key, n=8):
        o = VMAP[key]
        return VEC[:, o:o + n]

    m0 = mk.mark()
    ZT = mk.sb([128, 8, 2], BF16, "ZT")
    memset("dve", ZT[:], 0.0, ["ZT"])
    for s in SEQ:
        for a in (0, 1):
            L = seq_len(s)
            h = hts[(s, a)].rearrange("k p t -> p k t")
            mk.dma("sp", [(h[:, :, 0:1], ZT[:, :, 0:1])], ["ZT"], [], "z0", slow=True)
            mk.dma("sp", [(h[:, :, L + 1:L + 2], ZT[:, :, 1:2])], ["ZT"], [], "z0", slow=True)

    CC = mk.sb([128, 8, 3], F32, "CC")
    ST = mk.sb([128, 8, 3], F32, "ST")
    ld(CC[:], cct_d, [], ["CC"], "c1")
    act(ST[:], CC[:], AF.Silu, ["CC"], ["ST"])
    MODR = mk.sb([3, 6 * D], F32, "MODR")
    ADAB = mk.sb([3, 6 * D], F32, "ADAB")
    WB = [mk.sb([128, 8, 512], F32, "WB%d" % i) for i in range(2)]
    for l in range(2):
        ld(ADAB[:], ada_b_d[l].partition_broadcast(3), [], ["ADAB"], "c2")
        wv = ada_w_d[l].rearrange("(k p) n -> p k n", p=128)
        for jb in range(12):
            sl = jb % 2
            ld(WB[sl][:], wv[:, :, jb * 512:(jb + 1) * 512], [], ["WB%d" % sl], "wb%d" % sl)
            bk = jb % 2
            for k in range(8):
                mm(PS[0:3, 512 * bk:512 * bk + 512], ST[:, k, :], WB[sl][:, k, :], k == 0, k == 7,
                   ["ST", "WB%d" % sl], pk(bk))
            tt("dve", MODR[0:3, jb * 512:(jb + 1) * 512], PS[0:3, 512 * bk:512 * bk + 512],
               ADAB[0:3, jb * 512:(jb + 1) * 512], ALU.add, pk(bk) + ["ADAB"], [("MODR", jb)])
        allk = [("MODR", jb) for jb in range(12)]
        ld(modd[l], MODR[0:3, :], allk, [], "c3")
        for k in range(48):
            tr(PS[:, 3584 + 3 * k:3584 + 3 * k + 3], MODR[0:3, 128 * k:128 * (k + 1)], ident[0:3, 0:3],
               allk + ["ident"], pk(7))
        cp("dve", MODP[l][:].rearrange("p k r -> p (k r)"), PS[:, 3584:3584 + 144], pk(7), ["MODP%d" % l])
    TMP8 = mk.sb([128, 8], F32, "TMP8")
    sites = [(0, 8, 0, None, None), (0, 32, 24, ("ln1_g", 0), ("ln1_b", 0)),
             (1, 8, 0, ("ln2_g", 0), ("ln2_b", 0)), (1, 32, 24, ("ln1_g", 1), ("ln1_b", 1))]
    for si, (l, sco, sho, gk, bk_) in enumerate(sites):
        for r in range(3):
            A_ = SCAL[:, si, r, 0, :]
            B_ = SCAL[:, si, r, 1, :]
            sc = MODP[l][:, sco:sco + 8, r]
            sh = MODP[l][:, sho:sho + 8, r]
            ts("dve", TMP8[:], sc, 1.0, None, ALU.add, None, ["MODP%d" % l], ["TMP8"])
            if gk is None:
                cp("dve", A_, TMP8[:], ["TMP8"], ["SCAL"])
                cp("dve", B_, sh, ["MODP%d" % l], ["SCAL"])
            else:
                tt("dve", A_, TMP8[:], vcol(gk), ALU.mult, ["TMP8", "VEC"], ["SCAL"])
                tt("dve", B_, TMP8[:], vcol(bk_), ALU.mult, ["TMP8", "VEC"], ["SCAL"])
                tt("dve", B_, B_, sh, ALU.add, ["SCAL", "MODP%d" % l], ["SCAL"])
    LV = mk.sb([128, 4, 64], F32, "LV")
    ld(LV[:].rearrange("p a d -> p (a d)"), lam_d.rearrange("a d -> (a d)").partition_broadcast(128), [], ["LV"], "c4")
    L2 = mk.sb([128, 2, 64], F32, "L2")
    E2 = mk.sb([128, 2], F32, "E2")
    tt("dve", L2[:, 0, :], LV[:, 0, :], LV[:, 1, :], ALU.mult, ["LV"], ["L2"])
    tt("dve", L2[:, 1, :], LV[:, 2, :], LV[:, 3, :], ALU.mult, ["LV"], ["L2"])
    mk.op("dve", lambda e: e.reduce_sum(out=E2[:], in_=L2[:], axis=mybir.AxisListType.X), ["L2"], ["E2"])
    act(E2[:], E2[:], AF.Exp, ["E2"], ["E2"])
    tt("dve", NLAM[:], E2[:, 1:2], E2[:, 0:1], ALU.subtract, ["E2"], ["NLAM"])
    ts("dve", NLAM[:], NLAM[:], -LAMBDA_INIT0, None, ALU.add, None, ["NLAM"], ["NLAM"])
    ld(GSUB[:], subg_d.partition_broadcast(128), [], ["GSUB"], "c5")
    ts("dve", GSUB[:], GSUB[:], 1.0 - LAMBDA_INIT0, None, ALU.mult, None, ["GSUB"], ["GSUB"])
    act(CST[:], VEC[:, VMAP[("apar", 0)]:VMAP[("apar", 0)] + 16], AF.Exp, ["VEC"], ["CST"], scale=-1.0)
    ts("dve", CST[:], CST[:], 1.0, None, ALU.add, None, ["CST"], ["CST"])
    act(CST[:], CST[:], AF.Ln, ["CST"], ["CST"])
    ts("dve", CST[:], CST[:], -8.0, None, ALU.mult, None, ["CST"], ["CST"])
    mk.release(m0)

    def make_ht(src, srckeys, HT, htkey, site, r, pb):
        for k in range(8):
            b_ = pb + k // 4
            tr(bank(b_, 128, 128 * (k % 4)), src[:, 128 * k:128 * (k + 1)], ident[:], srckeys + ["ident"],
               [("p", b_)])
        for k in range(8):
            hm = CFG.get("ht_mode", 3)
            if hm == 0 or (hm == 1 and k >= 4) or (hm == 2 and k < 4):
                continue
            b_ = pb + k // 4
            A_ = SCAL[:, site, r, 0, k:k + 1]
            B_ = SCAL[:, site, r, 1, k:k + 1]
            if k < 4:
                act(HT[:, k, :], bank(b_, 128, 128 * (k % 4)), AF.Identity, [("p", b_), "SCAL"],
                    [(htkey, k)], bias=B_, scale=A_)
            else:
                ts("dve", HT[:, k, :], bank(b_, 128, 128 * (k % 4)), A_, B_, ALU.mult, ALU.add,
                   [("p", b_), "SCAL"], [(htkey, k)])

    def epilogue(ps2, pskeys, Xsrc, GB, Gt, Bt, bkeys, site, r, xdst, hdst, E, i, pb):
        sl = i % 2
        sy = i % len(E["Y"])
        Xt, Y, HT = E["Xt"][sl], E["Y"][sy], E["HT"][sy]
        kx, ky, kh = "Xt%d" % sl, "Y%d" % sy, "HTe%d" % sy
        ld(Xt[:], Xsrc, [], [kx], "ex%d" % sl)
        tt("dve", Y[:], ps2, GB[:], ALU.mult, pskeys + bkeys, [ky])
        stt("dve", Y[:], Xt[:], ALPHA, Y[:], ALU.mult, ALU.add, [kx, ky], [ky])
        BS, MV, RS = E["BS"][sl], E["MV"][sl], E["RS"][sl]
        kb = "BS%d" % sl
        mk.op("dve", lambda e: e.bn_stats(out=BS[:, 0, :], in_=Y[:, 0:512]), [ky], [(kb, 0)])
        mk.op("dve", lambda e: e.bn_stats(out=BS[:, 1, :], in_=Y[:, 512:1024]), [ky], [(kb, 1)])
        mk.op("dve", lambda e: e.bn_aggr(out=MV[:], in_=BS[:]), [(kb, 0), (kb, 1)], [(kb, 2)])
        act(RS[:, 0:1], MV[:, 1:2], AF.Ln, [(kb, 2), "eps"], [(kb, 3)], bias=epsT[:, 0:1])
        act(RS[:, 1:2], RS[:, 0:1], AF.Exp, [(kb, 3)], [(kb, 4)], scale=-0.5)
        stt("dve", RS[:, 2:3], MV[:, 0:1], -1.0, RS[:, 1:2], ALU.mult, ALU.mult, [(kb, 2), (kb, 4)], [(kb, 5)])
        act(Y[:], Y[:], AF.Identity, [ky, (kb, 4), (kb, 5)], [ky], bias=RS[:, 2:3], scale=RS[:, 1:2])
        tt("pool", Xt[:], Y[:], Gt[:], ALU.mult, [ky] + bkeys, [kx])
        tt("pool", Xt[:], Xt[:], Bt[:], ALU.add, [kx] + bkeys, [kx])
        ld(xdst, Xt[:], [kx], [], "es%d" % sl)
        if hdst is not None:
            make_ht(Y, [ky], HT, kh, site, r, pb)
            ld(hdst, HT[:], [(kh, k) for k in range(8)], [], "eh%d" % sy)

    def epi_alloc(ny=2):
        E = {}
        E["Xt"] = [mk.sb([128, D], F32, "Xt") for _ in range(2)]
        E["Y"] = [mk.sb([128, D], F32, "Y") for _ in range(ny)]
        E["HT"] = [mk.sb([128, 8, 128], BF16, "HTe") for _ in range(ny)]
        E["BS"] = [mk.sb([128, 2, 6], F32, "BS") for _ in range(2)]
        E["MV"] = [mk.sb([128, 2], F32, "MV") for _ in range(2)]
        E["RS"] = [mk.sb([128, 3], F32, "RS") for _ in range(2)]
        return E

    def load_bc(GB, Gt, Bt, l, goff, r, gname, bname):
        ld(GB[:], modd[l, r, goff:goff + D].partition_broadcast(128), [], ["GB"], "bc0")
        ld(Gt[:], ln_d[gname][l].partition_broadcast(128), [], ["Gt"], "bc1")
        ld(Bt[:], ln_d[bname][l].partition_broadcast(128), [], ["Bt"], "bc2")

    def phase_qkv():
        m = mk.mark()
        WIN = mk.sb([128, 8, 3 * D], BF16, "WIN")
        wv = awin_d[0].rearrange("(k p) n -> p k n", p=128)
        for k in range(8):
            if CFG.get("nowin"):
                break
            ldc([(WIN[:, k, :], wv[:, k, :])], [], [("WIN", k)], "w%d" % (k % 4))
        wink = [("WIN", k) for k in range(8)]
        XT = [mk.sb([128, D], F32, "XT") for _ in range(2)]
        RP = [mk.sb([128, 2, 64], F32, "RP") for _ in range(2)]
        HT = [mk.sb([128, 8, 128], BF16, "HT") for _ in range(2)]
        RQ = mk.sb([128, D], F32, "RQ")
        T1 = mk.sb([128, D], F32, "T1")
        T2 = mk.sb([128, D], F32, "T2")
        RB = mk.sb([128, D], F32, "RB")
        VV = [mk.sb([128, 2, 512], BF16, "VV") for _ in range(2)]
        TQ = [mk.sb([128, 8, 128], BF16, "TQ") for _ in range(2)]
        TB = [mk.sb([128, 8, 128], BF16, "TB") for _ in range(2)]
        tiles = [(b, i) for b in range(NBB) for i in range(34)]
        if "qkv_tiles" in CFG:
            tiles = tiles[:CFG["qkv_tiles"]]

        def src_of(b, i):
            if i < 32:
                return x_d[b, 128 * i:128 * (i + 1), :]
            return ctx_d[b, 128 * (i - 32):128 * (i - 31), :]

        def issue_load(n):
            b, i = tiles[n]
            sl = n % 2
            if not CFG.get("nox"):
                ld(XT[sl][:], src_of(b, i), [], ["XT%d" % sl], "xl%d" % sl)
            if i < 32 and not CFG.get("norp"):
                ld(RP[sl][:], rope_d[128 * i:128 * (i + 1)], [], ["RP%d" % sl], "rl%d" % sl)

        issue_load(0)
        for n, (b, i) in enumerate(tiles):
            sl = n % 2
            if n + 1 < len(tiles):
                issue_load(n + 1)
            lat = i < 32
            r = b if lat else 2
            tok0 = 128 * i if lat else T + 128 * (i - 32)
            htk = "HT%d" % sl
            if not CFG.get("noht"):
                make_ht(XT[sl], ["XT%d" % sl], HT[sl], htk, 0, r, 6)
            hk = [(htk, k) for k in range(8)]
            stg = CFG.get("qkv_stage", 9)
            if stg < 1:
                continue
            for j in range(6):
                for k in range(8):
                    mm(bank(j), HT[sl][:, k, :], WIN[:, k, 512 * j:512 * (j + 1)], k == 0, k == 7,
                       hk + wink, pk(j))
            if stg < 2:
                continue
            cp("act", RQ[:], PS[:, 0:1024], pk(0) + pk(1), ["RQ"])
            cp("dve", RB[:], PS[:, 1536:2560], pk(3) + pk(4), ["RB"])
            cp("act", VV[sl][:, 0, :], bank(2), pk(2), [("VV%d" % sl, 0)])
            cp("dve", VV[sl][:, 1, :], bank(5), pk(5), [("VV%d" % sl, 1)])
            if lat and not CFG.get("norope"):
                rp = RP[sl]
                rk = "RP%d" % sl
                tt("dve", T1[:].rearrange("p (g d) -> p g d", g=16), RQ[:].rearrange("p (g d) -> p g d", g=16),
                   rp[:, 0:1, :].to_broadcast([128, 16, 64]), ALU.mult, ["RQ", rk], ["T1"])
                rqv = RQ[:].rearrange("p (g a h f) -> p g a h f", g=16, a=2, h=2, f=16)
                t2v = T2[:].rearrange("p (g a h f) -> p g a h f", g=16, a=2, h=2, f=16)
                sv = rp[:, 1:2, :].rearrange("p o (a h f) -> p o a h f", a=2, h=2, f=16)
                tt("pool", t2v[:, :, :, 0, :], rqv[:, :, :, 1, :], sv[:, :, :, 0, :].to_broadcast([128, 16, 2, 16]),
                   ALU.mult, ["RQ", rk], [("T2", 0)])
                tt("pool", t2v[:, :, :, 1, :], rqv[:, :, :, 0, :], sv[:, :, :, 1, :].to_broadcast([128, 16, 2, 16]),
                   ALU.mult, ["RQ", rk], [("T2", 1)])
                tt("dve", T1[:], T1[:], T2[:], ALU.add, ["T1", ("T2", 0), ("T2", 1)], ["T1"])
                qsrc, qk = T1, ["T1"]
            else:
                qsrc, qk = RQ, ["RQ"]
            if stg < 3:
                continue
            for (src, sk, dst, dk) in ((qsrc, qk, TQ[sl], "TQ%d" % sl), (RB, ["RB"], TB[sl], "TB%d" % sl)):
                for k in range(8):
                    b_ = 6 + k // 4
                    tr(bank(b_, 128, 128 * (k % 4)), src[:, 128 * k:128 * (k + 1)], ident[:], sk + ["ident"],
                       [("p", b_)])
                act(dst[:, 0:4, :].rearrange("p k t -> p (k t)"), bank(6), AF.Identity,
                    [("p", 6)], [(dk, 0)], scale=0.125)
                cp("dve", dst[:, 4:8, :].rearrange("p k t -> p (k t)"), bank(7),
                   [("p", 7)], [(dk, 1)])
            if stg < 4:
                continue
            ld(vas[b][tok0:tok0 + 128, :], VV[sl][:, 0, :], [("VV%d" % sl, 0)], [], "sva%d" % sl)
            ld(vbs[b][tok0:tok0 + 128, :], VV[sl][:, 1, :], [("VV%d" % sl, 1)], [], "svb%d" % sl)
            for si_, (dst_d, tile_, kk, half) in enumerate(((qat, TQ, "TQ%d" % sl, 0), (kat, TQ, "TQ%d" % sl, 1),
                                                          (qbt, TB, "TB%d" % sl, 0), (kbt, TB, "TB%d" % sl, 1))):
                ld(dst_d[b].rearrange("h p t -> p h t")[:, :, tok0:tok0 + 128],
                   tile_[sl][:, 4 * half:4 * half + 4, :], [(kk, half)], [], "sq%d%d" % (si_, sl))
        mk.release(m)

    def phase_diff():
        m = mk.mark()
        KT = [mk.sb([128, TK], BF16, "KT") for _ in range(2)]
        QT = [mk.sb([128, TK], BF16, "QT") for _ in range(2)]
        V1 = [mk.sb([128, 34, 129], BF16, "V1") for _ in range(2)]
        PT = [[mk.sb([128, 512], BF16, "PT") for _ in range(2)] for _ in range(2)]
        AQ = mk.sb([128, 128], F32, "AQ")
        OO = mk.sb([128, 128], F32, "OO")
        JK = mk.sb([128, 128], F32, "JK")
        R2 = mk.sb([128, 4], F32, "R2")
        AOB = [mk.sb([128, 4, 128], F32, "AOB") for _ in range(2)]
        for i in range(2):
            memset("pool", V1[i][:, :, 128:129], 1.0, [("V1%d" % i, "one")])
        nblk = 0
        hh = 0
        for b in range(NBB):
            for h in range(4):
                sl = hh % 2
                hh += 1
                ld(KT[sl][:], kat[b][h], [], ["KT%d" % sl], "ak%d" % sl)
                ld(QT[sl][:], qat[b][h], [], ["QT%d" % sl], "aq%d" % sl)
                ld(V1[sl][:, :, 0:128], vas[b].rearrange("(t p) c -> p t c", p=128)[:, :, 128 * h:128 * (h + 1)],
                   [], ["V1%d" % sl], "av%d" % sl)
                vkeys = ["V1%d" % sl, ("V1%d" % sl, "one")]
                blocks = [(512 * j, 512, list(range(34))) for j in range(8)] + [(T, 256, [32, 33])]
                for (q0, nq, kts) in blocks:
                    nqt = nq // 128
                    memset("dve", PS[:, 2048:4096].rearrange("p (b c) -> p b c", b=4)[:, 0:nqt, 0:258], 0.0,
                           [("acc", q) for q in range(nqt)])
                    for it, kt in enumerate(kts):
                        par = it % 2
                        for mp in range(2):
                            bk = 2 * par + mp
                            mm(bank(bk, nq), KT[sl][64 * mp:64 * (mp + 1), 128 * kt:128 * (kt + 1)],
                               QT[sl][64 * mp:64 * (mp + 1), q0:q0 + nq], True, True,
                               ["KT%d" % sl, "QT%d" % sl], pk(bk))
                            act(PT[par][mp][:, 0:nq], bank(bk, nq), AF.Exp, pk(bk), ["PT%d%d" % (par, mp)])
                        for mp in range(2):
                            for qt in range(nqt):
                                mm(bank(4 + qt, 129, 129 * mp), PT[par][mp][:, 128 * qt:128 * (qt + 1)],
                                   V1[sl][:, kt, :], False, False, ["PT%d%d" % (par, mp)] + vkeys + [("acc", qt)],
                                   [("acc", qt)], skip=True)
                    ob = AOB[nblk % 2]
                    okey = "AOB%d" % (nblk % 2)
                    nblk += 1
                    for qt in range(nqt):
                        base = 512 * (4 + qt)
                        ak = [("acc", qt)]
                        mk.op("dve", lambda e, base=base: e.reciprocal(out=R2[:, 0:2], in_=PS[:, base + 128:base + 258:129]),
                              ak, [("R2", 0)])
                        tt("dve", R2[:, 2:3], R2[:, 1:2], NLAM[:], ALU.mult, [("R2", 0), "NLAM"], [("R2", 1)])
                        ts("dve", AQ[:], PS[:, base:base + 128], R2[:, 0:1], None, ALU.mult, None, ak + [("R2", 0)], ["AQ"])
                        stt("dve", OO[:], PS[:, base + 129:base + 257], R2[:, 2:3], AQ[:], ALU.mult, ALU.add,
                            ak + [("R2", 1), "AQ"], ["OO"])
                        act(JK[:], OO[:], AF.Square, ["OO"], ["JK", ("R2", 2)], accum=R2[:, 3:4])
                        act(R2[:, 3:4], R2[:, 3:4], AF.Ln, [("R2", 2), "eps"], [("R2", 3)], bias=epsT[:, 0:1], scale=1.0 / 128)
                        act(R2[:, 3:4], R2[:, 3:4], AF.Exp, [("R2", 3)], [("R2", 4)], scale=-0.5)
                        stt("dve", ob[:, qt, :], OO[:], R2[:, 3:4], GSUB[:], ALU.mult, ALU.mult,
                            ["OO", ("R2", 4), "GSUB"], [(okey, qt)])
                    ld(aos[b][q0:q0 + nq, 128 * h:128 * (h + 1)].rearrange("(q p) c -> p q c", p=128),
                       ob[:, 0:nqt, :], [(okey, q) for q in range(nqt)], [], "sa%d" % (nblk % 2))
        mk.release(m)

    def phase_nbr():
        m = mk.mark()
        QB2 = mk.sb([128, 4, TK], BF16, "QB2")
        KB2 = mk.sb([128, 4, TK], BF16, "KB2")
        VBe = mk.sb([128, 34, 8, 65], BF16, "VBe")
        VBo = mk.sb([128, 31, 8, 65], BF16, "VBo")
        BI = [mk.sb([128, 4, 8, 64], F32, "BI") for _ in range(2)]
        MK2 = mk.sb([128, 1, 64], F32, "MK2")
        SBF = [mk.sb([128, 512], F32, "SBF") for _ in range(2)]
        PT = [mk.sb([128, 6, 512], BF16, "PTn") for _ in range(2)]
        RI = mk.sb([64, 8, 1], F32, "RI")
        OB = [mk.sb([64, 512], F32, "OB") for _ in range(2)]
        memset("pool", VBe[:, :, :, 64:65], 1.0, [("VBe", "one")])
        memset("pool", VBo[:, :, :, 64:65], 1.0, [("VBo", "one")])
        ld(MK2[:, 0, :], mask_d, [], ["MK2"], "nm")

        def gen_bias(dst, key, delta):
            for j in range(4):
                for i2 in range(2):
                    rr = delta + 2 * j + i2 + 7
                    ld(dst[64 * i2:64 * (i2 + 1), j, :, :], rpbt_d[rr], [], [(key, j, i2)], "nb%d" % i2)
                tt("dve", dst[:, j, :, :], dst[:, j, :, :], MK2[:].to_broadcast([128, 8, 64]), ALU.add,
                   [(key, j, 0), (key, j, 1), "MK2"], [(key, j)])

        gen_bias(BI[0], "BI0", -4)
        nrow = 0
        for b in range(NBB):
            ld(QB2[:], qbt[b].rearrange("h p t -> p h t"), [], ["QB2"], "nq")
            ld(KB2[:], kbt[b].rearrange("h p t -> p h t"), [], ["KB2"], "nk")
            vb_v = vbs[b].rearrange("(t p) (h d) -> p t h d", p=128, h=8)
            for t4 in range(0, 34, 6):
                t5 = min(34, t4 + 6)
                mk.dma("sp", [(VBe[:, t_, :, 0:64], vb_v[:, t_]) for t_ in range(t4, t5)], [], [("VBe", t4)], "nv")
            vbo_v = vbs[b][64:64 + 31 * 128, :].rearrange("(t p) (h d) -> p t h d", p=128, h=8)
            for t4 in range(0, 31, 6):
                t5 = min(31, t4 + 6)
                mk.dma("sp", [(VBo[:, t_, :, 0:64], vbo_v[:, t_]) for t_ in range(t4, t5)], [], [("VBo", t4)], "nv")
            vek = [("VBe", t4) for t4 in range(0, 34, 6)] + [("VBe", "one")]
            vok = [("VBo", t4) for t4 in range(0, 31, 6)] + [("VBo", "one")]
            rows = [("lat", r) for r in range(64)] + [("ctx", g) for g in range(4)]
            if "nbr_rows" in CFG:
                rows = rows[:CFG["nbr_rows"]]
            nst = CFG.get("nbr_stage", 9)
            for (kind, r) in rows:
                par = nrow % 2
                nrow += 1
                if kind == "lat":
                    rs = min(max(r - 4, 0), 56)
                    delta = rs - r
                    if delta == -4:
                        bi, bik = BI[0], "BI0"
                    else:
                        bi, bik = BI[1], "BI1"
                        gen_bias(BI[1], "BI1", delta)
                    q0 = 64 * r
                    kts = []
                    for j in range(4):
                        ks = 64 * (rs + 2 * j)
                        if rs % 2 == 0:
                            kts.append((ks, VBe, (rs + 2 * j) // 2, vek, j))
                        else:
                            kts.append((ks, VBo, (rs + 2 * j - 1) // 2, vok, j))
                    kts.append((T, VBe, 32, vek, None))
                    kts.append((T + 128, VBe, 33, vek, None))
                else:
                    q0 = T + 64 * r
                    kts = [(T, VBe, 32, vek, None), (T + 128, VBe, 33, vek, None)]
                pt = PT[par]
                ptk = "PTn%d" % par
                for n, (ks, vt, vi, vk, j) in enumerate(kts):
                    pp = n % 2
                    for h in range(8):
                        hp, hq = h % 2, h // 2
                        mm(bank(2 * pp + hp, 64, 64 * hq), KB2[64 * hp:64 * (hp + 1), hq, ks:ks + 128],
                           QB2[64 * hp:64 * (hp + 1), hq, q0:q0 + 64], True, True, ["KB2", "QB2"], pk(2 * pp + hp))
                    s2 = PS[:, 1024 * pp:1024 * pp + 1024].rearrange("p (b c) -> p b c", b=2)[:, :, 0:256]
                    sk2 = pk(2 * pp) + pk(2 * pp + 1)
                    if j is not None:
                        sb_ = SBF[n % 2]
                        tt("dve", sb_[:].rearrange("p (b c) -> p b c", b=2), s2,
                           bi[:, j, :, :].rearrange("p (b h) c -> p b (h c)", b=2), ALU.add,
                           sk2 + [(bik, j)], ["SBF%d" % (n % 2)])
                        act(pt[:, n, :], sb_[:], AF.Exp, ["SBF%d" % (n % 2)], [(ptk, n)])
                    else:
                        act(pt[:, n, :].rearrange("p (b c) -> p b c", b=2), s2, AF.Exp, sk2, [(ptk, n)])
                if nst < 2:
                    continue
                ab = 4 + 2 * par
                for h in range(8):
                    b_ = ab + h // 4
                    for n, (ks, vt, vi, vk, j) in enumerate(kts):
                        mm(PS[0:64, 512 * b_ + 65 * (h % 4):512 * b_ + 65 * (h % 4) + 65],
                           pt[:, n, 64 * ((h % 2) * 4 + h // 2):64 * ((h % 2) * 4 + h // 2 + 1)], vt[:, vi, h, :],
                           n == 0, n == len(kts) - 1,
                           [(ptk, n)] + vk, pk(b_))
                if nst < 3:
                    continue
                ob = OB[par]
                obk = "OB%d" % par
                for g in range(2):
                    b_ = ab + g
                    accv = PS[0:64, 512 * b_:512 * b_ + 260].rearrange("p (h c) -> p h c", h=4)
                    mk.op("dve", lambda e, accv=accv, g=g: e.reciprocal(out=RI[:, 4 * g:4 * g + 4, :], in_=accv[:, :, 64:65]),
                          pk(b_), [("RI", g)])
                    tt("dve", ob[:, 256 * g:256 * (g + 1)].rearrange("p (h d) -> p h d", h=4), accv[:, :, 0:64],
                       RI[:, 4 * g:4 * g + 4, :].to_broadcast([64, 4, 64]), ALU.mult, pk(b_) + [("RI", g)], [(obk, g)])
                ld(aos[b][q0:q0 + 64, 512:1024], ob[:], [(obk, 0), (obk, 1)], [], "no%d" % par)
        mk.release(m)

    def phase_oproj(l):
        m = mk.mark()
        WO = mk.sb([128, 8, D], BF16, "WO")
        wsrc = awout_d[0] if l == 0 else rwout_d[0]
        ldc([(WO[:], wsrc.rearrange("(k p) n -> p k n", p=128))], [], ["WO"], "w0")
        GB = mk.sb([128, D], F32, "GB")
        Gt = mk.sb([128, D], F32, "Gt")
        Bt = mk.sb([128, D], F32, "Bt")
        E = epi_alloc()
        site = 1 if l == 0 else 3
        seqs = SEQ if l == 0 else [s for s in SEQ if s[0] == "lat"]
        n = 0
        if l == 0:
            AO = [mk.sb([128, D], F32, "AO") for _ in range(2)]
            AT = [mk.sb([128, 8, 128], BF16, "AT") for _ in range(2)]
        else:
            MB = [mk.sb([128, 8, 512], BF16, "MB") for _ in range(2)]
        for s in seqs:
            r = seq_var(s)
            load_bc(GB, Gt, Bt, l, 2 * D, r, "ln1_g", "ln1_b")
            b = s[1]
            L = seq_len(s)
            base = 0 if s[0] == "lat" else T
            xsrc = seq_src(s) if l == 0 else xs[s]
            hdst_all = hts[(s, 0)].rearrange("k p t -> p k t")
            for i in range(L // 128):
                sl = n % 2
                pa = 2 * (n % 2)
                if l == 0:
                    ld(AO[sl][:], aos[b][base + 128 * i:base + 128 * (i + 1), :], [], ["AO%d" % sl], "ol%d" % sl)
                    for k in range(8):
                        tr(bank(4 + k // 4, 128, 128 * (k % 4)), AO[sl][:, 128 * k:128 * (k + 1)], ident[:],
                           ["AO%d" % sl, "ident"], [("p", 4 + k // 4)])
                    cp("act", AT[sl][:, 0:4, :].rearrange("p k t -> p (k t)"), bank(4), [("p", 4)],
                       [("AT%d" % sl, 0)])
                    cp("dve", AT[sl][:, 4:8, :].rearrange("p k t -> p (k t)"), bank(5), [("p", 5)],
                       [("AT%d" % sl, 1)])
                    lk = [("AT%d" % sl, 0), ("AT%d" % sl, 1)]
                    lhs = lambda k: AT[sl][:, k, :]
                else:
                    if i % 4 == 0:
                        ms = (n // 4) % 2
                        ld(MB[ms][:], mts[b].rearrange("k p t -> p k t")[:, :, 128 * i:128 * i + 512], [],
                           ["MB%d" % ms], "ol%d" % ms)
                    ms = (n // 4) % 2
                    lk = ["MB%d" % ms]
                    lhs = lambda k: MB[ms][:, k, 128 * (i % 4):128 * (i % 4 + 1)]
                for k in range(8):
                    for hf in range(2):
                        mm(bank(pa + hf), lhs(k), WO[:, k, 512 * hf:512 * (hf + 1)], k == 0, k == 7,
                           lk + ["WO"], pk((pa + hf)))
                epilogue(PS[:, 512 * pa:512 * pa + 1024], pk(pa) + pk(pa + 1),
                         xsrc[128 * i:128 * (i + 1), :], GB, Gt, Bt, ["GB", "Gt", "Bt"], site, r,
                         xs[s][128 * i:128 * (i + 1), :], hdst_all[:, :, 1 + 128 * i:1 + 128 * (i + 1)], E, n, 6)
                n += 1
        mk.release(m)

    def phase_ffn(l):
        m = mk.mark()
        WU = mk.sb([128, 8, 2 * DFF], BF16, "WU")
        WD = mk.sb([128, 22, D], BF16, "WD")
        wv = wup_d[l].rearrange("(k p) n -> p k n", p=128)
        for k in range(8):
            ldc([(WU[:, k, :], wv[:, k, :])], [], [("WU", k)], "w%d" % (k % 4))
        wdv = wdn_d[l].rearrange("(k p) n -> p k n", p=128)
        for k0 in range(0, 22, 6):
            k1 = min(22, k0 + 6)
            ldc([(WD[:, k0:k1, :], wdv[:, k0:k1, :])], [], [("WD", k0)], "w%d" % ((k0 // 6) % 4))
        wuk = [("WU", k) for k in range(8)]
        wdk = [("WD", k0) for k0 in range(0, 22, 6)]
        GB = mk.sb([128, D], F32, "GB")
        Gt = mk.sb([128, D], F32, "Gt")
        Bt = mk.sb([128, D], F32, "Bt")
        E = epi_alloc(1)
        HB = mk.sb([128, 8, 514], BF16, "HB")
        U = [mk.sb([128, 514], F32, "U") for _ in range(2)]
        C = [mk.sb([128, 512], F32, "C") for _ in range(4)]
        AC = mk.sb([128, 22, 512], BF16, "AC")
        last = l == 1
        site = 2 if l == 0 else None
        seqs = SEQ if l == 0 else [s for s in SEQ if s[0] == "lat"]
        fw = [VMAP[("fcw", l, k)] for k in range(3)]
        fb = VMAP[("fcb", l)]
        n = 0
        uc = 0
        for s in seqs:
            r = seq_var(s)
            load_bc(GB, Gt, Bt, l, 5 * D, r, "ln2_g", "ln2_b")
            L = seq_len(s)
            hsrc = hts[(s, 0)].rearrange("k p t -> p k t")
            hdst_all = hts[(s, 1)].rearrange("k p t -> p k t")
            nb = min(512, L)
            for t0 in range(0, L, nb):
                ld(HB[:, :, 0:nb + 2], hsrc[:, :, t0:t0 + nb + 2], [], ["HB"], "fh")
                for i in range(22):
                    for gv in range(2):
                        c = i + 22 * gv
                        u = uc % 4
                        uc += 1
                        mb = u % 2
                        hb_ = 2 if u % 2 == 0 else 7
                        hoff = 512 * hb_
                        for k in range(8):
                            mm(bank(mb, nb), WU[:, k, 128 * c:128 * (c + 1)], HB[:, k, 1:nb + 1], k == 0, k == 7,
                               wuk + ["HB"], pk(mb))
                            mm(PS[:, hoff:hoff + 2], WU[:, k, 128 * c:128 * (c + 1)], HB[:, k, 0:nb + 2:nb + 1],
                               k == 0, k == 7, wuk + ["HB"], pk(hb_))
                        Ut, Ct = U[u % 2], C[u]
                        cp("act", Ut[:, 1:nb + 1], bank(mb, nb), pk(mb), [("U%d" % (u % 2), 0)])
                        cp("act", Ut[:, 0:nb + 2:nb + 1], PS[:, hoff:hoff + 2], pk(hb_), [("U%d" % (u % 2), 1)])
                        uk = [("U%d" % (u % 2), 0), ("U%d" % (u % 2), 1)]
                        ce = "dve"
                        ts("pool", Ct[:, 0:nb], Ut[:, 1:nb + 1], VEC[:, fw[1] + c:fw[1] + c + 1], VEC[:, fb + c:fb + c + 1],
                           ALU.mult, ALU.add, uk + ["VEC"], ["C%d" % u])
                        stt(ce, Ct[:, 0:nb], Ut[:, 0:nb], VEC[:, fw[0] + c:fw[0] + c + 1], Ct[:, 0:nb], ALU.mult, ALU.add,
                            uk + ["VEC", "C%d" % u], ["C%d" % u])
                        stt(ce, Ct[:, 0:nb], Ut[:, 2:nb + 2], VEC[:, fw[2] + c:fw[2] + c + 1], Ct[:, 0:nb], ALU.mult, ALU.add,
                            uk + ["VEC", "C%d" % u], ["C%d" % u])
                        if gv == 0:
                            act(Ct[:, 0:nb], Ct[:, 0:nb], AF.Gelu_apprx_tanh, ["C%d" % u], ["C%d" % u])
                            cg, cgk = Ct, "C%d" % u
                        else:
                            tt("pool", AC[:, i, 0:nb], cg[:, 0:nb], Ct[:, 0:nb], ALU.mult, [cgk, "C%d" % u], [("AC", i)])
                ack = [("AC", i) for i in range(22)]
                for tq in range(nb // 128):
                    pa = 3
                    for k in range(22):
                        for hf in range(2):
                            mm(bank(pa + hf), AC[:, k, 128 * tq:128 * (tq + 1)], WD[:, k, 512 * hf:512 * (hf + 1)],
                               k == 0, k == 21, ack + wdk, pk((pa + hf)))
                    tok = t0 + 128 * tq
                    if last:
                        xdst = out_d[s[1], tok:tok + 128, :]
                        hdst = None
                    else:
                        xdst = xs[s][tok:tok + 128, :]
                        hdst = hdst_all[:, :, 1 + tok:1 + tok + 128]
                    epilogue(PS[:, 512 * pa:512 * pa + 1024], pk(pa) + pk(pa + 1),
                             xs[s][tok:tok + 128, :], GB, Gt, Bt, ["GB", "Gt", "Bt"], site, r, xdst, hdst, E, n, 5)
                    n += 1
        mk.release(m)

    def phase_rglru():
        m = mk.mark()
        CX0, LX0, NX = 2, 261, 4358
        XR = mk.sb([128, NX], F32, "XR")
        XC = mk.sb([128, NX], F32, "XC")
        XCb = mk.sb([128, NX], BF16, "XCb")
        A_ = mk.sb([128, NX], F32, "A")
        BT_ = mk.sb([128, NX], F32, "BT")
        TM = mk.sb([128, NX], F32, "TM")
        RF = mk.sb([128, NX], F32, "RF")
        RR = mk.sb([128, NX], F32, "RR")
        GY = mk.sb([128, T], BF16, "GY")
        MT = mk.sb([128, T], BF16, "MT")
        WI = mk.sb([128, 8, 2, 128], BF16, "WI")
        GW = mk.sb([128, 4, 128], BF16, "GW")
        HB = [mk.sb([128, 8, 512], BF16, "HBr") for _ in range(2)]
        memset("dve", XR[:], 0.0, ["XRz"])
        wv = rwin_d[0].rearrange("(k p) n -> p k n", p=128)
        nh = 0
        regions = [(CX0, TC), (LX0, T)]
        for b in range(NBB):
            hs = {"ctx": hts[(("ctx", b), 1)].rearrange("k p t -> p k t"),
                  "lat": hts[(("lat", b), 1)].rearrange("k p t -> p k t")}
            blocks = [("ctx", 0, 256, CX0)] + [("lat", 512 * j, 512, LX0 + 512 * j) for j in range(8)]
            for c in range(8):
                ldc([(WI[:, :, 0, :], wv[:, :, 128 * c:128 * (c + 1)]),
                     (WI[:, :, 1, :], wv[:, :, D + 128 * c:D + 128 * (c + 1)])], [], ["WI"], "w0")
                ldc([(GW[:, 0, :], rga_d[0, 0, c]), (GW[:, 1, :], rgx_d[0, 0, c]),
                     (GW[:, 2, :], rga_d[0, 1, c]), (GW[:, 3, :], rgx_d[0, 1, c])], [], ["GW"], "w1")
                for (kind, t0, nt, xo) in blocks:
                    sl = nh % 2
                    nh += 1
                    ld(HB[sl][:, :, 0:nt], hs[kind][:, :, 1 + t0:1 + t0 + nt], [], ["HBr%d" % sl], "rh%d" % sl)
                    bk = 2 * sl
                    for k in range(8):
                        mm(bank(bk, nt), WI[:, k, 1, :], HB[sl][:, k, 0:nt], k == 0, k == 7, ["WI", "HBr%d" % sl], pk(bk))
                    cp("act", XR[:, xo:xo + nt], bank(bk, nt), pk(bk) + ["XRz"], [("XR", xo)])
                    if kind == "lat":
                        for k in range(8):
                            mm(bank(bk + 1, nt), WI[:, k, 0, :], HB[sl][:, k, 0:nt], k == 0, k == 7,
                               ["WI", "HBr%d" % sl], pk((bk + 1)))
                        act(GY[:, t0:t0 + nt], bank(bk + 1, nt), AF.Gelu_apprx_tanh, pk((bk + 1)), [("GY", t0)])
                xrk = [("XR", xo) for (_, _, _, xo) in blocks] + ["XRz"]
                w = [VMAP[("rcw", k)] + c for k in range(4)]
                cb = VMAP[("rcb",)] + c
                for (o, n_) in regions:
                    ce = "dve"
                    ts("pool", XC[:, o:o + n_], XR[:, o:o + n_], VEC[:, w[2]:w[2] + 1], VEC[:, cb:cb + 1], ALU.mult, ALU.add,
                       xrk + ["VEC"], [("XC", o)])
                    for (kk, sh) in ((0, -2), (1, -1), (3, 1)):
                        stt(ce, XC[:, o:o + n_], XR[:, o + sh:o + sh + n_], VEC[:, w[kk]:w[kk] + 1], XC[:, o:o + n_],
                            ALU.mult, ALU.add, xrk + ["VEC", ("XC", o)], [("XC", o)])
                    cp("pool" if n_ == T else "act", XCb[:, o:o + n_], XC[:, o:o + n_], [("XC", o)], [("XCb", o)])
                xck = [("XC", o) for (o, _) in regions]
                xcbk = [("XCb", o) for (o, _) in regions]
                for d in range(2):
                    res = RF if d == 0 else RR
                    resk = "RF" if d == 0 else "RR"
                    ba = VMAP[("rba", d)] + c
                    bx = VMAP[("rbx", d)] + c
                    gi = 0
                    for (kind, t0, nt, xo) in blocks:
                        bk = 4 + 2 * (gi % 2)
                        gi += 1
                        mm(bank(bk, nt), GW[:, 2 * d, :], XCb[:, xo:xo + nt], True, True, ["GW"] + xcbk, pk(bk))
                        mm(bank(bk + 1, nt), GW[:, 2 * d + 1, :], XCb[:, xo:xo + nt], True, True, ["GW"] + xcbk,
                           pk((bk + 1)))
                        ro = CX0 if kind == "ctx" else LX0
                        act(A_[:, xo:xo + nt], bank(bk, nt), AF.Sigmoid, pk(bk) + ["VEC"], [("A", xo), ("A2", ro)],
                            bias=VEC[:, ba:ba + 1])
                        act(BT_[:, xo:xo + nt], bank(bk + 1, nt), AF.Sigmoid, pk(bk + 1) + ["VEC"], [("BT", xo), ("BT2", ro)],
                            bias=VEC[:, bx:bx + 1])
                    ak = [("A", xo) for (_, _, _, xo) in blocks]
                    btk = [("BT", xo) for (_, _, _, xo) in blocks]
                    for (o, n_) in regions:
                        sl_ = slice(o, o + n_)
                        kA, kB, kT = ("A2", o), ("BT2", o), ("TM", o)
                        act(A_[:, sl_], A_[:, sl_], AF.Exp, ak + ["CST"], [kA], scale=CST[:, 8 * d + c:8 * d + c + 1])
                        tt("pool", TM[:, sl_], A_[:, sl_], A_[:, sl_], ALU.mult, [kA], [kT])
                        ts("dve", TM[:, sl_], TM[:, sl_], -1.0, 1.0, ALU.mult, ALU.add, [kT], [kT])
                        act(TM[:, sl_], TM[:, sl_], AF.Sqrt, [kT], [kT])
                        tt("pool", BT_[:, sl_], BT_[:, sl_], XC[:, sl_], ALU.mult, btk + xck, [kB])
                        tt("pool", BT_[:, sl_], BT_[:, sl_], TM[:, sl_], ALU.mult, [kB, kT], [kB])
                    (oc, ncx), (ol, nl) = regions
                    if d == 0:
                        mk.op("dve", lambda e: e.tensor_tensor_scan(out=RF[:, oc:oc + ncx], data0=A_[:, oc:oc + ncx],
                                                                    data1=BT_[:, oc:oc + ncx], initial=0.0,
                                                                    op0=ALU.mult, op1=ALU.add),
                              [("A2", oc), ("BT2", oc)], [("RFc",)])
                        mk.op("dve", lambda e: e.tensor_tensor_scan(out=RF[:, ol:ol + nl], data0=A_[:, ol:ol + nl],
                                                                    data1=BT_[:, ol:ol + nl],
                                                                    initial=RF[:, oc + ncx - 1:oc + ncx],
                                                                    op0=ALU.mult, op1=ALU.add),
                              [("A2", ol), ("BT2", ol), ("RFc",)], [("RFl",)])
                    else:
                        mk.op("dve", lambda e: e.tensor_tensor_scan(out=RR[:, oc:oc + ncx][:, ::-1],
                                                                    data0=A_[:, oc:oc + ncx][:, ::-1],
                                                                    data1=BT_[:, oc:oc + ncx][:, ::-1], initial=0.0,
                                                                    op0=ALU.mult, op1=ALU.add),
                              [("A2", oc), ("BT2", oc)], [("RRc",)])
                        mk.op("dve", lambda e: e.tensor_tensor_scan(out=RR[:, ol:ol + nl][:, ::-1],
                                                                    data0=A_[:, ol:ol + nl][:, ::-1],
                                                                    data1=BT_[:, ol:ol + nl][:, ::-1],
                                                                    initial=RR[:, oc:oc + 1],
                                                                    op0=ALU.mult, op1=ALU.add),
                              [("A2", ol), ("BT2", ol), ("RRc",)], [("RRl",)])
                tt("pool", RF[:, LX0:LX0 + T], RF[:, LX0:LX0 + T], RR[:, LX0:LX0 + T], ALU.add, [("RFl",), ("RRl",)], [("RFl",)])
                tt("pool", MT[:], RF[:, LX0:LX0 + T], GY[:], ALU.mult, [("RFl",)] + [("GY", 512 * j) for j in range(8)], ["MT"])
                ld(mts[b][c], MT[:], ["MT"], [], "rm")
        mk.release(m)

    order = ["qkv", "diff", "nbr", "op0", "ffn0", "rg", "op1", "ffn1"]
    fns = {"qkv": phase_qkv, "diff": phase_diff, "nbr": phase_nbr, "op0": lambda: phase_oproj(0),
           "ffn0": lambda: phase_ffn(0), "rg": phase_rglru, "op1": lambda: phase_oproj(1), "ffn1": lambda: phase_ffn(1)}
    for ph in order:
        if stop == "p0":
            break
        fns[ph]()
        if stop == ph:
            break
    mk.barrier()
    named = {"modd": modd, "ao0": aos[0], "xs_lat0": xs[("lat", 0)], "xs_ctx0": xs[("ctx", 0)],
             "qat0": qat[0], "kat0": kat[0], "va0": vas[0], "qbt0": qbt[0], "kbt0": kbt[0], "vb0": vbs[0],
             "hts0_lat0": hts[(("lat", 0), 0)], "hts1_lat0": hts[(("lat", 0), 1)], "mt0": mts[0]}
    for nm in dbg:
        src = named[nm]
        dst = nc.dram_tensor("dbg_" + nm, list(src.shape), src.dtype, kind="ExternalOutput").ap()
        mk.dma("sp", [(dst, src)], [], [], "dbg")
    mk.emit()
    return nc, mk


def prep_inputs(inputs, core):
    g = lambda k: np.asarray(inputs[k])
    b0 = 2 * core
    m = {}
    m["x"] = np.ascontiguousarray(g("x")[b0:b0 + 2])
    m["ctx"] = np.ascontiguousarray(g("ctx")[b0:b0 + 2])
    cc = np.stack([g("c")[b0], g("c")[b0 + 1], g("c_ctx")], 0).astype(np.float32)
    m["cct"] = np.ascontiguousarray(cc.reshape(3, 8, 128).transpose(2, 1, 0))
    vec = np.zeros((128, NV), np.float32)
    for l in range(2):
        for nm in ("ln1_g", "ln1_b", "ln2_g", "ln2_b"):
            vec[:, VMAP[(nm, l)]:VMAP[(nm, l)] + 8] = _pl(g(nm)[l])
        for k in range(3):
            vec[:, VMAP[("fcw", l, k)]:VMAP[("fcw", l, k)] + 44] = _pl(g("ffn_conv_w")[l, k])
        vec[:, VMAP[("fcb", l)]:VMAP[("fcb", l)] + 44] = _pl(g("ffn_conv_b")[l])
    for k in range(4):
        vec[:, VMAP[("rcw", k)]:VMAP[("rcw", k)] + 8] = _pl(g("rnn_conv_w")[0, k])
    vec[:, VMAP[("rcb",)]:VMAP[("rcb",)] + 8] = _pl(g("rnn_conv_b")[0])
    for d in range(2):
        vec[:, VMAP[("apar", d)]:VMAP[("apar", d)] + 8] = _pl(g("rg_a_param")[0, d])
        vec[:, VMAP[("rba", d)]:VMAP[("rba", d)] + 8] = _pl(g("rg_ba")[0, d].reshape(-1))
        vec[:, VMAP[("rbx", d)]:VMAP[("rbx", d)] + 8] = _pl(g("rg_bx")[0, d].reshape(-1))
    m["vec"] = vec
    m["rope"] = _ROPE
    rpb = g("na_rpb")[0].astype(np.float32)
    kc = np.arange(64)[:, None]
    cq = np.arange(64)[None, :]
    rel = kc - cq + 15
    ok = (rel >= 0) & (rel <= 30)
    gath = rpb[:, :, np.clip(rel, 0, 30)] * ok[None, None]
    perm = [0, 2, 4, 6, 1, 3, 5, 7]
    m["rpbt"] = np.ascontiguousarray(gath[perm].transpose(1, 2, 0, 3)).astype(np.float32)
    m["mask"] = _MASK
    m["lamv"] = np.stack([g("diff_lq1")[0], g("diff_lk1")[0], g("diff_lq2")[0], g("diff_lk2")[0]], 0).astype(np.float32)
    m["subg"] = np.ascontiguousarray(g("diff_subln_g")[0]).astype(np.float32)
    for k in ("ada_w", "ada_b", "ln1_g", "ln1_b", "ln2_g", "ln2_b", "ffn_w_up", "ffn_w_down", "att_w_in",
              "att_w_out", "rnn_w_in", "rg_wa", "rg_wx", "rnn_w_out"):
        m[k] = np.ascontiguousarray(g(k)).astype(np.float32)
    return m


def _const_tables():
    t = np.arange(T)
    row = (t // 64).astype(np.float64)[:, None]
    col = (t % 64).astype(np.float64)[:, None]
    inv = 1.0 / (10000.0 ** (np.arange(16, dtype=np.float64) / 16))
    inv = inv.astype(np.float32).astype(np.float64)
    ang = np.concatenate([row * inv, row * inv, col * inv, col * inv], -1).astype(np.float32)
    cos = np.cos(ang).astype(np.float32)
    sin = np.sin(ang).astype(np.float32)
    sgn = np.tile(np.concatenate([-np.ones(16), np.ones(16)]), 2).astype(np.float32)
    rope = np.stack([cos, sin * sgn[None]], 1).astype(np.float32)
    c = np.arange(64)
    cs = np.clip(c - 8, 0, 48)
    kc = np.arange(64)[:, None]
    inside = (kc >= cs[None]) & (kc < cs[None] + 16)
    mask = np.where(inside, 0.0, NEG).astype(np.float32)
    return np.ascontiguousarray(rope), np.ascontiguousarray(np.concatenate([mask, mask], 0))


_ROPE, _MASK = _const_tables()
_CACHE = {}
CFG = {}


def kernel(**inputs):
    if "nc" not in _CACHE:
        _CACHE["nc"] = build()[0]
    nc = _CACHE["nc"]
    in_maps = [prep_inputs(inputs, c) for c in range(8)]
    res = run_bass_kernel_spmd(nc, in_maps, core_ids=list(range(8)))
    out = np.concatenate([r["out"] for r in res.results], axis=0)
    return out.astype(np.float32)
```

```python
import math
import contextlib
import numpy as np
import concourse.bass as bass
import concourse.mybir as mybir
from concourse.bass_utils import run_bass_kernel_spmd

F32 = mybir.dt.float32
BF16 = mybir.dt.bfloat16
AF = mybir.ActivationFunctionType
ALU = mybir.AluOpType

ENGS = ["pe", "act", "dve", "pool", "sp"]

D = 1024
T = 4096
TC = 256
TK = T + TC
DFF = 2816
ALPHA = 4.0 ** 0.25
EPS = 1e-5
LAMBDA_INIT0 = 0.8 - 0.6 * math.exp(0.0)
NEG = -30000.0


class MK:
    def __init__(self, nc):
        self.nc = nc
        self.ops = {e: [] for e in ENGS}
        self.cnt = {e: 0 for e in ENGS}
        self.dcnt = {}
        self.seen = {e: {} for e in ENGS}
        self.lastw = {}
        self.readers = {}
        self.sb_off = 16640
        self.sb_names = 0
        self.sb_max = 0

    def sb(self, shape, dtype, name=None):
        nbytes = int(np.prod(shape[1:])) * (4 if dtype == F32 else 2)
        nbytes = (nbytes + 63) // 64 * 64
        self.sb_names += 1
        nm = "%s_%d" % (name or "t", self.sb_names)
        t = self.nc.alloc_sbuf_tensor_at(nm, list(shape), dtype, offset=self.sb_off)
        self.sb_off += nbytes
        self.sb_max = max(self.sb_max, self.sb_off)
        assert self.sb_off <= 229376, ("sbuf overflow", nm, self.sb_off)
        return t

    def mark(self):
        return self.sb_off

    def release(self, m):
        self.barrier()
        self.sb_off = m

    def _deps(self, eng, reads, writes, is_dma):
        deps = {}

        def add(tok, raw):
            if tok is None:
                return
            sk, val, teng = tok
            if not is_dma and teng == eng and sk[0] == "e":
                if eng == "pe":
                    return
            if deps.get(sk, 0) < val:
                deps[sk] = val

        for k in reads:
            add(self.lastw.get(k), True)
        for k in writes:
            add(self.lastw.get(k), False)
            for sk, (val, teng) in self.readers.get(k, {}).items():
                add((sk, val, teng), False)
        waits = []
        seen = self.seen[eng]
        for sk, val in deps.items():
            if seen.get(sk, 0) >= val:
                continue
            seen[sk] = val
            waits.append((sk, val))
        return waits

    def _commit(self, tok, reads, writes):
        sk, val, teng = tok
        for k in writes:
            self.lastw[k] = tok
            self.readers[k] = {}
        for k in reads:
            self.readers.setdefault(k, {})[sk] = (val, teng)

    def op(self, eng, fn, reads=(), writes=()):
        if eng != "pe":
            pr = [k for k in reads if isinstance(k, tuple) and k and k[0] == "p"]
            if pr:
                writes = list(writes) + [k for k in pr if k not in writes]
        waits = self._deps(eng, reads, writes, False)
        self.cnt[eng] += 1
        tok = (("e", eng), self.cnt[eng], eng)
        self.ops[eng].append((waits, fn, tok, 1))
        self._commit(tok, reads, writes)
        return tok

    def dma(self, eng, pairs, reads, writes, slot, slow=False):
        sk = ("d", slot)
        prev = self.dcnt.get(slot, 0)
        waits = self._deps(eng, reads, writes, True)
        if prev and self.seen[eng].get(sk, 0) < prev:
            self.seen[eng][sk] = prev
            waits.append((sk, prev))
        n = len(pairs)
        self.dcnt[slot] = prev + 16 * n
        tok = (sk, prev + 16 * n, None)

        def fn(e, pairs=pairs, slow=slow):
            if slow:
                return [e.dma_start(out=o, in_=i, allow_slow_non_contiguous=True) for (o, i) in pairs]
            return [e.dma_start(out=o, in_=i) for (o, i) in pairs]

        self.ops[eng].append((waits, fn, tok, 16))
        self._commit(tok, reads, writes)
        return tok

    def barrier(self):
        final = {}
        for e in ENGS:
            if self.cnt[e]:
                final[("e", e)] = self.cnt[e]
        for s, v in self.dcnt.items():
            final[("d", s)] = v
        for e in ENGS:
            waits = []
            for sk, val in final.items():
                if sk == ("e", e):
                    continue
                if self.seen[e].get(sk, 0) < val:
                    self.seen[e][sk] = val
                    waits.append((sk, val))
            if waits:
                self.ops[e].append((waits, None, None, 0))
        self.lastw = {}
        self.readers = {}

    def emit(self):
        nc = self.nc
        self.barrier()
        waited = {e: set() for e in ENGS}
        for e in ENGS:
            for waits, fn, tok, inc in self.ops[e]:
                for sk, val in waits:
                    if sk[0] == "e":
                        waited[sk[1]].add(val)
        remap = {}
        for e in ENGS:
            vals = sorted(waited[e])
            remap[e] = {v: i + 1 for i, v in enumerate(vals)}
        sems = {}
        with contextlib.ExitStack() as st:
            for e in ENGS:
                sems[("e", e)] = st.enter_context(nc.semaphore("s_" + e))
            for s in self.dcnt:
                sems[("d", s)] = st.enter_context(nc.semaphore("d_" + str(s)))
            block = st.enter_context(nc.Block())

            def run(engname):
                def body(eng):
                    for waits, fn, tok, inc in self.ops[engname]:
                        for sk, val in waits:
                            v = remap[sk[1]][val] if sk[0] == "e" else val
                            eng.wait_ge(sems[sk], v)
                        if fn is None:
                            continue
                        r = fn(eng)
                        if inc == 16:
                            for ins in r:
                                ins.then_inc(sems[tok[0]], 16)
                        elif tok[1] in remap[engname]:
                            r.then_inc(sems[tok[0]], 1)

                return body

            block.tensor(run("pe"))
            block.scalar(run("act"))
            block.vector(run("dve"))
            block.gpsimd(run("pool"))
            block.sync(run("sp"))


def _vec_map():
    m = {}
    o = 0
    for l in range(2):
        for nm in ("ln1_g", "ln1_b", "ln2_g", "ln2_b"):
            m[(nm, l)] = o
            o += 8
    for l in range(2):
        for k in range(3):
            m[("fcw", l, k)] = o
            o += 44
    for l in range(2):
        m[("fcb", l)] = o
        o += 44
    for k in range(4):
        m[("rcw", k)] = o
        o += 8
    m[("rcb",)] = o
    o += 8
    for d in range(2):
        m[("apar", d)] = o
        o += 8
    for d in range(2):
        m[("rba", d)] = o
        o += 8
    for d in range(2):
        m[("rbx", d)] = o
        o += 8
    return m, o


VMAP, NV = _vec_map()


def _pl(v):
    v = np.asarray(v, np.float32).reshape(-1, 128)
    return np.ascontiguousarray(v.T)


def build(stop=None, dbg=()):
    nc = bass.Bass("TRN2", target_bir_lowering=False)
    mk = MK(nc)

    def din(name, shape, dt=F32):
        return nc.dram_tensor(name, list(shape), dt, kind="ExternalInput").ap()

    def dscr(name, shape, dt=F32):
        return nc.dram_tensor(name, list(shape), dt, kind="Internal").ap()

    x_d = din("x", [2, T, D])
    ctx_d = din("ctx", [2, TC, D])
    cct_d = din("cct", [128, 8, 3])
    vec_d = din("vec", [128, NV])
    rope_d = din("rope", [T, 2, 64])
    rpbt_d = din("rpbt", [15, 64, 8, 64])
    mask_d = din("mask", [128, 64])
    lam_d = din("lamv", [4, 64])
    subg_d = din("subg", [128])
    ada_w_d = din("ada_w", [2, D, 6 * D])
    ada_b_d = din("ada_b", [2, 6 * D])
    ln_d = {nm: din(nm, [2, D]) for nm in ("ln1_g", "ln1_b", "ln2_g", "ln2_b")}
    wup_d = din("ffn_w_up", [2, D, 2 * DFF])
    wdn_d = din("ffn_w_down", [2, DFF, D])
    awin_d = din("att_w_in", [1, D, 3 * D])
    awout_d = din("att_w_out", [1, D, D])
    rwin_d = din("rnn_w_in", [1, D, 2 * D])
    rga_d = din("rg_wa", [1, 2, 8, 128, 128])
    rgx_d = din("rg_wx", [1, 2, 8, 128, 128])
    rwout_d = din("rnn_w_out", [1, D, D])
    out_d = nc.dram_tensor("out", [2, T, D], F32, kind="ExternalOutput").ap()

    modd = dscr("modd", [2, 3, 6 * D])
    NBB = CFG.get("nb", 2)
    SEQ = [("lat", 0), ("ctx", 0), ("lat", 1), ("ctx", 1)][:2 * NBB]
    SEQ_ALL = [("lat", 0), ("ctx", 0), ("lat", 1), ("ctx", 1)]
    xs = {s: dscr("xs_%s%d" % s, [T if s[0] == "lat" else TC, D]) for s in SEQ_ALL}
    hts = {(s, a): dscr("hts%d_%s%d" % ((a,) + s), [8, 128, (T if s[0] == "lat" else TC) + 2], BF16)
           for s in SEQ_ALL for a in (0, 1)}
    qat = [dscr("qat%d" % b, [4, 128, TK], BF16) for b in range(2)]
    kat = [dscr("kat%d" % b, [4, 128, TK], BF16) for b in range(2)]
    qbt = [dscr("qbt%d" % b, [4, 128, TK], BF16) for b in range(2)]
    kbt = [dscr("kbt%d" % b, [4, 128, TK], BF16) for b in range(2)]
    vas = [dscr("va%d" % b, [TK, 512], BF16) for b in range(2)]
    vbs = [dscr("vb%d" % b, [TK, 512], BF16) for b in range(2)]
    aos = [dscr("ao%d" % b, [TK, D]) for b in range(2)]
    mts = [dscr("mt%d" % b, [8, 128, T], BF16) for b in range(2)]

    def seq_src(s):
        return x_d[s[1]] if s[0] == "lat" else ctx_d[s[1]]

    def seq_len(s):
        return T if s[0] == "lat" else TC

    def seq_var(s):
        return s[1] if s[0] == "lat" else 2

    PS = nc.alloc_psum_tensor("PS", [128, 4096], F32)

    def bank(i, n=512, off=0):
        return PS[:, 512 * i + off:512 * i + off + n]

    def pk(i):
        return [("p", i)]

    def mm(out, lhsT, rhs, start, stop, r, w, skip=False):
        mk.op("pe", lambda e: e.matmul(out, lhsT=lhsT, rhs=rhs, start=start, stop=stop, skip_group_check=skip), r, w)

    def tr(out, in_, idn, r, w):
        mk.op("pe", lambda e: e.transpose(out, in_, idn), r, w)

    def act(out, in_, func, r, w, bias=None, scale=None, accum=None):
        kw = {}
        if bias is not None:
            kw["bias"] = bias
        if scale is not None:
            kw["scale"] = scale
        if accum is not None:
            kw["accum_out"] = accum
        mk.op("act", lambda e: e.activation(out=out, in_=in_, func=func, **kw), r, w)

    def ts(eng, out, in0, s1, s2, op0, op1, r, w):
        if s2 is None:
            if op0 == ALU.mult:
                mk.op(eng, lambda e: e.tensor_scalar_mul(out=out, in0=in0, scalar1=s1), r, w)
            else:
                assert op0 == ALU.add
                mk.op(eng, lambda e: e.tensor_scalar_add(out=out, in0=in0, scalar1=s1), r, w)
        else:
            mk.op(eng, lambda e: e.tensor_scalar(out=out, in0=in0, scalar1=s1, scalar2=s2, op0=op0, op1=op1), r, w)

    def stt(eng, out, in0, scalar, in1, op0, op1, r, w):
        mk.op(eng, lambda e: e.scalar_tensor_tensor(out=out, in0=in0, scalar=scalar, in1=in1, op0=op0, op1=op1), r, w)

    def tt(eng, out, in0, in1, op, r, w):
        mk.op(eng, lambda e: e.tensor_tensor(out=out, in0=in0, in1=in1, op=op), r, w)

    def cp(eng, out, in_, r, w):
        if eng == "act":
            mk.op("act", lambda e: e.copy(out=out, in_=in_), r, w)
        else:
            mk.op(eng, lambda e: e.tensor_copy(out=out, in_=in_), r, w)

    def memset(eng, ap, val, w):
        mk.op(eng, lambda e: e.memset(ap, val), [], w)

    def ld(out, in_, r, w, slot):
        mk.dma("sp", [(out, in_)], r, w, slot)

    def ldc(pairs, r, w, slot):
        mk.dma("pool", pairs, r, w, slot)

    ident = mk.sb([128, 128], F32, "ident")
    memset("pool", ident[:], 0.0, ["ident"])
    mk.op("pool", lambda e: e.affine_select(out=ident[:], in_=ident[:], compare_op=ALU.not_equal, fill=1.0,
                                             base=0, pattern=[[-1, 128]], channel_multiplier=1), ["ident"], ["ident"])
    epsT = mk.sb([128, 1], F32, "eps")
    memset("dve", epsT[:], EPS, ["eps"])
    VEC = mk.sb([128, NV], F32, "VEC")
    ld(VEC[:], vec_d, [], ["VEC"], "c0")
    SCAL = mk.sb([128, 4, 3, 2, 8], F32, "SCAL")
    CST = mk.sb([128, 16], F32, "CST")
    NLAM = mk.sb([128, 1], F32, "NLAM")
    GSUB = mk.sb([128, 128], F32, "GSUB")

    def vcol(key, n=8):
        o = VMAP[key]
        return VEC[:, o:o + n]

    m0 = mk.mark()
    MODP = [mk.sb([128, 48, 3], F32, "MODP%d" % l) for l in range(2)]
    ZT = mk.sb([128, 8, 2], BF16, "ZT")
    memset("dve", ZT[:], 0.0, ["ZT"])
    for s in SEQ:
        for a in (0, 1):
            L = seq_len(s)
            h = hts[(s, a)].rearrange("k p t -> p k t")
            mk.dma("sp", [(h[:, :, 0:1], ZT[:, :, 0:1])], ["ZT"], [], "z0", slow=True)
            mk.dma("sp", [(h[:, :, L + 1:L + 2], ZT[:, :, 1:2])], ["ZT"], [], "z0", slow=True)

    CC = mk.sb([128, 8, 3], F32, "CC")
    ST = mk.sb([128, 8, 3], F32, "ST")
    ld(CC[:], cct_d, [], ["CC"], "c1")
    act(ST[:], CC[:], AF.Silu, ["CC"], ["ST"])
    MODR = mk.sb([3, 6 * D], F32, "MODR")
    ADAB = mk.sb([3, 6 * D], F32, "ADAB")
    WB = [mk.sb([128, 8, 512], F32, "WB%d" % i) for i in range(2)]
    for l in range(2):
        ld(ADAB[:], ada_b_d[l].partition_broadcast(3), [], ["ADAB"], "c2")
        wv = ada_w_d[l].rearrange("(k p) n -> p k n", p=128)
        for jb in range(12):
            sl = jb % 2
            ld(WB[sl][:], wv[:, :, jb * 512:(jb + 1) * 512], [], ["WB%d" % sl], "wb%d" % sl)
            bk = jb % 2
            for k in range(8):
                mm(PS[0:3, 512 * bk:512 * bk + 512], ST[:, k, :], WB[sl][:, k, :], k == 0, k == 7,
                   ["ST", "WB%d" % sl], pk(bk))
            tt("dve", MODR[0:3, jb * 512:(jb + 1) * 512], PS[0:3, 512 * bk:512 * bk + 512],
               ADAB[0:3, jb * 512:(jb + 1) * 512], ALU.add, pk(bk) + ["ADAB"], [("MODR", jb)])
        allk = [("MODR", jb) for jb in range(12)]
        ld(modd[l], MODR[0:3, :], allk, [], "c3")
        for k in range(48):
            tr(PS[:, 3584 + 3 * k:3584 + 3 * k + 3], MODR[0:3, 128 * k:128 * (k + 1)], ident[0:3, 0:3],
               allk + ["ident"], pk(7))
        cp("dve", MODP[l][:].rearrange("p k r -> p (k r)"), PS[:, 3584:3584 + 144], pk(7), ["MODP%d" % l])
    TMP8 = mk.sb([128, 8], F32, "TMP8")
    sites = [(0, 8, 0, None, None), (0, 32, 24, ("ln1_g", 0), ("ln1_b", 0)),
             (1, 8, 0, ("ln2_g", 0), ("ln2_b", 0)), (1, 32, 24, ("ln1_g", 1), ("ln1_b", 1))]
    for si, (l, sco, sho, gk, bk_) in enumerate(sites):
        for r in range(3):
            A_ = SCAL[:, si, r, 0, :]
            B_ = SCAL[:, si, r, 1, :]
            sc = MODP[l][:, sco:sco + 8, r]
            sh = MODP[l][:, sho:sho + 8, r]
            ts("dve", TMP8[:], sc, 1.0, None, ALU.add, None, ["MODP%d" % l], ["TMP8"])
            if gk is None:
                cp("dve", A_, TMP8[:], ["TMP8"], ["SCAL"])
                cp("dve", B_, sh, ["MODP%d" % l], ["SCAL"])
            else:
                tt("dve", A_, TMP8[:], vcol(gk), ALU.mult, ["TMP8", "VEC"], ["SCAL"])
                tt("dve", B_, TMP8[:], vcol(bk_), ALU.mult, ["TMP8", "VEC"], ["SCAL"])
                tt("dve", B_, B_, sh, ALU.add, ["SCAL", "MODP%d" % l], ["SCAL"])
    LV = mk.sb([128, 4, 64], F32, "LV")
    ld(LV[:].rearrange("p a d -> p (a d)"), lam_d.rearrange("a d -> (a d)").partition_broadcast(128), [], ["LV"], "c4")
    L2 = mk.sb([128, 2, 64], F32, "L2")
    E2 = mk.sb([128, 2], F32, "E2")
    tt("dve", L2[:, 0, :], LV[:, 0, :], LV[:, 1, :], ALU.mult, ["LV"], ["L2"])
    tt("dve", L2[:, 1, :], LV[:, 2, :], LV[:, 3, :], ALU.mult, ["LV"], ["L2"])
    mk.op("dve", lambda e: e.reduce_sum(out=E2[:], in_=L2[:], axis=mybir.AxisListType.X), ["L2"], ["E2"])
    act(E2[:], E2[:], AF.Exp, ["E2"], ["E2"])
    tt("dve", NLAM[:], E2[:, 1:2], E2[:, 0:1], ALU.subtract, ["E2"], ["NLAM"])
    ts("dve", NLAM[:], NLAM[:], -LAMBDA_INIT0, None, ALU.add, None, ["NLAM"], ["NLAM"])
    ld(GSUB[:], subg_d.partition_broadcast(128), [], ["GSUB"], "c5")
    ts("dve", GSUB[:], GSUB[:], 1.0 - LAMBDA_INIT0, None, ALU.mult, None, ["GSUB"], ["GSUB"])
    act(CST[:], VEC[:, VMAP[("apar", 0)]:VMAP[("apar", 0)] + 16], AF.Exp, ["VEC"], ["CST"], scale=-1.0)
    ts("dve", CST[:], CST[:], 1.0, None, ALU.add, None, ["CST"], ["CST"])
    act(CST[:], CST[:], AF.Ln, ["CST"], ["CST"])
    ts("dve", CST[:], CST[:], -8.0, None, ALU.mult, None, ["CST"], ["CST"])
    mk.release(m0)

    def make_ht(src, srckeys, HT, htkey, site, r, pb):
        for k in range(8):
            b_ = pb + k // 4
            tr(bank(b_, 128, 128 * (k % 4)), src[:, 128 * k:128 * (k + 1)], ident[:], srckeys + ["ident"],
               [("p", b_)])
        for k in range(8):
            hm = CFG.get("ht_mode", 3)
            if hm == 0 or (hm == 1 and k >= 4) or (hm == 2 and k < 4):
                continue
            b_ = pb + k // 4
            A_ = SCAL[:, site, r, 0, k:k + 1]
            B_ = SCAL[:, site, r, 1, k:k + 1]
            if k < 4:
                act(HT[:, k, :], bank(b_, 128, 128 * (k % 4)), AF.Identity, [("p", b_), "SCAL"],
                    [(htkey, k)], bias=B_, scale=A_)
            else:
                ts("dve", HT[:, k, :], bank(b_, 128, 128 * (k % 4)), A_, B_, ALU.mult, ALU.add,
                   [("p", b_), "SCAL"], [(htkey, k)])

    def epilogue(ps2, pskeys, Xsrc, GB, Gt, Bt, bkeys, site, r, xdst, hdst, E, i, pb):
        sl = i % 2
        sy = i % len(E["Y"])
        Xt, Y, HT = E["Xt"][sl], E["Y"][sy], E["HT"][sy]
        kx, ky, kh = "Xt%d" % sl, "Y%d" % sy, "HTe%d" % sy
        ld(Xt[:], Xsrc, [], [kx], "ex%d" % sl)
        tt("dve", Y[:], ps2, GB[:], ALU.mult, pskeys + bkeys, [ky])
        stt("dve", Y[:], Xt[:], ALPHA, Y[:], ALU.mult, ALU.add, [kx, ky], [ky])
        BS, MV, RS = E["BS"][sl], E["MV"][sl], E["RS"][sl]
        kb = "BS%d" % sl
        mk.op("dve", lambda e: e.bn_stats(out=BS[:, 0, :], in_=Y[:, 0:512]), [ky], [(kb, 0)])
        mk.op("dve", lambda e: e.bn_stats(out=BS[:, 1, :], in_=Y[:, 512:1024]), [ky], [(kb, 1)])
        mk.op("dve", lambda e: e.bn_aggr(out=MV[:], in_=BS[:]), [(kb, 0), (kb, 1)], [(kb, 2)])
        act(RS[:, 0:1], MV[:, 1:2], AF.Ln, [(kb, 2), "eps"], [(kb, 3)], bias=epsT[:, 0:1])
        act(RS[:, 1:2], RS[:, 0:1], AF.Exp, [(kb, 3)], [(kb, 4)], scale=-0.5)
        stt("dve", RS[:, 2:3], MV[:, 0:1], -1.0, RS[:, 1:2], ALU.mult, ALU.mult, [(kb, 2), (kb, 4)], [(kb, 5)])
        act(Y[:], Y[:], AF.Identity, [ky, (kb, 4), (kb, 5)], [ky], bias=RS[:, 2:3], scale=RS[:, 1:2])
        tt("pool", Xt[:], Y[:], Gt[:], ALU.mult, [ky] + bkeys, [kx])
        tt("pool", Xt[:], Xt[:], Bt[:], ALU.add, [kx] + bkeys, [kx])
        ld(xdst, Xt[:], [kx], [], "es%d" % sl)
        if hdst is not None:
            make_ht(Y, [ky], HT, kh, site, r, pb)
            ld(hdst, HT[:], [(kh, k) for k in range(8)], [], "eh%d" % sy)

    def epi_alloc(ny=2, nht=0):
        E = {}
        E["Xt"] = [mk.sb([128, D], F32, "Xt") for _ in range(2)]
        E["Y"] = [mk.sb([128, D], F32, "Y") for _ in range(ny)]
        E["HT"] = [mk.sb([128, 8, 128], BF16, "HTe") for _ in range(nht if nht else ny)]
        E["BS"] = [mk.sb([128, 2, 6], F32, "BS") for _ in range(2)]
        E["MV"] = [mk.sb([128, 2], F32, "MV") for _ in range(2)]
        E["RS"] = [mk.sb([128, 3], F32, "RS") for _ in range(2)]
        return E

    def load_bc(GB, Gt, Bt, l, goff, r, gname, bname):
        ld(GB[:], modd[l, r, goff:goff + D].partition_broadcast(128), [], ["GB"], "bc0")
        ld(Gt[:], ln_d[gname][l].partition_broadcast(128), [], ["Gt"], "bc1")
        ld(Bt[:], ln_d[bname][l].partition_broadcast(128), [], ["Bt"], "bc2")

    def phase_qkv():
        m = mk.mark()
        WIN = mk.sb([128, 8, 3 * D], BF16, "WIN")
        wv = awin_d[0].rearrange("(k p) n -> p k n", p=128)
        for k in range(8):
            if CFG.get("nowin"):
                break
            ldc([(WIN[:, k, :], wv[:, k, :])], [], [("WIN", k)], "w%d" % (k % 4))
        wink = [("WIN", k) for k in range(8)]
        XT = [mk.sb([128, D], F32, "XT") for _ in range(2)]
        RP = [mk.sb([128, 2, 64], F32, "RP") for _ in range(2)]
        HT = [mk.sb([128, 8, 128], BF16, "HT") for _ in range(2)]
        RQ = mk.sb([128, D], F32, "RQ")
        T1 = mk.sb([128, D], F32, "T1")
        T2 = mk.sb([128, D], F32, "T2")
        RB = mk.sb([128, D], F32, "RB")
        VV = [mk.sb([128, 2, 512], BF16, "VV") for _ in range(2)]
        TQ = [mk.sb([128, 8, 128], BF16, "TQ") for _ in range(2)]
        TB = [mk.sb([128, 8, 128], BF16, "TB") for _ in range(2)]
        tiles = [(b, i) for b in range(NBB) for i in range(34)]
        if "qkv_tiles" in CFG:
            tiles = tiles[:CFG["qkv_tiles"]]

        def src_of(b, i):
            if i < 32:
                return x_d[b, 128 * i:128 * (i + 1), :]
            return ctx_d[b, 128 * (i - 32):128 * (i - 31), :]

        def issue_load(n):
            b, i = tiles[n]
            sl = n % 2
            if not CFG.get("nox"):
                ld(XT[sl][:], src_of(b, i), [], ["XT%d" % sl], "xl%d" % sl)
            if i < 32 and not CFG.get("norp"):
                ld(RP[sl][:], rope_d[128 * i:128 * (i + 1)], [], ["RP%d" % sl], "rl%d" % sl)

        issue_load(0)
        for n, (b, i) in enumerate(tiles):
            sl = n % 2
            if n + 1 < len(tiles):
                issue_load(n + 1)
            lat = i < 32
            r = b if lat else 2
            tok0 = 128 * i if lat else T + 128 * (i - 32)
            htk = "HT%d" % sl
            if not CFG.get("noht"):
                make_ht(XT[sl], ["XT%d" % sl], HT[sl], htk, 0, r, 6)
            hk = [(htk, k) for k in range(8)]
            stg = CFG.get("qkv_stage", 9)
            if stg < 1:
                continue
            for j in range(6):
                for k in range(8):
                    mm(bank(j), HT[sl][:, k, :], WIN[:, k, 512 * j:512 * (j + 1)], k == 0, k == 7,
                       hk + wink, pk(j))
            if stg < 2:
                continue
            cp("act", RQ[:], PS[:, 0:1024], pk(0) + pk(1), ["RQ"])
            cp("dve", RB[:], PS[:, 1536:2560], pk(3) + pk(4), ["RB"])
            cp("act", VV[sl][:, 0, :], bank(2), pk(2), [("VV%d" % sl, 0)])
            cp("dve", VV[sl][:, 1, :], bank(5), pk(5), [("VV%d" % sl, 1)])
            if lat and not CFG.get("norope"):
                rp = RP[sl]
                rk = "RP%d" % sl
                tt("dve", T1[:].rearrange("p (g d) -> p g d", g=16), RQ[:].rearrange("p (g d) -> p g d", g=16),
                   rp[:, 0:1, :].to_broadcast([128, 16, 64]), ALU.mult, ["RQ", rk], ["T1"])
                rqv = RQ[:].rearrange("p (g a h f) -> p g a h f", g=16, a=2, h=2, f=16)
                t2v = T2[:].rearrange("p (g a h f) -> p g a h f", g=16, a=2, h=2, f=16)
                sv = rp[:, 1:2, :].rearrange("p o (a h f) -> p o a h f", a=2, h=2, f=16)
                tt("pool", t2v[:, :, :, 0, :], rqv[:, :, :, 1, :], sv[:, :, :, 0, :].to_broadcast([128, 16, 2, 16]),
                   ALU.mult, ["RQ", rk], [("T2", 0)])
                tt("pool", t2v[:, :, :, 1, :], rqv[:, :, :, 0, :], sv[:, :, :, 1, :].to_broadcast([128, 16, 2, 16]),
                   ALU.mult, ["RQ", rk], [("T2", 1)])
                tt("dve", T1[:], T1[:], T2[:], ALU.add, ["T1", ("T2", 0), ("T2", 1)], ["T1"])
                qsrc, qk = T1, ["T1"]
            else:
                qsrc, qk = RQ, ["RQ"]
            if stg < 3:
                continue
            for (src, sk, dst, dk) in ((qsrc, qk, TQ[sl], "TQ%d" % sl), (RB, ["RB"], TB[sl], "TB%d" % sl)):
                for k in range(8):
                    b_ = 6 + k // 4
                    tr(bank(b_, 128, 128 * (k % 4)), src[:, 128 * k:128 * (k + 1)], ident[:], sk + ["ident"],
                       [("p", b_)])
                act(dst[:, 0:4, :].rearrange("p k t -> p (k t)"), bank(6), AF.Identity,
                    [("p", 6)], [(dk, 0)], scale=0.125)
                cp("dve", dst[:, 4:8, :].rearrange("p k t -> p (k t)"), bank(7),
                   [("p", 7)], [(dk, 1)])
            if stg < 4:
                continue
            ld(vas[b][tok0:tok0 + 128, :], VV[sl][:, 0, :], [("VV%d" % sl, 0)], [], "sva%d" % sl)
            ld(vbs[b][tok0:tok0 + 128, :], VV[sl][:, 1, :], [("VV%d" % sl, 1)], [], "svb%d" % sl)
            for si_, (dst_d, tile_, kk, half) in enumerate(((qat, TQ, "TQ%d" % sl, 0), (kat, TQ, "TQ%d" % sl, 1),
                                                          (qbt, TB, "TB%d" % sl, 0), (kbt, TB, "TB%d" % sl, 1))):
                ld(dst_d[b].rearrange("h p t -> p h t")[:, :, tok0:tok0 + 128],
                   tile_[sl][:, 4 * half:4 * half + 4, :], [(kk, half)], [], "sq%d%d" % (si_, sl))
        mk.release(m)

    def phase_diff():
        m = mk.mark()
        KT = [mk.sb([128, TK], BF16, "KT") for _ in range(2)]
        QT = [mk.sb([128, TK], BF16, "QT") for _ in range(2)]
        V1 = [mk.sb([128, 34, 129], BF16, "V1") for _ in range(2)]
        NPT = 3
        PT = [mk.sb([128, 2, 512], BF16, "PT") for _ in range(NPT)]
        AQ = mk.sb([128, 4, 128], F32, "AQ")
        OO = mk.sb([128, 4, 128], F32, "OO")
        SQ = mk.sb([128, 4, 128], F32, "SQ")
        RC = mk.sb([128, 4, 2], F32, "RC")
        T1 = mk.sb([128, 4, 1], F32, "T1")
        SS = mk.sb([128, 4, 1], F32, "SS")
        G3 = mk.sb([128, 1, 128], F32, "G3")
        cp("pool", G3[:, 0, :], GSUB[:], ["GSUB"], ["G3"])
        AOB = [mk.sb([128, 4, 128], F32, "AOB") for _ in range(2)]
        for i in range(2):
            memset("pool", V1[i][:, :, 128:129], 1.0, [("V1%d" % i, "one")])
        ACC = PS[:, 2048:4096].rearrange("p (b c) -> p b c", b=4)
        nblk = 0
        hh = 0
        npt = 0
        for b in range(NBB):
            for h in range(4):
                sl = hh % 2
                hh += 1
                ld(KT[sl][:], kat[b][h], [], ["KT%d" % sl], "ak%d" % sl)
                ld(QT[sl][:], qat[b][h], [], ["QT%d" % sl], "aq%d" % sl)
                ld(V1[sl][:, :, 0:128], vas[b].rearrange("(t p) c -> p t c", p=128)[:, :, 128 * h:128 * (h + 1)],
                   [], ["V1%d" % sl], "av%d" % sl)
                vkeys = ["V1%d" % sl, ("V1%d" % sl, "one")]
                blocks = [(512 * j, 512, list(range(34))) for j in range(8)] + [(T, 256, [32, 33])]
                for (q0, nq, kts) in blocks:
                    nqt = nq // 128
                    acck = [("acc", q) for q in range(nqt)]
                    memset("dve", ACC[:, 0:nqt, 0:258], 0.0, acck)

                    def qk(it):
                        kt = kts[it]
                        sp_ = it % 2
                        for mp in range(2):
                            bk = 2 * sp_ + mp
                            mm(bank(bk, nq), KT[sl][64 * mp:64 * (mp + 1), 128 * kt:128 * (kt + 1)],
                               QT[sl][64 * mp:64 * (mp + 1), q0:q0 + nq], True, True,
                               ["KT%d" % sl, "QT%d" % sl], pk(bk))

                    qk(0)
                    for it, kt in enumerate(kts):
                        if it + 1 < len(kts):
                            qk(it + 1)
                        sp_ = it % 2
                        pb_ = npt % NPT
                        npt += 1
                        sview = PS[:, 1024 * sp_:1024 * sp_ + 1024].rearrange("p (b c) -> p b c", b=2)[:, :, 0:nq]
                        act(PT[pb_][:, :, 0:nq], sview, AF.Exp, pk(2 * sp_) + pk(2 * sp_ + 1), ["PT%d" % pb_])
                        for mp in range(2):
                            for qt in range(nqt):
                                mm(bank(4 + qt, 129, 129 * mp), PT[pb_][:, mp, 128 * qt:128 * (qt + 1)],
                                   V1[sl][:, kt, :], False, False, ["PT%d" % pb_] + vkeys + [("acc", qt)],
                                   [("acc", qt)], skip=True)
                    ob = AOB[nblk % 2]
                    okey = "AOB%d" % (nblk % 2)
                    nblk += 1
                    mk.op("dve", lambda e, nqt=nqt: e.reciprocal(out=RC[:, 0:nqt, :], in_=ACC[:, 0:nqt, 128:258:129]),
                          acck, ["RC"])
                    ts("dve", T1[:, 0:nqt, :], RC[:, 0:nqt, 1:2], NLAM[:, 0:1], None, ALU.mult, None, ["RC", "NLAM"], ["T1"])
                    tt("dve", AQ[:, 0:nqt, :], ACC[:, 0:nqt, 0:128], RC[:, 0:nqt, 0:1].to_broadcast([128, nqt, 128]),
                       ALU.mult, acck + ["RC"], ["AQ"])
                    tt("dve", SQ[:, 0:nqt, :], ACC[:, 0:nqt, 129:257], T1[:, 0:nqt, :].to_broadcast([128, nqt, 128]),
                       ALU.mult, acck + ["T1"], ["SQ"])
                    tt("dve", OO[:, 0:nqt, :], AQ[:, 0:nqt, :], SQ[:, 0:nqt, :], ALU.add, ["AQ", "SQ"], ["OO"])
                    tt("pool", SQ[:, 0:nqt, :], OO[:, 0:nqt, :], OO[:, 0:nqt, :], ALU.mult, ["OO"], ["SQ"])
                    mk.op("dve", lambda e, nqt=nqt: e.reduce_sum(out=SS[:, 0:nqt, :], in_=SQ[:, 0:nqt, :],
                                                                 axis=mybir.AxisListType.X), ["SQ"], ["SS"])
                    act(SS[:, 0:nqt, :], SS[:, 0:nqt, :], AF.Ln, ["SS", "eps"], ["SS"], bias=epsT[:, 0:1], scale=1.0 / 128)
                    act(SS[:, 0:nqt, :], SS[:, 0:nqt, :], AF.Exp, ["SS"], ["SS"], scale=-0.5)
                    tt("pool", OO[:, 0:nqt, :], OO[:, 0:nqt, :], SS[:, 0:nqt, :].to_broadcast([128, nqt, 128]),
                       ALU.mult, ["OO", "SS"], ["OO"])
                    tt("pool", ob[:, 0:nqt, :], OO[:, 0:nqt, :], G3[:].to_broadcast([128, nqt, 128]), ALU.mult,
                       ["OO", "G3"], [okey])
                    ld(aos[b][q0:q0 + nq, 128 * h:128 * (h + 1)].rearrange("(q p) c -> p q c", p=128),
                       ob[:, 0:nqt, :], [okey], [], "sa%d" % (nblk % 2))
        mk.release(m)

    def phase_nbr():
        m = mk.mark()
        QB2 = mk.sb([128, 4, TK], BF16, "QB2")
        KB2 = mk.sb([128, 4, TK], BF16, "KB2")
        VBe = mk.sb([128, 34, 8, 65], BF16, "VBe")
        VBo = mk.sb([128, 31, 8, 65], BF16, "VBo")
        BI = [mk.sb([128, 4, 8, 64], F32, "BI") for _ in range(2)]
        MK2 = mk.sb([128, 1, 64], F32, "MK2")
        SBF = [mk.sb([128, 512], F32, "SBF") for _ in range(2)]
        PT = [mk.sb([128, 6, 512], BF16, "PTn") for _ in range(2)]
        RI = mk.sb([64, 8, 1], F32, "RI")
        OB = [mk.sb([64, 512], F32, "OB") for _ in range(2)]
        memset("pool", VBe[:, :, :, 64:65], 1.0, [("VBe", "one")])
        memset("pool", VBo[:, :, :, 64:65], 1.0, [("VBo", "one")])
        ld(MK2[:, 0, :], mask_d, [], ["MK2"], "nm")

        def gen_bias(dst, key, delta):
            for j in range(4):
                for i2 in range(2):
                    rr = delta + 2 * j + i2 + 7
                    ld(dst[64 * i2:64 * (i2 + 1), j, :, :], rpbt_d[rr], [], [(key, j, i2)], "nb%d" % i2)
                tt("dve", dst[:, j, :, :], dst[:, j, :, :], MK2[:].to_broadcast([128, 8, 64]), ALU.add,
                   [(key, j, 0), (key, j, 1), "MK2"], [(key, j)])

        gen_bias(BI[0], "BI0", -4)
        nrow = 0
        for b in range(NBB):
            ld(QB2[:], qbt[b].rearrange("h p t -> p h t"), [], ["QB2"], "nq")
            ld(KB2[:], kbt[b].rearrange("h p t -> p h t"), [], ["KB2"], "nk")
            vb_v = vbs[b].rearrange("(t p) (h d) -> p t h d", p=128, h=8)
            for t4 in range(0, 34, 6):
                t5 = min(34, t4 + 6)
                mk.dma("sp", [(VBe[:, t_, :, 0:64], vb_v[:, t_]) for t_ in range(t4, t5)], [], [("VBe", t4)], "nv")
            vbo_v = vbs[b][64:64 + 31 * 128, :].rearrange("(t p) (h d) -> p t h d", p=128, h=8)
            for t4 in range(0, 31, 6):
                t5 = min(31, t4 + 6)
                mk.dma("sp", [(VBo[:, t_, :, 0:64], vbo_v[:, t_]) for t_ in range(t4, t5)], [], [("VBo", t4)], "nv")
            vek = [("VBe", t4) for t4 in range(0, 34, 6)] + [("VBe", "one")]
            vok = [("VBo", t4) for t4 in range(0, 31, 6)] + [("VBo", "one")]
            rows = [("lat", r) for r in range(64)] + [("ctx", g) for g in range(4)]
            if "nbr_rows" in CFG:
                rows = rows[:CFG["nbr_rows"]]
            nst = CFG.get("nbr_stage", 9)
            for (kind, r) in rows:
                par = nrow % 2
                nrow += 1
                if kind == "lat":
                    rs = min(max(r - 4, 0), 56)
                    delta = rs - r
                    if delta == -4:
                        bi, bik = BI[0], "BI0"
                    else:
                        bi, bik = BI[1], "BI1"
                        gen_bias(BI[1], "BI1", delta)
                    q0 = 64 * r
                    kts = []
                    for j in range(4):
                        ks = 64 * (rs + 2 * j)
                        if rs % 2 == 0:
                            kts.append((ks, VBe, (rs + 2 * j) // 2, vek, j))
                        else:
                            kts.append((ks, VBo, (rs + 2 * j - 1) // 2, vok, j))
                    kts.append((T, VBe, 32, vek, None))
                    kts.append((T + 128, VBe, 33, vek, None))
                else:
                    q0 = T + 64 * r
                    kts = [(T, VBe, 32, vek, None), (T + 128, VBe, 33, vek, None)]
                pt = PT[par]
                ptk = "PTn%d" % par
                for n, (ks, vt, vi, vk, j) in enumerate(kts):
                    pp = n % 2
                    for h in range(8):
                        hp, hq = h % 2, h // 2
                        mm(bank(2 * pp + hp, 64, 64 * hq), KB2[64 * hp:64 * (hp + 1), hq, ks:ks + 128],
                           QB2[64 * hp:64 * (hp + 1), hq, q0:q0 + 64], True, True, ["KB2", "QB2"], pk(2 * pp + hp))
                    s2 = PS[:, 1024 * pp:1024 * pp + 1024].rearrange("p (b c) -> p b c", b=2)[:, :, 0:256]
                    sk2 = pk(2 * pp) + pk(2 * pp + 1)
                    if j is not None:
                        sb_ = SBF[n % 2]
                        tt("dve", sb_[:].rearrange("p (b c) -> p b c", b=2), s2,
                           bi[:, j, :, :].rearrange("p (b h) c -> p b (h c)", b=2), ALU.add,
                           sk2 + [(bik, j)], ["SBF%d" % (n % 2)])
                        act(pt[:, n, :], sb_[:], AF.Exp, ["SBF%d" % (n % 2)], [(ptk, n)])
                    else:
                        act(pt[:, n, :].rearrange("p (b c) -> p b c", b=2), s2, AF.Exp, sk2, [(ptk, n)])
                if nst < 2:
                    continue
                ab = 4 + 2 * par
                for h in range(8):
                    b_ = ab + h // 4
                    for n, (ks, vt, vi, vk, j) in enumerate(kts):
                        mm(PS[0:64, 512 * b_ + 65 * (h % 4):512 * b_ + 65 * (h % 4) + 65],
                           pt[:, n, 64 * ((h % 2) * 4 + h // 2):64 * ((h % 2) * 4 + h // 2 + 1)], vt[:, vi, h, :],
                           n == 0, n == len(kts) - 1,
                           [(ptk, n)] + vk, pk(b_))
                if nst < 3:
                    continue
                ob = OB[par]
                obk = "OB%d" % par
                for g in range(2):
                    b_ = ab + g
                    accv = PS[0:64, 512 * b_:512 * b_ + 260].rearrange("p (h c) -> p h c", h=4)
                    mk.op("dve", lambda e, accv=accv, g=g: e.reciprocal(out=RI[:, 4 * g:4 * g + 4, :], in_=accv[:, :, 64:65]),
                          pk(b_), [("RI", g)])
                    tt("dve", ob[:, 256 * g:256 * (g + 1)].rearrange("p (h d) -> p h d", h=4), accv[:, :, 0:64],
                       RI[:, 4 * g:4 * g + 4, :].to_broadcast([64, 4, 64]), ALU.mult, pk(b_) + [("RI", g)], [(obk, g)])
                ld(aos[b][q0:q0 + 64, 512:1024], ob[:], [(obk, 0), (obk, 1)], [], "no%d" % par)
        mk.release(m)

    def phase_oproj(l):
        m = mk.mark()
        WO = mk.sb([128, 8, D], BF16, "WO")
        wsrc = awout_d[0] if l == 0 else rwout_d[0]
        ldc([(WO[:], wsrc.rearrange("(k p) n -> p k n", p=128))], [], ["WO"], "w0")
        GB = mk.sb([128, D], F32, "GB")
        Gt = mk.sb([128, D], F32, "Gt")
        Bt = mk.sb([128, D], F32, "Bt")
        E = epi_alloc()
        site = 1 if l == 0 else 3
        seqs = SEQ if l == 0 else [s for s in SEQ if s[0] == "lat"]
        n = 0
        if l == 0:
            AO = [mk.sb([128, D], F32, "AO") for _ in range(2)]
            AT = [mk.sb([128, 8, 128], BF16, "AT") for _ in range(2)]
        else:
            MB = [mk.sb([128, 8, 512], BF16, "MB") for _ in range(2)]
        for s in seqs:
            r = seq_var(s)
            load_bc(GB, Gt, Bt, l, 2 * D, r, "ln1_g", "ln1_b")
            b = s[1]
            L = seq_len(s)
            base = 0 if s[0] == "lat" else T
            xsrc = seq_src(s) if l == 0 else xs[s]
            hdst_all = hts[(s, 0)].rearrange("k p t -> p k t")
            for i in range(L // 128):
                sl = n % 2
                pa = 2 * (n % 2)
                if l == 0:
                    ld(AO[sl][:], aos[b][base + 128 * i:base + 128 * (i + 1), :], [], ["AO%d" % sl], "ol%d" % sl)
                    for k in range(8):
                        tr(bank(4 + k // 4, 128, 128 * (k % 4)), AO[sl][:, 128 * k:128 * (k + 1)], ident[:],
                           ["AO%d" % sl, "ident"], [("p", 4 + k // 4)])
                    cp("act", AT[sl][:, 0:4, :].rearrange("p k t -> p (k t)"), bank(4), [("p", 4)],
                       [("AT%d" % sl, 0)])
                    cp("dve", AT[sl][:, 4:8, :].rearrange("p k t -> p (k t)"), bank(5), [("p", 5)],
                       [("AT%d" % sl, 1)])
                    lk = [("AT%d" % sl, 0), ("AT%d" % sl, 1)]
                    lhs = lambda k: AT[sl][:, k, :]
                else:
                    if i % 4 == 0:
                        ms = (n // 4) % 2
                        ld(MB[ms][:], mts[b].rearrange("k p t -> p k t")[:, :, 128 * i:128 * i + 512], [],
                           ["MB%d" % ms], "ol%d" % ms)
                    ms = (n // 4) % 2
                    lk = ["MB%d" % ms]
                    lhs = lambda k: MB[ms][:, k, 128 * (i % 4):128 * (i % 4 + 1)]
                for k in range(8):
                    for hf in range(2):
                        mm(bank(pa + hf), lhs(k), WO[:, k, 512 * hf:512 * (hf + 1)], k == 0, k == 7,
                           lk + ["WO"], pk((pa + hf)))
                epilogue(PS[:, 512 * pa:512 * pa + 1024], pk(pa) + pk(pa + 1),
                         xsrc[128 * i:128 * (i + 1), :], GB, Gt, Bt, ["GB", "Gt", "Bt"], site, r,
                         xs[s][128 * i:128 * (i + 1), :], hdst_all[:, :, 1 + 128 * i:1 + 128 * (i + 1)], E, n, 6)
                n += 1
        mk.release(m)

    def phase_ffn(l):
        m = mk.mark()
        WU = mk.sb([128, 8, 2 * DFF], BF16, "WU")
        WD = mk.sb([128, 22, D], BF16, "WD")
        wv = wup_d[l].rearrange("(k p) n -> p k n", p=128)
        for k in range(8):
            ldc([(WU[:, k, :], wv[:, k, :])], [], [("WU", k)], "w%d" % (k % 4))
        wdv = wdn_d[l].rearrange("(k p) n -> p k n", p=128)
        wuk = [("WU", k) for k in range(8)]
        WDG = [(k0, min(22, k0 + 6)) for k0 in range(0, 22, 6)]
        wdk = [("WD", k0) for (k0, _) in WDG]
        Gt = mk.sb([128, D], F32, "Gt")
        Bt = mk.sb([128, D], F32, "Bt")
        E = epi_alloc(2, 1)
        HB = mk.sb([128, 8, 514], BF16, "HB")
        U = [mk.sb([128, 514], F32, "U") for _ in range(2)]
        C = [mk.sb([128, 512], F32, "C") for _ in range(3)]
        AC = mk.sb([128, 22, 512], BF16, "AC")
        UH = mk.sb([128, 44, 18], F32, "UH")
        HS = mk.sb([128, 8, 16], BF16, "HS")
        US = [C[0], C[1]]
        memset("pool", UH[:], 0.0, ["UH"])
        memset("pool", HS[:], 0.0, ["HS"])
        last = l == 1
        site = 2 if l == 0 else None
        seqs = SEQ if l == 0 else [s for s in SEQ if s[0] == "lat"]
        fw = [VMAP[("fcw", l, k)] for k in range(3)]
        fb = VMAP[("fcb", l)]
        MB = [0, 1, 2, 7]
        bkeys = ["Gt", "Bt"]
        n = 0
        uc = 0
        for s in seqs:
            r = seq_var(s)
            L = seq_len(s)
            hsrc = hts[(s, 0)].rearrange("k p t -> p k t")
            hdst_all = hts[(s, 1)].rearrange("k p t -> p k t")
            nb = min(512, L)
            nblk = L // nb
            GBt = E["Y"][0]
            ld(GBt[:], modd[l, r, 5 * D:6 * D].partition_broadcast(128), [], ["Y0"], "bc0")
            ld(Gt[:], ln_d["ln2_g"][l].partition_broadcast(128), [], ["Gt"], "bc1")
            ld(Bt[:], ln_d["ln2_b"][l].partition_broadcast(128), [], ["Bt"], "bc2")
            for gi, (k0, k1) in enumerate(WDG):
                ldc([(WD[:, k0:k1, :], wdv[:, k0:k1, :])], [], [("WD", k0)], "w%d" % (gi % 4))
                for kk in range(k0, k1):
                    tt("pool" if kk % 2 else "dve", WD[:, kk, :], WD[:, kk, :], GBt[:], ALU.mult, [("WD", k0), "Y0"], [("WD", k0)])
            if nblk > 1:
                for bd in range(1, nblk):
                    mk.dma("sp", [(HS[:, :, 2 * bd:2 * bd + 2], hsrc[:, :, 512 * bd:512 * bd + 2])], [], ["HS"], "fz", slow=True)
                for cb in range(11):
                    for k in range(8):
                        mm(PS[0:16, 512 * 3:512 * 3 + 512], HS[:, k, :], WU[:, k, 512 * cb:512 * (cb + 1)], k == 0, k == 7,
                           ["HS"] + wuk, pk(3))
                    cp("act", US[cb % 2][0:16, :], PS[0:16, 512 * 3:512 * 3 + 512], pk(3), ["C%d" % (cb % 2)])
                    for q in range(4):
                        c = 4 * cb + q
                        b_ = 5 + c // 22
                        tr(PS[:, 512 * b_ + 16 * (c % 22):512 * b_ + 16 * (c % 22) + 16], US[cb % 2][0:16, 128 * q:128 * (q + 1)],
                           ident[0:16, 0:16], ["C%d" % (cb % 2), "ident"], pk(b_))
                cp("dve", UH[:, 0:22, 0:16], PS[:, 512 * 5:512 * 5 + 352].rearrange("p (c e) -> p c e", e=16), pk(5), ["UH"])
                cp("dve", UH[:, 22:44, 0:16], PS[:, 512 * 6:512 * 6 + 352].rearrange("p (c e) -> p c e", e=16), pk(6), ["UH"])
            else:
                memset("pool", UH[:], 0.0, ["UH"])

            def xload(nn, tok):
                sl = nn % 2
                ld(E["Xt"][sl][:], xs[s][tok:tok + 128, :], [], ["Xt%d" % sl], "ex%d" % sl)

            def epiA(nn, ps2, pskeys, xdst):
                sl = nn % 2
                Xt, Y = E["Xt"][sl], E["Y"][sl]
                kx, ky = "Xt%d" % sl, "Y%d" % sl
                stt("dve", Y[:], Xt[:], ALPHA, ps2, ALU.mult, ALU.add, [kx] + pskeys, [ky])
                BS, MV, RS = E["BS"][sl], E["MV"][sl], E["RS"][sl]
                kb = "BS%d" % sl
                mk.op("dve", lambda e: e.bn_stats(out=BS[:, 0, :], in_=Y[:, 0:512]), [ky], [(kb, 0)])
                mk.op("dve", lambda e: e.bn_stats(out=BS[:, 1, :], in_=Y[:, 512:1024]), [ky], [(kb, 1)])
                mk.op("dve", lambda e: e.bn_aggr(out=MV[:], in_=BS[:]), [(kb, 0), (kb, 1)], [(kb, 2)])
                act(RS[:, 0:1], MV[:, 1:2], AF.Ln, [(kb, 2), "eps"], [(kb, 3)], bias=epsT[:, 0:1])
                act(RS[:, 1:2], RS[:, 0:1], AF.Exp, [(kb, 3)], [(kb, 4)], scale=-0.5)
                stt("dve", RS[:, 2:3], MV[:, 0:1], -1.0, RS[:, 1:2], ALU.mult, ALU.mult, [(kb, 2), (kb, 4)], [(kb, 5)])
                act(Y[:], Y[:], AF.Identity, [ky, (kb, 4), (kb, 5)], [ky], bias=RS[:, 2:3], scale=RS[:, 1:2])
                tt("pool", Xt[:], Y[:], Gt[:], ALU.mult, [ky] + bkeys, [kx])
                tt("pool", Xt[:], Xt[:], Bt[:], ALU.add, [kx] + bkeys, [kx])
                ld(xdst, Xt[:], [kx], [], "es%d" % sl)

            def epiB(nn, hdst):
                sl = nn % 2
                make_ht(E["Y"][sl], ["Y%d" % sl], E["HT"][0], "HTe0", site, r, 5)
                ld(hdst, E["HT"][0][:], [("HTe0", k) for k in range(8)], [], "eh0")

            def tail(pend):
                i_, cg, cgk, cv, cvk = pend
                act(cg[:, 0:nb], cg[:, 0:nb], AF.Gelu_apprx_tanh, [cgk], [cgk])
                tt("pool", AC[:, i_, 0:nb], cg[:, 0:nb], cv[:, 0:nb], ALU.mult, [cgk, cvk], [("AC", i_)])

            for j in range(nblk):
                t0 = nb * j
                if j == 0:
                    ld(HB[:, :, 0:nb + 2], hsrc[:, :, t0:t0 + nb + 2], [], ["HB"], "fh")
                xload(n, t0)
                pend = None
                for i in range(22):
                    cur = []
                    for gv in range(2):
                        c = i + 22 * gv
                        u = uc % 4
                        cb_ = uc % 3
                        uc += 1
                        mb = MB[u]
                        for k in range(8):
                            mm(bank(mb, nb), WU[:, k, 128 * c:128 * (c + 1)], HB[:, k, 1:nb + 1], k == 0, k == 7,
                               wuk + ["HB"], pk(mb))
                        Ut, Ct = U[u % 2], C[cb_]
                        ukk = "U%d" % (u % 2)
                        ck = "C%d" % cb_
                        cp("act", Ut[:, 1:nb + 1], bank(mb, nb), pk(mb), [(ukk, 0)])
                        cp("pool", Ut[:, 0:nb + 2:nb + 1], UH[:, c, 2 * j:2 * j + 4:3], ["UH"], [(ukk, 1)])
                        uk = [(ukk, 0), (ukk, 1)]
                        act(Ct[:, 0:nb], bank(mb, nb), AF.Identity, pk(mb) + ["VEC"], [ck],
                            bias=VEC[:, fb + c:fb + c + 1], scale=VEC[:, fw[1] + c:fw[1] + c + 1])
                        stt("dve", Ct[:, 0:nb], Ut[:, 0:nb], VEC[:, fw[0] + c:fw[0] + c + 1], Ct[:, 0:nb], ALU.mult, ALU.add,
                            uk + ["VEC", ck], [ck])
                        stt("dve", Ct[:, 0:nb], Ut[:, 2:nb + 2], VEC[:, fw[2] + c:fw[2] + c + 1], Ct[:, 0:nb], ALU.mult, ALU.add,
                            uk + ["VEC", ck], [ck])
                        cur += [Ct, ck]
                        if gv == 0 and pend is not None:
                            tail(pend)
                            pend = None
                    pend = (i, cur[0], cur[1], cur[2], cur[3])
                tail(pend)
                if j + 1 < nblk:
                    ld(HB[:, :, 0:nb + 2], hsrc[:, :, t0 + nb:t0 + 2 * nb + 2], [], ["HB"], "fh")
                ack = [("AC", i) for i in range(22)]
                nt_ = nb // 128
                prevB = None
                for tq in range(nt_):
                    tok = t0 + 128 * tq
                    for k in range(22):
                        for hf in range(2):
                            mm(bank(3 + hf), AC[:, k, 128 * tq:128 * (tq + 1)], WD[:, k, 512 * hf:512 * (hf + 1)],
                               k == 0, k == 21, ack + wdk, pk(3 + hf))
                    if tq + 1 < nt_:
                        xload(n + 1, tok + 128)
                    if last:
                        xdst = out_d[s[1], tok:tok + 128, :]
                    else:
                        xdst = xs[s][tok:tok + 128, :]
                    epiA(n, PS[:, 512 * 3:512 * 3 + 1024], pk(3) + pk(4), xdst)
                    if prevB is not None:
                        epiB(*prevB)
                        prevB = None
                    if not last:
                        prevB = (n, hdst_all[:, :, 1 + tok:1 + tok + 128])
                    n += 1
                if prevB is not None:
                    epiB(*prevB)
        mk.release(m)

    def phase_rglru():
        m = mk.mark()
        CX0, LX0, NX = 2, 261, 4358
        XR = mk.sb([128, NX], F32, "XR")
        XC = mk.sb([128, NX], F32, "XC")
        XCb = mk.sb([128, NX], BF16, "XCb")
        A_ = mk.sb([128, NX], F32, "A")
        BT_ = mk.sb([128, NX], F32, "BT")
        TM = mk.sb([128, NX], F32, "TM")
        RF = mk.sb([128, NX], F32, "RF")
        RR = mk.sb([128, NX], F32, "RR")
        GY = mk.sb([128, T], BF16, "GY")
        MT = mk.sb([128, T], BF16, "MT")
        WI = mk.sb([128, 8, 2, 128], BF16, "WI")
        GW = mk.sb([128, 4, 128], BF16, "GW")
        HB = [mk.sb([128, 8, 512], BF16, "HBr") for _ in range(2)]
        memset("dve", XR[:], 0.0, ["XRz"])
        wv = rwin_d[0].rearrange("(k p) n -> p k n", p=128)
        nh = 0
        regions = [(CX0, TC), (LX0, T)]
        for b in range(NBB):
            hs = {"ctx": hts[(("ctx", b), 1)].rearrange("k p t -> p k t"),
                  "lat": hts[(("lat", b), 1)].rearrange("k p t -> p k t")}
            blocks = [("ctx", 0, 256, CX0)] + [("lat", 512 * j, 512, LX0 + 512 * j) for j in range(8)]
            for c in range(8):
                ldc([(WI[:, :, 0, :], wv[:, :, 128 * c:128 * (c + 1)]),
                     (WI[:, :, 1, :], wv[:, :, D + 128 * c:D + 128 * (c + 1)])], [], ["WI"], "w0")
                ldc([(GW[:, 0, :], rga_d[0, 0, c]), (GW[:, 1, :], rgx_d[0, 0, c]),
                     (GW[:, 2, :], rga_d[0, 1, c]), (GW[:, 3, :], rgx_d[0, 1, c])], [], ["GW"], "w1")
                for (kind, t0, nt, xo) in blocks:
                    sl = nh % 2
                    nh += 1
                    ld(HB[sl][:, :, 0:nt], hs[kind][:, :, 1 + t0:1 + t0 + nt], [], ["HBr%d" % sl], "rh%d" % sl)
                    bk = 2 * sl
                    for k in range(8):
                        mm(bank(bk, nt), WI[:, k, 1, :], HB[sl][:, k, 0:nt], k == 0, k == 7, ["WI", "HBr%d" % sl], pk(bk))
                    cp("act", XR[:, xo:xo + nt], bank(bk, nt), pk(bk) + ["XRz"], [("XR", xo)])
                    if kind == "lat":
                        for k in range(8):
                            mm(bank(bk + 1, nt), WI[:, k, 0, :], HB[sl][:, k, 0:nt], k == 0, k == 7,
                               ["WI", "HBr%d" % sl], pk((bk + 1)))
                        act(GY[:, t0:t0 + nt], bank(bk + 1, nt), AF.Gelu_apprx_tanh, pk((bk + 1)), [("GY", t0)])
                xrk = [("XR", xo) for (_, _, _, xo) in blocks] + ["XRz"]
                w = [VMAP[("rcw", k)] + c for k in range(4)]
                cb = VMAP[("rcb",)] + c
                for (o, n_) in regions:
                    ce = "dve"
                    ts("pool", XC[:, o:o + n_], XR[:, o:o + n_], VEC[:, w[2]:w[2] + 1], VEC[:, cb:cb + 1], ALU.mult, ALU.add,
                       xrk + ["VEC"], [("XC", o)])
                    for (kk, sh) in ((0, -2), (1, -1), (3, 1)):
                        stt(ce, XC[:, o:o + n_], XR[:, o + sh:o + sh + n_], VEC[:, w[kk]:w[kk] + 1], XC[:, o:o + n_],
                            ALU.mult, ALU.add, xrk + ["VEC", ("XC", o)], [("XC", o)])
                    cp("pool" if n_ == T else "act", XCb[:, o:o + n_], XC[:, o:o + n_], [("XC", o)], [("XCb", o)])
                xck = [("XC", o) for (o, _) in regions]
                xcbk = [("XCb", o) for (o, _) in regions]
                for d in range(2):
                    res = RF if d == 0 else RR
                    resk = "RF" if d == 0 else "RR"
                    ba = VMAP[("rba", d)] + c
                    bx = VMAP[("rbx", d)] + c
                    gi = 0
                    for (kind, t0, nt, xo) in blocks:
                        bk = 4 + 2 * (gi % 2)
                        gi += 1
                        mm(bank(bk, nt), GW[:, 2 * d, :], XCb[:, xo:xo + nt], True, True, ["GW"] + xcbk, pk(bk))
                        mm(bank(bk + 1, nt), GW[:, 2 * d + 1, :], XCb[:, xo:xo + nt], True, True, ["GW"] + xcbk,
                           pk((bk + 1)))
                        ro = CX0 if kind == "ctx" else LX0
                        act(A_[:, xo:xo + nt], bank(bk, nt), AF.Sigmoid, pk(bk) + ["VEC"], [("A", xo), ("A2", ro)],
                            bias=VEC[:, ba:ba + 1])
                        act(BT_[:, xo:xo + nt], bank(bk + 1, nt), AF.Sigmoid, pk(bk + 1) + ["VEC"], [("BT", xo), ("BT2", ro)],
                            bias=VEC[:, bx:bx + 1])
                    ak = [("A", xo) for (_, _, _, xo) in blocks]
                    btk = [("BT", xo) for (_, _, _, xo) in blocks]
                    for (o, n_) in regions:
                        sl_ = slice(o, o + n_)
                        kA, kB, kT = ("A2", o), ("BT2", o), ("TM", o)
                        act(A_[:, sl_], A_[:, sl_], AF.Exp, ak + ["CST"], [kA], scale=CST[:, 8 * d + c:8 * d + c + 1])
                        tt("pool", TM[:, sl_], A_[:, sl_], A_[:, sl_], ALU.mult, [kA], [kT])
                        ts("dve", TM[:, sl_], TM[:, sl_], -1.0, 1.0, ALU.mult, ALU.add, [kT], [kT])
                        act(TM[:, sl_], TM[:, sl_], AF.Sqrt, [kT], [kT])
                        tt("pool", BT_[:, sl_], BT_[:, sl_], XC[:, sl_], ALU.mult, btk + xck, [kB])
                        tt("pool", BT_[:, sl_], BT_[:, sl_], TM[:, sl_], ALU.mult, [kB, kT], [kB])
                    (oc, ncx), (ol, nl) = regions
                    if d == 0:
                        mk.op("dve", lambda e: e.tensor_tensor_scan(out=RF[:, oc:oc + ncx], data0=A_[:, oc:oc + ncx],
                                                                    data1=BT_[:, oc:oc + ncx], initial=0.0,
                                                                    op0=ALU.mult, op1=ALU.add),
                              [("A2", oc), ("BT2", oc)], [("RFc",)])
                        mk.op("dve", lambda e: e.tensor_tensor_scan(out=RF[:, ol:ol + nl], data0=A_[:, ol:ol + nl],
                                                                    data1=BT_[:, ol:ol + nl],
                                                                    initial=RF[:, oc + ncx - 1:oc + ncx],
                                                                    op0=ALU.mult, op1=ALU.add),
                              [("A2", ol), ("BT2", ol), ("RFc",)], [("RFl",)])
                    else:
                        mk.op("dve", lambda e: e.tensor_tensor_scan(out=RR[:, oc:oc + ncx][:, ::-1],
                                                                    data0=A_[:, oc:oc + ncx][:, ::-1],
                                                                    data1=BT_[:, oc:oc + ncx][:, ::-1], initial=0.0,
                                                                    op0=ALU.mult, op1=ALU.add),
                              [("A2", oc), ("BT2", oc)], [("RRc",)])
                        mk.op("dve", lambda e: e.tensor_tensor_scan(out=RR[:, ol:ol + nl][:, ::-1],
                                                                    data0=A_[:, ol:ol + nl][:, ::-1],
                                                                    data1=BT_[:, ol:ol + nl][:, ::-1],
                                                                    initial=RR[:, oc:oc + 1],
                                                                    op0=ALU.mult, op1=ALU.add),
                              [("A2", ol), ("BT2", ol), ("RRc",)], [("RRl",)])
                tt("pool", RF[:, LX0:LX0 + T], RF[:, LX0:LX0 + T], RR[:, LX0:LX0 + T], ALU.add, [("RFl",), ("RRl",)], [("RFl",)])
                tt("pool", MT[:], RF[:, LX0:LX0 + T], GY[:], ALU.mult, [("RFl",)] + [("GY", 512 * j) for j in range(8)], ["MT"])
                ld(mts[b][c], MT[:], ["MT"], [], "rm")
        mk.release(m)

    order = ["qkv", "diff", "nbr", "op0", "ffn0", "rg", "op1", "ffn1"]
    fns = {"qkv": phase_qkv, "diff": phase_diff, "nbr": phase_nbr, "op0": lambda: phase_oproj(0),
           "ffn0": lambda: phase_ffn(0), "rg": phase_rglru, "op1": lambda: phase_oproj(1), "ffn1": lambda: phase_ffn(1)}
    for ph in order:
        if stop == "p0":
            break
        if "only" in CFG and ph not in CFG["only"]:
            continue
        fns[ph]()
        if stop == ph:
            break
    mk.barrier()
    named = {"modd": modd, "ao0": aos[0], "xs_lat0": xs[("lat", 0)], "xs_ctx0": xs[("ctx", 0)],
             "qat0": qat[0], "kat0": kat[0], "va0": vas[0], "qbt0": qbt[0], "kbt0": kbt[0], "vb0": vbs[0],
             "hts0_lat0": hts[(("lat", 0), 0)], "hts1_lat0": hts[(("lat", 0), 1)], "mt0": mts[0]}
    for nm in dbg:
        src = named[nm]
        dst = nc.dram_tensor("dbg_" + nm, list(src.shape), src.dtype, kind="ExternalOutput").ap()
        mk.dma("sp", [(dst, src)], [], [], "dbg")
    mk.emit()
    return nc, mk


def prep_inputs(inputs, core):
    g = lambda k: np.asarray(inputs[k])
    b0 = 2 * core
    m = {}
    m["x"] = np.ascontiguousarray(g("x")[b0:b0 + 2])
    m["ctx"] = np.ascontiguousarray(g("ctx")[b0:b0 + 2])
    cc = np.stack([g("c")[b0], g("c")[b0 + 1], g("c_ctx")], 0).astype(np.float32)
    m["cct"] = np.ascontiguousarray(cc.reshape(3, 8, 128).transpose(2, 1, 0))
    vec = np.zeros((128, NV), np.float32)
    for l in range(2):
        for nm in ("ln1_g", "ln1_b", "ln2_g", "ln2_b"):
            vec[:, VMAP[(nm, l)]:VMAP[(nm, l)] + 8] = _pl(g(nm)[l])
        for k in range(3):
            vec[:, VMAP[("fcw", l, k)]:VMAP[("fcw", l, k)] + 44] = _pl(g("ffn_conv_w")[l, k])
        vec[:, VMAP[("fcb", l)]:VMAP[("fcb", l)] + 44] = _pl(g("ffn_conv_b")[l])
    for k in range(4):
        vec[:, VMAP[("rcw", k)]:VMAP[("rcw", k)] + 8] = _pl(g("rnn_conv_w")[0, k])
    vec[:, VMAP[("rcb",)]:VMAP[("rcb",)] + 8] = _pl(g("rnn_conv_b")[0])
    for d in range(2):
        vec[:, VMAP[("apar", d)]:VMAP[("apar", d)] + 8] = _pl(g("rg_a_param")[0, d])
        vec[:, VMAP[("rba", d)]:VMAP[("rba", d)] + 8] = _pl(g("rg_ba")[0, d].reshape(-1))
        vec[:, VMAP[("rbx", d)]:VMAP[("rbx", d)] + 8] = _pl(g("rg_bx")[0, d].reshape(-1))
    m["vec"] = vec
    m["rope"] = _ROPE
    rpb = g("na_rpb")[0].astype(np.float32)
    kc = np.arange(64)[:, None]
    cq = np.arange(64)[None, :]
    rel = kc - cq + 15
    ok = (rel >= 0) & (rel <= 30)
    gath = rpb[:, :, np.clip(rel, 0, 30)] * ok[None, None]
    perm = [0, 2, 4, 6, 1, 3, 5, 7]
    m["rpbt"] = np.ascontiguousarray(gath[perm].transpose(1, 2, 0, 3)).astype(np.float32)
    m["mask"] = _MASK
    m["lamv"] = np.stack([g("diff_lq1")[0], g("diff_lk1")[0], g("diff_lq2")[0], g("diff_lk2")[0]], 0).astype(np.float32)
    m["subg"] = np.ascontiguousarray(g("diff_subln_g")[0]).astype(np.float32)
    for k in ("ada_w", "ada_b", "ln1_g", "ln1_b", "ln2_g", "ln2_b", "ffn_w_up", "ffn_w_down", "att_w_in",
              "att_w_out", "rnn_w_in", "rg_wa", "rg_wx", "rnn_w_out"):
        m[k] = np.ascontiguousarray(g(k)).astype(np.float32)
    return m


def _const_tables():
    t = np.arange(T)
    row = (t // 64).astype(np.float64)[:, None]
    col = (t % 64).astype(np.float64)[:, None]
    inv = 1.0 / (10000.0 ** (np.arange(16, dtype=np.float64) / 16))
    inv = inv.astype(np.float32).astype(np.float64)
    ang = np.concatenate([row * inv, row * inv, col * inv, col * inv], -1).astype(np.float32)
    cos = np.cos(ang).astype(np.float32)
    sin = np.sin(ang).astype(np.float32)
    sgn = np.tile(np.concatenate([-np.ones(16), np.ones(16)]), 2).astype(np.float32)
    rope = np.stack([cos, sin * sgn[None]], 1).astype(np.float32)
    c = np.arange(64)
    cs = np.clip(c - 8, 0, 48)
    kc = np.arange(64)[:, None]
    inside = (kc >= cs[None]) & (kc < cs[None] + 16)
    mask = np.where(inside, 0.0, NEG).astype(np.float32)
    return np.ascontiguousarray(rope), np.ascontiguousarray(np.concatenate([mask, mask], 0))


_ROPE, _MASK = _const_tables()
_CACHE = {}
CFG = {}


def kernel(**inputs):
    if "nc" not in _CACHE:
        _CACHE["nc"] = build()[0]
    nc = _CACHE["nc"]
    in_maps = [prep_inputs(inputs, c) for c in range(8)]
    res = run_bass_kernel_spmd(nc, in_maps, core_ids=list(range(8)))
    out = np.concatenate([r["out"] for r in res.results], axis=0)
    return out.astype(np.float32)
```

```python
import math
import contextlib
import numpy as np
import concourse.bass as bass
import concourse.mybir as mybir
from concourse.bass_utils import run_bass_kernel_spmd

F32 = mybir.dt.float32
BF16 = mybir.dt.bfloat16
AF = mybir.ActivationFunctionType
ALU = mybir.AluOpType

ENGS = ["pe", "act", "dve", "pool", "sp"]

D = 1024
T = 4096
TC = 256
TK = T + TC
DFF = 2816
ALPHA = 4.0 ** 0.25
EPS = 1e-5
LAMBDA_INIT0 = 0.8 - 0.6 * math.exp(0.0)
NEG = -30000.0


class MK:
    def __init__(self, nc):
        self.nc = nc
        self.ops = {e: [] for e in ENGS}
        self.cnt = {e: 0 for e in ENGS}
        self.dcnt = {}
        self.seen = {e: {} for e in ENGS}
        self.lastw = {}
        self.readers = {}
        self.sb_off = 16640
        self.sb_names = 0
        self.sb_max = 0

    def sb(self, shape, dtype, name=None):
        nbytes = int(np.prod(shape[1:])) * (4 if dtype == F32 else 2)
        nbytes = (nbytes + 63) // 64 * 64
        self.sb_names += 1
        nm = "%s_%d" % (name or "t", self.sb_names)
        t = self.nc.alloc_sbuf_tensor_at(nm, list(shape), dtype, offset=self.sb_off)
        self.sb_off += nbytes
        self.sb_max = max(self.sb_max, self.sb_off)
        assert self.sb_off <= 229376, ("sbuf overflow", nm, self.sb_off)
        return t

    def mark(self):
        return self.sb_off

    def release(self, m):
        self.barrier()
        self.sb_off = m

    def _deps(self, eng, reads, writes, is_dma):
        deps = {}

        def add(tok, raw):
            if tok is None:
                return
            sk, val, teng = tok
            if not is_dma and teng == eng and sk[0] == "e":
                if eng == "pe":
                    return
            if deps.get(sk, 0) < val:
                deps[sk] = val

        for k in reads:
            add(self.lastw.get(k), True)
        for k in writes:
            add(self.lastw.get(k), False)
            for sk, (val, teng) in self.readers.get(k, {}).items():
                add((sk, val, teng), False)
        waits = []
        seen = self.seen[eng]
        for sk, val in deps.items():
            if seen.get(sk, 0) >= val:
                continue
            seen[sk] = val
            waits.append((sk, val))
        return waits

    def _commit(self, tok, reads, writes):
        sk, val, teng = tok
        for k in writes:
            self.lastw[k] = tok
            self.readers[k] = {}
        for k in reads:
            self.readers.setdefault(k, {})[sk] = (val, teng)

    def op(self, eng, fn, reads=(), writes=()):
        if eng != "pe":
            pr = [k for k in reads if isinstance(k, tuple) and k and k[0] == "p"]
            if pr:
                writes = list(writes) + [k for k in pr if k not in writes]
        waits = self._deps(eng, reads, writes, False)
        self.cnt[eng] += 1
        tok = (("e", eng), self.cnt[eng], eng)
        self.ops[eng].append((waits, fn, tok, 1))
        self._commit(tok, reads, writes)
        return tok

    def dma(self, eng, pairs, reads, writes, slot, slow=False):
        sk = ("d", slot)
        prev = self.dcnt.get(slot, 0)
        waits = self._deps(eng, reads, writes, True)
        if prev and self.seen[eng].get(sk, 0) < prev:
            self.seen[eng][sk] = prev
            waits.append((sk, prev))
        n = len(pairs)
        self.dcnt[slot] = prev + 16 * n
        tok = (sk, prev + 16 * n, None)

        def fn(e, pairs=pairs, slow=slow):
            if slow:
                return [e.dma_start(out=o, in_=i, allow_slow_non_contiguous=True) for (o, i) in pairs]
            return [e.dma_start(out=o, in_=i) for (o, i) in pairs]

        self.ops[eng].append((waits, fn, tok, 16))
        self._commit(tok, reads, writes)
        return tok

    def barrier(self):
        final = {}
        for e in ENGS:
            if self.cnt[e]:
                final[("e", e)] = self.cnt[e]
        for s, v in self.dcnt.items():
            final[("d", s)] = v
        for e in ENGS:
            waits = []
            for sk, val in final.items():
                if sk == ("e", e):
                    continue
                if self.seen[e].get(sk, 0) < val:
                    self.seen[e][sk] = val
                    waits.append((sk, val))
            if waits:
                self.ops[e].append((waits, None, None, 0))
        self.lastw = {}
        self.readers = {}

    def emit(self):
        nc = self.nc
        self.barrier()
        waited = {e: set() for e in ENGS}
        for e in ENGS:
            for waits, fn, tok, inc in self.ops[e]:
                for sk, val in waits:
                    if sk[0] == "e":
                        waited[sk[1]].add(val)
        remap = {}
        for e in ENGS:
            vals = sorted(waited[e])
            remap[e] = {v: i + 1 for i, v in enumerate(vals)}
        sems = {}
        with contextlib.ExitStack() as st:
            for e in ENGS:
                sems[("e", e)] = st.enter_context(nc.semaphore("s_" + e))
            for s in self.dcnt:
                sems[("d", s)] = st.enter_context(nc.semaphore("d_" + str(s)))
            block = st.enter_context(nc.Block())

            def run(engname):
                def body(eng):
                    for waits, fn, tok, inc in self.ops[engname]:
                        for sk, val in waits:
                            v = remap[sk[1]][val] if sk[0] == "e" else val
                            eng.wait_ge(sems[sk], v)
                        if fn is None:
                            continue
                        r = fn(eng)
                        if inc == 16:
                            for ins in r:
                                ins.then_inc(sems[tok[0]], 16)
                        elif tok[1] in remap[engname]:
                            r.then_inc(sems[tok[0]], 1)

                return body

            block.tensor(run("pe"))
            block.scalar(run("act"))
            block.vector(run("dve"))
            block.gpsimd(run("pool"))
            block.sync(run("sp"))


def _vec_map():
    m = {}
    o = 0
    for l in range(2):
        for nm in ("ln1_g", "ln1_b", "ln2_g", "ln2_b"):
            m[(nm, l)] = o
            o += 8
    for l in range(2):
        for k in range(3):
            m[("fcw", l, k)] = o
            o += 44
    for l in range(2):
        m[("fcb", l)] = o
        o += 44
    for k in range(4):
        m[("rcw", k)] = o
        o += 8
    m[("rcb",)] = o
    o += 8
    for d in range(2):
        m[("apar", d)] = o
        o += 8
    for d in range(2):
        m[("rba", d)] = o
        o += 8
    for d in range(2):
        m[("rbx", d)] = o
        o += 8
    return m, o


VMAP, NV = _vec_map()


def _pl(v):
    v = np.asarray(v, np.float32).reshape(-1, 128)
    return np.ascontiguousarray(v.T)


def build(stop=None, dbg=()):
    nc = bass.Bass("TRN2", target_bir_lowering=False)
    mk = MK(nc)

    def din(name, shape, dt=F32):
        return nc.dram_tensor(name, list(shape), dt, kind="ExternalInput").ap()

    def dscr(name, shape, dt=F32):
        return nc.dram_tensor(name, list(shape), dt, kind="Internal").ap()

    x_d = din("x", [2, T, D])
    ctx_d = din("ctx", [2, TC, D])
    cct_d = din("cct", [128, 8, 3])
    vec_d = din("vec", [128, NV])
    rope_d = din("rope", [T, 2, 64])
    rpbt_d = din("rpbt", [15, 64, 8, 64])
    mask_d = din("mask", [128, 64])
    lam_d = din("lamv", [4, 64])
    subg_d = din("subg", [128])
    ada_w_d = din("ada_w", [2, D, 6 * D])
    ada_b_d = din("ada_b", [2, 6 * D])
    ln_d = {nm: din(nm, [2, D]) for nm in ("ln1_g", "ln1_b", "ln2_g", "ln2_b")}
    wup_d = din("ffn_w_up", [2, D, 2 * DFF])
    wdn_d = din("ffn_w_down", [2, DFF, D])
    awin_d = din("att_w_in", [1, D, 3 * D])
    awout_d = din("att_w_out", [1, D, D])
    rwin_d = din("rnn_w_in", [1, D, 2 * D])
    rga_d = din("rg_wa", [1, 2, 8, 128, 128])
    rgx_d = din("rg_wx", [1, 2, 8, 128, 128])
    rwout_d = din("rnn_w_out", [1, D, D])
    out_d = nc.dram_tensor("out", [2, T, D], F32, kind="ExternalOutput").ap()

    modd = dscr("modd", [2, 3, 6 * D])
    NBB = CFG.get("nb", 2)
    SEQ = [("lat", 0), ("ctx", 0), ("lat", 1), ("ctx", 1)][:2 * NBB]
    SEQ_ALL = [("lat", 0), ("ctx", 0), ("lat", 1), ("ctx", 1)]
    xs = {s: dscr("xs_%s%d" % s, [T if s[0] == "lat" else TC, D]) for s in SEQ_ALL}
    hts = {(s, a): dscr("hts%d_%s%d" % ((a,) + s), [8, 128, (T if s[0] == "lat" else TC) + 2], BF16)
           for s in SEQ_ALL for a in (0, 1)}
    qat = [dscr("qat%d" % b, [4, 128, TK], BF16) for b in range(2)]
    kat = [dscr("kat%d" % b, [4, 128, TK], BF16) for b in range(2)]
    qbt = [dscr("qbt%d" % b, [4, 128, TK], BF16) for b in range(2)]
    kbt = [dscr("kbt%d" % b, [4, 128, TK], BF16) for b in range(2)]
    vas = [dscr("va%d" % b, [TK, 512], BF16) for b in range(2)]
    vbs = [dscr("vb%d" % b, [TK, 512], BF16) for b in range(2)]
    aos = [dscr("ao%d" % b, [TK, D]) for b in range(2)]
    mts = [dscr("mt%d" % b, [8, 128, T], BF16) for b in range(2)]

    def seq_src(s):
        return x_d[s[1]] if s[0] == "lat" else ctx_d[s[1]]

    def seq_len(s):
        return T if s[0] == "lat" else TC

    def seq_var(s):
        return s[1] if s[0] == "lat" else 2

    PS = nc.alloc_psum_tensor("PS", [128, 4096], F32)

    def bank(i, n=512, off=0):
        return PS[:, 512 * i + off:512 * i + off + n]

    def pk(i):
        return [("p", i)]

    def mm(out, lhsT, rhs, start, stop, r, w, skip=False):
        mk.op("pe", lambda e: e.matmul(out, lhsT=lhsT, rhs=rhs, start=start, stop=stop, skip_group_check=skip), r, w)

    def tr(out, in_, idn, r, w):
        mk.op("pe", lambda e: e.transpose(out, in_, idn), r, w)

    def act(out, in_, func, r, w, bias=None, scale=None, accum=None):
        kw = {}
        if bias is not None:
            kw["bias"] = bias
        if scale is not None:
            kw["scale"] = scale
        if accum is not None:
            kw["accum_out"] = accum
        mk.op("act", lambda e: e.activation(out=out, in_=in_, func=func, **kw), r, w)

    def ts(eng, out, in0, s1, s2, op0, op1, r, w):
        if s2 is None:
            if op0 == ALU.mult:
                mk.op(eng, lambda e: e.tensor_scalar_mul(out=out, in0=in0, scalar1=s1), r, w)
            else:
                assert op0 == ALU.add
                mk.op(eng, lambda e: e.tensor_scalar_add(out=out, in0=in0, scalar1=s1), r, w)
        else:
            mk.op(eng, lambda e: e.tensor_scalar(out=out, in0=in0, scalar1=s1, scalar2=s2, op0=op0, op1=op1), r, w)

    def stt(eng, out, in0, scalar, in1, op0, op1, r, w):
        mk.op(eng, lambda e: e.scalar_tensor_tensor(out=out, in0=in0, scalar=scalar, in1=in1, op0=op0, op1=op1), r, w)

    def tt(eng, out, in0, in1, op, r, w):
        mk.op(eng, lambda e: e.tensor_tensor(out=out, in0=in0, in1=in1, op=op), r, w)

    def cp(eng, out, in_, r, w):
        if eng == "act":
            mk.op("act", lambda e: e.copy(out=out, in_=in_), r, w)
        else:
            mk.op(eng, lambda e: e.tensor_copy(out=out, in_=in_), r, w)

    def memset(eng, ap, val, w):
        mk.op(eng, lambda e: e.memset(ap, val), [], w)

    def ld(out, in_, r, w, slot):
        mk.dma("sp", [(out, in_)], r, w, slot)

    def ldc(pairs, r, w, slot):
        mk.dma("pool", pairs, r, w, slot)

    ident = mk.sb([128, 128], F32, "ident")
    memset("pool", ident[:], 0.0, ["ident"])
    mk.op("pool", lambda e: e.affine_select(out=ident[:], in_=ident[:], compare_op=ALU.not_equal, fill=1.0,
                                             base=0, pattern=[[-1, 128]], channel_multiplier=1), ["ident"], ["ident"])
    epsT = mk.sb([128, 1], F32, "eps")
    memset("dve", epsT[:], EPS, ["eps"])
    VEC = mk.sb([128, NV], F32, "VEC")
    ld(VEC[:], vec_d, [], ["VEC"], "c0")
    SCAL = mk.sb([128, 4, 3, 2, 8], F32, "SCAL")
    CST = mk.sb([128, 16], F32, "CST")
    NLAM = mk.sb([128, 1], F32, "NLAM")
    GSUB = mk.sb([128, 128], F32, "GSUB")

    def vcol(key, n=8):
        o = VMAP[key]
        return VEC[:, o:o + n]

    m0 = mk.mark()
    MODP = [mk.sb([128, 48, 3], F32, "MODP%d" % l) for l in range(2)]
    ZT = mk.sb([128, 8, 2], BF16, "ZT")
    memset("dve", ZT[:], 0.0, ["ZT"])
    for s in SEQ:
        for a in (0, 1):
            L = seq_len(s)
            h = hts[(s, a)].rearrange("k p t -> p k t")
            mk.dma("sp", [(h[:, :, 0:1], ZT[:, :, 0:1])], ["ZT"], [], "z0", slow=True)
            mk.dma("sp", [(h[:, :, L + 1:L + 2], ZT[:, :, 1:2])], ["ZT"], [], "z0", slow=True)

    CC = mk.sb([128, 8, 3], F32, "CC")
    ST = mk.sb([128, 8, 3], F32, "ST")
    ld(CC[:], cct_d, [], ["CC"], "c1")
    act(ST[:], CC[:], AF.Silu, ["CC"], ["ST"])
    MODR = mk.sb([3, 6 * D], F32, "MODR")
    ADAB = mk.sb([3, 6 * D], F32, "ADAB")
    WB = [mk.sb([128, 8, 512], F32, "WB%d" % i) for i in range(2)]
    for l in range(2):
        ld(ADAB[:], ada_b_d[l].partition_broadcast(3), [], ["ADAB"], "c2")
        wv = ada_w_d[l].rearrange("(k p) n -> p k n", p=128)
        for jb in range(12):
            sl = jb % 2
            ld(WB[sl][:], wv[:, :, jb * 512:(jb + 1) * 512], [], ["WB%d" % sl], "wb%d" % sl)
            bk = jb % 2
            for k in range(8):
                mm(PS[0:3, 512 * bk:512 * bk + 512], ST[:, k, :], WB[sl][:, k, :], k == 0, k == 7,
                   ["ST", "WB%d" % sl], pk(bk))
            tt("dve", MODR[0:3, jb * 512:(jb + 1) * 512], PS[0:3, 512 * bk:512 * bk + 512],
               ADAB[0:3, jb * 512:(jb + 1) * 512], ALU.add, pk(bk) + ["ADAB"], [("MODR", jb)])
        allk = [("MODR", jb) for jb in range(12)]
        ld(modd[l], MODR[0:3, :], allk, [], "c3")
        for k in range(48):
            tr(PS[:, 3584 + 3 * k:3584 + 3 * k + 3], MODR[0:3, 128 * k:128 * (k + 1)], ident[0:3, 0:3],
               allk + ["ident"], pk(7))
        cp("dve", MODP[l][:].rearrange("p k r -> p (k r)"), PS[:, 3584:3584 + 144], pk(7), ["MODP%d" % l])
    TMP8 = mk.sb([128, 8], F32, "TMP8")
    sites = [(0, 8, 0, None, None), (0, 32, 24, ("ln1_g", 0), ("ln1_b", 0)),
             (1, 8, 0, ("ln2_g", 0), ("ln2_b", 0)), (1, 32, 24, ("ln1_g", 1), ("ln1_b", 1))]
    for si, (l, sco, sho, gk, bk_) in enumerate(sites):
        for r in range(3):
            A_ = SCAL[:, si, r, 0, :]
            B_ = SCAL[:, si, r, 1, :]
            sc = MODP[l][:, sco:sco + 8, r]
            sh = MODP[l][:, sho:sho + 8, r]
            ts("dve", TMP8[:], sc, 1.0, None, ALU.add, None, ["MODP%d" % l], ["TMP8"])
            if gk is None:
                cp("dve", A_, TMP8[:], ["TMP8"], ["SCAL"])
                cp("dve", B_, sh, ["MODP%d" % l], ["SCAL"])
            else:
                tt("dve", A_, TMP8[:], vcol(gk), ALU.mult, ["TMP8", "VEC"], ["SCAL"])
                tt("dve", B_, TMP8[:], vcol(bk_), ALU.mult, ["TMP8", "VEC"], ["SCAL"])
                tt("dve", B_, B_, sh, ALU.add, ["SCAL", "MODP%d" % l], ["SCAL"])
    LV = mk.sb([128, 4, 64], F32, "LV")
    ld(LV[:].rearrange("p a d -> p (a d)"), lam_d.rearrange("a d -> (a d)").partition_broadcast(128), [], ["LV"], "c4")
    L2 = mk.sb([128, 2, 64], F32, "L2")
    E2 = mk.sb([128, 2], F32, "E2")
    tt("dve", L2[:, 0, :], LV[:, 0, :], LV[:, 1, :], ALU.mult, ["LV"], ["L2"])
    tt("dve", L2[:, 1, :], LV[:, 2, :], LV[:, 3, :], ALU.mult, ["LV"], ["L2"])
    mk.op("dve", lambda e: e.reduce_sum(out=E2[:], in_=L2[:], axis=mybir.AxisListType.X), ["L2"], ["E2"])
    act(E2[:], E2[:], AF.Exp, ["E2"], ["E2"])
    tt("dve", NLAM[:], E2[:, 1:2], E2[:, 0:1], ALU.subtract, ["E2"], ["NLAM"])
    ts("dve", NLAM[:], NLAM[:], -LAMBDA_INIT0, None, ALU.add, None, ["NLAM"], ["NLAM"])
    ld(GSUB[:], subg_d.partition_broadcast(128), [], ["GSUB"], "c5")
    ts("dve", GSUB[:], GSUB[:], 1.0 - LAMBDA_INIT0, None, ALU.mult, None, ["GSUB"], ["GSUB"])
    act(CST[:], VEC[:, VMAP[("apar", 0)]:VMAP[("apar", 0)] + 16], AF.Exp, ["VEC"], ["CST"], scale=-1.0)
    ts("dve", CST[:], CST[:], 1.0, None, ALU.add, None, ["CST"], ["CST"])
    act(CST[:], CST[:], AF.Ln, ["CST"], ["CST"])
    ts("dve", CST[:], CST[:], -8.0, None, ALU.mult, None, ["CST"], ["CST"])
    mk.release(m0)

    def make_ht(src, srckeys, HT, htkey, site, r, pb):
        for k in range(8):
            b_ = pb + k // 4
            tr(bank(b_, 128, 128 * (k % 4)), src[:, 128 * k:128 * (k + 1)], ident[:], srckeys + ["ident"],
               [("p", b_)])
        for k in range(8):
            hm = CFG.get("ht_mode", 3)
            if hm == 0 or (hm == 1 and k >= 4) or (hm == 2 and k < 4):
                continue
            b_ = pb + k // 4
            A_ = SCAL[:, site, r, 0, k:k + 1]
            B_ = SCAL[:, site, r, 1, k:k + 1]
            if k < 4:
                act(HT[:, k, :], bank(b_, 128, 128 * (k % 4)), AF.Identity, [("p", b_), "SCAL"],
                    [(htkey, k)], bias=B_, scale=A_)
            else:
                ts("dve", HT[:, k, :], bank(b_, 128, 128 * (k % 4)), A_, B_, ALU.mult, ALU.add,
                   [("p", b_), "SCAL"], [(htkey, k)])

    def epilogue(ps2, pskeys, Xsrc, GB, Gt, Bt, bkeys, site, r, xdst, hdst, E, i, pb):
        sl = i % 2
        sy = i % len(E["Y"])
        Xt, Y, HT = E["Xt"][sl], E["Y"][sy], E["HT"][sy]
        kx, ky, kh = "Xt%d" % sl, "Y%d" % sy, "HTe%d" % sy
        ld(Xt[:], Xsrc, [], [kx], "ex%d" % sl)
        tt("dve", Y[:], ps2, GB[:], ALU.mult, pskeys + bkeys, [ky])
        stt("dve", Y[:], Xt[:], ALPHA, Y[:], ALU.mult, ALU.add, [kx, ky], [ky])
        BS, MV, RS = E["BS"][sl], E["MV"][sl], E["RS"][sl]
        kb = "BS%d" % sl
        mk.op("dve", lambda e: e.bn_stats(out=BS[:, 0, :], in_=Y[:, 0:512]), [ky], [(kb, 0)])
        mk.op("dve", lambda e: e.bn_stats(out=BS[:, 1, :], in_=Y[:, 512:1024]), [ky], [(kb, 1)])
        mk.op("dve", lambda e: e.bn_aggr(out=MV[:], in_=BS[:]), [(kb, 0), (kb, 1)], [(kb, 2)])
        act(RS[:, 0:1], MV[:, 1:2], AF.Ln, [(kb, 2), "eps"], [(kb, 3)], bias=epsT[:, 0:1])
        act(RS[:, 1:2], RS[:, 0:1], AF.Exp, [(kb, 3)], [(kb, 4)], scale=-0.5)
        stt("dve", RS[:, 2:3], MV[:, 0:1], -1.0, RS[:, 1:2], ALU.mult, ALU.mult, [(kb, 2), (kb, 4)], [(kb, 5)])
        act(Y[:], Y[:], AF.Identity, [ky, (kb, 4), (kb, 5)], [ky], bias=RS[:, 2:3], scale=RS[:, 1:2])
        tt("pool", Xt[:], Y[:], Gt[:], ALU.mult, [ky] + bkeys, [kx])
        tt("pool", Xt[:], Xt[:], Bt[:], ALU.add, [kx] + bkeys, [kx])
        ld(xdst, Xt[:], [kx], [], "es%d" % sl)
        if hdst is not None:
            make_ht(Y, [ky], HT, kh, site, r, pb)
            ld(hdst, HT[:], [(kh, k) for k in range(8)], [], "eh%d" % sy)

    def epi_alloc(ny=2, nht=0):
        E = {}
        E["Xt"] = [mk.sb([128, D], F32, "Xt") for _ in range(2)]
        E["Y"] = [mk.sb([128, D], F32, "Y") for _ in range(ny)]
        E["HT"] = [mk.sb([128, 8, 128], BF16, "HTe") for _ in range(nht if nht else ny)]
        E["BS"] = [mk.sb([128, 2, 6], F32, "BS") for _ in range(2)]
        E["MV"] = [mk.sb([128, 2], F32, "MV") for _ in range(2)]
        E["RS"] = [mk.sb([128, 3], F32, "RS") for _ in range(2)]
        return E

    def load_bc(GB, Gt, Bt, l, goff, r, gname, bname):
        ld(GB[:], modd[l, r, goff:goff + D].partition_broadcast(128), [], ["GB"], "bc0")
        ld(Gt[:], ln_d[gname][l].partition_broadcast(128), [], ["Gt"], "bc1")
        ld(Bt[:], ln_d[bname][l].partition_broadcast(128), [], ["Bt"], "bc2")

    def phase_qkv():
        m = mk.mark()
        WIN = mk.sb([128, 8, 3 * D], BF16, "WIN")
        wv = awin_d[0].rearrange("(k p) n -> p k n", p=128)
        for k in range(8):
            if CFG.get("nowin"):
                break
            ldc([(WIN[:, k, :], wv[:, k, :])], [], [("WIN", k)], "w%d" % (k % 4))
        wink = [("WIN", k) for k in range(8)]
        XT = [mk.sb([128, D], F32, "XT") for _ in range(2)]
        RP = [mk.sb([128, 2, 64], F32, "RP") for _ in range(2)]
        HT = [mk.sb([128, 8, 128], BF16, "HT") for _ in range(2)]
        RQ = mk.sb([128, D], F32, "RQ")
        T1 = mk.sb([128, D], F32, "T1")
        T2 = mk.sb([128, D], F32, "T2")
        RB = mk.sb([128, D], F32, "RB")
        VV = [mk.sb([128, 2, 512], BF16, "VV") for _ in range(2)]
        TQ = [mk.sb([128, 8, 128], BF16, "TQ") for _ in range(2)]
        TB = [mk.sb([128, 8, 128], BF16, "TB") for _ in range(2)]
        tiles = [(b, i) for b in range(NBB) for i in range(34)]
        if "qkv_tiles" in CFG:
            tiles = tiles[:CFG["qkv_tiles"]]

        def src_of(b, i):
            if i < 32:
                return x_d[b, 128 * i:128 * (i + 1), :]
            return ctx_d[b, 128 * (i - 32):128 * (i - 31), :]

        def issue_load(n):
            b, i = tiles[n]
            sl = n % 2
            if not CFG.get("nox"):
                ld(XT[sl][:], src_of(b, i), [], ["XT%d" % sl], "xl%d" % sl)
            if i < 32 and not CFG.get("norp"):
                ld(RP[sl][:], rope_d[128 * i:128 * (i + 1)], [], ["RP%d" % sl], "rl%d" % sl)

        issue_load(0)
        for n, (b, i) in enumerate(tiles):
            sl = n % 2
            if n + 1 < len(tiles):
                issue_load(n + 1)
            lat = i < 32
            r = b if lat else 2
            tok0 = 128 * i if lat else T + 128 * (i - 32)
            htk = "HT%d" % sl
            if not CFG.get("noht"):
                make_ht(XT[sl], ["XT%d" % sl], HT[sl], htk, 0, r, 6)
            hk = [(htk, k) for k in range(8)]
            stg = CFG.get("qkv_stage", 9)
            if stg < 1:
                continue
            for j in range(6):
                for k in range(8):
                    mm(bank(j), HT[sl][:, k, :], WIN[:, k, 512 * j:512 * (j + 1)], k == 0, k == 7,
                       hk + wink, pk(j))
            if stg < 2:
                continue
            cp("act", RQ[:], PS[:, 0:1024], pk(0) + pk(1), ["RQ"])
            cp("dve", RB[:], PS[:, 1536:2560], pk(3) + pk(4), ["RB"])
            cp("act", VV[sl][:, 0, :], bank(2), pk(2), [("VV%d" % sl, 0)])
            cp("dve", VV[sl][:, 1, :], bank(5), pk(5), [("VV%d" % sl, 1)])
            if lat and not CFG.get("norope"):
                rp = RP[sl]
                rk = "RP%d" % sl
                tt("dve", T1[:].rearrange("p (g d) -> p g d", g=16), RQ[:].rearrange("p (g d) -> p g d", g=16),
                   rp[:, 0:1, :].to_broadcast([128, 16, 64]), ALU.mult, ["RQ", rk], ["T1"])
                rqv = RQ[:].rearrange("p (g a h f) -> p g a h f", g=16, a=2, h=2, f=16)
                t2v = T2[:].rearrange("p (g a h f) -> p g a h f", g=16, a=2, h=2, f=16)
                sv = rp[:, 1:2, :].rearrange("p o (a h f) -> p o a h f", a=2, h=2, f=16)
                tt("pool", t2v[:, :, :, 0, :], rqv[:, :, :, 1, :], sv[:, :, :, 0, :].to_broadcast([128, 16, 2, 16]),
                   ALU.mult, ["RQ", rk], [("T2", 0)])
                tt("pool", t2v[:, :, :, 1, :], rqv[:, :, :, 0, :], sv[:, :, :, 1, :].to_broadcast([128, 16, 2, 16]),
                   ALU.mult, ["RQ", rk], [("T2", 1)])
                tt("dve", T1[:], T1[:], T2[:], ALU.add, ["T1", ("T2", 0), ("T2", 1)], ["T1"])
                qsrc, qk = T1, ["T1"]
            else:
                qsrc, qk = RQ, ["RQ"]
            if stg < 3:
                continue
            for (src, sk, dst, dk) in ((qsrc, qk, TQ[sl], "TQ%d" % sl), (RB, ["RB"], TB[sl], "TB%d" % sl)):
                for k in range(8):
                    b_ = 6 + k // 4
                    tr(bank(b_, 128, 128 * (k % 4)), src[:, 128 * k:128 * (k + 1)], ident[:], sk + ["ident"],
                       [("p", b_)])
                act(dst[:, 0:4, :].rearrange("p k t -> p (k t)"), bank(6), AF.Identity,
                    [("p", 6)], [(dk, 0)], scale=0.125)
                cp("dve", dst[:, 4:8, :].rearrange("p k t -> p (k t)"), bank(7),
                   [("p", 7)], [(dk, 1)])
            if stg < 4:
                continue
            ld(vas[b][tok0:tok0 + 128, :], VV[sl][:, 0, :], [("VV%d" % sl, 0)], [], "sva%d" % sl)
            ld(vbs[b][tok0:tok0 + 128, :], VV[sl][:, 1, :], [("VV%d" % sl, 1)], [], "svb%d" % sl)
            for si_, (dst_d, tile_, kk, half) in enumerate(((qat, TQ, "TQ%d" % sl, 0), (kat, TQ, "TQ%d" % sl, 1),
                                                          (qbt, TB, "TB%d" % sl, 0), (kbt, TB, "TB%d" % sl, 1))):
                ld(dst_d[b].rearrange("h p t -> p h t")[:, :, tok0:tok0 + 128],
                   tile_[sl][:, 4 * half:4 * half + 4, :], [(kk, half)], [], "sq%d%d" % (si_, sl))
        mk.release(m)

    def phase_diff():
        m = mk.mark()
        KT = [mk.sb([128, TK], BF16, "KT") for _ in range(2)]
        QT = [mk.sb([128, TK], BF16, "QT") for _ in range(2)]
        V1 = [mk.sb([128, 34, 129], BF16, "V1") for _ in range(2)]
        NPT = 3
        PT = [mk.sb([128, 2, 512], BF16, "PT") for _ in range(NPT)]
        AQ = mk.sb([128, 4, 128], F32, "AQ")
        OO = mk.sb([128, 4, 128], F32, "OO")
        SQ = mk.sb([128, 4, 128], F32, "SQ")
        RC = mk.sb([128, 4, 2], F32, "RC")
        T1 = mk.sb([128, 4, 1], F32, "T1")
        SS = mk.sb([128, 4, 1], F32, "SS")
        G3 = mk.sb([128, 1, 128], F32, "G3")
        cp("pool", G3[:, 0, :], GSUB[:], ["GSUB"], ["G3"])
        AOB = [mk.sb([128, 4, 128], F32, "AOB") for _ in range(2)]
        for i in range(2):
            memset("pool", V1[i][:, :, 128:129], 1.0, [("V1%d" % i, "one")])
        ACC = PS[:, 2048:4096].rearrange("p (b c) -> p b c", b=4)
        nblk = 0
        hh = 0
        npt = 0
        for b in range(NBB):
            for h in range(4):
                sl = hh % 2
                hh += 1
                ld(KT[sl][:], kat[b][h], [], ["KT%d" % sl], "ak%d" % sl)
                ld(QT[sl][:], qat[b][h], [], ["QT%d" % sl], "aq%d" % sl)
                ld(V1[sl][:, :, 0:128], vas[b].rearrange("(t p) c -> p t c", p=128)[:, :, 128 * h:128 * (h + 1)],
                   [], ["V1%d" % sl], "av%d" % sl)
                vkeys = ["V1%d" % sl, ("V1%d" % sl, "one")]
                blocks = [(512 * j, 512, list(range(34))) for j in range(8)] + [(T, 256, [32, 33])]
                for (q0, nq, kts) in blocks:
                    nqt = nq // 128
                    acck = [("acc", q) for q in range(nqt)]
                    memset("dve", ACC[:, 0:nqt, 0:258], 0.0, acck)

                    def qk(it):
                        kt = kts[it]
                        sp_ = it % 2
                        for mp in range(2):
                            bk = 2 * sp_ + mp
                            mm(bank(bk, nq), KT[sl][64 * mp:64 * (mp + 1), 128 * kt:128 * (kt + 1)],
                               QT[sl][64 * mp:64 * (mp + 1), q0:q0 + nq], True, True,
                               ["KT%d" % sl, "QT%d" % sl], pk(bk))

                    qk(0)
                    for it, kt in enumerate(kts):
                        if it + 1 < len(kts):
                            qk(it + 1)
                        sp_ = it % 2
                        pb_ = npt % NPT
                        npt += 1
                        sview = PS[:, 1024 * sp_:1024 * sp_ + 1024].rearrange("p (b c) -> p b c", b=2)[:, :, 0:nq]
                        act(PT[pb_][:, :, 0:nq], sview, AF.Exp, pk(2 * sp_) + pk(2 * sp_ + 1), ["PT%d" % pb_])
                        for mp in range(2):
                            for qt in range(nqt):
                                mm(bank(4 + qt, 129, 129 * mp), PT[pb_][:, mp, 128 * qt:128 * (qt + 1)],
                                   V1[sl][:, kt, :], False, False, ["PT%d" % pb_] + vkeys + [("acc", qt)],
                                   [("acc", qt)], skip=True)
                    ob = AOB[nblk % 2]
                    okey = "AOB%d" % (nblk % 2)
                    nblk += 1
                    mk.op("dve", lambda e, nqt=nqt: e.reciprocal(out=RC[:, 0:nqt, :], in_=ACC[:, 0:nqt, 128:258:129]),
                          acck, ["RC"])
                    ts("dve", T1[:, 0:nqt, :], RC[:, 0:nqt, 1:2], NLAM[:, 0:1], None, ALU.mult, None, ["RC", "NLAM"], ["T1"])
                    tt("dve", AQ[:, 0:nqt, :], ACC[:, 0:nqt, 0:128], RC[:, 0:nqt, 0:1].to_broadcast([128, nqt, 128]),
                       ALU.mult, acck + ["RC"], ["AQ"])
                    tt("dve", SQ[:, 0:nqt, :], ACC[:, 0:nqt, 129:257], T1[:, 0:nqt, :].to_broadcast([128, nqt, 128]),
                       ALU.mult, acck + ["T1"], ["SQ"])
                    tt("dve", OO[:, 0:nqt, :], AQ[:, 0:nqt, :], SQ[:, 0:nqt, :], ALU.add, ["AQ", "SQ"], ["OO"])
                    tt("pool", SQ[:, 0:nqt, :], OO[:, 0:nqt, :], OO[:, 0:nqt, :], ALU.mult, ["OO"], ["SQ"])
                    mk.op("dve", lambda e, nqt=nqt: e.reduce_sum(out=SS[:, 0:nqt, :], in_=SQ[:, 0:nqt, :],
                                                                 axis=mybir.AxisListType.X), ["SQ"], ["SS"])
                    act(SS[:, 0:nqt, :], SS[:, 0:nqt, :], AF.Ln, ["SS", "eps"], ["SS"], bias=epsT[:, 0:1], scale=1.0 / 128)
                    act(SS[:, 0:nqt, :], SS[:, 0:nqt, :], AF.Exp, ["SS"], ["SS"], scale=-0.5)
                    tt("pool", OO[:, 0:nqt, :], OO[:, 0:nqt, :], SS[:, 0:nqt, :].to_broadcast([128, nqt, 128]),
                       ALU.mult, ["OO", "SS"], ["OO"])
                    tt("pool", ob[:, 0:nqt, :], OO[:, 0:nqt, :], G3[:].to_broadcast([128, nqt, 128]), ALU.mult,
                       ["OO", "G3"], [okey])
                    ld(aos[b][q0:q0 + nq, 128 * h:128 * (h + 1)].rearrange("(q p) c -> p q c", p=128),
                       ob[:, 0:nqt, :], [okey], [], "sa%d" % (nblk % 2))
        mk.release(m)

    def phase_nbr():
        m = mk.mark()
        QB2 = mk.sb([128, 4, TK], BF16, "QB2")
        KB2 = mk.sb([128, 4, TK], BF16, "KB2")
        VBe = mk.sb([128, 34, 8, 65], BF16, "VBe")
        VBo = mk.sb([128, 31, 8, 65], BF16, "VBo")
        BI = [mk.sb([128, 4, 8, 64], F32, "BI") for _ in range(2)]
        MK2 = mk.sb([128, 1, 64], F32, "MK2")
        SBF = [mk.sb([128, 512], F32, "SBF") for _ in range(2)]
        PT = [mk.sb([128, 6, 512], BF16, "PTn") for _ in range(2)]
        RI = mk.sb([64, 8, 1], F32, "RI")
        OB = [mk.sb([64, 512], F32, "OB") for _ in range(2)]
        memset("pool", VBe[:, :, :, 64:65], 1.0, [("VBe", "one")])
        memset("pool", VBo[:, :, :, 64:65], 1.0, [("VBo", "one")])
        ld(MK2[:, 0, :], mask_d, [], ["MK2"], "nm")

        def gen_bias(dst, key, delta):
            for j in range(4):
                for i2 in range(2):
                    rr = delta + 2 * j + i2 + 7
                    ld(dst[64 * i2:64 * (i2 + 1), j, :, :], rpbt_d[rr], [], [(key, j, i2)], "nb%d" % i2)
                tt("dve", dst[:, j, :, :], dst[:, j, :, :], MK2[:].to_broadcast([128, 8, 64]), ALU.add,
                   [(key, j, 0), (key, j, 1), "MK2"], [(key, j)])

        gen_bias(BI[0], "BI0", -4)
        nrow = 0
        for b in range(NBB):
            ld(QB2[:], qbt[b].rearrange("h p t -> p h t"), [], ["QB2"], "nq")
            ld(KB2[:], kbt[b].rearrange("h p t -> p h t"), [], ["KB2"], "nk")
            vb_v = vbs[b].rearrange("(t p) (h d) -> p t h d", p=128, h=8)
            for t4 in range(0, 34, 6):
                t5 = min(34, t4 + 6)
                mk.dma("sp", [(VBe[:, t_, :, 0:64], vb_v[:, t_]) for t_ in range(t4, t5)], [], [("VBe", t4)], "nv")
            vbo_v = vbs[b][64:64 + 31 * 128, :].rearrange("(t p) (h d) -> p t h d", p=128, h=8)
            for t4 in range(0, 31, 6):
                t5 = min(31, t4 + 6)
                mk.dma("sp", [(VBo[:, t_, :, 0:64], vbo_v[:, t_]) for t_ in range(t4, t5)], [], [("VBo", t4)], "nv")
            vek = [("VBe", t4) for t4 in range(0, 34, 6)] + [("VBe", "one")]
            vok = [("VBo", t4) for t4 in range(0, 31, 6)] + [("VBo", "one")]
            rows = [("lat", r) for r in range(64)] + [("ctx", g) for g in range(4)]
            if "nbr_rows" in CFG:
                rows = rows[:CFG["nbr_rows"]]
            nst = CFG.get("nbr_stage", 9)
            for (kind, r) in rows:
                par = nrow % 2
                nrow += 1
                if kind == "lat":
                    rs = min(max(r - 4, 0), 56)
                    delta = rs - r
                    if delta == -4:
                        bi, bik = BI[0], "BI0"
                    else:
                        bi, bik = BI[1], "BI1"
                        gen_bias(BI[1], "BI1", delta)
                    q0 = 64 * r
                    kts = []
                    for j in range(4):
                        ks = 64 * (rs + 2 * j)
                        if rs % 2 == 0:
                            kts.append((ks, VBe, (rs + 2 * j) // 2, vek, j))
                        else:
                            kts.append((ks, VBo, (rs + 2 * j - 1) // 2, vok, j))
                    kts.append((T, VBe, 32, vek, None))
                    kts.append((T + 128, VBe, 33, vek, None))
                else:
                    q0 = T + 64 * r
                    kts = [(T, VBe, 32, vek, None), (T + 128, VBe, 33, vek, None)]
                pt = PT[par]
                ptk = "PTn%d" % par
                for n, (ks, vt, vi, vk, j) in enumerate(kts):
                    pp = n % 2
                    for h in range(8):
                        hp, hq = h % 2, h // 2
                        mm(bank(2 * pp + hp, 64, 64 * hq), KB2[64 * hp:64 * (hp + 1), hq, ks:ks + 128],
                           QB2[64 * hp:64 * (hp + 1), hq, q0:q0 + 64], True, True, ["KB2", "QB2"], pk(2 * pp + hp))
                    s2 = PS[:, 1024 * pp:1024 * pp + 1024].rearrange("p (b c) -> p b c", b=2)[:, :, 0:256]
                    sk2 = pk(2 * pp) + pk(2 * pp + 1)
                    if j is not None:
                        sb_ = SBF[n % 2]
                        tt("dve", sb_[:].rearrange("p (b c) -> p b c", b=2), s2,
                           bi[:, j, :, :].rearrange("p (b h) c -> p b (h c)", b=2), ALU.add,
                           sk2 + [(bik, j)], ["SBF%d" % (n % 2)])
                        act(pt[:, n, :], sb_[:], AF.Exp, ["SBF%d" % (n % 2)], [(ptk, n)])
                    else:
                        act(pt[:, n, :].rearrange("p (b c) -> p b c", b=2), s2, AF.Exp, sk2, [(ptk, n)])
                if nst < 2:
                    continue
                ab = 4 + 2 * par
                for h in range(8):
                    b_ = ab + h // 4
                    for n, (ks, vt, vi, vk, j) in enumerate(kts):
                        mm(PS[0:64, 512 * b_ + 65 * (h % 4):512 * b_ + 65 * (h % 4) + 65],
                           pt[:, n, 64 * ((h % 2) * 4 + h // 2):64 * ((h % 2) * 4 + h // 2 + 1)], vt[:, vi, h, :],
                           n == 0, n == len(kts) - 1,
                           [(ptk, n)] + vk, pk(b_))
                if nst < 3:
                    continue
                ob = OB[par]
                obk = "OB%d" % par
                for g in range(2):
                    b_ = ab + g
                    accv = PS[0:64, 512 * b_:512 * b_ + 260].rearrange("p (h c) -> p h c", h=4)
                    mk.op("dve", lambda e, accv=accv, g=g: e.reciprocal(out=RI[:, 4 * g:4 * g + 4, :], in_=accv[:, :, 64:65]),
                          pk(b_), [("RI", g)])
                    tt("dve", ob[:, 256 * g:256 * (g + 1)].rearrange("p (h d) -> p h d", h=4), accv[:, :, 0:64],
                       RI[:, 4 * g:4 * g + 4, :].to_broadcast([64, 4, 64]), ALU.mult, pk(b_) + [("RI", g)], [(obk, g)])
                ld(aos[b][q0:q0 + 64, 512:1024], ob[:], [(obk, 0), (obk, 1)], [], "no%d" % par)
        mk.release(m)

    def phase_oproj(l):
        m = mk.mark()
        WO = mk.sb([128, 8, D], BF16, "WO")
        wsrc = awout_d[0] if l == 0 else rwout_d[0]
        ldc([(WO[:], wsrc.rearrange("(k p) n -> p k n", p=128))], [], ["WO"], "w0")
        GB = mk.sb([128, D], F32, "GB")
        Gt = mk.sb([128, D], F32, "Gt")
        Bt = mk.sb([128, D], F32, "Bt")
        E = epi_alloc()
        bkeys = ["GB", "Gt", "Bt"]
        site = 1 if l == 0 else 3
        seqs = SEQ if l == 0 else [s for s in SEQ if s[0] == "lat"]
        if l == 0:
            AO = [mk.sb([128, D], F32, "AO") for _ in range(3)]
            AT = [mk.sb([128, 8, 128], BF16, "AT") for _ in range(2)]
        else:
            MB = [mk.sb([128, 8, 512], BF16, "MB") for _ in range(2)]
        X3 = [mk.sb([128, D], F32, "X3") for _ in range(3)]
        n0 = 0
        for s in seqs:
            r = seq_var(s)
            load_bc(GB, Gt, Bt, l, 2 * D, r, "ln1_g", "ln1_b")
            b = s[1]
            L = seq_len(s)
            NT = L // 128
            base = 0 if s[0] == "lat" else T
            xsrc = seq_src(s) if l == 0 else xs[s]
            hdst_all = hts[(s, 0)].rearrange("k p t -> p k t")

            def stageL(i):
                n = n0 + i
                s3 = n % 3
                ld(X3[s3][:], xsrc[128 * i:128 * (i + 1), :], [], ["X3%d" % s3], "ex%d" % s3)
                if l == 0:
                    ld(AO[s3][:], aos[b][base + 128 * i:base + 128 * (i + 1), :], [], ["AO%d" % s3], "ol%d" % s3)
                elif i % 4 == 0:
                    ms = (i // 4) % 2
                    ld(MB[ms][:], mts[b].rearrange("k p t -> p k t")[:, :, 128 * i:128 * i + 512], [],
                       ["MB%d" % ms], "om%d" % ms)

            def stageT(i):
                n = n0 + i
                sl = n % 2
                s3 = n % 3
                if l == 0:
                    for k in range(8):
                        tr(bank(4 + k // 4, 128, 128 * (k % 4)), AO[s3][:, 128 * k:128 * (k + 1)], ident[:],
                           ["AO%d" % s3, "ident"], pk(4 + k // 4))
                    cp("act", AT[sl][:, 0:4, :].rearrange("p k t -> p (k t)"), bank(4), pk(4), [("AT%d" % sl, 0)])
                    cp("dve", AT[sl][:, 4:8, :].rearrange("p k t -> p (k t)"), bank(5), pk(5), [("AT%d" % sl, 1)])

            def stageM(i):
                n = n0 + i
                sl = n % 2
                pa = 2 * sl
                for k in range(8):
                    if l == 0:
                        lhs, lk = AT[sl][:, k, :], [("AT%d" % sl, 0), ("AT%d" % sl, 1)]
                    else:
                        ms = (i // 4) % 2
                        lhs, lk = MB[ms][:, k, 128 * (i % 4):128 * (i % 4 + 1)], ["MB%d" % ms]
                    for hf in range(2):
                        mm(bank(pa + hf), lhs, WO[:, k, 512 * hf:512 * (hf + 1)], k == 0, k == 7, lk + ["WO"], pk(pa + hf))

            def stageA(i):
                n = n0 + i
                sl = n % 2
                pa = 2 * sl
                Xt, Y = E["Xt"][sl], E["Y"][sl]
                kx, ky = "Xt%d" % sl, "Y%d" % sl
                s3 = n % 3
                tt("dve", Y[:], PS[:, 512 * pa:512 * pa + 1024], GB[:], ALU.mult, pk(pa) + pk(pa + 1) + ["GB"], [ky])
                stt("dve", Y[:], X3[s3][:], ALPHA, Y[:], ALU.mult, ALU.add, ["X3%d" % s3, ky], [ky])
                BS, MV, RS = E["BS"][sl], E["MV"][sl], E["RS"][sl]
                kb = "BS%d" % sl
                mk.op("dve", lambda e: e.bn_stats(out=BS[:, 0, :], in_=Y[:, 0:512]), [ky], [(kb, 0)])
                mk.op("dve", lambda e: e.bn_stats(out=BS[:, 1, :], in_=Y[:, 512:1024]), [ky], [(kb, 1)])
                mk.op("dve", lambda e: e.bn_aggr(out=MV[:], in_=BS[:]), [(kb, 0), (kb, 1)], [(kb, 2)])
                act(RS[:, 0:1], MV[:, 1:2], AF.Ln, [(kb, 2), "eps"], [(kb, 3)], bias=epsT[:, 0:1])
                act(RS[:, 1:2], RS[:, 0:1], AF.Exp, [(kb, 3)], [(kb, 4)], scale=-0.5)
                stt("dve", RS[:, 2:3], MV[:, 0:1], -1.0, RS[:, 1:2], ALU.mult, ALU.mult, [(kb, 2), (kb, 4)], [(kb, 5)])
                act(Y[:], Y[:], AF.Identity, [ky, (kb, 4), (kb, 5)], [ky], bias=RS[:, 2:3], scale=RS[:, 1:2])
                tt("pool", Xt[:], Y[:], Gt[:], ALU.mult, [ky, "Gt"], [kx])
                tt("pool", Xt[:], Xt[:], Bt[:], ALU.add, [kx, "Bt"], [kx])
                mk.dma("pool", [(xs[s][128 * i:128 * (i + 1), :], Xt[:])], [kx], [], "pes%d" % sl)

            def stageB(i):
                n = n0 + i
                sl = n % 2
                make_ht(E["Y"][sl], ["Y%d" % sl], E["HT"][sl], "HTe%d" % sl, site, r, 6)
                mk.dma("pool", [(hdst_all[:, :, 1 + 128 * i:1 + 128 * (i + 1)], E["HT"][sl][:])],
                       [("HTe%d" % sl, k) for k in range(8)], [], "peh%d" % sl)

            stageL(0)
            if NT > 1:
                stageL(1)
            stageT(0)
            for i in range(NT):
                if i + 2 < NT:
                    stageL(i + 2)
                if i + 1 < NT:
                    stageT(i + 1)
                stageM(i)
                stageA(i)
                if i >= 1:
                    stageB(i - 1)
            stageB(NT - 1)
            n0 += NT
        mk.release(m)

    def phase_ffn(l):
        m = mk.mark()
        WU = mk.sb([128, 8, 2 * DFF], BF16, "WU")
        WD = mk.sb([128, 22, D], BF16, "WD")
        wv = wup_d[l].rearrange("(k p) n -> p k n", p=128)
        for k in range(8):
            ldc([(WU[:, k, :], wv[:, k, :])], [], [("WU", k)], "w%d" % (k % 4))
        wdv = wdn_d[l].rearrange("(k p) n -> p k n", p=128)
        wuk = [("WU", k) for k in range(8)]
        WDG = [(k0, min(22, k0 + 6)) for k0 in range(0, 22, 6)]
        wdk = [("WD", k0) for (k0, _) in WDG]
        Gt = mk.sb([128, D], F32, "Gt")
        Bt = mk.sb([128, D], F32, "Bt")
        E = epi_alloc(2, 1)
        HB = mk.sb([128, 8, 514], BF16, "HB")
        UW = mk.sb([128, 44, 8, 2], F32, "UW")
        C = [mk.sb([128, 512], F32, "C") for _ in range(3)]
        AC = mk.sb([128, 22, 512], BF16, "AC")
        UH = mk.sb([128, 44, 18], F32, "UH")
        HS = mk.sb([128, 8, 16], BF16, "HS")
        US = [C[0], C[1]]
        memset("pool", UH[:], 0.0, ["UH"])
        memset("pool", HS[:], 0.0, ["HS"])
        last = l == 1
        site = 2 if l == 0 else None
        seqs = SEQ if l == 0 else [s for s in SEQ if s[0] == "lat"]
        fw = [VMAP[("fcw", l, k)] for k in range(3)]
        fb = VMAP[("fcb", l)]
        MB = [0, 1, 2, 7]
        bkeys = ["Gt", "Bt"]
        n = 0
        uc = 0
        for s in seqs:
            r = seq_var(s)
            L = seq_len(s)
            hsrc = hts[(s, 0)].rearrange("k p t -> p k t")
            hdst_all = hts[(s, 1)].rearrange("k p t -> p k t")
            nb = min(512, L)
            nblk = L // nb
            GBt = E["Y"][0]
            ld(GBt[:], modd[l, r, 5 * D:6 * D].partition_broadcast(128), [], ["Y0"], "bc0")
            ld(Gt[:], ln_d["ln2_g"][l].partition_broadcast(128), [], ["Gt"], "bc1")
            ld(Bt[:], ln_d["ln2_b"][l].partition_broadcast(128), [], ["Bt"], "bc2")
            for gi, (k0, k1) in enumerate(WDG):
                ldc([(WD[:, k0:k1, :], wdv[:, k0:k1, :])], [], [("WD", k0)], "w%d" % (gi % 4))
                for kk in range(k0, k1):
                    tt("pool" if kk % 2 else "dve", WD[:, kk, :], WD[:, kk, :], GBt[:], ALU.mult, [("WD", k0), "Y0"], [("WD", k0)])
            if nblk > 1:
                for bd in range(1, nblk):
                    mk.dma("sp", [(HS[:, :, 2 * bd:2 * bd + 2], hsrc[:, :, 512 * bd:512 * bd + 2])], [], ["HS"], "fz", slow=True)
                for cb in range(11):
                    for k in range(8):
                        mm(PS[0:16, 512 * 3:512 * 3 + 512], HS[:, k, :], WU[:, k, 512 * cb:512 * (cb + 1)], k == 0, k == 7,
                           ["HS"] + wuk, pk(3))
                    cp("act", US[cb % 2][0:16, :], PS[0:16, 512 * 3:512 * 3 + 512], pk(3), ["C%d" % (cb % 2)])
                    for q in range(4):
                        c = 4 * cb + q
                        b_ = 5 + c // 22
                        tr(PS[:, 512 * b_ + 16 * (c % 22):512 * b_ + 16 * (c % 22) + 16], US[cb % 2][0:16, 128 * q:128 * (q + 1)],
                           ident[0:16, 0:16], ["C%d" % (cb % 2), "ident"], pk(b_))
                cp("dve", UH[:, 0:22, 0:16], PS[:, 512 * 5:512 * 5 + 352].rearrange("p (c e) -> p c e", e=16), pk(5), ["UH"])
                cp("dve", UH[:, 22:44, 0:16], PS[:, 512 * 6:512 * 6 + 352].rearrange("p (c e) -> p c e", e=16), pk(6), ["UH"])
            else:
                memset("pool", UH[:], 0.0, ["UH"])

            w0b = VEC[:, fw[0]:fw[0] + 44].rearrange("p (c o) -> p c o", o=1).to_broadcast([128, 44, 8])
            w2b = VEC[:, fw[2]:fw[2] + 44].rearrange("p (c o) -> p c o", o=1).to_broadcast([128, 44, 8])
            tt("dve", UW[:, :, :, 0], UH[:, :, 0:15:2], w0b, ALU.mult, ["UH", "VEC"], [("UW", 0)])
            tt("dve", UW[:, :, :, 1], UH[:, :, 3:18:2], w2b, ALU.mult, ["UH", "VEC"], [("UW", 1)])
            uwk = [("UW", 0), ("UW", 1)]

            def xload(nn, tok):
                sl = nn % 2
                ld(E["Xt"][sl][:], xs[s][tok:tok + 128, :], [], ["Xt%d" % sl], "ex%d" % sl)

            def epiA(nn, ps2, pskeys, xdst):
                sl = nn % 2
                Xt, Y = E["Xt"][sl], E["Y"][sl]
                kx, ky = "Xt%d" % sl, "Y%d" % sl
                stt("dve", Y[:], Xt[:], ALPHA, ps2, ALU.mult, ALU.add, [kx] + pskeys, [ky])
                BS, MV, RS = E["BS"][sl], E["MV"][sl], E["RS"][sl]
                kb = "BS%d" % sl
                mk.op("dve", lambda e: e.bn_stats(out=BS[:, 0, :], in_=Y[:, 0:512]), [ky], [(kb, 0)])
                mk.op("dve", lambda e: e.bn_stats(out=BS[:, 1, :], in_=Y[:, 512:1024]), [ky], [(kb, 1)])
                mk.op("dve", lambda e: e.bn_aggr(out=MV[:], in_=BS[:]), [(kb, 0), (kb, 1)], [(kb, 2)])
                act(RS[:, 0:1], MV[:, 1:2], AF.Ln, [(kb, 2), "eps"], [(kb, 3)], bias=epsT[:, 0:1])
                act(RS[:, 1:2], RS[:, 0:1], AF.Exp, [(kb, 3)], [(kb, 4)], scale=-0.5)
                stt("dve", RS[:, 2:3], MV[:, 0:1], -1.0, RS[:, 1:2], ALU.mult, ALU.mult, [(kb, 2), (kb, 4)], [(kb, 5)])
                act(Y[:], Y[:], AF.Identity, [ky, (kb, 4), (kb, 5)], [ky], bias=RS[:, 2:3], scale=RS[:, 1:2])
                tt("pool", Xt[:], Y[:], Gt[:], ALU.mult, [ky] + bkeys, [kx])
                tt("pool", Xt[:], Xt[:], Bt[:], ALU.add, [kx] + bkeys, [kx])
                ld(xdst, Xt[:], [kx], [], "es%d" % sl)

            def epiB(nn, hdst):
                sl = nn % 2
                make_ht(E["Y"][sl], ["Y%d" % sl], E["HT"][0], "HTe0", site, r, 5)
                ld(hdst, E["HT"][0][:], [("HTe0", k) for k in range(8)], [], "eh0")

            def tail(pend):
                i_, cg, cgk, cv, cvk = pend
                act(cg[:, 0:nb], cg[:, 0:nb], AF.Gelu_apprx_tanh, [cgk], [cgk])
                tt("pool", AC[:, i_, 0:nb], cg[:, 0:nb], cv[:, 0:nb], ALU.mult, [cgk, cvk], [("AC", i_)])

            for j in range(nblk):
                t0 = nb * j
                if j == 0:
                    ld(HB[:, :, 0:nb + 2], hsrc[:, :, t0:t0 + nb + 2], [], ["HB"], "fh")
                xload(n, t0)
                pend = None
                for i in range(22):
                    cur = []
                    for gv in range(2):
                        c = i + 22 * gv
                        u = uc % 4
                        cb_ = uc % 3
                        uc += 1
                        mb = MB[u]
                        for k in range(8):
                            mm(bank(mb, nb), WU[:, k, 128 * c:128 * (c + 1)], HB[:, k, 1:nb + 1], k == 0, k == 7,
                               wuk + ["HB"], pk(mb))
                        Ct = C[cb_]
                        ck = "C%d" % cb_
                        act(Ct[:, 0:nb], bank(mb, nb), AF.Identity, pk(mb) + ["VEC"], [ck],
                            bias=VEC[:, fb + c:fb + c + 1], scale=VEC[:, fw[1] + c:fw[1] + c + 1])
                        tt("pool", Ct[:, 0:nb:nb - 1], Ct[:, 0:nb:nb - 1], UW[:, c, j, :], ALU.add, uwk + [ck], [ck])
                        stt("dve", Ct[:, 1:nb], bank(mb, nb - 1), VEC[:, fw[0] + c:fw[0] + c + 1], Ct[:, 1:nb], ALU.mult, ALU.add,
                            pk(mb) + ["VEC", ck], [ck])
                        stt("dve", Ct[:, 0:nb - 1], bank(mb, nb - 1, 1), VEC[:, fw[2] + c:fw[2] + c + 1], Ct[:, 0:nb - 1], ALU.mult, ALU.add,
                            pk(mb) + ["VEC", ck], [ck])
                        cur += [Ct, ck]
                        if gv == 0 and pend is not None:
                            tail(pend)
                            pend = None
                    pend = (i, cur[0], cur[1], cur[2], cur[3])
                tail(pend)
                if j + 1 < nblk:
                    ld(HB[:, :, 0:nb + 2], hsrc[:, :, t0 + nb:t0 + 2 * nb + 2], [], ["HB"], "fh")
                ack = [("AC", i) for i in range(22)]
                nt_ = nb // 128
                prevB = None
                for tq in range(nt_):
                    tok = t0 + 128 * tq
                    for k in range(22):
                        for hf in range(2):
                            mm(bank(3 + hf), AC[:, k, 128 * tq:128 * (tq + 1)], WD[:, k, 512 * hf:512 * (hf + 1)],
                               k == 0, k == 21, [("AC", k)] + wdk, pk(3 + hf))
                    if tq + 1 < nt_:
                        xload(n + 1, tok + 128)
                    if last:
                        xdst = out_d[s[1], tok:tok + 128, :]
                    else:
                        xdst = xs[s][tok:tok + 128, :]
                    epiA(n, PS[:, 512 * 3:512 * 3 + 1024], pk(3) + pk(4), xdst)
                    if prevB is not None:
                        epiB(*prevB)
                        prevB = None
                    if not last:
                        prevB = (n, hdst_all[:, :, 1 + tok:1 + tok + 128])
                    n += 1
                if prevB is not None:
                    epiB(*prevB)
        mk.release(m)

    def phase_rglru():
        m = mk.mark()
        CX0, LX0, NX = 2, 261, 4358
        XR = mk.sb([128, NX], F32, "XR")
        XC = mk.sb([128, NX], F32, "XC")
        XCb = mk.sb([128, NX], BF16, "XCb")
        A_ = mk.sb([128, NX], F32, "A")
        BT_ = mk.sb([128, NX], F32, "BT")
        TM = mk.sb([128, NX], F32, "TM")
        RF = mk.sb([128, NX], F32, "RF")
        RR = mk.sb([128, NX], F32, "RR")
        GY = mk.sb([128, T], BF16, "GY")
        MT = mk.sb([128, T], BF16, "MT")
        WI = mk.sb([128, 8, 2, 128], BF16, "WI")
        GW = mk.sb([128, 4, 128], BF16, "GW")
        NHB = 4
        HB = [mk.sb([128, 8, 512], BF16, "HBr") for _ in range(NHB)]
        memset("dve", XR[:], 0.0, ["XRz"])
        wv = rwin_d[0].rearrange("(k p) n -> p k n", p=128)
        nh = 0
        regions = [(CX0, TC), (LX0, T)]
        BLOCKS = [("ctx", 0, 256, CX0)] + [("lat", 512 * j, 512, LX0 + 512 * j) for j in range(8)]
        items = [(b, c, bi) for b in range(NBB) for c in range(8) for bi in range(9)]
        issued = [0]

        def prefetch(upto):
            while issued[0] <= min(upto, len(items) - 1):
                b_, c_, bi_ = items[issued[0]]
                kind_, t0_, nt_, xo_ = BLOCKS[bi_]
                hb = issued[0] % NHB
                src = hts[((kind_, b_), 1)].rearrange("k p t -> p k t")
                ld(HB[hb][:, :, 0:nt_], src[:, :, 1 + t0_:1 + t0_ + nt_], [], ["HBr%d" % hb], "rh%d" % hb)
                issued[0] += 1

        for b in range(NBB):
            blocks = BLOCKS
            for c in range(8):
                ldc([(WI[:, :, 0, :], wv[:, :, 128 * c:128 * (c + 1)]),
                     (WI[:, :, 1, :], wv[:, :, D + 128 * c:D + 128 * (c + 1)])], [], ["WI"], "w0")
                ldc([(GW[:, 0, :], rga_d[0, 0, c]), (GW[:, 1, :], rgx_d[0, 0, c]),
                     (GW[:, 2, :], rga_d[0, 1, c]), (GW[:, 3, :], rgx_d[0, 1, c])], [], ["GW"], "w1")
                for (kind, t0, nt, xo) in blocks:
                    prefetch(nh + NHB - 1)
                    sl = nh % NHB
                    bk = 2 * (nh % 2)
                    nh += 1
                    for k in range(8):
                        mm(bank(bk, nt), WI[:, k, 1, :], HB[sl][:, k, 0:nt], k == 0, k == 7, ["WI", "HBr%d" % sl], pk(bk))
                    cp("act", XR[:, xo:xo + nt], bank(bk, nt), pk(bk) + ["XRz"], [("XR", xo)])
                    if kind == "lat":
                        for k in range(8):
                            mm(bank(bk + 1, nt), WI[:, k, 0, :], HB[sl][:, k, 0:nt], k == 0, k == 7,
                               ["WI", "HBr%d" % sl], pk((bk + 1)))
                        act(GY[:, t0:t0 + nt], bank(bk + 1, nt), AF.Gelu_apprx_tanh, pk((bk + 1)), [("GY", t0)])
                xrk = [("XR", xo) for (_, _, _, xo) in blocks] + ["XRz"]
                w = [VMAP[("rcw", k)] + c for k in range(4)]
                cb = VMAP[("rcb",)] + c
                for (o, n_) in regions:
                    ce = "dve"
                    ts("pool", XC[:, o:o + n_], XR[:, o:o + n_], VEC[:, w[2]:w[2] + 1], VEC[:, cb:cb + 1], ALU.mult, ALU.add,
                       xrk + ["VEC"], [("XC", o)])
                    for (kk, sh) in ((0, -2), (1, -1), (3, 1)):
                        stt(ce, XC[:, o:o + n_], XR[:, o + sh:o + sh + n_], VEC[:, w[kk]:w[kk] + 1], XC[:, o:o + n_],
                            ALU.mult, ALU.add, xrk + ["VEC", ("XC", o)], [("XC", o)])
                    cp("pool" if n_ == T else "act", XCb[:, o:o + n_], XC[:, o:o + n_], [("XC", o)], [("XCb", o)])
                xck = [("XC", o) for (o, _) in regions]
                xcbk = [("XCb", o) for (o, _) in regions]
                for d in range(2):
                    res = RF if d == 0 else RR
                    resk = "RF" if d == 0 else "RR"
                    ba = VMAP[("rba", d)] + c
                    bx = VMAP[("rbx", d)] + c
                    gi = 0
                    for (kind, t0, nt, xo) in blocks:
                        bk = 4 + 2 * (gi % 2)
                        gi += 1
                        mm(bank(bk, nt), GW[:, 2 * d, :], XCb[:, xo:xo + nt], True, True, ["GW"] + xcbk, pk(bk))
                        mm(bank(bk + 1, nt), GW[:, 2 * d + 1, :], XCb[:, xo:xo + nt], True, True, ["GW"] + xcbk,
                           pk((bk + 1)))
                        ro = CX0 if kind == "ctx" else LX0
                        act(A_[:, xo:xo + nt], bank(bk, nt), AF.Sigmoid, pk(bk) + ["VEC"], [("A", xo), ("A2", ro)],
                            bias=VEC[:, ba:ba + 1])
                        act(BT_[:, xo:xo + nt], bank(bk + 1, nt), AF.Sigmoid, pk(bk + 1) + ["VEC"], [("BT", xo), ("BT2", ro)],
                            bias=VEC[:, bx:bx + 1])
                    ak = [("A", xo) for (_, _, _, xo) in blocks]
                    btk = [("BT", xo) for (_, _, _, xo) in blocks]
                    for (o, n_) in regions:
                        sl_ = slice(o, o + n_)
                        kA, kB, kT = ("A2", o), ("BT2", o), ("TM", o)
                        act(A_[:, sl_], A_[:, sl_], AF.Exp, ak + ["CST"], [kA], scale=CST[:, 8 * d + c:8 * d + c + 1])
                        tt("pool", TM[:, sl_], A_[:, sl_], A_[:, sl_], ALU.mult, [kA], [kT])
                        ts("dve", TM[:, sl_], TM[:, sl_], -1.0, 1.0, ALU.mult, ALU.add, [kT], [kT])
                        act(TM[:, sl_], TM[:, sl_], AF.Sqrt, [kT], [kT])
                        tt("pool", BT_[:, sl_], BT_[:, sl_], XC[:, sl_], ALU.mult, btk + xck, [kB])
                        tt("pool", BT_[:, sl_], BT_[:, sl_], TM[:, sl_], ALU.mult, [kB, kT], [kB])
                    (oc, ncx), (ol, nl) = regions
                    if d == 0:
                        mk.op("dve", lambda e: e.tensor_tensor_scan(out=RF[:, oc:oc + ncx], data0=A_[:, oc:oc + ncx],
                                                                    data1=BT_[:, oc:oc + ncx], initial=0.0,
                                                                    op0=ALU.mult, op1=ALU.add),
                              [("A2", oc), ("BT2", oc)], [("RFc",)])
                        mk.op("dve", lambda e: e.tensor_tensor_scan(out=RF[:, ol:ol + nl], data0=A_[:, ol:ol + nl],
                                                                    data1=BT_[:, ol:ol + nl],
                                                                    initial=RF[:, oc + ncx - 1:oc + ncx],
                                                                    op0=ALU.mult, op1=ALU.add),
                              [("A2", ol), ("BT2", ol), ("RFc",)], [("RFl",)])
                    else:
                        mk.op("dve", lambda e: e.tensor_tensor_scan(out=RR[:, oc:oc + ncx][:, ::-1],
                                                                    data0=A_[:, oc:oc + ncx][:, ::-1],
                                                                    data1=BT_[:, oc:oc + ncx][:, ::-1], initial=0.0,
                                                                    op0=ALU.mult, op1=ALU.add),
                              [("A2", oc), ("BT2", oc)], [("RRc",)])
                        mk.op("dve", lambda e: e.tensor_tensor_scan(out=RR[:, ol:ol + nl][:, ::-1],
                                                                    data0=A_[:, ol:ol + nl][:, ::-1],
                                                                    data1=BT_[:, ol:ol + nl][:, ::-1],
                                                                    initial=RR[:, oc:oc + 1],
                                                                    op0=ALU.mult, op1=ALU.add),
                              [("A2", ol), ("BT2", ol), ("RRc",)], [("RRl",)])
                tt("pool", RF[:, LX0:LX0 + T], RF[:, LX0:LX0 + T], RR[:, LX0:LX0 + T], ALU.add, [("RFl",), ("RRl",)], [("RFl",)])
                tt("pool", MT[:], RF[:, LX0:LX0 + T], GY[:], ALU.mult, [("RFl",)] + [("GY", 512 * j) for j in range(8)], ["MT"])
                ld(mts[b][c], MT[:], ["MT"], [], "rm")
        mk.release(m)

    order = ["qkv", "diff", "nbr", "op0", "ffn0", "rg", "op1", "ffn1"]
    fns = {"qkv": phase_qkv, "diff": phase_diff, "nbr": phase_nbr, "op0": lambda: phase_oproj(0),
           "ffn0": lambda: phase_ffn(0), "rg": phase_rglru, "op1": lambda: phase_oproj(1), "ffn1": lambda: phase_ffn(1)}
    for ph in order:
        if stop == "p0":
            break
        if "only" in CFG and ph not in CFG["only"]:
            continue
        fns[ph]()
        if stop == ph:
            break
    mk.barrier()
    named = {"modd": modd, "ao0": aos[0], "xs_lat0": xs[("lat", 0)], "xs_ctx0": xs[("ctx", 0)],
             "qat0": qat[0], "kat0": kat[0], "va0": vas[0], "qbt0": qbt[0], "kbt0": kbt[0], "vb0": vbs[0],
             "hts0_lat0": hts[(("lat", 0), 0)], "hts1_lat0": hts[(("lat", 0), 1)], "mt0": mts[0]}
    for nm in dbg:
        src = named[nm]
        dst = nc.dram_tensor("dbg_" + nm, list(src.shape), src.dtype, kind="ExternalOutput").ap()
        mk.dma("sp", [(dst, src)], [], [], "dbg")
    mk.emit()
    return nc, mk


def prep_inputs(inputs, core):
    g = lambda k: np.asarray(inputs[k])
    b0 = 2 * core
    m = {}
    m["x"] = np.ascontiguousarray(g("x")[b0:b0 + 2])
    m["ctx"] = np.ascontiguousarray(g("ctx")[b0:b0 + 2])
    cc = np.stack([g("c")[b0], g("c")[b0 + 1], g("c_ctx")], 0).astype(np.float32)
    m["cct"] = np.ascontiguousarray(cc.reshape(3, 8, 128).transpose(2, 1, 0))
    vec = np.zeros((128, NV), np.float32)
    for l in range(2):
        for nm in ("ln1_g", "ln1_b", "ln2_g", "ln2_b"):
            vec[:, VMAP[(nm, l)]:VMAP[(nm, l)] + 8] = _pl(g(nm)[l])
        for k in range(3):
            vec[:, VMAP[("fcw", l, k)]:VMAP[("fcw", l, k)] + 44] = _pl(g("ffn_conv_w")[l, k])
        vec[:, VMAP[("fcb", l)]:VMAP[("fcb", l)] + 44] = _pl(g("ffn_conv_b")[l])
    for k in range(4):
        vec[:, VMAP[("rcw", k)]:VMAP[("rcw", k)] + 8] = _pl(g("rnn_conv_w")[0, k])
    vec[:, VMAP[("rcb",)]:VMAP[("rcb",)] + 8] = _pl(g("rnn_conv_b")[0])
    for d in range(2):
        vec[:, VMAP[("apar", d)]:VMAP[("apar", d)] + 8] = _pl(g("rg_a_param")[0, d])
        vec[:, VMAP[("rba", d)]:VMAP[("rba", d)] + 8] = _pl(g("rg_ba")[0, d].reshape(-1))
        vec[:, VMAP[("rbx", d)]:VMAP[("rbx", d)] + 8] = _pl(g("rg_bx")[0, d].reshape(-1))
    m["vec"] = vec
    m["rope"] = _ROPE
    rpb = g("na_rpb")[0].astype(np.float32)
    kc = np.arange(64)[:, None]
    cq = np.arange(64)[None, :]
    rel = kc - cq + 15
    ok = (rel >= 0) & (rel <= 30)
    gath = rpb[:, :, np.clip(rel, 0, 30)] * ok[None, None]
    perm = [0, 2, 4, 6, 1, 3, 5, 7]
    m["rpbt"] = np.ascontiguousarray(gath[perm].transpose(1, 2, 0, 3)).astype(np.float32)
    m["mask"] = _MASK
    m["lamv"] = np.stack([g("diff_lq1")[0], g("diff_lk1")[0], g("diff_lq2")[0], g("diff_lk2")[0]], 0).astype(np.float32)
    m["subg"] = np.ascontiguousarray(g("diff_subln_g")[0]).astype(np.float32)
    for k in ("ada_w", "ada_b", "ln1_g", "ln1_b", "ln2_g", "ln2_b", "ffn_w_up", "ffn_w_down", "att_w_in",
              "att_w_out", "rnn_w_in", "rg_wa", "rg_wx", "rnn_w_out"):
        m[k] = np.ascontiguousarray(g(k)).astype(np.float32)
    return m


def _const_tables():
    t = np.arange(T)
    row = (t // 64).astype(np.float64)[:, None]
    col = (t % 64).astype(np.float64)[:, None]
    inv = 1.0 / (10000.0 ** (np.arange(16, dtype=np.float64) / 16))
    inv = inv.astype(np.float32).astype(np.float64)
    ang = np.concatenate([row * inv, row * inv, col * inv, col * inv], -1).astype(np.float32)
    cos = np.cos(ang).astype(np.float32)
    sin = np.sin(ang).astype(np.float32)
    sgn = np.tile(np.concatenate([-np.ones(16), np.ones(16)]), 2).astype(np.float32)
    rope = np.stack([cos, sin * sgn[None]], 1).astype(np.float32)
    c = np.arange(64)
    cs = np.clip(c - 8, 0, 48)
    kc = np.arange(64)[:, None]
    inside = (kc >= cs[None]) & (kc < cs[None] + 16)
    mask = np.where(inside, 0.0, NEG).astype(np.float32)
    return np.ascontiguousarray(rope), np.ascontiguousarray(np.concatenate([mask, mask], 0))


_ROPE, _MASK = _const_tables()
_CACHE = {}
CFG = {}


def kernel(**inputs):
    if "nc" not in _CACHE:
        _CACHE["nc"] = build()[0]
    nc = _CACHE["nc"]
    in_maps = [prep_inputs(inputs, c) for c in range(8)]
    res = run_bass_kernel_spmd(nc, in_maps, core_ids=list(range(8)))
    out = np.concatenate([r["out"] for r in res.results], axis=0)
    return out.astype(np.float32)
```

```python
import math
import contextlib
import numpy as np
import concourse.bass as bass
import concourse.mybir as mybir
from concourse.bass_utils import run_bass_kernel_spmd

F32 = mybir.dt.float32
BF16 = mybir.dt.bfloat16
AF = mybir.ActivationFunctionType
ALU = mybir.AluOpType

ENGS = ["pe", "act", "dve", "pool", "sp"]

D = 1024
T = 4096
TC = 256
TK = T + TC
DFF = 2816
ALPHA = 4.0 ** 0.25
EPS = 1e-5
LAMBDA_INIT0 = 0.8 - 0.6 * math.exp(0.0)
NEG = -30000.0


class MK:
    def __init__(self, nc):
        self.nc = nc
        self.ops = {e: [] for e in ENGS}
        self.cnt = {e: 0 for e in ENGS}
        self.dcnt = {}
        self.seen = {e: {} for e in ENGS}
        self.lastw = {}
        self.readers = {}
        self.sb_off = 16640
        self.sb_names = 0
        self.sb_max = 0

    def sb(self, shape, dtype, name=None):
        nbytes = int(np.prod(shape[1:])) * (4 if dtype == F32 else 2)
        nbytes = (nbytes + 63) // 64 * 64
        self.sb_names += 1
        nm = "%s_%d" % (name or "t", self.sb_names)
        t = self.nc.alloc_sbuf_tensor_at(nm, list(shape), dtype, offset=self.sb_off)
        self.sb_off += nbytes
        self.sb_max = max(self.sb_max, self.sb_off)
        assert self.sb_off <= 229376, ("sbuf overflow", nm, self.sb_off)
        return t

    def mark(self):
        return self.sb_off

    def release(self, m):
        self.barrier()
        self.sb_off = m

    def _deps(self, eng, reads, writes, is_dma):
        deps = {}

        def add(tok, raw):
            if tok is None:
                return
            sk, val, teng = tok
            if not is_dma and teng == eng and sk[0] == "e":
                if eng == "pe":
                    return
            if deps.get(sk, 0) < val:
                deps[sk] = val

        for k in reads:
            add(self.lastw.get(k), True)
        for k in writes:
            add(self.lastw.get(k), False)
            for sk, (val, teng) in self.readers.get(k, {}).items():
                add((sk, val, teng), False)
        waits = []
        seen = self.seen[eng]
        for sk, val in deps.items():
            if seen.get(sk, 0) >= val:
                continue
            seen[sk] = val
            waits.append((sk, val))
        return waits

    def _commit(self, tok, reads, writes):
        sk, val, teng = tok
        for k in writes:
            self.lastw[k] = tok
            self.readers[k] = {}
        for k in reads:
            self.readers.setdefault(k, {})[sk] = (val, teng)

    def op(self, eng, fn, reads=(), writes=()):
        if eng != "pe":
            pr = [k for k in reads if isinstance(k, tuple) and k and k[0] == "p"]
            if pr:
                writes = list(writes) + [k for k in pr if k not in writes]
        waits = self._deps(eng, reads, writes, False)
        self.cnt[eng] += 1
        tok = (("e", eng), self.cnt[eng], eng)
        self.ops[eng].append((waits, fn, tok, 1))
        self._commit(tok, reads, writes)
        return tok

    def dma(self, eng, pairs, reads, writes, slot, slow=False):
        sk = ("d", slot)
        prev = self.dcnt.get(slot, 0)
        waits = self._deps(eng, reads, writes, True)
        if prev and self.seen[eng].get(sk, 0) < prev:
            self.seen[eng][sk] = prev
            waits.append((sk, prev))
        n = len(pairs)
        self.dcnt[slot] = prev + 16 * n
        tok = (sk, prev + 16 * n, None)

        def fn(e, pairs=pairs, slow=slow):
            if slow:
                return [e.dma_start(out=o, in_=i, allow_slow_non_contiguous=True) for (o, i) in pairs]
            return [e.dma_start(out=o, in_=i) for (o, i) in pairs]

        self.ops[eng].append((waits, fn, tok, 16))
        self._commit(tok, reads, writes)
        return tok

    def barrier(self):
        final = {}
        for e in ENGS:
            if self.cnt[e]:
                final[("e", e)] = self.cnt[e]
        for s, v in self.dcnt.items():
            final[("d", s)] = v
        for e in ENGS:
            waits = []
            for sk, val in final.items():
                if sk == ("e", e):
                    continue
                if self.seen[e].get(sk, 0) < val:
                    self.seen[e][sk] = val
                    waits.append((sk, val))
            if waits:
                self.ops[e].append((waits, None, None, 0))
        self.lastw = {}
        self.readers = {}

    def emit(self):
        nc = self.nc
        self.barrier()
        waited = {e: set() for e in ENGS}
        for e in ENGS:
            for waits, fn, tok, inc in self.ops[e]:
                for sk, val in waits:
                    if sk[0] == "e":
                        waited[sk[1]].add(val)
        remap = {}
        for e in ENGS:
            vals = sorted(waited[e])
            remap[e] = {v: i + 1 for i, v in enumerate(vals)}
        sems = {}
        with contextlib.ExitStack() as st:
            for e in ENGS:
                sems[("e", e)] = st.enter_context(nc.semaphore("s_" + e))
            for s in self.dcnt:
                sems[("d", s)] = st.enter_context(nc.semaphore("d_" + str(s)))
            block = st.enter_context(nc.Block())

            def run(engname):
                def body(eng):
                    for waits, fn, tok, inc in self.ops[engname]:
                        for sk, val in waits:
                            v = remap[sk[1]][val] if sk[0] == "e" else val
                            eng.wait_ge(sems[sk], v)
                        if fn is None:
                            continue
                        r = fn(eng)
                        if inc == 16:
                            for ins in r:
                                ins.then_inc(sems[tok[0]], 16)
                        elif tok[1] in remap[engname]:
                            r.then_inc(sems[tok[0]], 1)

                return body

            block.tensor(run("pe"))
            block.scalar(run("act"))
            block.vector(run("dve"))
            block.gpsimd(run("pool"))
            block.sync(run("sp"))


def _vec_map():
    m = {}
    o = 0
    for l in range(2):
        for nm in ("ln1_g", "ln1_b", "ln2_g", "ln2_b"):
            m[(nm, l)] = o
            o += 8
    for l in range(2):
        for k in range(3):
            m[("fcw", l, k)] = o
            o += 44
    for l in range(2):
        m[("fcb", l)] = o
        o += 44
    for k in range(4):
        m[("rcw", k)] = o
        o += 8
    m[("rcb",)] = o
    o += 8
    for d in range(2):
        m[("apar", d)] = o
        o += 8
    for d in range(2):
        m[("rba", d)] = o
        o += 8
    for d in range(2):
        m[("rbx", d)] = o
        o += 8
    return m, o


VMAP, NV = _vec_map()


def _pl(v):
    v = np.asarray(v, np.float32).reshape(-1, 128)
    return np.ascontiguousarray(v.T)


def build(stop=None, dbg=()):
    nc = bass.Bass("TRN2", target_bir_lowering=False)
    mk = MK(nc)

    def din(name, shape, dt=F32):
        return nc.dram_tensor(name, list(shape), dt, kind="ExternalInput").ap()

    def dscr(name, shape, dt=F32):
        return nc.dram_tensor(name, list(shape), dt, kind="Internal").ap()

    x_d = din("x", [2, T, D])
    ctx_d = din("ctx", [2, TC, D])
    cct_d = din("cct", [128, 8, 3])
    vec_d = din("vec", [128, NV])
    rope_d = din("rope", [T, 2, 64])
    rpbt_d = din("rpbt", [15, 64, 8, 64])
    mask_d = din("mask", [128, 64])
    lam_d = din("lamv", [4, 64])
    subg_d = din("subg", [128])
    ada_w_d = din("ada_w", [2, D, 6 * D])
    ada_b_d = din("ada_b", [2, 6 * D])
    ln_d = {nm: din(nm, [2, D]) for nm in ("ln1_g", "ln1_b", "ln2_g", "ln2_b")}
    wup_d = din("ffn_w_up", [2, D, 2 * DFF])
    wdn_d = din("ffn_w_down", [2, DFF, D])
    awin_d = din("att_w_in", [1, D, 3 * D])
    awout_d = din("att_w_out", [1, D, D])
    rwin_d = din("rnn_w_in", [1, D, 2 * D])
    rga_d = din("rg_wa", [1, 2, 8, 128, 128])
    rgx_d = din("rg_wx", [1, 2, 8, 128, 128])
    rwout_d = din("rnn_w_out", [1, D, D])
    out_d = nc.dram_tensor("out", [2, T, D], F32, kind="ExternalOutput").ap()

    modd = dscr("modd", [2, 3, 6 * D])
    NBB = CFG.get("nb", 2)
    SEQ = [("lat", 0), ("ctx", 0), ("lat", 1), ("ctx", 1)][:2 * NBB]
    SEQ_ALL = [("lat", 0), ("ctx", 0), ("lat", 1), ("ctx", 1)]
    xs = {s: dscr("xs_%s%d" % s, [T if s[0] == "lat" else TC, D]) for s in SEQ_ALL}
    hts = {(s, a): dscr("hts%d_%s%d" % ((a,) + s), [8, 128, (T if s[0] == "lat" else TC) + 2], BF16)
           for s in SEQ_ALL for a in (0, 1)}
    qat = [dscr("qat%d" % b, [4, 128, TK], BF16) for b in range(2)]
    kat = [dscr("kat%d" % b, [4, 128, TK], BF16) for b in range(2)]
    qbt = [dscr("qbt%d" % b, [4, 128, TK], BF16) for b in range(2)]
    kbt = [dscr("kbt%d" % b, [4, 128, TK], BF16) for b in range(2)]
    vas = [dscr("va%d" % b, [TK, 512], BF16) for b in range(2)]
    vbs = [dscr("vb%d" % b, [TK, 512], BF16) for b in range(2)]
    aos = [dscr("ao%d" % b, [TK, D]) for b in range(2)]
    mts = [dscr("mt%d" % b, [8, 128, T], BF16) for b in range(2)]

    def seq_src(s):
        return x_d[s[1]] if s[0] == "lat" else ctx_d[s[1]]

    def seq_len(s):
        return T if s[0] == "lat" else TC

    def seq_var(s):
        return s[1] if s[0] == "lat" else 2

    PS = nc.alloc_psum_tensor("PS", [128, 4096], F32)

    def bank(i, n=512, off=0):
        return PS[:, 512 * i + off:512 * i + off + n]

    def pk(i):
        return [("p", i)]

    def mm(out, lhsT, rhs, start, stop, r, w, skip=False):
        mk.op("pe", lambda e: e.matmul(out, lhsT=lhsT, rhs=rhs, start=start, stop=stop, skip_group_check=skip), r, w)

    def tr(out, in_, idn, r, w):
        mk.op("pe", lambda e: e.transpose(out, in_, idn), r, w)

    def act(out, in_, func, r, w, bias=None, scale=None, accum=None):
        kw = {}
        if bias is not None:
            kw["bias"] = bias
        if scale is not None:
            kw["scale"] = scale
        if accum is not None:
            kw["accum_out"] = accum
        mk.op("act", lambda e: e.activation(out=out, in_=in_, func=func, **kw), r, w)

    def ts(eng, out, in0, s1, s2, op0, op1, r, w):
        if s2 is None:
            if op0 == ALU.mult:
                mk.op(eng, lambda e: e.tensor_scalar_mul(out=out, in0=in0, scalar1=s1), r, w)
            else:
                assert op0 == ALU.add
                mk.op(eng, lambda e: e.tensor_scalar_add(out=out, in0=in0, scalar1=s1), r, w)
        else:
            mk.op(eng, lambda e: e.tensor_scalar(out=out, in0=in0, scalar1=s1, scalar2=s2, op0=op0, op1=op1), r, w)

    def stt(eng, out, in0, scalar, in1, op0, op1, r, w):
        mk.op(eng, lambda e: e.scalar_tensor_tensor(out=out, in0=in0, scalar=scalar, in1=in1, op0=op0, op1=op1), r, w)

    def tt(eng, out, in0, in1, op, r, w):
        mk.op(eng, lambda e: e.tensor_tensor(out=out, in0=in0, in1=in1, op=op), r, w)

    def cp(eng, out, in_, r, w):
        if eng == "act":
            mk.op("act", lambda e: e.copy(out=out, in_=in_), r, w)
        else:
            mk.op(eng, lambda e: e.tensor_copy(out=out, in_=in_), r, w)

    def memset(eng, ap, val, w):
        mk.op(eng, lambda e: e.memset(ap, val), [], w)

    def ld(out, in_, r, w, slot):
        mk.dma("sp", [(out, in_)], r, w, slot)

    def ldc(pairs, r, w, slot):
        mk.dma("pool", pairs, r, w, slot)

    ident = mk.sb([128, 128], F32, "ident")
    memset("pool", ident[:], 0.0, ["ident"])
    mk.op("pool", lambda e: e.affine_select(out=ident[:], in_=ident[:], compare_op=ALU.not_equal, fill=1.0,
                                             base=0, pattern=[[-1, 128]], channel_multiplier=1), ["ident"], ["ident"])
    identb = mk.sb([128, 128], BF16, "identb")
    cp("pool", identb[:], ident[:], ["ident"], ["identb"])
    epsT = mk.sb([128, 1], F32, "eps")
    memset("dve", epsT[:], EPS, ["eps"])
    VEC = mk.sb([128, NV], F32, "VEC")
    ld(VEC[:], vec_d, [], ["VEC"], "c0")
    SCAL = mk.sb([128, 4, 3, 2, 8], F32, "SCAL")
    CST = mk.sb([128, 16], F32, "CST")
    NLAM = mk.sb([128, 1], F32, "NLAM")
    GSUB = mk.sb([128, 128], F32, "GSUB")

    def vcol(key, n=8):
        o = VMAP[key]
        return VEC[:, o:o + n]

    m0 = mk.mark()
    MODP = [mk.sb([128, 48, 3], F32, "MODP%d" % l) for l in range(2)]
    ZT = mk.sb([128, 8, 2], BF16, "ZT")
    memset("dve", ZT[:], 0.0, ["ZT"])
    for s in SEQ:
        for a in (0, 1):
            L = seq_len(s)
            h = hts[(s, a)].rearrange("k p t -> p k t")
            mk.dma("sp", [(h[:, :, 0:1], ZT[:, :, 0:1])], ["ZT"], [], "z0", slow=True)
            mk.dma("sp", [(h[:, :, L + 1:L + 2], ZT[:, :, 1:2])], ["ZT"], [], "z0", slow=True)

    CC = mk.sb([128, 8, 3], F32, "CC")
    ST = mk.sb([128, 8, 3], F32, "ST")
    ld(CC[:], cct_d, [], ["CC"], "c1")
    act(ST[:], CC[:], AF.Silu, ["CC"], ["ST"])
    MODR = mk.sb([3, 6 * D], F32, "MODR")
    ADAB = mk.sb([3, 6 * D], F32, "ADAB")
    WB = [mk.sb([128, 8, 512], F32, "WB%d" % i) for i in range(2)]
    for l in range(2):
        ld(ADAB[:], ada_b_d[l].partition_broadcast(3), [], ["ADAB"], "c2")
        wv = ada_w_d[l].rearrange("(k p) n -> p k n", p=128)
        for jb in range(12):
            sl = jb % 2
            ld(WB[sl][:], wv[:, :, jb * 512:(jb + 1) * 512], [], ["WB%d" % sl], "wb%d" % sl)
            bk = jb % 2
            for k in range(8):
                mm(PS[0:3, 512 * bk:512 * bk + 512], ST[:, k, :], WB[sl][:, k, :], k == 0, k == 7,
                   ["ST", "WB%d" % sl], pk(bk))
            tt("dve", MODR[0:3, jb * 512:(jb + 1) * 512], PS[0:3, 512 * bk:512 * bk + 512],
               ADAB[0:3, jb * 512:(jb + 1) * 512], ALU.add, pk(bk) + ["ADAB"], [("MODR", jb)])
        allk = [("MODR", jb) for jb in range(12)]
        ld(modd[l], MODR[0:3, :], allk, [], "c3")
        for k in range(48):
            tr(PS[:, 3584 + 3 * k:3584 + 3 * k + 3], MODR[0:3, 128 * k:128 * (k + 1)], ident[0:3, 0:3],
               allk + ["ident"], pk(7))
        cp("dve", MODP[l][:].rearrange("p k r -> p (k r)"), PS[:, 3584:3584 + 144], pk(7), ["MODP%d" % l])
    TMP8 = mk.sb([128, 8], F32, "TMP8")
    sites = [(0, 8, 0, None, None), (0, 32, 24, ("ln1_g", 0), ("ln1_b", 0)),
             (1, 8, 0, ("ln2_g", 0), ("ln2_b", 0)), (1, 32, 24, ("ln1_g", 1), ("ln1_b", 1))]
    for si, (l, sco, sho, gk, bk_) in enumerate(sites):
        for r in range(3):
            A_ = SCAL[:, si, r, 0, :]
            B_ = SCAL[:, si, r, 1, :]
            sc = MODP[l][:, sco:sco + 8, r]
            sh = MODP[l][:, sho:sho + 8, r]
            ts("dve", TMP8[:], sc, 1.0, None, ALU.add, None, ["MODP%d" % l], ["TMP8"])
            if gk is None:
                cp("dve", A_, TMP8[:], ["TMP8"], ["SCAL"])
                cp("dve", B_, sh, ["MODP%d" % l], ["SCAL"])
            else:
                tt("dve", A_, TMP8[:], vcol(gk), ALU.mult, ["TMP8", "VEC"], ["SCAL"])
                tt("dve", B_, TMP8[:], vcol(bk_), ALU.mult, ["TMP8", "VEC"], ["SCAL"])
                tt("dve", B_, B_, sh, ALU.add, ["SCAL", "MODP%d" % l], ["SCAL"])
    LV = mk.sb([128, 4, 64], F32, "LV")
    ld(LV[:].rearrange("p a d -> p (a d)"), lam_d.rearrange("a d -> (a d)").partition_broadcast(128), [], ["LV"], "c4")
    L2 = mk.sb([128, 2, 64], F32, "L2")
    E2 = mk.sb([128, 2], F32, "E2")
    tt("dve", L2[:, 0, :], LV[:, 0, :], LV[:, 1, :], ALU.mult, ["LV"], ["L2"])
    tt("dve", L2[:, 1, :], LV[:, 2, :], LV[:, 3, :], ALU.mult, ["LV"], ["L2"])
    mk.op("dve", lambda e: e.reduce_sum(out=E2[:], in_=L2[:], axis=mybir.AxisListType.X), ["L2"], ["E2"])
    act(E2[:], E2[:], AF.Exp, ["E2"], ["E2"])
    tt("dve", NLAM[:], E2[:, 1:2], E2[:, 0:1], ALU.subtract, ["E2"], ["NLAM"])
    ts("dve", NLAM[:], NLAM[:], -LAMBDA_INIT0, None, ALU.add, None, ["NLAM"], ["NLAM"])
    ld(GSUB[:], subg_d.partition_broadcast(128), [], ["GSUB"], "c5")
    ts("dve", GSUB[:], GSUB[:], 1.0 - LAMBDA_INIT0, None, ALU.mult, None, ["GSUB"], ["GSUB"])
    act(CST[:], VEC[:, VMAP[("apar", 0)]:VMAP[("apar", 0)] + 16], AF.Exp, ["VEC"], ["CST"], scale=-1.0)
    ts("dve", CST[:], CST[:], 1.0, None, ALU.add, None, ["CST"], ["CST"])
    act(CST[:], CST[:], AF.Ln, ["CST"], ["CST"])
    ts("dve", CST[:], CST[:], -8.0, None, ALU.mult, None, ["CST"], ["CST"])
    mk.release(m0)

    def make_ht(src, srckeys, HT, htkey, site, r, pb):
        for k in range(8):
            b_ = pb + k // 4
            tr(bank(b_, 128, 128 * (k % 4)), src[:, 128 * k:128 * (k + 1)], ident[:], srckeys + ["ident"],
               [("p", b_)])
        for k in range(8):
            hm = CFG.get("ht_mode", 3)
            if hm == 0 or (hm == 1 and k >= 4) or (hm == 2 and k < 4):
                continue
            b_ = pb + k // 4
            A_ = SCAL[:, site, r, 0, k:k + 1]
            B_ = SCAL[:, site, r, 1, k:k + 1]
            if k < 4:
                act(HT[:, k, :], bank(b_, 128, 128 * (k % 4)), AF.Identity, [("p", b_), "SCAL"],
                    [(htkey, k)], bias=B_, scale=A_)
            else:
                ts("dve", HT[:, k, :], bank(b_, 128, 128 * (k % 4)), A_, B_, ALU.mult, ALU.add,
                   [("p", b_), "SCAL"], [(htkey, k)])

    def epilogue(ps2, pskeys, Xsrc, GB, Gt, Bt, bkeys, site, r, xdst, hdst, E, i, pb):
        sl = i % 2
        sy = i % len(E["Y"])
        Xt, Y, HT = E["Xt"][sl], E["Y"][sy], E["HT"][sy]
        kx, ky, kh = "Xt%d" % sl, "Y%d" % sy, "HTe%d" % sy
        ld(Xt[:], Xsrc, [], [kx], "ex%d" % sl)
        tt("dve", Y[:], ps2, GB[:], ALU.mult, pskeys + bkeys, [ky])
        stt("dve", Y[:], Xt[:], ALPHA, Y[:], ALU.mult, ALU.add, [kx, ky], [ky])
        BS, MV, RS = E["BS"][sl], E["MV"][sl], E["RS"][sl]
        kb = "BS%d" % sl
        mk.op("dve", lambda e: e.bn_stats(out=BS[:, 0, :], in_=Y[:, 0:512]), [ky], [(kb, 0)])
        mk.op("dve", lambda e: e.bn_stats(out=BS[:, 1, :], in_=Y[:, 512:1024]), [ky], [(kb, 1)])
        mk.op("dve", lambda e: e.bn_aggr(out=MV[:], in_=BS[:]), [(kb, 0), (kb, 1)], [(kb, 2)])
        act(RS[:, 0:1], MV[:, 1:2], AF.Ln, [(kb, 2), "eps"], [(kb, 3)], bias=epsT[:, 0:1])
        act(RS[:, 1:2], RS[:, 0:1], AF.Exp, [(kb, 3)], [(kb, 4)], scale=-0.5)
        stt("dve", RS[:, 2:3], MV[:, 0:1], -1.0, RS[:, 1:2], ALU.mult, ALU.mult, [(kb, 2), (kb, 4)], [(kb, 5)])
        act(Y[:], Y[:], AF.Identity, [ky, (kb, 4), (kb, 5)], [ky], bias=RS[:, 2:3], scale=RS[:, 1:2])
        tt("pool", Xt[:], Y[:], Gt[:], ALU.mult, [ky] + bkeys, [kx])
        tt("pool", Xt[:], Xt[:], Bt[:], ALU.add, [kx] + bkeys, [kx])
        ld(xdst, Xt[:], [kx], [], "es%d" % sl)
        if hdst is not None:
            make_ht(Y, [ky], HT, kh, site, r, pb)
            ld(hdst, HT[:], [(kh, k) for k in range(8)], [], "eh%d" % sy)

    def epi_alloc(ny=2, nht=0):
        E = {}
        E["Xt"] = [mk.sb([128, D], F32, "Xt") for _ in range(2)]
        E["Y"] = [mk.sb([128, D], F32, "Y") for _ in range(ny)]
        E["HT"] = [mk.sb([128, 8, 128], BF16, "HTe") for _ in range(nht if nht else ny)]
        E["BS"] = [mk.sb([128, 2, 6], F32, "BS") for _ in range(2)]
        E["MV"] = [mk.sb([128, 2], F32, "MV") for _ in range(2)]
        E["RS"] = [mk.sb([128, 3], F32, "RS") for _ in range(2)]
        return E

    def load_bc(GB, Gt, Bt, l, goff, r, gname, bname):
        ld(GB[:], modd[l, r, goff:goff + D].partition_broadcast(128), [], ["GB"], "bc0")
        ld(Gt[:], ln_d[gname][l].partition_broadcast(128), [], ["Gt"], "bc1")
        ld(Bt[:], ln_d[bname][l].partition_broadcast(128), [], ["Bt"], "bc2")

    def phase_qkv():
        m = mk.mark()
        WIN = mk.sb([128, 8, 3 * D], BF16, "WIN")
        wv = awin_d[0].rearrange("(k p) n -> p k n", p=128)
        for k in range(8):
            ldc([(WIN[:, k, :], wv[:, k, :])], [], [("WIN", k)], "w%d" % (k % 4))
        wink = [("WIN", k) for k in range(8)]
        XT = [mk.sb([128, D], F32, "XT") for _ in range(3)]
        RP = [mk.sb([128, 2, 64], F32, "RP") for _ in range(3)]
        HT = [mk.sb([128, 8, 128], BF16, "HT") for _ in range(2)]
        RQ = [mk.sb([128, D], F32, "RQ") for _ in range(2)]
        T1 = [mk.sb([128, D], F32, "T1") for _ in range(2)]
        T2 = [mk.sb([128, D], F32, "T2") for _ in range(2)]
        RB = [mk.sb([128, D], BF16, "RB") for _ in range(2)]
        TQb = [mk.sb([128, D], BF16, "TQb") for _ in range(2)]
        VV = [mk.sb([128, 2, 512], BF16, "VV") for _ in range(2)]
        TQ = [mk.sb([128, 8, 128], BF16, "TQ") for _ in range(2)]
        TB = [mk.sb([128, 8, 128], BF16, "TB") for _ in range(2)]
        tiles = [(b, i) for b in range(NBB) for i in range(34)]
        if "qkv_tiles" in CFG:
            tiles = tiles[:CFG["qkv_tiles"]]
        NT = len(tiles)

        def src_of(b, i):
            if i < 32:
                return x_d[b, 128 * i:128 * (i + 1), :]
            return ctx_d[b, 128 * (i - 32):128 * (i - 31), :]

        def stageL(n):
            b, i = tiles[n]
            s3 = n % 3
            ld(XT[s3][:], src_of(b, i), [], ["XT%d" % s3], "xl%d" % s3)
            if i < 32:
                ld(RP[s3][:], rope_d[128 * i:128 * (i + 1)], [], ["RP%d" % s3], "rl%d" % s3)

        def stageM(n):
            b, i = tiles[n]
            sl, s3 = n % 2, n % 3
            lat = i < 32
            r = b if lat else 2
            htk = "HT%d" % sl
            make_ht(XT[s3], ["XT%d" % s3], HT[sl], htk, 0, r, 6)
            hk = [(htk, k) for k in range(8)]
            for j in range(6):
                for k in range(8):
                    mm(bank(j), HT[sl][:, k, :], WIN[:, k, 512 * j:512 * (j + 1)], k == 0, k == 7, hk + wink, pk(j))
            cp("act", RQ[sl][:], PS[:, 0:1024], pk(0) + pk(1), ["RQ%d" % sl])
            cp("dve", RB[sl][:], PS[:, 1536:2560], pk(3) + pk(4), ["RB%d" % sl])
            cp("act", VV[sl][:, 0, :], bank(2), pk(2), [("VV%d" % sl, 0)])
            cp("dve", VV[sl][:, 1, :], bank(5), pk(5), [("VV%d" % sl, 1)])
            if lat:
                rp = RP[s3]
                rk = "RP%d" % s3
                rq, t1, t2 = RQ[sl], T1[sl], T2[sl]
                tt("dve", t1[:].rearrange("p (g d) -> p g d", g=16), rq[:].rearrange("p (g d) -> p g d", g=16),
                   rp[:, 0:1, :].to_broadcast([128, 16, 64]), ALU.mult, ["RQ%d" % sl, rk], ["T1%d" % sl])
                rqv = rq[:].rearrange("p (g a h f) -> p g a h f", g=16, a=2, h=2, f=16)
                t2v = t2[:].rearrange("p (g a h f) -> p g a h f", g=16, a=2, h=2, f=16)
                sv = rp[:, 1:2, :].rearrange("p o (a h f) -> p o a h f", a=2, h=2, f=16)
                tt("pool", t2v[:, :, :, 0, :], rqv[:, :, :, 1, :], sv[:, :, :, 0, :].to_broadcast([128, 16, 2, 16]),
                   ALU.mult, ["RQ%d" % sl, rk], [("T2%d" % sl, 0)])
                tt("pool", t2v[:, :, :, 1, :], rqv[:, :, :, 0, :], sv[:, :, :, 1, :].to_broadcast([128, 16, 2, 16]),
                   ALU.mult, ["RQ%d" % sl, rk], [("T2%d" % sl, 1)])
                tt("dve", TQb[sl][:], t1[:], t2[:], ALU.add, ["T1%d" % sl, ("T2%d" % sl, 0), ("T2%d" % sl, 1)], ["TQb%d" % sl])
            else:
                cp("pool", TQb[sl][:], RQ[sl][:], ["RQ%d" % sl], ["TQb%d" % sl])
            ld(vas[b][tok0_of(n):tok0_of(n) + 128, :], VV[sl][:, 0, :], [("VV%d" % sl, 0)], [], "sva%d" % sl)
            ld(vbs[b][tok0_of(n):tok0_of(n) + 128, :], VV[sl][:, 1, :], [("VV%d" % sl, 1)], [], "svb%d" % sl)

        def tok0_of(n):
            b, i = tiles[n]
            return 128 * i if i < 32 else T + 128 * (i - 32)

        def stageX(n):
            b, i = tiles[n]
            sl = n % 2
            lat = i < 32
            tok0 = tok0_of(n)
            for (src, sk, dst, dk) in ((TQb[sl], ["TQb%d" % sl], TQ[sl], "TQ%d" % sl), (RB[sl], ["RB%d" % sl], TB[sl], "TB%d" % sl)):
                for k in range(8):
                    b_ = 6 + k // 4
                    tr(PS[:, 512 * b_:512 * (b_ + 1)].bitcast(BF16)[:, 128 * (k % 4):128 * (k % 4 + 1)],
                       src[:, 128 * k:128 * (k + 1)], identb[:], sk + ["identb"], pk(b_))
                act(dst[:, 0:4, :].rearrange("p k t -> p (k t)"), bank(6).bitcast(BF16)[:, 0:512], AF.Identity, pk(6),
                    [(dk, 0)], scale=0.125)
                cp("dve", dst[:, 4:8, :].rearrange("p k t -> p (k t)"), bank(7).bitcast(BF16)[:, 0:512], pk(7), [(dk, 1)])
            for si_, (dst_d, tile_, kk, half) in enumerate(((qat, TQ, "TQ%d" % sl, 0), (kat, TQ, "TQ%d" % sl, 1),
                                                          (qbt, TB, "TB%d" % sl, 0), (kbt, TB, "TB%d" % sl, 1))):
                ld(dst_d[b].rearrange("h p t -> p h t")[:, :, tok0:tok0 + 128],
                   tile_[sl][:, 4 * half:4 * half + 4, :], [(kk, half)], [], "sq%d%d" % (si_, sl))

        if NT:
            stageL(0)
            if NT > 1:
                stageL(1)
            stageM(0)
            for n in range(NT):
                if n + 2 < NT:
                    stageL(n + 2)
                if n + 1 < NT:
                    stageM(n + 1)
                stageX(n)
        mk.release(m)

    def phase_diff():
        m = mk.mark()
        KT = [mk.sb([128, TK], BF16, "KT") for _ in range(2)]
        QT = [mk.sb([128, TK], BF16, "QT") for _ in range(2)]
        V1 = [mk.sb([128, 34, 129], BF16, "V1") for _ in range(2)]
        NPT = 3
        PT = [mk.sb([128, 2, 512], BF16, "PT") for _ in range(NPT)]
        AQ = mk.sb([128, 4, 128], F32, "AQ")
        OO = mk.sb([128, 4, 128], F32, "OO")
        SQ = mk.sb([128, 4, 128], F32, "SQ")
        RC = mk.sb([128, 4, 2], F32, "RC")
        T1 = mk.sb([128, 4, 1], F32, "T1")
        SS = mk.sb([128, 4, 1], F32, "SS")
        G3 = mk.sb([128, 1, 128], F32, "G3")
        cp("pool", G3[:, 0, :], GSUB[:], ["GSUB"], ["G3"])
        AOB = [mk.sb([128, 4, 128], F32, "AOB") for _ in range(2)]
        for i in range(2):
            memset("pool", V1[i][:, :, 128:129], 1.0, [("V1%d" % i, "one")])
        ACC = PS[:, 2048:4096].rearrange("p (b c) -> p b c", b=4)
        nblk = 0
        hh = 0
        npt = 0
        for b in range(NBB):
            for h in range(4):
                sl = hh % 2
                hh += 1
                ld(KT[sl][:], kat[b][h], [], ["KT%d" % sl], "ak%d" % sl)
                ld(QT[sl][:], qat[b][h], [], ["QT%d" % sl], "aq%d" % sl)
                ld(V1[sl][:, :, 0:128], vas[b].rearrange("(t p) c -> p t c", p=128)[:, :, 128 * h:128 * (h + 1)],
                   [], ["V1%d" % sl], "av%d" % sl)
                vkeys = ["V1%d" % sl, ("V1%d" % sl, "one")]
                blocks = [(512 * j, 512, list(range(34))) for j in range(8)] + [(T, 256, [32, 33])]
                for (q0, nq, kts) in blocks:
                    nqt = nq // 128
                    acck = [("acc", q) for q in range(nqt)]
                    memset("dve", ACC[:, 0:nqt, 0:258], 0.0, acck)

                    def qk(it):
                        kt = kts[it]
                        sp_ = it % 2
                        for mp in range(2):
                            bk = 2 * sp_ + mp
                            mm(bank(bk, nq), KT[sl][64 * mp:64 * (mp + 1), 128 * kt:128 * (kt + 1)],
                               QT[sl][64 * mp:64 * (mp + 1), q0:q0 + nq], True, True,
                               ["KT%d" % sl, "QT%d" % sl], pk(bk))

                    qk(0)
                    for it, kt in enumerate(kts):
                        if it + 1 < len(kts):
                            qk(it + 1)
                        sp_ = it % 2
                        pb_ = npt % NPT
                        npt += 1
                        sview = PS[:, 1024 * sp_:1024 * sp_ + 1024].rearrange("p (b c) -> p b c", b=2)[:, :, 0:nq]
                        act(PT[pb_][:, :, 0:nq], sview, AF.Exp, pk(2 * sp_) + pk(2 * sp_ + 1), ["PT%d" % pb_])
                        for mp in range(2):
                            for qt in range(nqt):
                                mm(bank(4 + qt, 129, 129 * mp), PT[pb_][:, mp, 128 * qt:128 * (qt + 1)],
                                   V1[sl][:, kt, :], False, False, ["PT%d" % pb_] + vkeys + [("acc", qt)],
                                   [("acc", qt)], skip=True)
                    ob = AOB[nblk % 2]
                    okey = "AOB%d" % (nblk % 2)
                    nblk += 1
                    mk.op("dve", lambda e, nqt=nqt: e.reciprocal(out=RC[:, 0:nqt, :], in_=ACC[:, 0:nqt, 128:258:129]),
                          acck, ["RC"])
                    ts("dve", T1[:, 0:nqt, :], RC[:, 0:nqt, 1:2], NLAM[:, 0:1], None, ALU.mult, None, ["RC", "NLAM"], ["T1"])
                    tt("dve", AQ[:, 0:nqt, :], ACC[:, 0:nqt, 0:128], RC[:, 0:nqt, 0:1].to_broadcast([128, nqt, 128]),
                       ALU.mult, acck + ["RC"], ["AQ"])
                    tt("dve", SQ[:, 0:nqt, :], ACC[:, 0:nqt, 129:257], T1[:, 0:nqt, :].to_broadcast([128, nqt, 128]),
                       ALU.mult, acck + ["T1"], ["SQ"])
                    tt("dve", OO[:, 0:nqt, :], AQ[:, 0:nqt, :], SQ[:, 0:nqt, :], ALU.add, ["AQ", "SQ"], ["OO"])
                    tt("pool", SQ[:, 0:nqt, :], OO[:, 0:nqt, :], OO[:, 0:nqt, :], ALU.mult, ["OO"], ["SQ"])
                    mk.op("dve", lambda e, nqt=nqt: e.reduce_sum(out=SS[:, 0:nqt, :], in_=SQ[:, 0:nqt, :],
                                                                 axis=mybir.AxisListType.X), ["SQ"], ["SS"])
                    act(SS[:, 0:nqt, :], SS[:, 0:nqt, :], AF.Ln, ["SS", "eps"], ["SS"], bias=epsT[:, 0:1], scale=1.0 / 128)
                    act(SS[:, 0:nqt, :], SS[:, 0:nqt, :], AF.Exp, ["SS"], ["SS"], scale=-0.5)
                    tt("pool", OO[:, 0:nqt, :], OO[:, 0:nqt, :], SS[:, 0:nqt, :].to_broadcast([128, nqt, 128]),
                       ALU.mult, ["OO", "SS"], ["OO"])
                    tt("pool", ob[:, 0:nqt, :], OO[:, 0:nqt, :], G3[:].to_broadcast([128, nqt, 128]), ALU.mult,
                       ["OO", "G3"], [okey])
                    ld(aos[b][q0:q0 + nq, 128 * h:128 * (h + 1)].rearrange("(q p) c -> p q c", p=128),
                       ob[:, 0:nqt, :], [okey], [], "sa%d" % (nblk % 2))
        mk.release(m)

    def phase_nbr():
        m = mk.mark()
        QB2 = mk.sb([128, 4, TK], BF16, "QB2")
        KB2 = mk.sb([128, 4, TK], BF16, "KB2")
        VBe = mk.sb([128, 34, 8, 65], BF16, "VBe")
        VBo = mk.sb([128, 31, 8, 65], BF16, "VBo")
        BI = [mk.sb([128, 4, 8, 64], F32, "BI") for _ in range(2)]
        MK2 = mk.sb([128, 1, 64], F32, "MK2")
        SBF = [mk.sb([128, 512], F32, "SBF") for _ in range(2)]
        PT = [mk.sb([128, 6, 512], BF16, "PTn") for _ in range(2)]
        RI = mk.sb([64, 8, 1], F32, "RI")
        OB = [mk.sb([64, 512], F32, "OB") for _ in range(2)]
        memset("pool", VBe[:, :, :, 64:65], 1.0, [("VBe", "one")])
        memset("pool", VBo[:, :, :, 64:65], 1.0, [("VBo", "one")])
        ld(MK2[:, 0, :], mask_d, [], ["MK2"], "nm")

        def gen_bias(dst, key, delta):
            for j in range(4):
                for i2 in range(2):
                    rr = delta + 2 * j + i2 + 7
                    ld(dst[64 * i2:64 * (i2 + 1), j, :, :], rpbt_d[rr], [], [(key, j, i2)], "nb%d" % i2)
                tt("dve", dst[:, j, :, :], dst[:, j, :, :], MK2[:].to_broadcast([128, 8, 64]), ALU.add,
                   [(key, j, 0), (key, j, 1), "MK2"], [(key, j)])

        gen_bias(BI[0], "BI0", -4)
        nrow = 0
        for b in range(NBB):
            ld(QB2[:], qbt[b].rearrange("h p t -> p h t"), [], ["QB2"], "nq")
            ld(KB2[:], kbt[b].rearrange("h p t -> p h t"), [], ["KB2"], "nk")
            vb_v = vbs[b].rearrange("(t p) (h d) -> p t h d", p=128, h=8)
            for t4 in range(0, 34, 6):
                t5 = min(34, t4 + 6)
                mk.dma("sp", [(VBe[:, t_, :, 0:64], vb_v[:, t_]) for t_ in range(t4, t5)], [], [("VBe", t4)], "nv")
            vbo_v = vbs[b][64:64 + 31 * 128, :].rearrange("(t p) (h d) -> p t h d", p=128, h=8)
            for t4 in range(0, 31, 6):
                t5 = min(31, t4 + 6)
                mk.dma("sp", [(VBo[:, t_, :, 0:64], vbo_v[:, t_]) for t_ in range(t4, t5)], [], [("VBo", t4)], "nv")
            vek = [("VBe", t4) for t4 in range(0, 34, 6)] + [("VBe", "one")]
            vok = [("VBo", t4) for t4 in range(0, 31, 6)] + [("VBo", "one")]
            rows = [("lat", r) for r in range(64)] + [("ctx", g) for g in range(4)]
            if "nbr_rows" in CFG:
                rows = rows[:CFG["nbr_rows"]]
            nst = CFG.get("nbr_stage", 9)
            for (kind, r) in rows:
                par = nrow % 2
                nrow += 1
                if kind == "lat":
                    rs = min(max(r - 4, 0), 56)
                    delta = rs - r
                    if delta == -4:
                        bi, bik = BI[0], "BI0"
                    else:
                        bi, bik = BI[1], "BI1"
                        gen_bias(BI[1], "BI1", delta)
                    q0 = 64 * r
                    kts = []
                    for j in range(4):
                        ks = 64 * (rs + 2 * j)
                        if rs % 2 == 0:
                            kts.append((ks, VBe, (rs + 2 * j) // 2, vek, j))
                        else:
                            kts.append((ks, VBo, (rs + 2 * j - 1) // 2, vok, j))
                    kts.append((T, VBe, 32, vek, None))
                    kts.append((T + 128, VBe, 33, vek, None))
                else:
                    q0 = T + 64 * r
                    kts = [(T, VBe, 32, vek, None), (T + 128, VBe, 33, vek, None)]
                pt = PT[par]
                ptk = "PTn%d" % par
                for n, (ks, vt, vi, vk, j) in enumerate(kts):
                    pp = n % 2
                    for h in range(8):
                        hp, hq = h % 2, h // 2
                        mm(bank(2 * pp + hp, 64, 64 * hq), KB2[64 * hp:64 * (hp + 1), hq, ks:ks + 128],
                           QB2[64 * hp:64 * (hp + 1), hq, q0:q0 + 64], True, True, ["KB2", "QB2"], pk(2 * pp + hp))
                    s2 = PS[:, 1024 * pp:1024 * pp + 1024].rearrange("p (b c) -> p b c", b=2)[:, :, 0:256]
                    sk2 = pk(2 * pp) + pk(2 * pp + 1)
                    if j is not None:
                        sb_ = SBF[n % 2]
                        tt("dve", sb_[:].rearrange("p (b c) -> p b c", b=2), s2,
                           bi[:, j, :, :].rearrange("p (b h) c -> p b (h c)", b=2), ALU.add,
                           sk2 + [(bik, j)], ["SBF%d" % (n % 2)])
                        act(pt[:, n, :], sb_[:], AF.Exp, ["SBF%d" % (n % 2)], [(ptk, n)])
                    else:
                        act(pt[:, n, :].rearrange("p (b c) -> p b c", b=2), s2, AF.Exp, sk2, [(ptk, n)])
                if nst < 2:
                    continue
                ab = 4 + 2 * par
                for h in range(8):
                    b_ = ab + h // 4
                    for n, (ks, vt, vi, vk, j) in enumerate(kts):
                        mm(PS[0:64, 512 * b_ + 65 * (h % 4):512 * b_ + 65 * (h % 4) + 65],
                           pt[:, n, 64 * ((h % 2) * 4 + h // 2):64 * ((h % 2) * 4 + h // 2 + 1)], vt[:, vi, h, :],
                           n == 0, n == len(kts) - 1,
                           [(ptk, n)] + vk, pk(b_))
                if nst < 3:
                    continue
                ob = OB[par]
                obk = "OB%d" % par
                for g in range(2):
                    b_ = ab + g
                    accv = PS[0:64, 512 * b_:512 * b_ + 260].rearrange("p (h c) -> p h c", h=4)
                    mk.op("dve", lambda e, accv=accv, g=g: e.reciprocal(out=RI[:, 4 * g:4 * g + 4, :], in_=accv[:, :, 64:65]),
                          pk(b_), [("RI", g)])
                    tt("dve", ob[:, 256 * g:256 * (g + 1)].rearrange("p (h d) -> p h d", h=4), accv[:, :, 0:64],
                       RI[:, 4 * g:4 * g + 4, :].to_broadcast([64, 4, 64]), ALU.mult, pk(b_) + [("RI", g)], [(obk, g)])
                ld(aos[b][q0:q0 + 64, 512:1024], ob[:], [(obk, 0), (obk, 1)], [], "no%d" % par)
        mk.release(m)

    def phase_oproj(l):
        m = mk.mark()
        WO = mk.sb([128, 8, D], BF16, "WO")
        wsrc = awout_d[0] if l == 0 else rwout_d[0]
        ldc([(WO[:], wsrc.rearrange("(k p) n -> p k n", p=128))], [], ["WO"], "w0")
        GB = mk.sb([128, D], F32, "GB")
        Gt = mk.sb([128, D], F32, "Gt")
        Bt = mk.sb([128, D], F32, "Bt")
        E = epi_alloc()
        bkeys = ["GB", "Gt", "Bt"]
        site = 1 if l == 0 else 3
        seqs = SEQ if l == 0 else [s for s in SEQ if s[0] == "lat"]
        if l == 0:
            AO = [mk.sb([128, D], F32, "AO") for _ in range(3)]
            AT = [mk.sb([128, 8, 128], BF16, "AT") for _ in range(2)]
        else:
            MB = [mk.sb([128, 8, 512], BF16, "MB") for _ in range(2)]
        X3 = [mk.sb([128, D], F32, "X3") for _ in range(3)]
        n0 = 0
        for s in seqs:
            r = seq_var(s)
            load_bc(GB, Gt, Bt, l, 2 * D, r, "ln1_g", "ln1_b")
            b = s[1]
            L = seq_len(s)
            NT = L // 128
            base = 0 if s[0] == "lat" else T
            xsrc = seq_src(s) if l == 0 else xs[s]
            hdst_all = hts[(s, 0)].rearrange("k p t -> p k t")

            def stageL(i):
                n = n0 + i
                s3 = n % 3
                ld(X3[s3][:], xsrc[128 * i:128 * (i + 1), :], [], ["X3%d" % s3], "ex%d" % s3)
                if l == 0:
                    ld(AO[s3][:], aos[b][base + 128 * i:base + 128 * (i + 1), :], [], ["AO%d" % s3], "ol%d" % s3)
                elif i % 4 == 0:
                    ms = (i // 4) % 2
                    ld(MB[ms][:], mts[b].rearrange("k p t -> p k t")[:, :, 128 * i:128 * i + 512], [],
                       ["MB%d" % ms], "om%d" % ms)

            def stageT(i):
                n = n0 + i
                sl = n % 2
                s3 = n % 3
                if l == 0:
                    for k in range(8):
                        tr(bank(4 + k // 4, 128, 128 * (k % 4)), AO[s3][:, 128 * k:128 * (k + 1)], ident[:],
                           ["AO%d" % s3, "ident"], pk(4 + k // 4))
                    cp("act", AT[sl][:, 0:4, :].rearrange("p k t -> p (k t)"), bank(4), pk(4), [("AT%d" % sl, 0)])
                    cp("dve", AT[sl][:, 4:8, :].rearrange("p k t -> p (k t)"), bank(5), pk(5), [("AT%d" % sl, 1)])

            def stageM(i):
                n = n0 + i
                sl = n % 2
                pa = 2 * sl
                for k in range(8):
                    if l == 0:
                        lhs, lk = AT[sl][:, k, :], [("AT%d" % sl, 0), ("AT%d" % sl, 1)]
                    else:
                        ms = (i // 4) % 2
                        lhs, lk = MB[ms][:, k, 128 * (i % 4):128 * (i % 4 + 1)], ["MB%d" % ms]
                    for hf in range(2):
                        mm(bank(pa + hf), lhs, WO[:, k, 512 * hf:512 * (hf + 1)], k == 0, k == 7, lk + ["WO"], pk(pa + hf))

            def stageA(i):
                n = n0 + i
                sl = n % 2
                pa = 2 * sl
                Xt, Y = E["Xt"][sl], E["Y"][sl]
                kx, ky = "Xt%d" % sl, "Y%d" % sl
                s3 = n % 3
                tt("dve", Y[:], PS[:, 512 * pa:512 * pa + 1024], GB[:], ALU.mult, pk(pa) + pk(pa + 1) + ["GB"], [ky])
                stt("dve", Y[:], X3[s3][:], ALPHA, Y[:], ALU.mult, ALU.add, ["X3%d" % s3, ky], [ky])
                BS, MV, RS = E["BS"][sl], E["MV"][sl], E["RS"][sl]
                kb = "BS%d" % sl
                mk.op("dve", lambda e: e.bn_stats(out=BS[:, 0, :], in_=Y[:, 0:512]), [ky], [(kb, 0)])
                mk.op("dve", lambda e: e.bn_stats(out=BS[:, 1, :], in_=Y[:, 512:1024]), [ky], [(kb, 1)])
                mk.op("dve", lambda e: e.bn_aggr(out=MV[:], in_=BS[:]), [(kb, 0), (kb, 1)], [(kb, 2)])
                act(RS[:, 0:1], MV[:, 1:2], AF.Ln, [(kb, 2), "eps"], [(kb, 3)], bias=epsT[:, 0:1])
                act(RS[:, 1:2], RS[:, 0:1], AF.Exp, [(kb, 3)], [(kb, 4)], scale=-0.5)
                stt("dve", RS[:, 2:3], MV[:, 0:1], -1.0, RS[:, 1:2], ALU.mult, ALU.mult, [(kb, 2), (kb, 4)], [(kb, 5)])
                act(Y[:], Y[:], AF.Identity, [ky, (kb, 4), (kb, 5)], [ky], bias=RS[:, 2:3], scale=RS[:, 1:2])
                tt("pool", Xt[:], Y[:], Gt[:], ALU.mult, [ky, "Gt"], [kx])
                tt("pool", Xt[:], Xt[:], Bt[:], ALU.add, [kx, "Bt"], [kx])
                mk.dma("pool", [(xs[s][128 * i:128 * (i + 1), :], Xt[:])], [kx], [], "pes%d" % sl)

            def stageB(i):
                n = n0 + i
                sl = n % 2
                make_ht(E["Y"][sl], ["Y%d" % sl], E["HT"][sl], "HTe%d" % sl, site, r, 6)
                mk.dma("pool", [(hdst_all[:, :, 1 + 128 * i:1 + 128 * (i + 1)], E["HT"][sl][:])],
                       [("HTe%d" % sl, k) for k in range(8)], [], "peh%d" % sl)

            stageL(0)
            if NT > 1:
                stageL(1)
            stageT(0)
            for i in range(NT):
                if i + 2 < NT:
                    stageL(i + 2)
                if i + 1 < NT:
                    stageT(i + 1)
                stageM(i)
                stageA(i)
                if i >= 1:
                    stageB(i - 1)
            stageB(NT - 1)
            n0 += NT
        mk.release(m)

    def phase_ffn(l):
        m = mk.mark()
        WU = mk.sb([128, 8, 2 * DFF], BF16, "WU")
        WD = mk.sb([128, 22, D], BF16, "WD")
        wv = wup_d[l].rearrange("(k p) n -> p k n", p=128)
        for k in range(8):
            ldc([(WU[:, k, :], wv[:, k, :])], [], [("WU", k)], "w%d" % (k % 4))
        wdv = wdn_d[l].rearrange("(k p) n -> p k n", p=128)
        wuk = [("WU", k) for k in range(8)]
        WDG = [(k0, min(22, k0 + 6)) for k0 in range(0, 22, 6)]
        wdk = [("WD", k0) for (k0, _) in WDG]
        Gt = mk.sb([128, D], F32, "Gt")
        Bt = mk.sb([128, D], F32, "Bt")
        E = epi_alloc(2, 1)
        HB = mk.sb([128, 8, 514], BF16, "HB")
        UW = mk.sb([128, 44, 8, 2], F32, "UW")
        C = [mk.sb([128, 512], F32, "C") for _ in range(3)]
        AC = mk.sb([128, 22, 512], BF16, "AC")
        UH = mk.sb([128, 44, 18], F32, "UH")
        HS = mk.sb([128, 8, 16], BF16, "HS")
        US = [C[0], C[1]]
        memset("pool", UH[:], 0.0, ["UH"])
        memset("pool", HS[:], 0.0, ["HS"])
        last = l == 1
        site = 2 if l == 0 else None
        seqs = SEQ if l == 0 else [s for s in SEQ if s[0] == "lat"]
        fw = [VMAP[("fcw", l, k)] for k in range(3)]
        fb = VMAP[("fcb", l)]
        MB = [0, 1, 2, 7]
        bkeys = ["Gt", "Bt"]
        n = 0
        uc = 0
        for s in seqs:
            r = seq_var(s)
            L = seq_len(s)
            hsrc = hts[(s, 0)].rearrange("k p t -> p k t")
            hdst_all = hts[(s, 1)].rearrange("k p t -> p k t")
            nb = min(512, L)
            nblk = L // nb
            GBt = E["Y"][0]
            ld(GBt[:], modd[l, r, 5 * D:6 * D].partition_broadcast(128), [], ["Y0"], "bc0")
            ld(Gt[:], ln_d["ln2_g"][l].partition_broadcast(128), [], ["Gt"], "bc1")
            ld(Bt[:], ln_d["ln2_b"][l].partition_broadcast(128), [], ["Bt"], "bc2")
            for gi, (k0, k1) in enumerate(WDG):
                ldc([(WD[:, k0:k1, :], wdv[:, k0:k1, :])], [], [("WD", k0)], "w%d" % (gi % 4))
                for kk in range(k0, k1):
                    tt("pool" if kk % 2 else "dve", WD[:, kk, :], WD[:, kk, :], GBt[:], ALU.mult, [("WD", k0), "Y0"], [("WD", k0)])
            if nblk > 1:
                for bd in range(1, nblk):
                    mk.dma("sp", [(HS[:, :, 2 * bd:2 * bd + 2], hsrc[:, :, 512 * bd:512 * bd + 2])], [], ["HS"], "fz", slow=True)
                for cb in range(11):
                    for k in range(8):
                        mm(PS[0:16, 512 * 3:512 * 3 + 512], HS[:, k, :], WU[:, k, 512 * cb:512 * (cb + 1)], k == 0, k == 7,
                           ["HS"] + wuk, pk(3))
                    cp("act", US[cb % 2][0:16, :], PS[0:16, 512 * 3:512 * 3 + 512], pk(3), ["C%d" % (cb % 2)])
                    for q in range(4):
                        c = 4 * cb + q
                        b_ = 5 + c // 22
                        tr(PS[:, 512 * b_ + 16 * (c % 22):512 * b_ + 16 * (c % 22) + 16], US[cb % 2][0:16, 128 * q:128 * (q + 1)],
                           ident[0:16, 0:16], ["C%d" % (cb % 2), "ident"], pk(b_))
                cp("dve", UH[:, 0:22, 0:16], PS[:, 512 * 5:512 * 5 + 352].rearrange("p (c e) -> p c e", e=16), pk(5), ["UH"])
                cp("dve", UH[:, 22:44, 0:16], PS[:, 512 * 6:512 * 6 + 352].rearrange("p (c e) -> p c e", e=16), pk(6), ["UH"])
            else:
                memset("pool", UH[:], 0.0, ["UH"])

            w0b = VEC[:, fw[0]:fw[0] + 44].rearrange("p (c o) -> p c o", o=1).to_broadcast([128, 44, 8])
            w2b = VEC[:, fw[2]:fw[2] + 44].rearrange("p (c o) -> p c o", o=1).to_broadcast([128, 44, 8])
            tt("dve", UW[:, :, :, 0], UH[:, :, 0:15:2], w0b, ALU.mult, ["UH", "VEC"], [("UW", 0)])
            tt("dve", UW[:, :, :, 1], UH[:, :, 3:18:2], w2b, ALU.mult, ["UH", "VEC"], [("UW", 1)])
            uwk = [("UW", 0), ("UW", 1)]

            def xload(nn, tok):
                sl = nn % 2
                ld(E["Xt"][sl][:], xs[s][tok:tok + 128, :], [], ["Xt%d" % sl], "ex%d" % sl)

            def epiA(nn, ps2, pskeys, xdst):
                sl = nn % 2
                Xt, Y = E["Xt"][sl], E["Y"][sl]
                kx, ky = "Xt%d" % sl, "Y%d" % sl
                stt("dve", Y[:], Xt[:], ALPHA, ps2, ALU.mult, ALU.add, [kx] + pskeys, [ky])
                BS, MV, RS = E["BS"][sl], E["MV"][sl], E["RS"][sl]
                kb = "BS%d" % sl
                mk.op("dve", lambda e: e.bn_stats(out=BS[:, 0, :], in_=Y[:, 0:512]), [ky], [(kb, 0)])
                mk.op("dve", lambda e: e.bn_stats(out=BS[:, 1, :], in_=Y[:, 512:1024]), [ky], [(kb, 1)])
                mk.op("dve", lambda e: e.bn_aggr(out=MV[:], in_=BS[:]), [(kb, 0), (kb, 1)], [(kb, 2)])
                act(RS[:, 0:1], MV[:, 1:2], AF.Ln, [(kb, 2), "eps"], [(kb, 3)], bias=epsT[:, 0:1])
                act(RS[:, 1:2], RS[:, 0:1], AF.Exp, [(kb, 3)], [(kb, 4)], scale=-0.5)
                stt("dve", RS[:, 2:3], MV[:, 0:1], -1.0, RS[:, 1:2], ALU.mult, ALU.mult, [(kb, 2), (kb, 4)], [(kb, 5)])
                act(Y[:], Y[:], AF.Identity, [ky, (kb, 4), (kb, 5)], [ky], bias=RS[:, 2:3], scale=RS[:, 1:2])
                tt("pool", Xt[:], Y[:], Gt[:], ALU.mult, [ky] + bkeys, [kx])
                tt("pool", Xt[:], Xt[:], Bt[:], ALU.add, [kx] + bkeys, [kx])
                ld(xdst, Xt[:], [kx], [], "es%d" % sl)

            def epiB(nn, hdst):
                sl = nn % 2
                make_ht(E["Y"][sl], ["Y%d" % sl], E["HT"][0], "HTe0", site, r, 5)
                ld(hdst, E["HT"][0][:], [("HTe0", k) for k in range(8)], [], "eh0")

            def tail(pend):
                i_, cg, cgk, cv, cvk = pend
                act(cg[:, 0:nb], cg[:, 0:nb], AF.Gelu_apprx_tanh, [cgk], [cgk])
                tt("pool", AC[:, i_, 0:nb], cg[:, 0:nb], cv[:, 0:nb], ALU.mult, [cgk, cvk], [("AC", i_)])

            for j in range(nblk):
                t0 = nb * j
                if j == 0:
                    ld(HB[:, :, 0:nb + 2], hsrc[:, :, t0:t0 + nb + 2], [], ["HB"], "fh")
                xload(n, t0)
                pend = None
                for i in range(22):
                    cur = []
                    for gv in range(2):
                        c = i + 22 * gv
                        u = uc % 4
                        cb_ = uc % 3
                        uc += 1
                        mb = MB[u]
                        for k in range(8):
                            mm(bank(mb, nb), WU[:, k, 128 * c:128 * (c + 1)], HB[:, k, 1:nb + 1], k == 0, k == 7,
                               wuk + ["HB"], pk(mb))
                        Ct = C[cb_]
                        ck = "C%d" % cb_
                        act(Ct[:, 0:nb], bank(mb, nb), AF.Identity, pk(mb) + ["VEC"], [ck],
                            bias=VEC[:, fb + c:fb + c + 1], scale=VEC[:, fw[1] + c:fw[1] + c + 1])
                        tt("pool", Ct[:, 0:nb:nb - 1], Ct[:, 0:nb:nb - 1], UW[:, c, j, :], ALU.add, uwk + [ck], [ck])
                        stt("dve", Ct[:, 1:nb], bank(mb, nb - 1), VEC[:, fw[0] + c:fw[0] + c + 1], Ct[:, 1:nb], ALU.mult, ALU.add,
                            pk(mb) + ["VEC", ck], [ck])
                        stt("dve", Ct[:, 0:nb - 1], bank(mb, nb - 1, 1), VEC[:, fw[2] + c:fw[2] + c + 1], Ct[:, 0:nb - 1], ALU.mult, ALU.add,
                            pk(mb) + ["VEC", ck], [ck])
                        cur += [Ct, ck]
                        if gv == 0 and pend is not None:
                            tail(pend)
                            pend = None
                    pend = (i, cur[0], cur[1], cur[2], cur[3])
                tail(pend)
                if j + 1 < nblk:
                    ld(HB[:, :, 0:nb + 2], hsrc[:, :, t0 + nb:t0 + 2 * nb + 2], [], ["HB"], "fh")
                ack = [("AC", i) for i in range(22)]
                nt_ = nb // 128
                prevB = None
                for tq in range(nt_):
                    tok = t0 + 128 * tq
                    for k in range(22):
                        for hf in range(2):
                            mm(bank(3 + hf), AC[:, k, 128 * tq:128 * (tq + 1)], WD[:, k, 512 * hf:512 * (hf + 1)],
                               k == 0, k == 21, [("AC", k)] + wdk, pk(3 + hf))
                    if tq + 1 < nt_:
                        xload(n + 1, tok + 128)
                    if last:
                        xdst = out_d[s[1], tok:tok + 128, :]
                    else:
                        xdst = xs[s][tok:tok + 128, :]
                    epiA(n, PS[:, 512 * 3:512 * 3 + 1024], pk(3) + pk(4), xdst)
                    if prevB is not None:
                        epiB(*prevB)
                        prevB = None
                    if not last:
                        prevB = (n, hdst_all[:, :, 1 + tok:1 + tok + 128])
                    n += 1
                if prevB is not None:
                    epiB(*prevB)
        mk.release(m)

    def phase_rglru():
        m = mk.mark()
        CX0, LX0, NX = 2, 261, 4358
        XR = mk.sb([128, NX], F32, "XR")
        XC = mk.sb([128, NX], F32, "XC")
        XCb = mk.sb([128, NX], BF16, "XCb")
        A0 = mk.sb([128, NX], F32, "A0")
        A1 = mk.sb([128, NX], F32, "A1")
        BT0 = mk.sb([128, NX], F32, "BT0")
        TM = mk.sb([128, NX], F32, "TM")
        RF = mk.sb([128, NX], F32, "RF")
        RR = mk.sb([128, NX], F32, "RR")
        GY = [mk.sb([128, T], BF16, "GY") for _ in range(2)]
        MT = mk.sb([128, T], BF16, "MT")
        WI = [mk.sb([128, 8, 2, 128], BF16, "WI") for _ in range(2)]
        GW = [mk.sb([128, 4, 128], BF16, "GW") for _ in range(2)]
        ONE = mk.sb([128, 1], F32, "ONE")
        memset("pool", ONE[:], 1.0, ["ONE"])
        NHB = 2
        HB = [mk.sb([128, 8, 512], BF16, "HBr") for _ in range(NHB)]
        memset("dve", XR[:], 0.0, ["XRz"])
        wv = rwin_d[0].rearrange("(k p) n -> p k n", p=128)
        regions = [(CX0, TC), (LX0, T)]
        BLOCKS = [("ctx", 0, 256, CX0)] + [("lat", 512 * j, 512, LX0 + 512 * j) for j in range(8)]
        chunks = [(b, c) for b in range(NBB) for c in range(8)]
        items = [(b, c, bi) for (b, c) in chunks for bi in range(9)]
        issued = [0]
        nh = [0]

        def prefetch(upto):
            while issued[0] <= min(upto, len(items) - 1):
                b_, c_, bi_ = items[issued[0]]
                kind_, t0_, nt_, xo_ = BLOCKS[bi_]
                hb = issued[0] % NHB
                src = hts[((kind_, b_), 1)].rearrange("k p t -> p k t")
                ld(HB[hb][:, :, 0:nt_], src[:, :, 1 + t0_:1 + t0_ + nt_], [], ["HBr%d" % hb], "rh%d" % hb)
                issued[0] += 1

        def inproj(ci):
            b, c = chunks[ci]
            w_ = ci % 2
            ldc([(WI[w_][:, :, 0, :], wv[:, :, 128 * c:128 * (c + 1)]),
                 (WI[w_][:, :, 1, :], wv[:, :, D + 128 * c:D + 128 * (c + 1)])], [], ["WI%d" % w_], "w%d" % w_)
            ldc([(GW[w_][:, 0, :], rga_d[0, 0, c]), (GW[w_][:, 1, :], rgx_d[0, 0, c]),
                 (GW[w_][:, 2, :], rga_d[0, 1, c]), (GW[w_][:, 3, :], rgx_d[0, 1, c])], [], ["GW%d" % w_], "w%d" % (2 + w_))
            gy = GY[ci % 2]
            for (kind, t0, nt, xo) in BLOCKS:
                prefetch(nh[0] + NHB - 1)
                sl = nh[0] % NHB
                bk = 2 * (nh[0] % 2)
                nh[0] += 1
                for k in range(8):
                    mm(bank(bk, nt), WI[w_][:, k, 1, :], HB[sl][:, k, 0:nt], k == 0, k == 7, ["WI%d" % w_, "HBr%d" % sl], pk(bk))
                cp("act", XR[:, xo:xo + nt], bank(bk, nt), pk(bk) + ["XRz"], [("XR", xo)])
                if kind == "lat":
                    for k in range(8):
                        mm(bank(bk + 1, nt), WI[w_][:, k, 0, :], HB[sl][:, k, 0:nt], k == 0, k == 7,
                           ["WI%d" % w_, "HBr%d" % sl], pk(bk + 1))
                    act(gy[:, t0:t0 + nt], bank(bk + 1, nt), AF.Gelu_apprx_tanh, pk(bk + 1), [("GY%d" % (ci % 2), t0)])

        xrk = [("XR", xo) for (_, _, _, xo) in BLOCKS] + ["XRz"]
        xck = [("XC", o) for (o, _) in regions]
        xcbk = [("XCb", o) for (o, _) in regions]
        (oc, ncx), (ol, nl) = regions
        inproj(0)
        for ci, (b, c) in enumerate(chunks):
            w_ = ci % 2
            w = [VMAP[("rcw", k)] + c for k in range(4)]
            cb = VMAP[("rcb",)] + c
            for (o, n_) in regions:
                act(XC[:, o:o + n_], XR[:, o:o + n_], AF.Identity, xrk + ["VEC"], [("XC", o)],
                    bias=VEC[:, cb:cb + 1], scale=VEC[:, w[2]:w[2] + 1])
                for (kk, sh) in ((0, -2), (1, -1), (3, 1)):
                    stt("dve", XC[:, o:o + n_], XR[:, o + sh:o + sh + n_], VEC[:, w[kk]:w[kk] + 1], XC[:, o:o + n_],
                        ALU.mult, ALU.add, xrk + ["VEC", ("XC", o)], [("XC", o)])
                cp("pool", XCb[:, o:o + n_], XC[:, o:o + n_], [("XC", o)], [("XCb", o)])
            for d in range(2):
                Ad = A0 if d == 0 else A1
                Bd = BT0 if d == 0 else RR
                an, bn = "A%d" % d, "B%d" % d
                ba = VMAP[("rba", d)] + c
                bx = VMAP[("rbx", d)] + c
                for gi, (kind, t0, nt, xo) in enumerate(BLOCKS):
                    bk = 4 + 2 * (gi % 2)
                    ro = CX0 if kind == "ctx" else LX0
                    mm(bank(bk, nt), GW[w_][:, 2 * d, :], XCb[:, xo:xo + nt], True, True, ["GW%d" % w_] + xcbk, pk(bk))
                    mm(bank(bk + 1, nt), GW[w_][:, 2 * d + 1, :], XCb[:, xo:xo + nt], True, True, ["GW%d" % w_] + xcbk,
                       pk(bk + 1))
                    act(Ad[:, xo:xo + nt], bank(bk, nt), AF.Sigmoid, pk(bk) + ["VEC"], [(an, xo), (an, "r", ro)],
                        bias=VEC[:, ba:ba + 1])
                    act(Bd[:, xo:xo + nt], bank(bk + 1, nt), AF.Sigmoid, pk(bk + 1) + ["VEC"], [(bn, xo), (bn, "r", ro)],
                        bias=VEC[:, bx:bx + 1])
                ak = [(an, xo) for (_, _, _, xo) in BLOCKS]
                btk = [(bn, xo) for (_, _, _, xo) in BLOCKS]
                for (o, n_) in regions:
                    sl_ = slice(o, o + n_)
                    kA, kB, kT = (an, "r", o), (bn, "r", o), ("TM", d, o)
                    act(Ad[:, sl_], Ad[:, sl_], AF.Exp, ak + ["CST"], [kA], scale=CST[:, 8 * d + c:8 * d + c + 1])
                    act(TM[:, sl_], Ad[:, sl_], AF.Square, [kA], [("TM", o)])
                    act(TM[:, sl_], TM[:, sl_], AF.Sqrt, [("TM", o), "ONE"], [("TM", o)], bias=ONE[:, 0:1], scale=-1.0)
                    tt("dve", Bd[:, sl_], Bd[:, sl_], XC[:, sl_], ALU.mult, btk + xck, [kB])
                    tt("dve", Bd[:, sl_], Bd[:, sl_], TM[:, sl_], ALU.mult, [kB, ("TM", o)], [kB])
                if d == 0:
                    mk.op("dve", lambda e: e.tensor_tensor_scan(out=RF[:, oc:oc + ncx], data0=A0[:, oc:oc + ncx],
                                                                data1=BT0[:, oc:oc + ncx], initial=0.0,
                                                                op0=ALU.mult, op1=ALU.add),
                          [("A0", "r", oc), ("B0", "r", oc)], [("RFc",)])
                    mk.op("dve", lambda e: e.tensor_tensor_scan(out=RF[:, ol:ol + nl], data0=A0[:, ol:ol + nl],
                                                                data1=BT0[:, ol:ol + nl],
                                                                initial=RF[:, oc + ncx - 1:oc + ncx],
                                                                op0=ALU.mult, op1=ALU.add),
                          [("A0", "r", ol), ("B0", "r", ol), ("RFc",)], [("RFl",)])
                else:
                    mk.op("dve", lambda e: e.tensor_tensor_scan(out=RR[:, oc:oc + ncx][:, ::-1],
                                                                data0=A1[:, oc:oc + ncx][:, ::-1],
                                                                data1=RR[:, oc:oc + ncx][:, ::-1], initial=0.0,
                                                                op0=ALU.mult, op1=ALU.add),
                          [("A1", "r", oc), ("B1", "r", oc)], [("B1", "r", oc), ("RRc",)])
                    mk.op("dve", lambda e: e.tensor_tensor_scan(out=RR[:, ol:ol + nl][:, ::-1],
                                                                data0=A1[:, ol:ol + nl][:, ::-1],
                                                                data1=RR[:, ol:ol + nl][:, ::-1],
                                                                initial=RR[:, oc:oc + 1],
                                                                op0=ALU.mult, op1=ALU.add),
                          [("A1", "r", ol), ("B1", "r", ol), ("RRc",)], [("B1", "r", ol), ("RRl",)])
                if d == 1 and ci + 1 < len(chunks):
                    inproj(ci + 1)
            tt("pool", RF[:, LX0:LX0 + T], RF[:, LX0:LX0 + T], RR[:, LX0:LX0 + T], ALU.add, [("RFl",), ("RRl",), ("B1", "r", ol)], [("RFl",)])
            tt("dve", MT[:], RF[:, LX0:LX0 + T], GY[ci % 2][:], ALU.mult,
               [("RFl",)] + [("GY%d" % (ci % 2), 512 * j) for j in range(8)], ["MT"])
            ld(mts[b][c], MT[:], ["MT"], [], "rm")
        mk.release(m)

    order = ["qkv", "diff", "nbr", "op0", "ffn0", "rg", "op1", "ffn1"]
    fns = {"qkv": phase_qkv, "diff": phase_diff, "nbr": phase_nbr, "op0": lambda: phase_oproj(0),
           "ffn0": lambda: phase_ffn(0), "rg": phase_rglru, "op1": lambda: phase_oproj(1), "ffn1": lambda: phase_ffn(1)}
    for ph in order:
        if stop == "p0":
            break
        if "only" in CFG and ph not in CFG["only"]:
            continue
        fns[ph]()
        if stop == ph:
            break
    mk.barrier()
    named = {"modd": modd, "ao0": aos[0], "xs_lat0": xs[("lat", 0)], "xs_ctx0": xs[("ctx", 0)],
             "qat0": qat[0], "kat0": kat[0], "va0": vas[0], "qbt0": qbt[0], "kbt0": kbt[0], "vb0": vbs[0],
             "hts0_lat0": hts[(("lat", 0), 0)], "hts1_lat0": hts[(("lat", 0), 1)], "mt0": mts[0]}
    for nm in dbg:
        src = named[nm]
        dst = nc.dram_tensor("dbg_" + nm, list(src.shape), src.dtype, kind="ExternalOutput").ap()
        mk.dma("sp", [(dst, src)], [], [], "dbg")
    mk.emit()
    return nc, mk


def prep_inputs(inputs, core):
    g = lambda k: np.asarray(inputs[k])
    b0 = 2 * core
    m = {}
    m["x"] = np.ascontiguousarray(g("x")[b0:b0 + 2])
    m["ctx"] = np.ascontiguousarray(g("ctx")[b0:b0 + 2])
    cc = np.stack([g("c")[b0], g("c")[b0 + 1], g("c_ctx")], 0).astype(np.float32)
    m["cct"] = np.ascontiguousarray(cc.reshape(3, 8, 128).transpose(2, 1, 0))
    vec = np.zeros((128, NV), np.float32)
    for l in range(2):
        for nm in ("ln1_g", "ln1_b", "ln2_g", "ln2_b"):
            vec[:, VMAP[(nm, l)]:VMAP[(nm, l)] + 8] = _pl(g(nm)[l])
        for k in range(3):
            vec[:, VMAP[("fcw", l, k)]:VMAP[("fcw", l, k)] + 44] = _pl(g("ffn_conv_w")[l, k])
        vec[:, VMAP[("fcb", l)]:VMAP[("fcb", l)] + 44] = _pl(g("ffn_conv_b")[l])
    for k in range(4):
        vec[:, VMAP[("rcw", k)]:VMAP[("rcw", k)] + 8] = _pl(g("rnn_conv_w")[0, k])
    vec[:, VMAP[("rcb",)]:VMAP[("rcb",)] + 8] = _pl(g("rnn_conv_b")[0])
    for d in range(2):
        vec[:, VMAP[("apar", d)]:VMAP[("apar", d)] + 8] = _pl(g("rg_a_param")[0, d])
        vec[:, VMAP[("rba", d)]:VMAP[("rba", d)] + 8] = _pl(g("rg_ba")[0, d].reshape(-1))
        vec[:, VMAP[("rbx", d)]:VMAP[("rbx", d)] + 8] = _pl(g("rg_bx")[0, d].reshape(-1))
    m["vec"] = vec
    m["rope"] = _ROPE
    rpb = g("na_rpb")[0].astype(np.float32)
    kc = np.arange(64)[:, None]
    cq = np.arange(64)[None, :]
    rel = kc - cq + 15
    ok = (rel >= 0) & (rel <= 30)
    gath = rpb[:, :, np.clip(rel, 0, 30)] * ok[None, None]
    perm = [0, 2, 4, 6, 1, 3, 5, 7]
    m["rpbt"] = np.ascontiguousarray(gath[perm].transpose(1, 2, 0, 3)).astype(np.float32)
    m["mask"] = _MASK
    m["lamv"] = np.stack([g("diff_lq1")[0], g("diff_lk1")[0], g("diff_lq2")[0], g("diff_lk2")[0]], 0).astype(np.float32)
    m["subg"] = np.ascontiguousarray(g("diff_subln_g")[0]).astype(np.float32)
    for k in ("ada_w", "ada_b", "ln1_g", "ln1_b", "ln2_g", "ln2_b", "ffn_w_up", "ffn_w_down", "att_w_in",
              "att_w_out", "rnn_w_in", "rg_wa", "rg_wx", "rnn_w_out"):
        m[k] = np.ascontiguousarray(g(k)).astype(np.float32)
    return m


def _const_tables():
    t = np.arange(T)
    row = (t // 64).astype(np.float64)[:, None]
    col = (t % 64).astype(np.float64)[:, None]
    inv = 1.0 / (10000.0 ** (np.arange(16, dtype=np.float64) / 16))
    inv = inv.astype(np.float32).astype(np.float64)
    ang = np.concatenate([row * inv, row * inv, col * inv, col * inv], -1).astype(np.float32)
    cos = np.cos(ang).astype(np.float32)
    sin = np.sin(ang).astype(np.float32)
    sgn = np.tile(np.concatenate([-np.ones(16), np.ones(16)]), 2).astype(np.float32)
    rope = np.stack([cos, sin * sgn[None]], 1).astype(np.float32)
    c = np.arange(64)
    cs = np.clip(c - 8, 0, 48)
    kc = np.arange(64)[:, None]
    inside = (kc >= cs[None]) & (kc < cs[None] + 16)
    mask = np.where(inside, 0.0, NEG).astype(np.float32)
    return np.ascontiguousarray(rope), np.ascontiguousarray(np.concatenate([mask, mask], 0))


_ROPE, _MASK = _const_tables()
_CACHE = {}
CFG = {}


def kernel(**inputs):
    if "nc" not in _CACHE:
        _CACHE["nc"] = build()[0]
    nc = _CACHE["nc"]
    in_maps = [prep_inputs(inputs, c) for c in range(8)]
    res = run_bass_kernel_spmd(nc, in_maps, core_ids=list(range(8)))
    out = np.concatenate([r["out"] for r in res.results], axis=0)
    return out.astype(np.float32)
```

```python
import math
import contextlib
import numpy as np
import concourse.bass as bass
import concourse.mybir as mybir
from concourse.bass_utils import run_bass_kernel_spmd

F32 = mybir.dt.float32
BF16 = mybir.dt.bfloat16
AF = mybir.ActivationFunctionType
ALU = mybir.AluOpType

ENGS = ["pe", "act", "dve", "pool", "sp"]

D = 1024
T = 4096
TC = 256
TK = T + TC
DFF = 2816
ALPHA = 4.0 ** 0.25
EPS = 1e-5
LAMBDA_INIT0 = 0.8 - 0.6 * math.exp(0.0)
NEG = -30000.0


class MK:
    def __init__(self, nc):
        self.nc = nc
        self.ops = {e: [] for e in ENGS}
        self.cnt = {e: 0 for e in ENGS}
        self.dcnt = {}
        self.seen = {e: {} for e in ENGS}
        self.lastw = {}
        self.readers = {}
        self.sb_off = 16640
        self.sb_names = 0
        self.sb_max = 0

    def sb(self, shape, dtype, name=None):
        nbytes = int(np.prod(shape[1:])) * (4 if dtype == F32 else 2)
        nbytes = (nbytes + 63) // 64 * 64
        self.sb_names += 1
        nm = "%s_%d" % (name or "t", self.sb_names)
        t = self.nc.alloc_sbuf_tensor_at(nm, list(shape), dtype, offset=self.sb_off)
        self.sb_off += nbytes
        self.sb_max = max(self.sb_max, self.sb_off)
        assert self.sb_off <= 229376, ("sbuf overflow", nm, self.sb_off)
        return t

    def mark(self):
        return self.sb_off

    def release(self, m):
        self.barrier()
        self.sb_off = m

    def _deps(self, eng, reads, writes, is_dma):
        deps = {}

        def add(tok, raw):
            if tok is None:
                return
            sk, val, teng = tok
            if not is_dma and teng == eng and sk[0] == "e":
                if eng == "pe":
                    return
            if deps.get(sk, 0) < val:
                deps[sk] = val

        for k in reads:
            add(self.lastw.get(k), True)
        for k in writes:
            add(self.lastw.get(k), False)
            for sk, (val, teng) in self.readers.get(k, {}).items():
                add((sk, val, teng), False)
        waits = []
        seen = self.seen[eng]
        for sk, val in deps.items():
            if seen.get(sk, 0) >= val:
                continue
            seen[sk] = val
            waits.append((sk, val))
        return waits

    def _commit(self, tok, reads, writes):
        sk, val, teng = tok
        for k in writes:
            self.lastw[k] = tok
            self.readers[k] = {}
        for k in reads:
            self.readers.setdefault(k, {})[sk] = (val, teng)

    def op(self, eng, fn, reads=(), writes=()):
        if eng != "pe":
            pr = [k for k in reads if isinstance(k, tuple) and k and k[0] == "p"]
            if pr:
                writes = list(writes) + [k for k in pr if k not in writes]
        waits = self._deps(eng, reads, writes, False)
        self.cnt[eng] += 1
        tok = (("e", eng), self.cnt[eng], eng)
        self.ops[eng].append((waits, fn, tok, 1))
        self._commit(tok, reads, writes)
        return tok

    def dma(self, eng, pairs, reads, writes, slot, slow=False):
        sk = ("d", slot)
        prev = self.dcnt.get(slot, 0)
        waits = self._deps(eng, reads, writes, True)
        if prev and self.seen[eng].get(sk, 0) < prev:
            self.seen[eng][sk] = prev
            waits.append((sk, prev))
        n = len(pairs)
        self.dcnt[slot] = prev + 16 * n
        tok = (sk, prev + 16 * n, None)

        def fn(e, pairs=pairs, slow=slow):
            if slow:
                return [e.dma_start(out=o, in_=i, allow_slow_non_contiguous=True) for (o, i) in pairs]
            return [e.dma_start(out=o, in_=i) for (o, i) in pairs]

        self.ops[eng].append((waits, fn, tok, 16))
        self._commit(tok, reads, writes)
        return tok

    def barrier(self):
        final = {}
        for e in ENGS:
            if self.cnt[e]:
                final[("e", e)] = self.cnt[e]
        for s, v in self.dcnt.items():
            final[("d", s)] = v
        for e in ENGS:
            waits = []
            for sk, val in final.items():
                if sk == ("e", e):
                    continue
                if self.seen[e].get(sk, 0) < val:
                    self.seen[e][sk] = val
                    waits.append((sk, val))
            if waits:
                self.ops[e].append((waits, None, None, 0))
        self.lastw = {}
        self.readers = {}

    def emit(self):
        nc = self.nc
        self.barrier()
        waited = {e: set() for e in ENGS}
        for e in ENGS:
            for waits, fn, tok, inc in self.ops[e]:
                for sk, val in waits:
                    if sk[0] == "e":
                        waited[sk[1]].add(val)
        remap = {}
        for e in ENGS:
            vals = sorted(waited[e])
            remap[e] = {v: i + 1 for i, v in enumerate(vals)}
        sems = {}
        with contextlib.ExitStack() as st:
            for e in ENGS:
                sems[("e", e)] = st.enter_context(nc.semaphore("s_" + e))
            for s in self.dcnt:
                sems[("d", s)] = st.enter_context(nc.semaphore("d_" + str(s)))
            block = st.enter_context(nc.Block())

            def run(engname):
                def body(eng):
                    for waits, fn, tok, inc in self.ops[engname]:
                        for sk, val in waits:
                            v = remap[sk[1]][val] if sk[0] == "e" else val
                            eng.wait_ge(sems[sk], v)
                        if fn is None:
                            continue
                        r = fn(eng)
                        if inc == 16:
                            for ins in r:
                                ins.then_inc(sems[tok[0]], 16)
                        elif tok[1] in remap[engname]:
                            r.then_inc(sems[tok[0]], 1)

                return body

            block.tensor(run("pe"))
            block.scalar(run("act"))
            block.vector(run("dve"))
            block.gpsimd(run("pool"))
            block.sync(run("sp"))


def _vec_map():
    m = {}
    o = 0
    for l in range(2):
        for nm in ("ln1_g", "ln1_b", "ln2_g", "ln2_b"):
            m[(nm, l)] = o
            o += 8
    for l in range(2):
        for k in range(3):
            m[("fcw", l, k)] = o
            o += 44
    for l in range(2):
        m[("fcb", l)] = o
        o += 44
    for k in range(4):
        m[("rcw", k)] = o
        o += 8
    m[("rcb",)] = o
    o += 8
    for d in range(2):
        m[("apar", d)] = o
        o += 8
    for d in range(2):
        m[("rba", d)] = o
        o += 8
    for d in range(2):
        m[("rbx", d)] = o
        o += 8
    return m, o


VMAP, NV = _vec_map()


def _pl(v):
    v = np.asarray(v, np.float32).reshape(-1, 128)
    return np.ascontiguousarray(v.T)


def build(stop=None, dbg=()):
    nc = bass.Bass("TRN2", target_bir_lowering=False)
    mk = MK(nc)

    def din(name, shape, dt=F32):
        return nc.dram_tensor(name, list(shape), dt, kind="ExternalInput").ap()

    def dscr(name, shape, dt=F32):
        return nc.dram_tensor(name, list(shape), dt, kind="Internal").ap()

    x_d = din("x", [2, T, D])
    ctx_d = din("ctx", [2, TC, D])
    cct_d = din("cct", [128, 8, 3])
    vec_d = din("vec", [128, NV])
    rope_d = din("rope", [T, 2, 64])
    rpbt_d = din("rpbt", [15, 64, 8, 64])
    mask_d = din("mask", [128, 64])
    lam_d = din("lamv", [4, 64])
    subg_d = din("subg", [128])
    ada_w_d = din("ada_w", [2, D, 6 * D])
    ada_b_d = din("ada_b", [2, 6 * D])
    ln_d = {nm: din(nm, [2, D]) for nm in ("ln1_g", "ln1_b", "ln2_g", "ln2_b")}
    wup_d = din("ffn_w_up", [2, D, 2 * DFF])
    wdn_d = din("ffn_w_down", [2, DFF, D])
    awin_d = din("att_w_in", [1, D, 3 * D])
    awout_d = din("att_w_out", [1, D, D])
    rwin_d = din("rnn_w_in", [1, D, 2 * D])
    rga_d = din("rg_wa", [1, 2, 8, 128, 128])
    rgx_d = din("rg_wx", [1, 2, 8, 128, 128])
    rwout_d = din("rnn_w_out", [1, D, D])
    out_d = nc.dram_tensor("out", [2, T, D], F32, kind="ExternalOutput").ap()

    modd = dscr("modd", [2, 3, 6 * D])
    NBB = CFG.get("nb", 2)
    SEQ = [("lat", 0), ("ctx", 0), ("lat", 1), ("ctx", 1)][:2 * NBB]
    SEQ_ALL = [("lat", 0), ("ctx", 0), ("lat", 1), ("ctx", 1)]
    xs = {s: dscr("xs_%s%d" % s, [T if s[0] == "lat" else TC, D]) for s in SEQ_ALL}
    hts = {(s, a): dscr("hts%d_%s%d" % ((a,) + s), [8, 128, (T if s[0] == "lat" else TC) + 2], BF16)
           for s in SEQ_ALL for a in (0, 1)}
    qat = [dscr("qat%d" % b, [4, 128, TK], BF16) for b in range(2)]
    kat = [dscr("kat%d" % b, [4, 128, TK], BF16) for b in range(2)]
    qbt = [dscr("qbt%d" % b, [4, 128, TK], BF16) for b in range(2)]
    kbt = [dscr("kbt%d" % b, [4, 128, TK], BF16) for b in range(2)]
    vas = [dscr("va%d" % b, [TK, 512], BF16) for b in range(2)]
    vbs = [dscr("vb%d" % b, [TK, 512], BF16) for b in range(2)]
    aos = [dscr("ao%d" % b, [TK, D]) for b in range(2)]
    mts = [dscr("mt%d" % b, [8, 128, T], BF16) for b in range(2)]

    def seq_src(s):
        return x_d[s[1]] if s[0] == "lat" else ctx_d[s[1]]

    def seq_len(s):
        return T if s[0] == "lat" else TC

    def seq_var(s):
        return s[1] if s[0] == "lat" else 2

    PS = nc.alloc_psum_tensor("PS", [128, 4096], F32)

    def bank(i, n=512, off=0):
        return PS[:, 512 * i + off:512 * i + off + n]

    def pk(i):
        return [("p", i)]

    def mm(out, lhsT, rhs, start, stop, r, w, skip=False):
        mk.op("pe", lambda e: e.matmul(out, lhsT=lhsT, rhs=rhs, start=start, stop=stop, skip_group_check=skip), r, w)

    def tr(out, in_, idn, r, w):
        mk.op("pe", lambda e: e.transpose(out, in_, idn), r, w)

    def act(out, in_, func, r, w, bias=None, scale=None, accum=None):
        kw = {}
        if bias is not None:
            kw["bias"] = bias
        if scale is not None:
            kw["scale"] = scale
        if accum is not None:
            kw["accum_out"] = accum
        mk.op("act", lambda e: e.activation(out=out, in_=in_, func=func, **kw), r, w)

    def ts(eng, out, in0, s1, s2, op0, op1, r, w):
        if s2 is None:
            if op0 == ALU.mult:
                mk.op(eng, lambda e: e.tensor_scalar_mul(out=out, in0=in0, scalar1=s1), r, w)
            else:
                assert op0 == ALU.add
                mk.op(eng, lambda e: e.tensor_scalar_add(out=out, in0=in0, scalar1=s1), r, w)
        else:
            mk.op(eng, lambda e: e.tensor_scalar(out=out, in0=in0, scalar1=s1, scalar2=s2, op0=op0, op1=op1), r, w)

    def stt(eng, out, in0, scalar, in1, op0, op1, r, w):
        mk.op(eng, lambda e: e.scalar_tensor_tensor(out=out, in0=in0, scalar=scalar, in1=in1, op0=op0, op1=op1), r, w)

    def tt(eng, out, in0, in1, op, r, w):
        mk.op(eng, lambda e: e.tensor_tensor(out=out, in0=in0, in1=in1, op=op), r, w)

    def cp(eng, out, in_, r, w):
        if eng == "act":
            mk.op("act", lambda e: e.copy(out=out, in_=in_), r, w)
        else:
            mk.op(eng, lambda e: e.tensor_copy(out=out, in_=in_), r, w)

    def memset(eng, ap, val, w):
        mk.op(eng, lambda e: e.memset(ap, val), [], w)

    def ld(out, in_, r, w, slot):
        mk.dma("sp", [(out, in_)], r, w, slot)

    def ldc(pairs, r, w, slot):
        mk.dma("pool", pairs, r, w, slot)

    ident = mk.sb([128, 128], F32, "ident")
    memset("pool", ident[:], 0.0, ["ident"])
    mk.op("pool", lambda e: e.affine_select(out=ident[:], in_=ident[:], compare_op=ALU.not_equal, fill=1.0,
                                             base=0, pattern=[[-1, 128]], channel_multiplier=1), ["ident"], ["ident"])
    identb = mk.sb([128, 128], BF16, "identb")
    cp("pool", identb[:], ident[:], ["ident"], ["identb"])
    epsT = mk.sb([128, 1], F32, "eps")
    memset("dve", epsT[:], EPS, ["eps"])
    VEC = mk.sb([128, NV], F32, "VEC")
    ld(VEC[:], vec_d, [], ["VEC"], "c0")
    SCAL = mk.sb([128, 4, 3, 2, 8], F32, "SCAL")
    CST = mk.sb([128, 16], F32, "CST")
    NLAM = mk.sb([128, 1], F32, "NLAM")
    GSUB = mk.sb([128, 128], F32, "GSUB")

    def vcol(key, n=8):
        o = VMAP[key]
        return VEC[:, o:o + n]

    m0 = mk.mark()
    MODP = [mk.sb([128, 48, 3], F32, "MODP%d" % l) for l in range(2)]
    ZT = mk.sb([128, 8, 2], BF16, "ZT")
    memset("dve", ZT[:], 0.0, ["ZT"])
    for s in SEQ:
        for a in (0, 1):
            L = seq_len(s)
            h = hts[(s, a)].rearrange("k p t -> p k t")
            mk.dma("sp", [(h[:, :, 0:1], ZT[:, :, 0:1])], ["ZT"], [], "z0", slow=True)
            mk.dma("sp", [(h[:, :, L + 1:L + 2], ZT[:, :, 1:2])], ["ZT"], [], "z0", slow=True)

    CC = mk.sb([128, 8, 3], F32, "CC")
    ST = mk.sb([128, 8, 3], F32, "ST")
    ld(CC[:], cct_d, [], ["CC"], "c1")
    act(ST[:], CC[:], AF.Silu, ["CC"], ["ST"])
    MODR = mk.sb([3, 6 * D], F32, "MODR")
    ADAB = mk.sb([3, 6 * D], F32, "ADAB")
    WB = [mk.sb([128, 8, 512], F32, "WB%d" % i) for i in range(2)]
    for l in range(2):
        ld(ADAB[:], ada_b_d[l].partition_broadcast(3), [], ["ADAB"], "c2")
        wv = ada_w_d[l].rearrange("(k p) n -> p k n", p=128)
        for jb in range(12):
            sl = jb % 2
            ld(WB[sl][:], wv[:, :, jb * 512:(jb + 1) * 512], [], ["WB%d" % sl], "wb%d" % sl)
            bk = jb % 2
            for k in range(8):
                mm(PS[0:3, 512 * bk:512 * bk + 512], ST[:, k, :], WB[sl][:, k, :], k == 0, k == 7,
                   ["ST", "WB%d" % sl], pk(bk))
            tt("dve", MODR[0:3, jb * 512:(jb + 1) * 512], PS[0:3, 512 * bk:512 * bk + 512],
               ADAB[0:3, jb * 512:(jb + 1) * 512], ALU.add, pk(bk) + ["ADAB"], [("MODR", jb)])
        allk = [("MODR", jb) for jb in range(12)]
        ld(modd[l], MODR[0:3, :], allk, [], "c3")
        for k in range(48):
            tr(PS[:, 3584 + 3 * k:3584 + 3 * k + 3], MODR[0:3, 128 * k:128 * (k + 1)], ident[0:3, 0:3],
               allk + ["ident"], pk(7))
        cp("dve", MODP[l][:].rearrange("p k r -> p (k r)"), PS[:, 3584:3584 + 144], pk(7), ["MODP%d" % l])
    TMP8 = mk.sb([128, 8], F32, "TMP8")
    sites = [(0, 8, 0, None, None), (0, 32, 24, ("ln1_g", 0), ("ln1_b", 0)),
             (1, 8, 0, ("ln2_g", 0), ("ln2_b", 0)), (1, 32, 24, ("ln1_g", 1), ("ln1_b", 1))]
    for si, (l, sco, sho, gk, bk_) in enumerate(sites):
        for r in range(3):
            A_ = SCAL[:, si, r, 0, :]
            B_ = SCAL[:, si, r, 1, :]
            sc = MODP[l][:, sco:sco + 8, r]
            sh = MODP[l][:, sho:sho + 8, r]
            ts("dve", TMP8[:], sc, 1.0, None, ALU.add, None, ["MODP%d" % l], ["TMP8"])
            if gk is None:
                cp("dve", A_, TMP8[:], ["TMP8"], ["SCAL"])
                cp("dve", B_, sh, ["MODP%d" % l], ["SCAL"])
            else:
                tt("dve", A_, TMP8[:], vcol(gk), ALU.mult, ["TMP8", "VEC"], ["SCAL"])
                tt("dve", B_, TMP8[:], vcol(bk_), ALU.mult, ["TMP8", "VEC"], ["SCAL"])
                tt("dve", B_, B_, sh, ALU.add, ["SCAL", "MODP%d" % l], ["SCAL"])
    LV = mk.sb([128, 4, 64], F32, "LV")
    ld(LV[:].rearrange("p a d -> p (a d)"), lam_d.rearrange("a d -> (a d)").partition_broadcast(128), [], ["LV"], "c4")
    L2 = mk.sb([128, 2, 64], F32, "L2")
    E2 = mk.sb([128, 2], F32, "E2")
    tt("dve", L2[:, 0, :], LV[:, 0, :], LV[:, 1, :], ALU.mult, ["LV"], ["L2"])
    tt("dve", L2[:, 1, :], LV[:, 2, :], LV[:, 3, :], ALU.mult, ["LV"], ["L2"])
    mk.op("dve", lambda e: e.reduce_sum(out=E2[:], in_=L2[:], axis=mybir.AxisListType.X), ["L2"], ["E2"])
    act(E2[:], E2[:], AF.Exp, ["E2"], ["E2"])
    tt("dve", NLAM[:], E2[:, 1:2], E2[:, 0:1], ALU.subtract, ["E2"], ["NLAM"])
    ts("dve", NLAM[:], NLAM[:], -LAMBDA_INIT0, None, ALU.add, None, ["NLAM"], ["NLAM"])
    ld(GSUB[:], subg_d.partition_broadcast(128), [], ["GSUB"], "c5")
    ts("dve", GSUB[:], GSUB[:], 1.0 - LAMBDA_INIT0, None, ALU.mult, None, ["GSUB"], ["GSUB"])
    act(CST[:], VEC[:, VMAP[("apar", 0)]:VMAP[("apar", 0)] + 16], AF.Exp, ["VEC"], ["CST"], scale=-1.0)
    ts("dve", CST[:], CST[:], 1.0, None, ALU.add, None, ["CST"], ["CST"])
    act(CST[:], CST[:], AF.Ln, ["CST"], ["CST"])
    ts("dve", CST[:], CST[:], -8.0, None, ALU.mult, None, ["CST"], ["CST"])
    mk.release(m0)

    def make_ht(src, srckeys, HT, htkey, site, r, pb):
        for k in range(8):
            b_ = pb + k // 4
            tr(bank(b_, 128, 128 * (k % 4)), src[:, 128 * k:128 * (k + 1)], ident[:], srckeys + ["ident"],
               [("p", b_)])
        for k in range(8):
            hm = CFG.get("ht_mode", 3)
            if hm == 0 or (hm == 1 and k >= 4) or (hm == 2 and k < 4):
                continue
            b_ = pb + k // 4
            A_ = SCAL[:, site, r, 0, k:k + 1]
            B_ = SCAL[:, site, r, 1, k:k + 1]
            if k < 4:
                act(HT[:, k, :], bank(b_, 128, 128 * (k % 4)), AF.Identity, [("p", b_), "SCAL"],
                    [(htkey, k)], bias=B_, scale=A_)
            else:
                ts("dve", HT[:, k, :], bank(b_, 128, 128 * (k % 4)), A_, B_, ALU.mult, ALU.add,
                   [("p", b_), "SCAL"], [(htkey, k)])

    def epilogue(ps2, pskeys, Xsrc, GB, Gt, Bt, bkeys, site, r, xdst, hdst, E, i, pb):
        sl = i % 2
        sy = i % len(E["Y"])
        Xt, Y, HT = E["Xt"][sl], E["Y"][sy], E["HT"][sy]
        kx, ky, kh = "Xt%d" % sl, "Y%d" % sy, "HTe%d" % sy
        ld(Xt[:], Xsrc, [], [kx], "ex%d" % sl)
        tt("dve", Y[:], ps2, GB[:], ALU.mult, pskeys + bkeys, [ky])
        stt("dve", Y[:], Xt[:], ALPHA, Y[:], ALU.mult, ALU.add, [kx, ky], [ky])
        BS, MV, RS = E["BS"][sl], E["MV"][sl], E["RS"][sl]
        kb = "BS%d" % sl
        mk.op("dve", lambda e: e.bn_stats(out=BS[:, 0, :], in_=Y[:, 0:512]), [ky], [(kb, 0)])
        mk.op("dve", lambda e: e.bn_stats(out=BS[:, 1, :], in_=Y[:, 512:1024]), [ky], [(kb, 1)])
        mk.op("dve", lambda e: e.bn_aggr(out=MV[:], in_=BS[:]), [(kb, 0), (kb, 1)], [(kb, 2)])
        act(RS[:, 0:1], MV[:, 1:2], AF.Ln, [(kb, 2), "eps"], [(kb, 3)], bias=epsT[:, 0:1])
        act(RS[:, 1:2], RS[:, 0:1], AF.Exp, [(kb, 3)], [(kb, 4)], scale=-0.5)
        stt("dve", RS[:, 2:3], MV[:, 0:1], -1.0, RS[:, 1:2], ALU.mult, ALU.mult, [(kb, 2), (kb, 4)], [(kb, 5)])
        act(Y[:], Y[:], AF.Identity, [ky, (kb, 4), (kb, 5)], [ky], bias=RS[:, 2:3], scale=RS[:, 1:2])
        tt("pool", Xt[:], Y[:], Gt[:], ALU.mult, [ky] + bkeys, [kx])
        tt("pool", Xt[:], Xt[:], Bt[:], ALU.add, [kx] + bkeys, [kx])
        ld(xdst, Xt[:], [kx], [], "es%d" % sl)
        if hdst is not None:
            make_ht(Y, [ky], HT, kh, site, r, pb)
            ld(hdst, HT[:], [(kh, k) for k in range(8)], [], "eh%d" % sy)

    def epi_alloc(ny=2, nht=0):
        E = {}
        E["Xt"] = [mk.sb([128, D], F32, "Xt") for _ in range(2)]
        E["Y"] = [mk.sb([128, D], F32, "Y") for _ in range(ny)]
        E["HT"] = [mk.sb([128, 8, 128], BF16, "HTe") for _ in range(nht if nht else ny)]
        E["BS"] = [mk.sb([128, 2, 6], F32, "BS") for _ in range(2)]
        E["MV"] = [mk.sb([128, 2], F32, "MV") for _ in range(2)]
        E["RS"] = [mk.sb([128, 3], F32, "RS") for _ in range(2)]
        return E

    def load_bc(GB, Gt, Bt, l, goff, r, gname, bname):
        ld(GB[:], modd[l, r, goff:goff + D].partition_broadcast(128), [], ["GB"], "bc0")
        ld(Gt[:], ln_d[gname][l].partition_broadcast(128), [], ["Gt"], "bc1")
        ld(Bt[:], ln_d[bname][l].partition_broadcast(128), [], ["Bt"], "bc2")

    def phase_qkv():
        m = mk.mark()
        WIN = mk.sb([128, 8, 3 * D], BF16, "WIN")
        wv = awin_d[0].rearrange("(k p) n -> p k n", p=128)
        for k in range(8):
            ldc([(WIN[:, k, :], wv[:, k, :])], [], [("WIN", k)], "w%d" % (k % 4))
        wink = [("WIN", k) for k in range(8)]
        XT = [mk.sb([128, D], F32, "XT") for _ in range(3)]
        RP = [mk.sb([128, 2, 64], F32, "RP") for _ in range(3)]
        HT = [mk.sb([128, 8, 128], BF16, "HT") for _ in range(2)]
        RQ = [mk.sb([128, D], F32, "RQ") for _ in range(2)]
        T1 = [mk.sb([128, D], F32, "T1") for _ in range(2)]
        T2 = [mk.sb([128, D], F32, "T2") for _ in range(2)]
        RB = [mk.sb([128, D], BF16, "RB") for _ in range(2)]
        TQb = [mk.sb([128, D], BF16, "TQb") for _ in range(2)]
        VV = [mk.sb([128, 2, 512], BF16, "VV") for _ in range(2)]
        TQ = [mk.sb([128, 8, 128], BF16, "TQ") for _ in range(2)]
        TB = [mk.sb([128, 8, 128], BF16, "TB") for _ in range(2)]
        tiles = [(b, i) for b in range(NBB) for i in range(34)]
        if "qkv_tiles" in CFG:
            tiles = tiles[:CFG["qkv_tiles"]]
        NT = len(tiles)

        def src_of(b, i):
            if i < 32:
                return x_d[b, 128 * i:128 * (i + 1), :]
            return ctx_d[b, 128 * (i - 32):128 * (i - 31), :]

        def stageL(n):
            b, i = tiles[n]
            s3 = n % 3
            ld(XT[s3][:], src_of(b, i), [], ["XT%d" % s3], "xl%d" % s3)
            if i < 32:
                ld(RP[s3][:], rope_d[128 * i:128 * (i + 1)], [], ["RP%d" % s3], "rl%d" % s3)

        def stageM(n):
            b, i = tiles[n]
            sl, s3 = n % 2, n % 3
            lat = i < 32
            r = b if lat else 2
            htk = "HT%d" % sl
            make_ht(XT[s3], ["XT%d" % s3], HT[sl], htk, 0, r, 6)
            hk = [(htk, k) for k in range(8)]
            for j in range(6):
                for k in range(8):
                    mm(bank(j), HT[sl][:, k, :], WIN[:, k, 512 * j:512 * (j + 1)], k == 0, k == 7, hk + wink, pk(j))
            cp("act", RQ[sl][:], PS[:, 0:1024], pk(0) + pk(1), ["RQ%d" % sl])
            cp("dve", RB[sl][:], PS[:, 1536:2560], pk(3) + pk(4), ["RB%d" % sl])
            cp("act", VV[sl][:, 0, :], bank(2), pk(2), [("VV%d" % sl, 0)])
            cp("dve", VV[sl][:, 1, :], bank(5), pk(5), [("VV%d" % sl, 1)])
            if lat:
                rp = RP[s3]
                rk = "RP%d" % s3
                rq, t1, t2 = RQ[sl], T1[sl], T2[sl]
                tt("dve", t1[:].rearrange("p (g d) -> p g d", g=16), rq[:].rearrange("p (g d) -> p g d", g=16),
                   rp[:, 0:1, :].to_broadcast([128, 16, 64]), ALU.mult, ["RQ%d" % sl, rk], ["T1%d" % sl])
                rqv = rq[:].rearrange("p (g a h f) -> p g a h f", g=16, a=2, h=2, f=16)
                t2v = t2[:].rearrange("p (g a h f) -> p g a h f", g=16, a=2, h=2, f=16)
                sv = rp[:, 1:2, :].rearrange("p o (a h f) -> p o a h f", a=2, h=2, f=16)
                tt("pool", t2v[:, :, :, 0, :], rqv[:, :, :, 1, :], sv[:, :, :, 0, :].to_broadcast([128, 16, 2, 16]),
                   ALU.mult, ["RQ%d" % sl, rk], [("T2%d" % sl, 0)])
                tt("pool", t2v[:, :, :, 1, :], rqv[:, :, :, 0, :], sv[:, :, :, 1, :].to_broadcast([128, 16, 2, 16]),
                   ALU.mult, ["RQ%d" % sl, rk], [("T2%d" % sl, 1)])
                tt("dve", TQb[sl][:], t1[:], t2[:], ALU.add, ["T1%d" % sl, ("T2%d" % sl, 0), ("T2%d" % sl, 1)], ["TQb%d" % sl])
            else:
                cp("pool", TQb[sl][:], RQ[sl][:], ["RQ%d" % sl], ["TQb%d" % sl])
            ld(vas[b][tok0_of(n):tok0_of(n) + 128, :], VV[sl][:, 0, :], [("VV%d" % sl, 0)], [], "sva%d" % sl)
            ld(vbs[b][tok0_of(n):tok0_of(n) + 128, :], VV[sl][:, 1, :], [("VV%d" % sl, 1)], [], "svb%d" % sl)

        def tok0_of(n):
            b, i = tiles[n]
            return 128 * i if i < 32 else T + 128 * (i - 32)

        def stageX(n):
            b, i = tiles[n]
            sl = n % 2
            lat = i < 32
            tok0 = tok0_of(n)
            for (src, sk, dst, dk) in ((TQb[sl], ["TQb%d" % sl], TQ[sl], "TQ%d" % sl), (RB[sl], ["RB%d" % sl], TB[sl], "TB%d" % sl)):
                for k in range(8):
                    b_ = 6 + k // 4
                    tr(PS[:, 512 * b_:512 * (b_ + 1)].bitcast(BF16)[:, 128 * (k % 4):128 * (k % 4 + 1)],
                       src[:, 128 * k:128 * (k + 1)], identb[:], sk + ["identb"], pk(b_))
                act(dst[:, 0:4, :].rearrange("p k t -> p (k t)"), bank(6).bitcast(BF16)[:, 0:512], AF.Identity, pk(6),
                    [(dk, 0)], scale=0.125)
                cp("dve", dst[:, 4:8, :].rearrange("p k t -> p (k t)"), bank(7).bitcast(BF16)[:, 0:512], pk(7), [(dk, 1)])
            for si_, (dst_d, tile_, kk, half) in enumerate(((qat, TQ, "TQ%d" % sl, 0), (kat, TQ, "TQ%d" % sl, 1),
                                                          (qbt, TB, "TB%d" % sl, 0), (kbt, TB, "TB%d" % sl, 1))):
                ld(dst_d[b].rearrange("h p t -> p h t")[:, :, tok0:tok0 + 128],
                   tile_[sl][:, 4 * half:4 * half + 4, :], [(kk, half)], [], "sq%d%d" % (si_, sl))

        if NT:
            stageL(0)
            if NT > 1:
                stageL(1)
            stageM(0)
            for n in range(NT):
                if n + 2 < NT:
                    stageL(n + 2)
                if n + 1 < NT:
                    stageM(n + 1)
                stageX(n)
        mk.release(m)

    def phase_diff():
        m = mk.mark()
        KT = [mk.sb([128, TK], BF16, "KT") for _ in range(2)]
        QT = [mk.sb([128, TK], BF16, "QT") for _ in range(2)]
        V1 = [mk.sb([128, 34, 129], BF16, "V1") for _ in range(2)]
        NPT = 3
        PT = [mk.sb([128, 2, 512], BF16, "PT") for _ in range(NPT)]
        AQ = mk.sb([128, 4, 128], F32, "AQ")
        OO = mk.sb([128, 4, 128], F32, "OO")
        SQ = mk.sb([128, 4, 128], F32, "SQ")
        RC = mk.sb([128, 4, 2], F32, "RC")
        T1 = mk.sb([128, 4, 1], F32, "T1")
        SS = mk.sb([128, 4, 1], F32, "SS")
        G3 = mk.sb([128, 1, 128], F32, "G3")
        cp("pool", G3[:, 0, :], GSUB[:], ["GSUB"], ["G3"])
        AOB = [mk.sb([128, 4, 128], F32, "AOB") for _ in range(2)]
        for i in range(2):
            memset("pool", V1[i][:, :, 128:129], 1.0, [("V1%d" % i, "one")])
        ACC = PS[:, 2048:4096].rearrange("p (b c) -> p b c", b=4)
        nblk = 0
        hh = 0
        npt = 0
        for b in range(NBB):
            for h in range(4):
                sl = hh % 2
                hh += 1
                ld(KT[sl][:], kat[b][h], [], ["KT%d" % sl], "ak%d" % sl)
                ld(QT[sl][:], qat[b][h], [], ["QT%d" % sl], "aq%d" % sl)
                ld(V1[sl][:, :, 0:128], vas[b].rearrange("(t p) c -> p t c", p=128)[:, :, 128 * h:128 * (h + 1)],
                   [], ["V1%d" % sl], "av%d" % sl)
                vkeys = ["V1%d" % sl, ("V1%d" % sl, "one")]
                blocks = [(512 * j, 512, list(range(34))) for j in range(8)] + [(T, 256, [32, 33])]
                for (q0, nq, kts) in blocks:
                    nqt = nq // 128
                    acck = [("acc", q) for q in range(nqt)]
                    memset("dve", ACC[:, 0:nqt, 0:258], 0.0, acck)

                    def qk(it):
                        kt = kts[it]
                        sp_ = it % 2
                        for mp in range(2):
                            bk = 2 * sp_ + mp
                            mm(bank(bk, nq), KT[sl][64 * mp:64 * (mp + 1), 128 * kt:128 * (kt + 1)],
                               QT[sl][64 * mp:64 * (mp + 1), q0:q0 + nq], True, True,
                               ["KT%d" % sl, "QT%d" % sl], pk(bk))

                    qk(0)
                    for it, kt in enumerate(kts):
                        if it + 1 < len(kts):
                            qk(it + 1)
                        sp_ = it % 2
                        pb_ = npt % NPT
                        npt += 1
                        sview = PS[:, 1024 * sp_:1024 * sp_ + 1024].rearrange("p (b c) -> p b c", b=2)[:, :, 0:nq]
                        act(PT[pb_][:, :, 0:nq], sview, AF.Exp, pk(2 * sp_) + pk(2 * sp_ + 1), ["PT%d" % pb_])
                        for mp in range(2):
                            for qt in range(nqt):
                                mm(bank(4 + qt, 129, 129 * mp), PT[pb_][:, mp, 128 * qt:128 * (qt + 1)],
                                   V1[sl][:, kt, :], False, False, ["PT%d" % pb_] + vkeys + [("acc", qt)],
                                   [("acc", qt)], skip=True)
                    ob = AOB[nblk % 2]
                    okey = "AOB%d" % (nblk % 2)
                    nblk += 1
                    mk.op("dve", lambda e, nqt=nqt: e.reciprocal(out=RC[:, 0:nqt, :], in_=ACC[:, 0:nqt, 128:258:129]),
                          acck, ["RC"])
                    ts("dve", T1[:, 0:nqt, :], RC[:, 0:nqt, 1:2], NLAM[:, 0:1], None, ALU.mult, None, ["RC", "NLAM"], ["T1"])
                    tt("dve", AQ[:, 0:nqt, :], ACC[:, 0:nqt, 0:128], RC[:, 0:nqt, 0:1].to_broadcast([128, nqt, 128]),
                       ALU.mult, acck + ["RC"], ["AQ"])
                    tt("dve", SQ[:, 0:nqt, :], ACC[:, 0:nqt, 129:257], T1[:, 0:nqt, :].to_broadcast([128, nqt, 128]),
                       ALU.mult, acck + ["T1"], ["SQ"])
                    tt("dve", OO[:, 0:nqt, :], AQ[:, 0:nqt, :], SQ[:, 0:nqt, :], ALU.add, ["AQ", "SQ"], ["OO"])
                    tt("pool", SQ[:, 0:nqt, :], OO[:, 0:nqt, :], OO[:, 0:nqt, :], ALU.mult, ["OO"], ["SQ"])
                    mk.op("dve", lambda e, nqt=nqt: e.reduce_sum(out=SS[:, 0:nqt, :], in_=SQ[:, 0:nqt, :],
                                                                 axis=mybir.AxisListType.X), ["SQ"], ["SS"])
                    act(SS[:, 0:nqt, :], SS[:, 0:nqt, :], AF.Ln, ["SS", "eps"], ["SS"], bias=epsT[:, 0:1], scale=1.0 / 128)
                    act(SS[:, 0:nqt, :], SS[:, 0:nqt, :], AF.Exp, ["SS"], ["SS"], scale=-0.5)
                    tt("pool", OO[:, 0:nqt, :], OO[:, 0:nqt, :], SS[:, 0:nqt, :].to_broadcast([128, nqt, 128]),
                       ALU.mult, ["OO", "SS"], ["OO"])
                    tt("pool", ob[:, 0:nqt, :], OO[:, 0:nqt, :], G3[:].to_broadcast([128, nqt, 128]), ALU.mult,
                       ["OO", "G3"], [okey])
                    ld(aos[b][q0:q0 + nq, 128 * h:128 * (h + 1)].rearrange("(q p) c -> p q c", p=128),
                       ob[:, 0:nqt, :], [okey], [], "sa%d" % (nblk % 2))
        mk.release(m)

    def phase_nbr():
        m = mk.mark()
        QB2 = mk.sb([128, 4, TK], BF16, "QB2")
        KB2 = mk.sb([128, 4, TK], BF16, "KB2")
        VBe = mk.sb([128, 34, 8, 65], BF16, "VBe")
        VBo = mk.sb([128, 31, 8, 65], BF16, "VBo")
        BALL = mk.sb([128, 14, 8, 64], F32, "BALL")
        MK2 = mk.sb([128, 1, 64], F32, "MK2")
        SBF = [mk.sb([128, 512], F32, "SBF") for _ in range(2)]
        PT = [mk.sb([128, 6, 512], BF16, "PTn") for _ in range(2)]
        RI = mk.sb([64, 8, 1], F32, "RI")
        OB = [mk.sb([64, 512], F32, "OB") for _ in range(2)]
        memset("pool", VBe[:, :, :, 64:65], 1.0, [("VBe", "one")])
        memset("pool", VBo[:, :, :, 64:65], 1.0, [("VBo", "one")])
        ld(MK2[:, 0, :], mask_d, [], ["MK2"], "nm")

        for base_ in range(14):
            for i2 in range(2):
                ld(BALL[64 * i2:64 * (i2 + 1), base_, :, :], rpbt_d[base_ + i2], [], [("BALL", base_, i2)], "nb%d" % i2)
            tt("dve" if base_ % 2 else "pool", BALL[:, base_, :, :], BALL[:, base_, :, :], MK2[:].to_broadcast([128, 8, 64]), ALU.add,
               [("BALL", base_, 0), ("BALL", base_, 1), "MK2"], [("BALL", base_)])
        nrow = 0
        for b in range(NBB):
            ld(QB2[:], qbt[b].rearrange("h p t -> p h t"), [], ["QB2"], "nq")
            ld(KB2[:], kbt[b].rearrange("h p t -> p h t"), [], ["KB2"], "nk")
            vb_v = vbs[b].rearrange("(t p) (h d) -> p t h d", p=128, h=8)
            for t4 in range(0, 34, 6):
                t5 = min(34, t4 + 6)
                mk.dma("sp", [(VBe[:, t_, :, 0:64], vb_v[:, t_]) for t_ in range(t4, t5)], [], [("VBe", t4)], "nv")
            vbo_v = vbs[b][64:64 + 31 * 128, :].rearrange("(t p) (h d) -> p t h d", p=128, h=8)
            for t4 in range(0, 31, 6):
                t5 = min(31, t4 + 6)
                mk.dma("sp", [(VBo[:, t_, :, 0:64], vbo_v[:, t_]) for t_ in range(t4, t5)], [], [("VBo", t4)], "nv")
            vek = [("VBe", t4) for t4 in range(0, 34, 6)] + [("VBe", "one")]
            vok = [("VBo", t4) for t4 in range(0, 31, 6)] + [("VBo", "one")]
            rows = [("lat", r) for r in range(64)] + [("ctx", g) for g in range(4)]
            if "nbr_rows" in CFG:
                rows = rows[:CFG["nbr_rows"]]
            nst = CFG.get("nbr_stage", 9)
            for (kind, r) in rows:
                par = nrow % 2
                nrow += 1
                if kind == "lat":
                    rs = min(max(r - 4, 0), 56)
                    delta = rs - r
                    q0 = 64 * r
                    kts = []
                    for j in range(4):
                        ks = 64 * (rs + 2 * j)
                        if rs % 2 == 0:
                            kts.append((ks, VBe, (rs + 2 * j) // 2, vek, j))
                        else:
                            kts.append((ks, VBo, (rs + 2 * j - 1) // 2, vok, j))
                    kts.append((T, VBe, 32, vek, None))
                    kts.append((T + 128, VBe, 33, vek, None))
                else:
                    q0 = T + 64 * r
                    kts = [(T, VBe, 32, vek, None), (T + 128, VBe, 33, vek, None)]
                pt = PT[par]
                ptk = "PTn%d" % par
                for n, (ks, vt, vi, vk, j) in enumerate(kts):
                    pp = n % 2
                    for h in range(8):
                        hp, hq = h % 2, h // 2
                        mm(bank(2 * pp + hp, 64, 64 * hq), KB2[64 * hp:64 * (hp + 1), hq, ks:ks + 128],
                           QB2[64 * hp:64 * (hp + 1), hq, q0:q0 + 64], True, True, ["KB2", "QB2"], pk(2 * pp + hp))
                    s2 = PS[:, 1024 * pp:1024 * pp + 1024].rearrange("p (b c) -> p b c", b=2)[:, :, 0:256]
                    sk2 = pk(2 * pp) + pk(2 * pp + 1)
                    if j is not None:
                        sb_ = SBF[n % 2]
                        bidx = delta + 2 * j + 7
                        tt("dve", sb_[:].rearrange("p (b c) -> p b c", b=2), s2,
                           BALL[:, bidx, :, :].rearrange("p (b h) c -> p b (h c)", b=2), ALU.add,
                           sk2 + [("BALL", bidx)], ["SBF%d" % (n % 2)])
                        act(pt[:, n, :], sb_[:], AF.Exp, ["SBF%d" % (n % 2)], [(ptk, n)])
                    else:
                        act(pt[:, n, :].rearrange("p (b c) -> p b c", b=2), s2, AF.Exp, sk2, [(ptk, n)])
                if nst < 2:
                    continue
                ab = 4 + 2 * par
                for h in range(8):
                    b_ = ab + h // 4
                    for n, (ks, vt, vi, vk, j) in enumerate(kts):
                        mm(PS[0:64, 512 * b_ + 65 * (h % 4):512 * b_ + 65 * (h % 4) + 65],
                           pt[:, n, 64 * ((h % 2) * 4 + h // 2):64 * ((h % 2) * 4 + h // 2 + 1)], vt[:, vi, h, :],
                           n == 0, n == len(kts) - 1,
                           [(ptk, n)] + vk, pk(b_))
                if nst < 3:
                    continue
                ob = OB[par]
                obk = "OB%d" % par
                for g in range(2):
                    b_ = ab + g
                    accv = PS[0:64, 512 * b_:512 * b_ + 260].rearrange("p (h c) -> p h c", h=4)
                    mk.op("dve", lambda e, accv=accv, g=g: e.reciprocal(out=RI[:, 4 * g:4 * g + 4, :], in_=accv[:, :, 64:65]),
                          pk(b_), [("RI", g)])
                    tt("dve", ob[:, 256 * g:256 * (g + 1)].rearrange("p (h d) -> p h d", h=4), accv[:, :, 0:64],
                       RI[:, 4 * g:4 * g + 4, :].to_broadcast([64, 4, 64]), ALU.mult, pk(b_) + [("RI", g)], [(obk, g)])
                ld(aos[b][q0:q0 + 64, 512:1024], ob[:], [(obk, 0), (obk, 1)], [], "no%d" % par)
        mk.release(m)

    def phase_oproj(l):
        m = mk.mark()
        WO = mk.sb([128, 8, D], BF16, "WO")
        wsrc = awout_d[0] if l == 0 else rwout_d[0]
        ldc([(WO[:], wsrc.rearrange("(k p) n -> p k n", p=128))], [], ["WO"], "w0")
        GB = mk.sb([128, D], F32, "GB")
        Gt = mk.sb([128, D], F32, "Gt")
        Bt = mk.sb([128, D], F32, "Bt")
        E = epi_alloc()
        bkeys = ["GB", "Gt", "Bt"]
        site = 1 if l == 0 else 3
        seqs = SEQ if l == 0 else [s for s in SEQ if s[0] == "lat"]
        if l == 0:
            AO = [mk.sb([128, D], F32, "AO") for _ in range(3)]
            AT = [mk.sb([128, 8, 128], BF16, "AT") for _ in range(2)]
        else:
            MB = [mk.sb([128, 8, 512], BF16, "MB") for _ in range(2)]
        X3 = [mk.sb([128, D], F32, "X3") for _ in range(3)]
        n0 = 0
        for s in seqs:
            r = seq_var(s)
            load_bc(GB, Gt, Bt, l, 2 * D, r, "ln1_g", "ln1_b")
            b = s[1]
            L = seq_len(s)
            NT = L // 128
            base = 0 if s[0] == "lat" else T
            xsrc = seq_src(s) if l == 0 else xs[s]
            hdst_all = hts[(s, 0)].rearrange("k p t -> p k t")

            def stageL(i):
                n = n0 + i
                s3 = n % 3
                ld(X3[s3][:], xsrc[128 * i:128 * (i + 1), :], [], ["X3%d" % s3], "ex%d" % s3)
                if l == 0:
                    ld(AO[s3][:], aos[b][base + 128 * i:base + 128 * (i + 1), :], [], ["AO%d" % s3], "ol%d" % s3)
                elif i % 4 == 0:
                    ms = (i // 4) % 2
                    ld(MB[ms][:], mts[b].rearrange("k p t -> p k t")[:, :, 128 * i:128 * i + 512], [],
                       ["MB%d" % ms], "om%d" % ms)

            def stageT(i):
                n = n0 + i
                sl = n % 2
                s3 = n % 3
                if l == 0:
                    for k in range(8):
                        tr(bank(4 + k // 4, 128, 128 * (k % 4)), AO[s3][:, 128 * k:128 * (k + 1)], ident[:],
                           ["AO%d" % s3, "ident"], pk(4 + k // 4))
                    cp("act", AT[sl][:, 0:4, :].rearrange("p k t -> p (k t)"), bank(4), pk(4), [("AT%d" % sl, 0)])
                    cp("dve", AT[sl][:, 4:8, :].rearrange("p k t -> p (k t)"), bank(5), pk(5), [("AT%d" % sl, 1)])

            def stageM(i):
                n = n0 + i
                sl = n % 2
                pa = 2 * sl
                for k in range(8):
                    if l == 0:
                        lhs, lk = AT[sl][:, k, :], [("AT%d" % sl, 0), ("AT%d" % sl, 1)]
                    else:
                        ms = (i // 4) % 2
                        lhs, lk = MB[ms][:, k, 128 * (i % 4):128 * (i % 4 + 1)], ["MB%d" % ms]
                    for hf in range(2):
                        mm(bank(pa + hf), lhs, WO[:, k, 512 * hf:512 * (hf + 1)], k == 0, k == 7, lk + ["WO"], pk(pa + hf))

            def stageA(i):
                n = n0 + i
                sl = n % 2
                pa = 2 * sl
                Xt, Y = E["Xt"][sl], E["Y"][sl]
                kx, ky = "Xt%d" % sl, "Y%d" % sl
                s3 = n % 3
                tt("dve", Y[:], PS[:, 512 * pa:512 * pa + 1024], GB[:], ALU.mult, pk(pa) + pk(pa + 1) + ["GB"], [ky])
                stt("dve", Y[:], X3[s3][:], ALPHA, Y[:], ALU.mult, ALU.add, ["X3%d" % s3, ky], [ky])
                BS, MV, RS = E["BS"][sl], E["MV"][sl], E["RS"][sl]
                kb = "BS%d" % sl
                mk.op("dve", lambda e: e.bn_stats(out=BS[:, 0, :], in_=Y[:, 0:512]), [ky], [(kb, 0)])
                mk.op("dve", lambda e: e.bn_stats(out=BS[:, 1, :], in_=Y[:, 512:1024]), [ky], [(kb, 1)])
                mk.op("dve", lambda e: e.bn_aggr(out=MV[:], in_=BS[:]), [(kb, 0), (kb, 1)], [(kb, 2)])
                act(RS[:, 0:1], MV[:, 1:2], AF.Ln, [(kb, 2), "eps"], [(kb, 3)], bias=epsT[:, 0:1])
                act(RS[:, 1:2], RS[:, 0:1], AF.Exp, [(kb, 3)], [(kb, 4)], scale=-0.5)
                stt("dve", RS[:, 2:3], MV[:, 0:1], -1.0, RS[:, 1:2], ALU.mult, ALU.mult, [(kb, 2), (kb, 4)], [(kb, 5)])
                act(Y[:], Y[:], AF.Identity, [ky, (kb, 4), (kb, 5)], [ky], bias=RS[:, 2:3], scale=RS[:, 1:2])
                tt("pool", Xt[:], Y[:], Gt[:], ALU.mult, [ky, "Gt"], [kx])
                tt("pool", Xt[:], Xt[:], Bt[:], ALU.add, [kx, "Bt"], [kx])
                mk.dma("pool", [(xs[s][128 * i:128 * (i + 1), :], Xt[:])], [kx], [], "pes%d" % sl)

            def stageB(i):
                n = n0 + i
                sl = n % 2
                make_ht(E["Y"][sl], ["Y%d" % sl], E["HT"][sl], "HTe%d" % sl, site, r, 6)
                mk.dma("pool", [(hdst_all[:, :, 1 + 128 * i:1 + 128 * (i + 1)], E["HT"][sl][:])],
                       [("HTe%d" % sl, k) for k in range(8)], [], "peh%d" % sl)

            stageL(0)
            if NT > 1:
                stageL(1)
            stageT(0)
            for i in range(NT):
                if i + 2 < NT:
                    stageL(i + 2)
                if i + 1 < NT:
                    stageT(i + 1)
                stageM(i)
                stageA(i)
                if i >= 1:
                    stageB(i - 1)
            stageB(NT - 1)
            n0 += NT
        mk.release(m)

    def phase_ffn(l):
        m = mk.mark()
        WU = mk.sb([128, 8, 2 * DFF], BF16, "WU")
        WD = mk.sb([128, 22, D], BF16, "WD")
        wv = wup_d[l].rearrange("(k p) n -> p k n", p=128)
        for k in range(8):
            ldc([(WU[:, k, :], wv[:, k, :])], [], [("WU", k)], "w%d" % (k % 4))
        wdv = wdn_d[l].rearrange("(k p) n -> p k n", p=128)
        wuk = [("WU", k) for k in range(8)]
        WDG = [(k0, min(22, k0 + 6)) for k0 in range(0, 22, 6)]
        wdk = [("WD", k0) for (k0, _) in WDG]
        Gt = mk.sb([128, D], F32, "Gt")
        Bt = mk.sb([128, D], F32, "Bt")
        E = epi_alloc(2, 1)
        HB = mk.sb([128, 8, 514], BF16, "HB")
        UW = mk.sb([128, 44, 8, 2], F32, "UW")
        C = [mk.sb([128, 512], F32, "C") for _ in range(3)]
        m_ac = mk.mark()
        UH = mk.sb([128, 44, 18], F32, "UH")
        mk.sb_off = m_ac
        AC = mk.sb([128, 22, 512], BF16, "AC")
        ACK = [("AC", i) for i in range(22)]
        X3b = mk.sb([128, D], F32, "Xt")
        E["Xt"].append(X3b)
        HS = mk.sb([128, 8, 16], BF16, "HS")
        US = [C[0], C[1]]
        memset("pool", HS[:], 0.0, ["HS"])
        last = l == 1
        site = 2 if l == 0 else None
        seqs = SEQ if l == 0 else [s for s in SEQ if s[0] == "lat"]
        fw = [VMAP[("fcw", l, k)] for k in range(3)]
        fb = VMAP[("fcb", l)]
        MB = [0, 1, 2, 7]
        bkeys = ["Gt", "Bt"]
        n = 0
        uc = 0
        for s in seqs:
            r = seq_var(s)
            L = seq_len(s)
            hsrc = hts[(s, 0)].rearrange("k p t -> p k t")
            hdst_all = hts[(s, 1)].rearrange("k p t -> p k t")
            nb = min(512, L)
            nblk = L // nb
            GBt = E["Y"][0]
            ld(GBt[:], modd[l, r, 5 * D:6 * D].partition_broadcast(128), [], ["Y0"], "bc0")
            ld(Gt[:], ln_d["ln2_g"][l].partition_broadcast(128), [], ["Gt"], "bc1")
            ld(Bt[:], ln_d["ln2_b"][l].partition_broadcast(128), [], ["Bt"], "bc2")
            for gi, (k0, k1) in enumerate(WDG):
                ldc([(WD[:, k0:k1, :], wdv[:, k0:k1, :])], [], [("WD", k0)], "w%d" % (gi % 4))
                for kk in range(k0, k1):
                    tt("pool" if kk % 2 else "dve", WD[:, kk, :], WD[:, kk, :], GBt[:], ALU.mult, [("WD", k0), "Y0"], [("WD", k0)])
            if nblk > 1:
                for bd in range(1, nblk):
                    mk.dma("sp", [(HS[:, :, 2 * bd:2 * bd + 2], hsrc[:, :, 512 * bd:512 * bd + 2])], [], ["HS"], "fz", slow=True)
                for cb in range(11):
                    for k in range(8):
                        mm(PS[0:16, 512 * 3:512 * 3 + 512], HS[:, k, :], WU[:, k, 512 * cb:512 * (cb + 1)], k == 0, k == 7,
                           ["HS"] + wuk, pk(3))
                    cp("act", US[cb % 2][0:16, :], PS[0:16, 512 * 3:512 * 3 + 512], pk(3), ["C%d" % (cb % 2)])
                    for q in range(4):
                        c = 4 * cb + q
                        b_ = 5 + c // 22
                        tr(PS[:, 512 * b_ + 16 * (c % 22):512 * b_ + 16 * (c % 22) + 16], US[cb % 2][0:16, 128 * q:128 * (q + 1)],
                           ident[0:16, 0:16], ["C%d" % (cb % 2), "ident"], pk(b_))
                memset("pool", UH[:, :, 16:18], 0.0, ["UH"] + ACK)
                cp("dve", UH[:, 0:22, 0:16], PS[:, 512 * 5:512 * 5 + 352].rearrange("p (c e) -> p c e", e=16), pk(5), ["UH"] + ACK)
                cp("dve", UH[:, 22:44, 0:16], PS[:, 512 * 6:512 * 6 + 352].rearrange("p (c e) -> p c e", e=16), pk(6), ["UH"] + ACK)
            else:
                memset("pool", UH[:], 0.0, ["UH"] + ACK)

            w0b = VEC[:, fw[0]:fw[0] + 44].rearrange("p (c o) -> p c o", o=1).to_broadcast([128, 44, 8])
            w2b = VEC[:, fw[2]:fw[2] + 44].rearrange("p (c o) -> p c o", o=1).to_broadcast([128, 44, 8])
            tt("dve", UW[:, :, :, 0], UH[:, :, 0:15:2], w0b, ALU.mult, ["UH", "VEC"] + ACK, [("UW", 0)])
            tt("dve", UW[:, :, :, 1], UH[:, :, 3:18:2], w2b, ALU.mult, ["UH", "VEC"] + ACK, [("UW", 1)])
            uwk = [("UW", 0), ("UW", 1)]

            def xload(nn, tok):
                sl = nn % 3
                ld(E["Xt"][sl][:], xs[s][tok:tok + 128, :], [], ["Xt%d" % sl], "ex%d" % sl)

            def epiA(nn, ps2, pskeys, xdst):
                sl = nn % 2
                s3 = nn % 3
                Xt, Y = E["Xt"][s3], E["Y"][sl]
                kx, ky = "Xt%d" % s3, "Y%d" % sl
                stt("dve", Y[:], Xt[:], ALPHA, ps2, ALU.mult, ALU.add, [kx] + pskeys, [ky])
                BS, MV, RS = E["BS"][sl], E["MV"][sl], E["RS"][sl]
                kb = "BS%d" % sl
                mk.op("dve", lambda e: e.bn_stats(out=BS[:, 0, :], in_=Y[:, 0:512]), [ky], [(kb, 0)])
                mk.op("dve", lambda e: e.bn_stats(out=BS[:, 1, :], in_=Y[:, 512:1024]), [ky], [(kb, 1)])
                mk.op("dve", lambda e: e.bn_aggr(out=MV[:], in_=BS[:]), [(kb, 0), (kb, 1)], [(kb, 2)])
                act(RS[:, 0:1], MV[:, 1:2], AF.Ln, [(kb, 2), "eps"], [(kb, 3)], bias=epsT[:, 0:1])
                act(RS[:, 1:2], RS[:, 0:1], AF.Exp, [(kb, 3)], [(kb, 4)], scale=-0.5)
                stt("dve", RS[:, 2:3], MV[:, 0:1], -1.0, RS[:, 1:2], ALU.mult, ALU.mult, [(kb, 2), (kb, 4)], [(kb, 5)])
                act(Y[:], Y[:], AF.Identity, [ky, (kb, 4), (kb, 5)], [ky], bias=RS[:, 2:3], scale=RS[:, 1:2])
                tt("pool", Xt[:], Y[:], Gt[:], ALU.mult, [ky] + bkeys, [kx])
                tt("pool", Xt[:], Xt[:], Bt[:], ALU.add, [kx] + bkeys, [kx])
                ld(xdst, Xt[:], [kx], [], "es%d" % s3)

            def epiB(nn, hdst):
                sl = nn % 2
                make_ht(E["Y"][sl], ["Y%d" % sl], E["HT"][0], "HTe0", site, r, 5)
                ld(hdst, E["HT"][0][:], [("HTe0", k) for k in range(8)], [], "eh0")

            def tail(pend):
                i_, cg, cgk, cv, cvk = pend
                act(cg[:, 0:nb], cg[:, 0:nb], AF.Gelu_apprx_tanh, [cgk], [cgk])
                tt("pool", AC[:, i_, 0:nb], cg[:, 0:nb], cv[:, 0:nb], ALU.mult, [cgk, cvk], [("AC", i_)])

            for j in range(nblk):
                t0 = nb * j
                if j == 0:
                    ld(HB[:, :, 0:nb + 2], hsrc[:, :, t0:t0 + nb + 2], [], ["HB"], "fh")
                xload(n, t0)
                if nb > 128:
                    xload(n + 1, t0 + 128)
                pend = None
                for i in range(22):
                    cur = []
                    for gv in range(2):
                        c = i + 22 * gv
                        u = uc % 4
                        cb_ = uc % 3
                        uc += 1
                        mb = MB[u]
                        for k in range(8):
                            mm(bank(mb, nb), WU[:, k, 128 * c:128 * (c + 1)], HB[:, k, 1:nb + 1], k == 0, k == 7,
                               wuk + ["HB"], pk(mb))
                        Ct = C[cb_]
                        ck = "C%d" % cb_
                        act(Ct[:, 0:nb], bank(mb, nb), AF.Identity, pk(mb) + ["VEC"], [ck],
                            bias=VEC[:, fb + c:fb + c + 1], scale=VEC[:, fw[1] + c:fw[1] + c + 1])
                        tt("pool", Ct[:, 0:nb:nb - 1], Ct[:, 0:nb:nb - 1], UW[:, c, j, :], ALU.add, uwk + [ck], [ck])
                        stt("dve", Ct[:, 1:nb], bank(mb, nb - 1), VEC[:, fw[0] + c:fw[0] + c + 1], Ct[:, 1:nb], ALU.mult, ALU.add,
                            pk(mb) + ["VEC", ck], [ck])
                        stt("dve", Ct[:, 0:nb - 1], bank(mb, nb - 1, 1), VEC[:, fw[2] + c:fw[2] + c + 1], Ct[:, 0:nb - 1], ALU.mult, ALU.add,
                            pk(mb) + ["VEC", ck], [ck])
                        cur += [Ct, ck]
                        if gv == 0 and pend is not None:
                            tail(pend)
                            pend = None
                    pend = (i, cur[0], cur[1], cur[2], cur[3])
                tail(pend)
                if j + 1 < nblk:
                    ld(HB[:, :, 0:nb + 2], hsrc[:, :, t0 + nb:t0 + 2 * nb + 2], [], ["HB"], "fh")
                ack = [("AC", i) for i in range(22)]
                nt_ = nb // 128
                prevB = None
                for tq in range(nt_):
                    tok = t0 + 128 * tq
                    for k in range(22):
                        for hf in range(2):
                            mm(bank(3 + hf), AC[:, k, 128 * tq:128 * (tq + 1)], WD[:, k, 512 * hf:512 * (hf + 1)],
                               k == 0, k == 21, [("AC", k)] + wdk, pk(3 + hf))
                    if tq + 2 < nt_:
                        xload(n + 2, tok + 256)
                    if last:
                        xdst = out_d[s[1], tok:tok + 128, :]
                    else:
                        xdst = xs[s][tok:tok + 128, :]
                    epiA(n, PS[:, 512 * 3:512 * 3 + 1024], pk(3) + pk(4), xdst)
                    if prevB is not None:
                        epiB(*prevB)
                        prevB = None
                    if not last:
                        prevB = (n, hdst_all[:, :, 1 + tok:1 + tok + 128])
                    n += 1
                if prevB is not None:
                    epiB(*prevB)
        mk.release(m)

    def phase_rglru():
        m = mk.mark()
        CX0, LX0, NX = 2, 261, 4358
        XR = mk.sb([128, NX], F32, "XR")
        XC = mk.sb([128, NX], F32, "XC")
        XCb = mk.sb([128, NX], BF16, "XCb")
        A0 = mk.sb([128, NX], F32, "A0")
        A1 = mk.sb([128, NX], F32, "A1")
        BT0 = mk.sb([128, NX], F32, "BT0")
        TM = mk.sb([128, NX], F32, "TM")
        RF = mk.sb([128, NX], F32, "RF")
        RR = mk.sb([128, NX], F32, "RR")
        GY = [mk.sb([128, T], BF16, "GY") for _ in range(2)]
        MT = mk.sb([128, T], BF16, "MT")
        WI = [mk.sb([128, 8, 2, 128], BF16, "WI") for _ in range(2)]
        GW = [mk.sb([128, 4, 128], BF16, "GW") for _ in range(2)]
        ONE = mk.sb([128, 1], F32, "ONE")
        memset("pool", ONE[:], 1.0, ["ONE"])
        NHB = 2
        HB = [mk.sb([128, 8, 512], BF16, "HBr") for _ in range(NHB)]
        memset("dve", XR[:], 0.0, ["XRz"])
        wv = rwin_d[0].rearrange("(k p) n -> p k n", p=128)
        regions = [(CX0, TC), (LX0, T)]
        BLOCKS = [("ctx", 0, 256, CX0)] + [("lat", 512 * j, 512, LX0 + 512 * j) for j in range(8)]
        chunks = [(b, c) for b in range(NBB) for c in range(8)]
        items = [(b, c, bi) for (b, c) in chunks for bi in range(9)]
        issued = [0]
        nh = [0]

        def prefetch(upto):
            while issued[0] <= min(upto, len(items) - 1):
                b_, c_, bi_ = items[issued[0]]
                kind_, t0_, nt_, xo_ = BLOCKS[bi_]
                hb = issued[0] % NHB
                src = hts[((kind_, b_), 1)].rearrange("k p t -> p k t")
                ld(HB[hb][:, :, 0:nt_], src[:, :, 1 + t0_:1 + t0_ + nt_], [], ["HBr%d" % hb], "rh%d" % hb)
                issued[0] += 1

        def inproj(ci):
            b, c = chunks[ci]
            w_ = ci % 2
            ldc([(WI[w_][:, :, 0, :], wv[:, :, 128 * c:128 * (c + 1)]),
                 (WI[w_][:, :, 1, :], wv[:, :, D + 128 * c:D + 128 * (c + 1)])], [], ["WI%d" % w_], "w%d" % w_)
            ldc([(GW[w_][:, 0, :], rga_d[0, 0, c]), (GW[w_][:, 1, :], rgx_d[0, 0, c]),
                 (GW[w_][:, 2, :], rga_d[0, 1, c]), (GW[w_][:, 3, :], rgx_d[0, 1, c])], [], ["GW%d" % w_], "w%d" % (2 + w_))
            gy = GY[ci % 2]
            for (kind, t0, nt, xo) in BLOCKS:
                prefetch(nh[0] + NHB - 1)
                sl = nh[0] % NHB
                bk = 2 * (nh[0] % 2)
                nh[0] += 1
                for k in range(8):
                    mm(bank(bk, nt), WI[w_][:, k, 1, :], HB[sl][:, k, 0:nt], k == 0, k == 7, ["WI%d" % w_, "HBr%d" % sl], pk(bk))
                cp("act", XR[:, xo:xo + nt], bank(bk, nt), pk(bk) + ["XRz"], [("XR", xo)])
                if kind == "lat":
                    for k in range(8):
                        mm(bank(bk + 1, nt), WI[w_][:, k, 0, :], HB[sl][:, k, 0:nt], k == 0, k == 7,
                           ["WI%d" % w_, "HBr%d" % sl], pk(bk + 1))
                    act(gy[:, t0:t0 + nt], bank(bk + 1, nt), AF.Gelu_apprx_tanh, pk(bk + 1), [("GY%d" % (ci % 2), t0)])

        xrk = [("XR", xo) for (_, _, _, xo) in BLOCKS] + ["XRz"]
        xck = [("XC", o) for (o, _) in regions]
        xcbk = [("XCb", o) for (o, _) in regions]
        (oc, ncx), (ol, nl) = regions
        inproj(0)
        for ci, (b, c) in enumerate(chunks):
            w_ = ci % 2
            w = [VMAP[("rcw", k)] + c for k in range(4)]
            cb = VMAP[("rcb",)] + c
            for (o, n_) in regions:
                act(XC[:, o:o + n_], XR[:, o:o + n_], AF.Identity, xrk + ["VEC"], [("XC", o)],
                    bias=VEC[:, cb:cb + 1], scale=VEC[:, w[2]:w[2] + 1])
                for (kk, sh) in ((0, -2), (1, -1), (3, 1)):
                    stt("dve", XC[:, o:o + n_], XR[:, o + sh:o + sh + n_], VEC[:, w[kk]:w[kk] + 1], XC[:, o:o + n_],
                        ALU.mult, ALU.add, xrk + ["VEC", ("XC", o)], [("XC", o)])
                cp("pool", XCb[:, o:o + n_], XC[:, o:o + n_], [("XC", o)], [("XCb", o)])
            for d in range(2):
                Ad = A0 if d == 0 else A1
                Bd = BT0 if d == 0 else RR
                an, bn = "A%d" % d, "B%d" % d
                ba = VMAP[("rba", d)] + c
                bx = VMAP[("rbx", d)] + c
                for gi, (kind, t0, nt, xo) in enumerate(BLOCKS):
                    bk = 4 + 2 * (gi % 2)
                    ro = CX0 if kind == "ctx" else LX0
                    mm(bank(bk, nt), GW[w_][:, 2 * d, :], XCb[:, xo:xo + nt], True, True, ["GW%d" % w_] + xcbk, pk(bk))
                    mm(bank(bk + 1, nt), GW[w_][:, 2 * d + 1, :], XCb[:, xo:xo + nt], True, True, ["GW%d" % w_] + xcbk,
                       pk(bk + 1))
                    act(Ad[:, xo:xo + nt], bank(bk, nt), AF.Sigmoid, pk(bk) + ["VEC"], [(an, xo), (an, "r", ro)],
                        bias=VEC[:, ba:ba + 1])
                    act(Bd[:, xo:xo + nt], bank(bk + 1, nt), AF.Sigmoid, pk(bk + 1) + ["VEC"], [(bn, xo), (bn, "r", ro)],
                        bias=VEC[:, bx:bx + 1])
                ak = [(an, xo) for (_, _, _, xo) in BLOCKS]
                btk = [(bn, xo) for (_, _, _, xo) in BLOCKS]
                for (o, n_) in regions:
                    sl_ = slice(o, o + n_)
                    kA, kB, kT = (an, "r", o), (bn, "r", o), ("TM", d, o)
                    act(Ad[:, sl_], Ad[:, sl_], AF.Exp, ak + ["CST"], [kA], scale=CST[:, 8 * d + c:8 * d + c + 1])
                    act(TM[:, sl_], Ad[:, sl_], AF.Square, [kA], [("TM", o)])
                    act(TM[:, sl_], TM[:, sl_], AF.Sqrt, [("TM", o), "ONE"], [("TM", o)], bias=ONE[:, 0:1], scale=-1.0)
                    tt("dve", Bd[:, sl_], Bd[:, sl_], XC[:, sl_], ALU.mult, btk + xck, [kB])
                    tt("dve", Bd[:, sl_], Bd[:, sl_], TM[:, sl_], ALU.mult, [kB, ("TM", o)], [kB])
                if d == 0:
                    mk.op("dve", lambda e: e.tensor_tensor_scan(out=RF[:, oc:oc + ncx], data0=A0[:, oc:oc + ncx],
                                                                data1=BT0[:, oc:oc + ncx], initial=0.0,
                                                                op0=ALU.mult, op1=ALU.add),
                          [("A0", "r", oc), ("B0", "r", oc)], [("RFc",)])
                    mk.op("dve", lambda e: e.tensor_tensor_scan(out=RF[:, ol:ol + nl], data0=A0[:, ol:ol + nl],
                                                                data1=BT0[:, ol:ol + nl],
                                                                initial=RF[:, oc + ncx - 1:oc + ncx],
                                                                op0=ALU.mult, op1=ALU.add),
                          [("A0", "r", ol), ("B0", "r", ol), ("RFc",)], [("RFl",)])
                else:
                    mk.op("dve", lambda e: e.tensor_tensor_scan(out=RR[:, oc:oc + ncx][:, ::-1],
                                                                data0=A1[:, oc:oc + ncx][:, ::-1],
                                                                data1=RR[:, oc:oc + ncx][:, ::-1], initial=0.0,
                                                                op0=ALU.mult, op1=ALU.add),
                          [("A1", "r", oc), ("B1", "r", oc)], [("B1", "r", oc), ("RRc",)])
                    mk.op("dve", lambda e: e.tensor_tensor_scan(out=RR[:, ol:ol + nl][:, ::-1],
                                                                data0=A1[:, ol:ol + nl][:, ::-1],
                                                                data1=RR[:, ol:ol + nl][:, ::-1],
                                                                initial=RR[:, oc:oc + 1],
                                                                op0=ALU.mult, op1=ALU.add),
                          [("A1", "r", ol), ("B1", "r", ol), ("RRc",)], [("B1", "r", ol), ("RRl",)])
                if d == 1 and ci + 1 < len(chunks):
                    inproj(ci + 1)
            tt("pool", RF[:, LX0:LX0 + T], RF[:, LX0:LX0 + T], RR[:, LX0:LX0 + T], ALU.add, [("RFl",), ("RRl",), ("B1", "r", ol)], [("RFl",)])
            tt("dve", MT[:], RF[:, LX0:LX0 + T], GY[ci % 2][:], ALU.mult,
               [("RFl",)] + [("GY%d" % (ci % 2), 512 * j) for j in range(8)], ["MT"])
            ld(mts[b][c], MT[:], ["MT"], [], "rm")
        mk.release(m)

    order = ["qkv", "diff", "nbr", "op0", "ffn0", "rg", "op1", "ffn1"]
    fns = {"qkv": phase_qkv, "diff": phase_diff, "nbr": phase_nbr, "op0": lambda: phase_oproj(0),
           "ffn0": lambda: phase_ffn(0), "rg": phase_rglru, "op1": lambda: phase_oproj(1), "ffn1": lambda: phase_ffn(1)}
    for ph in order:
        if stop == "p0":
            break
        if "only" in CFG and ph not in CFG["only"]:
            continue
        fns[ph]()
        if stop == ph:
            break
    mk.barrier()
    named = {"modd": modd, "ao0": aos[0], "xs_lat0": xs[("lat", 0)], "xs_ctx0": xs[("ctx", 0)],
             "qat0": qat[0], "kat0": kat[0], "va0": vas[0], "qbt0": qbt[0], "kbt0": kbt[0], "vb0": vbs[0],
             "hts0_lat0": hts[(("lat", 0), 0)], "hts1_lat0": hts[(("lat", 0), 1)], "mt0": mts[0]}
    for nm in dbg:
        src = named[nm]
        dst = nc.dram_tensor("dbg_" + nm, list(src.shape), src.dtype, kind="ExternalOutput").ap()
        mk.dma("sp", [(dst, src)], [], [], "dbg")
    mk.emit()
    return nc, mk


def prep_inputs(inputs, core):
    g = lambda k: np.asarray(inputs[k])
    b0 = 2 * core
    m = {}
    m["x"] = np.ascontiguousarray(g("x")[b0:b0 + 2])
    m["ctx"] = np.ascontiguousarray(g("ctx")[b0:b0 + 2])
    cc = np.stack([g("c")[b0], g("c")[b0 + 1], g("c_ctx")], 0).astype(np.float32)
    m["cct"] = np.ascontiguousarray(cc.reshape(3, 8, 128).transpose(2, 1, 0))
    vec = np.zeros((128, NV), np.float32)
    for l in range(2):
        for nm in ("ln1_g", "ln1_b", "ln2_g", "ln2_b"):
            vec[:, VMAP[(nm, l)]:VMAP[(nm, l)] + 8] = _pl(g(nm)[l])
        for k in range(3):
            vec[:, VMAP[("fcw", l, k)]:VMAP[("fcw", l, k)] + 44] = _pl(g("ffn_conv_w")[l, k])
        vec[:, VMAP[("fcb", l)]:VMAP[("fcb", l)] + 44] = _pl(g("ffn_conv_b")[l])
    for k in range(4):
        vec[:, VMAP[("rcw", k)]:VMAP[("rcw", k)] + 8] = _pl(g("rnn_conv_w")[0, k])
    vec[:, VMAP[("rcb",)]:VMAP[("rcb",)] + 8] = _pl(g("rnn_conv_b")[0])
    for d in range(2):
        vec[:, VMAP[("apar", d)]:VMAP[("apar", d)] + 8] = _pl(g("rg_a_param")[0, d])
        vec[:, VMAP[("rba", d)]:VMAP[("rba", d)] + 8] = _pl(g("rg_ba")[0, d].reshape(-1))
        vec[:, VMAP[("rbx", d)]:VMAP[("rbx", d)] + 8] = _pl(g("rg_bx")[0, d].reshape(-1))
    m["vec"] = vec
    m["rope"] = _ROPE
    rpb = g("na_rpb")[0].astype(np.float32)
    kc = np.arange(64)[:, None]
    cq = np.arange(64)[None, :]
    rel = kc - cq + 15
    ok = (rel >= 0) & (rel <= 30)
    gath = rpb[:, :, np.clip(rel, 0, 30)] * ok[None, None]
    perm = [0, 2, 4, 6, 1, 3, 5, 7]
    m["rpbt"] = np.ascontiguousarray(gath[perm].transpose(1, 2, 0, 3)).astype(np.float32)
    m["mask"] = _MASK
    m["lamv"] = np.stack([g("diff_lq1")[0], g("diff_lk1")[0], g("diff_lq2")[0], g("diff_lk2")[0]], 0).astype(np.float32)
    m["subg"] = np.ascontiguousarray(g("diff_subln_g")[0]).astype(np.float32)
    for k in ("ada_w", "ada_b", "ln1_g", "ln1_b", "ln2_g", "ln2_b", "ffn_w_up", "ffn_w_down", "att_w_in",
              "att_w_out", "rnn_w_in", "rg_wa", "rg_wx", "rnn_w_out"):
        m[k] = np.ascontiguousarray(g(k)).astype(np.float32)
    return m


def _const_tables():
    t = np.arange(T)
    row = (t // 64).astype(np.float64)[:, None]
    col = (t % 64).astype(np.float64)[:, None]
    inv = 1.0 / (10000.0 ** (np.arange(16, dtype=np.float64) / 16))
    inv = inv.astype(np.float32).astype(np.float64)
    ang = np.concatenate([row * inv, row * inv, col * inv, col * inv], -1).astype(np.float32)
    cos = np.cos(ang).astype(np.float32)
    sin = np.sin(ang).astype(np.float32)
    sgn = np.tile(np.concatenate([-np.ones(16), np.ones(16)]), 2).astype(np.float32)
    rope = np.stack([cos, sin * sgn[None]], 1).astype(np.float32)
    c = np.arange(64)
    cs = np.clip(c - 8, 0, 48)
    kc = np.arange(64)[:, None]
    inside = (kc >= cs[None]) & (kc < cs[None] + 16)
    mask = np.where(inside, 0.0, NEG).astype(np.float32)
    return np.ascontiguousarray(rope), np.ascontiguousarray(np.concatenate([mask, mask], 0))


_ROPE, _MASK = _const_tables()
_CACHE = {}
CFG = {}


def kernel(**inputs):
    if "nc" not in _CACHE:
        _CACHE["nc"] = build()[0]
    nc = _CACHE["nc"]
    in_maps = [prep_inputs(inputs, c) for c in range(8)]
    res = run_bass_kernel_spmd(nc, in_maps, core_ids=list(range(8)))
    out = np.concatenate([r["out"] for r in res.results], axis=0)
    return out.astype(np.float32)
```

```python
import math
import contextlib
import numpy as np
import concourse.bass as bass
import concourse.mybir as mybir
from concourse.bass_utils import run_bass_kernel_spmd

F32 = mybir.dt.float32
BF16 = mybir.dt.bfloat16
AF = mybir.ActivationFunctionType
ALU = mybir.AluOpType

ENGS = ["pe", "act", "dve", "pool", "sp"]

D = 1024
T = 4096
TC = 256
TK = T + TC
DFF = 2816
ALPHA = 4.0 ** 0.25
EPS = 1e-5
LAMBDA_INIT0 = 0.8 - 0.6 * math.exp(0.0)
NEG = -30000.0


class MK:
    def __init__(self, nc):
        self.nc = nc
        self.ops = {e: [] for e in ENGS}
        self.cnt = {e: 0 for e in ENGS}
        self.dcnt = {}
        self.seen = {e: {} for e in ENGS}
        self.lastw = {}
        self.readers = {}
        self.sb_off = 16640
        self.sb_names = 0
        self.sb_max = 0

    def sb(self, shape, dtype, name=None):
        nbytes = int(np.prod(shape[1:])) * (4 if dtype == F32 else 2)
        nbytes = (nbytes + 63) // 64 * 64
        self.sb_names += 1
        nm = "%s_%d" % (name or "t", self.sb_names)
        t = self.nc.alloc_sbuf_tensor_at(nm, list(shape), dtype, offset=self.sb_off)
        self.sb_off += nbytes
        self.sb_max = max(self.sb_max, self.sb_off)
        assert self.sb_off <= 229376, ("sbuf overflow", nm, self.sb_off)
        return t

    def mark(self):
        return self.sb_off

    def release(self, m):
        self.barrier()
        self.sb_off = m

    def _deps(self, eng, reads, writes, is_dma):
        deps = {}

        def add(tok, raw):
            if tok is None:
                return
            sk, val, teng = tok
            if not is_dma and teng == eng and sk[0] == "e":
                if eng == "pe":
                    return
            if deps.get(sk, 0) < val:
                deps[sk] = val

        for k in reads:
            add(self.lastw.get(k), True)
        for k in writes:
            add(self.lastw.get(k), False)
            for sk, (val, teng) in self.readers.get(k, {}).items():
                add((sk, val, teng), False)
        waits = []
        seen = self.seen[eng]
        for sk, val in deps.items():
            if seen.get(sk, 0) >= val:
                continue
            seen[sk] = val
            waits.append((sk, val))
        return waits

    def _commit(self, tok, reads, writes):
        sk, val, teng = tok
        for k in writes:
            self.lastw[k] = tok
            self.readers[k] = {}
        for k in reads:
            self.readers.setdefault(k, {})[sk] = (val, teng)

    def op(self, eng, fn, reads=(), writes=()):
        if eng != "pe":
            pr = [k for k in reads if isinstance(k, tuple) and k and k[0] == "p"]
            if pr:
                writes = list(writes) + [k for k in pr if k not in writes]
        waits = self._deps(eng, reads, writes, False)
        self.cnt[eng] += 1
        tok = (("e", eng), self.cnt[eng], eng)
        self.ops[eng].append((waits, fn, tok, 1))
        self._commit(tok, reads, writes)
        return tok

    def dma(self, eng, pairs, reads, writes, slot, slow=False):
        sk = ("d", slot)
        prev = self.dcnt.get(slot, 0)
        waits = self._deps(eng, reads, writes, True)
        if prev and self.seen[eng].get(sk, 0) < prev:
            self.seen[eng][sk] = prev
            waits.append((sk, prev))
        n = len(pairs)
        self.dcnt[slot] = prev + 16 * n
        tok = (sk, prev + 16 * n, None)

        def fn(e, pairs=pairs, slow=slow):
            if slow:
                return [e.dma_start(out=o, in_=i, allow_slow_non_contiguous=True) for (o, i) in pairs]
            return [e.dma_start(out=o, in_=i) for (o, i) in pairs]

        self.ops[eng].append((waits, fn, tok, 16))
        self._commit(tok, reads, writes)
        return tok

    def barrier(self):
        final = {}
        for e in ENGS:
            if self.cnt[e]:
                final[("e", e)] = self.cnt[e]
        for s, v in self.dcnt.items():
            final[("d", s)] = v
        for e in ENGS:
            waits = []
            for sk, val in final.items():
                if sk == ("e", e):
                    continue
                if self.seen[e].get(sk, 0) < val:
                    self.seen[e][sk] = val
                    waits.append((sk, val))
            if waits:
                self.ops[e].append((waits, None, None, 0))
        self.lastw = {}
        self.readers = {}

    def emit(self):
        nc = self.nc
        self.barrier()
        waited = {e: set() for e in ENGS}
        for e in ENGS:
            for waits, fn, tok, inc in self.ops[e]:
                for sk, val in waits:
                    if sk[0] == "e":
                        waited[sk[1]].add(val)
        remap = {}
        for e in ENGS:
            vals = sorted(waited[e])
            remap[e] = {v: i + 1 for i, v in enumerate(vals)}
        sems = {}
        with contextlib.ExitStack() as st:
            for e in ENGS:
                sems[("e", e)] = st.enter_context(nc.semaphore("s_" + e))
            for s in self.dcnt:
                sems[("d", s)] = st.enter_context(nc.semaphore("d_" + str(s)))
            block = st.enter_context(nc.Block())

            def run(engname):
                def body(eng):
                    for waits, fn, tok, inc in self.ops[engname]:
                        for sk, val in waits:
                            v = remap[sk[1]][val] if sk[0] == "e" else val
                            eng.wait_ge(sems[sk], v)
                        if fn is None:
                            continue
                        r = fn(eng)
                        if inc == 16:
                            for ins in r:
                                ins.then_inc(sems[tok[0]], 16)
                        elif tok[1] in remap[engname]:
                            r.then_inc(sems[tok[0]], 1)

                return body

            block.tensor(run("pe"))
            block.scalar(run("act"))
            block.vector(run("dve"))
            block.gpsimd(run("pool"))
            block.sync(run("sp"))


def _vec_map():
    m = {}
    o = 0
    for l in range(2):
        for nm in ("ln1_g", "ln1_b", "ln2_g", "ln2_b"):
            m[(nm, l)] = o
            o += 8
    for l in range(2):
        for k in range(3):
            m[("fcw", l, k)] = o
            o += 44
    for l in range(2):
        m[("fcb", l)] = o
        o += 44
    for k in range(4):
        m[("rcw", k)] = o
        o += 8
    m[("rcb",)] = o
    o += 8
    for d in range(2):
        m[("apar", d)] = o
        o += 8
    for d in range(2):
        m[("rba", d)] = o
        o += 8
    for d in range(2):
        m[("rbx", d)] = o
        o += 8
    return m, o


VMAP, NV = _vec_map()


def _pl(v):
    v = np.asarray(v, np.float32).reshape(-1, 128)
    return np.ascontiguousarray(v.T)


def build(stop=None, dbg=()):
    nc = bass.Bass("TRN2", target_bir_lowering=False)
    mk = MK(nc)

    def din(name, shape, dt=F32):
        return nc.dram_tensor(name, list(shape), dt, kind="ExternalInput").ap()

    def dscr(name, shape, dt=F32):
        return nc.dram_tensor(name, list(shape), dt, kind="Internal").ap()

    x_d = din("x", [2, T, D])
    ctx_d = din("ctx", [2, TC, D])
    cct_d = din("cct", [128, 8, 3])
    vec_d = din("vec", [128, NV])
    rope_d = din("rope", [T, 2, 64])
    rpbt_d = din("rpbt", [15, 64, 8, 64])
    mask_d = din("mask", [128, 64])
    lam_d = din("lamv", [4, 64])
    subg_d = din("subg", [128])
    ada_w_d = din("ada_w", [2, D, 6 * D])
    ada_b_d = din("ada_b", [2, 6 * D])
    ln_d = {nm: din(nm, [2, D]) for nm in ("ln1_g", "ln1_b", "ln2_g", "ln2_b")}
    wup_d = din("ffn_w_up", [2, D, 2 * DFF])
    wdn_d = din("ffn_w_down", [2, DFF, D])
    awin_d = din("att_w_in", [1, D, 3 * D])
    awout_d = din("att_w_out", [1, D, D])
    rwin_d = din("rnn_w_in", [1, D, 2 * D])
    rga_d = din("rg_wa", [1, 2, 8, 128, 128])
    rgx_d = din("rg_wx", [1, 2, 8, 128, 128])
    rwout_d = din("rnn_w_out", [1, D, D])
    out_d = nc.dram_tensor("out", [2, T, D], F32, kind="ExternalOutput").ap()

    modd = dscr("modd", [2, 3, 6 * D])
    NBB = CFG.get("nb", 2)
    SEQ = [("lat", 0), ("ctx", 0), ("lat", 1), ("ctx", 1)][:2 * NBB]
    SEQ_ALL = [("lat", 0), ("ctx", 0), ("lat", 1), ("ctx", 1)]
    xs = {s: dscr("xs_%s%d" % s, [T if s[0] == "lat" else TC, D]) for s in SEQ_ALL}
    hts = {(s, a): dscr("hts%d_%s%d" % ((a,) + s), [8, 128, (T if s[0] == "lat" else TC) + 2], BF16)
           for s in SEQ_ALL for a in (0, 1)}
    qat = [dscr("qat%d" % b, [4, 128, TK], BF16) for b in range(2)]
    kat = [dscr("kat%d" % b, [4, 128, TK], BF16) for b in range(2)]
    qbt = [dscr("qbt%d" % b, [4, 128, TK], BF16) for b in range(2)]
    kbt = [dscr("kbt%d" % b, [4, 128, TK], BF16) for b in range(2)]
    vas = [dscr("va%d" % b, [TK, 512], BF16) for b in range(2)]
    vbs = [dscr("vb%d" % b, [TK, 512], BF16) for b in range(2)]
    aos = [dscr("ao%d" % b, [TK, D]) for b in range(2)]
    mts = [dscr("mt%d" % b, [8, 128, T], BF16) for b in range(2)]

    def seq_src(s):
        return x_d[s[1]] if s[0] == "lat" else ctx_d[s[1]]

    def seq_len(s):
        return T if s[0] == "lat" else TC

    def seq_var(s):
        return s[1] if s[0] == "lat" else 2

    PS = nc.alloc_psum_tensor("PS", [128, 4096], F32)

    def bank(i, n=512, off=0):
        return PS[:, 512 * i + off:512 * i + off + n]

    def pk(i):
        return [("p", i)]

    def mm(out, lhsT, rhs, start, stop, r, w, skip=False):
        mk.op("pe", lambda e: e.matmul(out, lhsT=lhsT, rhs=rhs, start=start, stop=stop, skip_group_check=skip), r, w)

    def tr(out, in_, idn, r, w):
        mk.op("pe", lambda e: e.transpose(out, in_, idn), r, w)

    def act(out, in_, func, r, w, bias=None, scale=None, accum=None):
        kw = {}
        if bias is not None:
            kw["bias"] = bias
        if scale is not None:
            kw["scale"] = scale
        if accum is not None:
            kw["accum_out"] = accum
        mk.op("act", lambda e: e.activation(out=out, in_=in_, func=func, **kw), r, w)

    def ts(eng, out, in0, s1, s2, op0, op1, r, w):
        if s2 is None:
            if op0 == ALU.mult:
                mk.op(eng, lambda e: e.tensor_scalar_mul(out=out, in0=in0, scalar1=s1), r, w)
            else:
                assert op0 == ALU.add
                mk.op(eng, lambda e: e.tensor_scalar_add(out=out, in0=in0, scalar1=s1), r, w)
        else:
            mk.op(eng, lambda e: e.tensor_scalar(out=out, in0=in0, scalar1=s1, scalar2=s2, op0=op0, op1=op1), r, w)

    def stt(eng, out, in0, scalar, in1, op0, op1, r, w):
        mk.op(eng, lambda e: e.scalar_tensor_tensor(out=out, in0=in0, scalar=scalar, in1=in1, op0=op0, op1=op1), r, w)

    def tt(eng, out, in0, in1, op, r, w):
        mk.op(eng, lambda e: e.tensor_tensor(out=out, in0=in0, in1=in1, op=op), r, w)

    def cp(eng, out, in_, r, w):
        if eng == "act":
            mk.op("act", lambda e: e.copy(out=out, in_=in_), r, w)
        else:
            mk.op(eng, lambda e: e.tensor_copy(out=out, in_=in_), r, w)

    def memset(eng, ap, val, w):
        mk.op(eng, lambda e: e.memset(ap, val), [], w)

    def ld(out, in_, r, w, slot):
        mk.dma("sp", [(out, in_)], r, w, slot)

    def ldc(pairs, r, w, slot):
        mk.dma("pool", pairs, r, w, slot)

    ident = mk.sb([128, 128], F32, "ident")
    memset("pool", ident[:], 0.0, ["ident"])
    mk.op("pool", lambda e: e.affine_select(out=ident[:], in_=ident[:], compare_op=ALU.not_equal, fill=1.0,
                                             base=0, pattern=[[-1, 128]], channel_multiplier=1), ["ident"], ["ident"])
    identb = mk.sb([128, 128], BF16, "identb")
    cp("pool", identb[:], ident[:], ["ident"], ["identb"])
    epsT = mk.sb([128, 1], F32, "eps")
    memset("dve", epsT[:], EPS, ["eps"])
    VEC = mk.sb([128, NV], F32, "VEC")
    ld(VEC[:], vec_d, [], ["VEC"], "c0")
    SCAL = mk.sb([128, 4, 3, 2, 8], F32, "SCAL")
    CST = mk.sb([128, 16], F32, "CST")
    NLAM = mk.sb([128, 1], F32, "NLAM")
    GSUB = mk.sb([128, 128], F32, "GSUB")

    def vcol(key, n=8):
        o = VMAP[key]
        return VEC[:, o:o + n]

    m0 = mk.mark()
    MODP = [mk.sb([128, 48, 3], F32, "MODP%d" % l) for l in range(2)]
    ZT = mk.sb([128, 8, 2], BF16, "ZT")
    memset("dve", ZT[:], 0.0, ["ZT"])
    for s in SEQ:
        for a in (0, 1):
            L = seq_len(s)
            h = hts[(s, a)].rearrange("k p t -> p k t")
            mk.dma("sp", [(h[:, :, 0:1], ZT[:, :, 0:1])], ["ZT"], [], "z0", slow=True)
            mk.dma("sp", [(h[:, :, L + 1:L + 2], ZT[:, :, 1:2])], ["ZT"], [], "z0", slow=True)

    CC = mk.sb([128, 8, 3], F32, "CC")
    ST = mk.sb([128, 8, 3], F32, "ST")
    ld(CC[:], cct_d, [], ["CC"], "c1")
    act(ST[:], CC[:], AF.Silu, ["CC"], ["ST"])
    MODR = mk.sb([3, 6 * D], F32, "MODR")
    ADAB = mk.sb([3, 6 * D], F32, "ADAB")
    WB = [mk.sb([128, 8, 512], F32, "WB%d" % i) for i in range(4)]
    for l in range(2):
        ld(ADAB[:], ada_b_d[l].partition_broadcast(3), [], ["ADAB"], "c2")
        wv = ada_w_d[l].rearrange("(k p) n -> p k n", p=128)
        for jb in range(12):
            sl = jb % 4
            ld(WB[sl][:], wv[:, :, jb * 512:(jb + 1) * 512], [], ["WB%d" % sl], "wb%d" % sl)
            bk = jb % 2
            for k in range(8):
                mm(PS[0:3, 512 * bk:512 * bk + 512], ST[:, k, :], WB[sl][:, k, :], k == 0, k == 7,
                   ["ST", "WB%d" % sl], pk(bk))
            tt("dve", MODR[0:3, jb * 512:(jb + 1) * 512], PS[0:3, 512 * bk:512 * bk + 512],
               ADAB[0:3, jb * 512:(jb + 1) * 512], ALU.add, pk(bk) + ["ADAB"], [("MODR", jb)])
        allk = [("MODR", jb) for jb in range(12)]
        ld(modd[l], MODR[0:3, :], allk, [], "c3")
        for k in range(48):
            tr(PS[:, 3584 + 3 * k:3584 + 3 * k + 3], MODR[0:3, 128 * k:128 * (k + 1)], ident[0:3, 0:3],
               allk + ["ident"], pk(7))
        cp("dve", MODP[l][:].rearrange("p k r -> p (k r)"), PS[:, 3584:3584 + 144], pk(7), ["MODP%d" % l])
    TMP8 = mk.sb([128, 8], F32, "TMP8")
    sites = [(0, 8, 0, None, None), (0, 32, 24, ("ln1_g", 0), ("ln1_b", 0)),
             (1, 8, 0, ("ln2_g", 0), ("ln2_b", 0)), (1, 32, 24, ("ln1_g", 1), ("ln1_b", 1))]
    for si, (l, sco, sho, gk, bk_) in enumerate(sites):
        for r in range(3):
            A_ = SCAL[:, si, r, 0, :]
            B_ = SCAL[:, si, r, 1, :]
            sc = MODP[l][:, sco:sco + 8, r]
            sh = MODP[l][:, sho:sho + 8, r]
            ts("dve", TMP8[:], sc, 1.0, None, ALU.add, None, ["MODP%d" % l], ["TMP8"])
            if gk is None:
                cp("dve", A_, TMP8[:], ["TMP8"], ["SCAL"])
                cp("dve", B_, sh, ["MODP%d" % l], ["SCAL"])
            else:
                tt("dve", A_, TMP8[:], vcol(gk), ALU.mult, ["TMP8", "VEC"], ["SCAL"])
                tt("dve", B_, TMP8[:], vcol(bk_), ALU.mult, ["TMP8", "VEC"], ["SCAL"])
                tt("dve", B_, B_, sh, ALU.add, ["SCAL", "MODP%d" % l], ["SCAL"])
    LV = mk.sb([128, 4, 64], F32, "LV")
    ld(LV[:].rearrange("p a d -> p (a d)"), lam_d.rearrange("a d -> (a d)").partition_broadcast(128), [], ["LV"], "c4")
    L2 = mk.sb([128, 2, 64], F32, "L2")
    E2 = mk.sb([128, 2], F32, "E2")
    tt("dve", L2[:, 0, :], LV[:, 0, :], LV[:, 1, :], ALU.mult, ["LV"], ["L2"])
    tt("dve", L2[:, 1, :], LV[:, 2, :], LV[:, 3, :], ALU.mult, ["LV"], ["L2"])
    mk.op("dve", lambda e: e.reduce_sum(out=E2[:], in_=L2[:], axis=mybir.AxisListType.X), ["L2"], ["E2"])
    act(E2[:], E2[:], AF.Exp, ["E2"], ["E2"])
    tt("dve", NLAM[:], E2[:, 1:2], E2[:, 0:1], ALU.subtract, ["E2"], ["NLAM"])
    ts("dve", NLAM[:], NLAM[:], -LAMBDA_INIT0, None, ALU.add, None, ["NLAM"], ["NLAM"])
    ld(GSUB[:], subg_d.partition_broadcast(128), [], ["GSUB"], "c5")
    ts("dve", GSUB[:], GSUB[:], 1.0 - LAMBDA_INIT0, None, ALU.mult, None, ["GSUB"], ["GSUB"])
    act(CST[:], VEC[:, VMAP[("apar", 0)]:VMAP[("apar", 0)] + 16], AF.Exp, ["VEC"], ["CST"], scale=-1.0)
    ts("dve", CST[:], CST[:], 1.0, None, ALU.add, None, ["CST"], ["CST"])
    act(CST[:], CST[:], AF.Ln, ["CST"], ["CST"])
    ts("dve", CST[:], CST[:], -8.0, None, ALU.mult, None, ["CST"], ["CST"])
    mk.release(m0)

    def make_ht(src, srckeys, HT, htkey, site, r, pb):
        for k in range(8):
            b_ = pb + k // 4
            tr(bank(b_, 128, 128 * (k % 4)), src[:, 128 * k:128 * (k + 1)], ident[:], srckeys + ["ident"],
               [("p", b_)])
        for k in range(8):
            hm = CFG.get("ht_mode", 3)
            if hm == 0 or (hm == 1 and k >= 4) or (hm == 2 and k < 4):
                continue
            b_ = pb + k // 4
            A_ = SCAL[:, site, r, 0, k:k + 1]
            B_ = SCAL[:, site, r, 1, k:k + 1]
            if k < 4:
                act(HT[:, k, :], bank(b_, 128, 128 * (k % 4)), AF.Identity, [("p", b_), "SCAL"],
                    [(htkey, k)], bias=B_, scale=A_)
            else:
                ts("dve", HT[:, k, :], bank(b_, 128, 128 * (k % 4)), A_, B_, ALU.mult, ALU.add,
                   [("p", b_), "SCAL"], [(htkey, k)])

    def epilogue(ps2, pskeys, Xsrc, GB, Gt, Bt, bkeys, site, r, xdst, hdst, E, i, pb):
        sl = i % 2
        sy = i % len(E["Y"])
        Xt, Y, HT = E["Xt"][sl], E["Y"][sy], E["HT"][sy]
        kx, ky, kh = "Xt%d" % sl, "Y%d" % sy, "HTe%d" % sy
        ld(Xt[:], Xsrc, [], [kx], "ex%d" % sl)
        tt("dve", Y[:], ps2, GB[:], ALU.mult, pskeys + bkeys, [ky])
        stt("dve", Y[:], Xt[:], ALPHA, Y[:], ALU.mult, ALU.add, [kx, ky], [ky])
        BS, MV, RS = E["BS"][sl], E["MV"][sl], E["RS"][sl]
        kb = "BS%d" % sl
        mk.op("dve", lambda e: e.bn_stats(out=BS[:, 0, :], in_=Y[:, 0:512]), [ky], [(kb, 0)])
        mk.op("dve", lambda e: e.bn_stats(out=BS[:, 1, :], in_=Y[:, 512:1024]), [ky], [(kb, 1)])
        mk.op("dve", lambda e: e.bn_aggr(out=MV[:], in_=BS[:]), [(kb, 0), (kb, 1)], [(kb, 2)])
        act(RS[:, 0:1], MV[:, 1:2], AF.Ln, [(kb, 2), "eps"], [(kb, 3)], bias=epsT[:, 0:1])
        act(RS[:, 1:2], RS[:, 0:1], AF.Exp, [(kb, 3)], [(kb, 4)], scale=-0.5)
        stt("dve", RS[:, 2:3], MV[:, 0:1], -1.0, RS[:, 1:2], ALU.mult, ALU.mult, [(kb, 2), (kb, 4)], [(kb, 5)])
        act(Y[:], Y[:], AF.Identity, [ky, (kb, 4), (kb, 5)], [ky], bias=RS[:, 2:3], scale=RS[:, 1:2])
        tt("pool", Xt[:], Y[:], Gt[:], ALU.mult, [ky] + bkeys, [kx])
        tt("pool", Xt[:], Xt[:], Bt[:], ALU.add, [kx] + bkeys, [kx])
        ld(xdst, Xt[:], [kx], [], "es%d" % sl)
        if hdst is not None:
            make_ht(Y, [ky], HT, kh, site, r, pb)
            ld(hdst, HT[:], [(kh, k) for k in range(8)], [], "eh%d" % sy)

    def epi_alloc(ny=2, nht=0):
        E = {}
        E["Xt"] = [mk.sb([128, D], F32, "Xt") for _ in range(2)]
        E["Y"] = [mk.sb([128, D], F32, "Y") for _ in range(ny)]
        E["HT"] = [mk.sb([128, 8, 128], BF16, "HTe") for _ in range(nht if nht else ny)]
        E["BS"] = [mk.sb([128, 2, 6], F32, "BS") for _ in range(2)]
        E["MV"] = [mk.sb([128, 2], F32, "MV") for _ in range(2)]
        E["RS"] = [mk.sb([128, 3], F32, "RS") for _ in range(2)]
        return E

    def load_bc(GB, Gt, Bt, l, goff, r, gname, bname):
        ld(GB[:], modd[l, r, goff:goff + D].partition_broadcast(128), [], ["GB"], "bc0")
        ld(Gt[:], ln_d[gname][l].partition_broadcast(128), [], ["Gt"], "bc1")
        ld(Bt[:], ln_d[bname][l].partition_broadcast(128), [], ["Bt"], "bc2")

    def phase_qkv():
        m = mk.mark()
        WIN = mk.sb([128, 8, 3 * D], BF16, "WIN")
        wv = awin_d[0].rearrange("(k p) n -> p k n", p=128)
        for k in range(8):
            ldc([(WIN[:, k, :], wv[:, k, :])], [], [("WIN", k)], "w%d" % (k % 4))
        wink = [("WIN", k) for k in range(8)]
        XT = [mk.sb([128, D], F32, "XT") for _ in range(3)]
        RP = [mk.sb([128, 2, 64], F32, "RP") for _ in range(3)]
        HT = [mk.sb([128, 8, 128], BF16, "HT") for _ in range(2)]
        RQ = [mk.sb([128, D], F32, "RQ") for _ in range(2)]
        T1 = [mk.sb([128, D], F32, "T1") for _ in range(2)]
        T2 = [mk.sb([128, D], F32, "T2") for _ in range(2)]
        RB = [mk.sb([128, D], BF16, "RB") for _ in range(2)]
        TQb = [mk.sb([128, D], BF16, "TQb") for _ in range(2)]
        VV = [mk.sb([128, 2, 512], BF16, "VV") for _ in range(2)]
        TQ = [mk.sb([128, 8, 128], BF16, "TQ") for _ in range(2)]
        TB = [mk.sb([128, 8, 128], BF16, "TB") for _ in range(2)]
        tiles = [(b, i) for b in range(NBB) for i in range(34)]
        if "qkv_tiles" in CFG:
            tiles = tiles[:CFG["qkv_tiles"]]
        NT = len(tiles)

        def src_of(b, i):
            if i < 32:
                return x_d[b, 128 * i:128 * (i + 1), :]
            return ctx_d[b, 128 * (i - 32):128 * (i - 31), :]

        def stageL(n):
            b, i = tiles[n]
            s3 = n % 3
            ld(XT[s3][:], src_of(b, i), [], ["XT%d" % s3], "xl%d" % s3)
            if i < 32:
                ld(RP[s3][:], rope_d[128 * i:128 * (i + 1)], [], ["RP%d" % s3], "rl%d" % s3)

        def stageM(n):
            b, i = tiles[n]
            sl, s3 = n % 2, n % 3
            lat = i < 32
            r = b if lat else 2
            htk = "HT%d" % sl
            make_ht(XT[s3], ["XT%d" % s3], HT[sl], htk, 0, r, 6)
            hk = [(htk, k) for k in range(8)]
            for j in range(6):
                for k in range(8):
                    mm(bank(j), HT[sl][:, k, :], WIN[:, k, 512 * j:512 * (j + 1)], k == 0, k == 7, hk + wink, pk(j))
            cp("act", RQ[sl][:], PS[:, 0:1024], pk(0) + pk(1), ["RQ%d" % sl])
            cp("dve", RB[sl][:], PS[:, 1536:2560], pk(3) + pk(4), ["RB%d" % sl])
            cp("act", VV[sl][:, 0, :], bank(2), pk(2), [("VV%d" % sl, 0)])
            cp("dve", VV[sl][:, 1, :], bank(5), pk(5), [("VV%d" % sl, 1)])
            if lat:
                rp = RP[s3]
                rk = "RP%d" % s3
                rq, t1, t2 = RQ[sl], T1[sl], T2[sl]
                tt("dve", t1[:].rearrange("p (g d) -> p g d", g=16), rq[:].rearrange("p (g d) -> p g d", g=16),
                   rp[:, 0:1, :].to_broadcast([128, 16, 64]), ALU.mult, ["RQ%d" % sl, rk], ["T1%d" % sl])
                rqv = rq[:].rearrange("p (g a h f) -> p g a h f", g=16, a=2, h=2, f=16)
                t2v = t2[:].rearrange("p (g a h f) -> p g a h f", g=16, a=2, h=2, f=16)
                sv = rp[:, 1:2, :].rearrange("p o (a h f) -> p o a h f", a=2, h=2, f=16)
                tt("pool", t2v[:, :, :, 0, :], rqv[:, :, :, 1, :], sv[:, :, :, 0, :].to_broadcast([128, 16, 2, 16]),
                   ALU.mult, ["RQ%d" % sl, rk], [("T2%d" % sl, 0)])
                tt("pool", t2v[:, :, :, 1, :], rqv[:, :, :, 0, :], sv[:, :, :, 1, :].to_broadcast([128, 16, 2, 16]),
                   ALU.mult, ["RQ%d" % sl, rk], [("T2%d" % sl, 1)])
                tt("dve", TQb[sl][:], t1[:], t2[:], ALU.add, ["T1%d" % sl, ("T2%d" % sl, 0), ("T2%d" % sl, 1)], ["TQb%d" % sl])
            else:
                cp("pool", TQb[sl][:], RQ[sl][:], ["RQ%d" % sl], ["TQb%d" % sl])
            ld(vas[b][tok0_of(n):tok0_of(n) + 128, :], VV[sl][:, 0, :], [("VV%d" % sl, 0)], [], "sva%d" % sl)
            ld(vbs[b][tok0_of(n):tok0_of(n) + 128, :], VV[sl][:, 1, :], [("VV%d" % sl, 1)], [], "svb%d" % sl)

        def tok0_of(n):
            b, i = tiles[n]
            return 128 * i if i < 32 else T + 128 * (i - 32)

        def stageX(n):
            b, i = tiles[n]
            sl = n % 2
            lat = i < 32
            tok0 = tok0_of(n)
            for (src, sk, dst, dk) in ((TQb[sl], ["TQb%d" % sl], TQ[sl], "TQ%d" % sl), (RB[sl], ["RB%d" % sl], TB[sl], "TB%d" % sl)):
                for k in range(8):
                    b_ = 6 + k // 4
                    tr(PS[:, 512 * b_:512 * (b_ + 1)].bitcast(BF16)[:, 128 * (k % 4):128 * (k % 4 + 1)],
                       src[:, 128 * k:128 * (k + 1)], identb[:], sk + ["identb"], pk(b_))
                act(dst[:, 0:4, :].rearrange("p k t -> p (k t)"), bank(6).bitcast(BF16)[:, 0:512], AF.Identity, pk(6),
                    [(dk, 0)], scale=0.125)
                cp("dve", dst[:, 4:8, :].rearrange("p k t -> p (k t)"), bank(7).bitcast(BF16)[:, 0:512], pk(7), [(dk, 1)])
            for si_, (dst_d, tile_, kk, half) in enumerate(((qat, TQ, "TQ%d" % sl, 0), (kat, TQ, "TQ%d" % sl, 1),
                                                          (qbt, TB, "TB%d" % sl, 0), (kbt, TB, "TB%d" % sl, 1))):
                ld(dst_d[b].rearrange("h p t -> p h t")[:, :, tok0:tok0 + 128],
                   tile_[sl][:, 4 * half:4 * half + 4, :], [(kk, half)], [], "sq%d%d" % (si_, sl))

        if NT:
            stageL(0)
            if NT > 1:
                stageL(1)
            stageM(0)
            for n in range(NT):
                if n + 2 < NT:
                    stageL(n + 2)
                if n + 1 < NT:
                    stageM(n + 1)
                stageX(n)
        mk.release(m)

    def phase_diff():
        m = mk.mark()
        KT = [mk.sb([128, TK], BF16, "KT") for _ in range(2)]
        QT = [mk.sb([128, TK], BF16, "QT") for _ in range(2)]
        V1 = [mk.sb([128, 34, 129], BF16, "V1") for _ in range(2)]
        NPT = 3
        PT = [mk.sb([128, 2, 512], BF16, "PT") for _ in range(NPT)]
        AQ = mk.sb([128, 4, 128], F32, "AQ")
        OO = mk.sb([128, 4, 128], F32, "OO")
        SQ = mk.sb([128, 4, 128], F32, "SQ")
        RC = mk.sb([128, 4, 2], F32, "RC")
        T1 = mk.sb([128, 4, 1], F32, "T1")
        SS = mk.sb([128, 4, 1], F32, "SS")
        G3 = mk.sb([128, 1, 128], F32, "G3")
        cp("pool", G3[:, 0, :], GSUB[:], ["GSUB"], ["G3"])
        AOB = [mk.sb([128, 4, 128], F32, "AOB") for _ in range(2)]
        for i in range(2):
            memset("pool", V1[i][:, :, 128:129], 1.0, [("V1%d" % i, "one")])
        ACC = PS[:, 2048:4096].rearrange("p (b c) -> p b c", b=4)
        nblk = 0
        hh = 0
        npt = 0
        for b in range(NBB):
            for h in range(4):
                sl = hh % 2
                hh += 1
                ld(KT[sl][:], kat[b][h], [], ["KT%d" % sl], "ak%d" % sl)
                ld(QT[sl][:], qat[b][h], [], ["QT%d" % sl], "aq%d" % sl)
                ld(V1[sl][:, :, 0:128], vas[b].rearrange("(t p) c -> p t c", p=128)[:, :, 128 * h:128 * (h + 1)],
                   [], ["V1%d" % sl], "av%d" % sl)
                vkeys = ["V1%d" % sl, ("V1%d" % sl, "one")]
                blocks = [(512 * j, 512, list(range(34))) for j in range(8)] + [(T, 256, [32, 33])]
                for (q0, nq, kts) in blocks:
                    nqt = nq // 128
                    acck = [("acc", q) for q in range(nqt)]
                    memset("dve", ACC[:, 0:nqt, 0:258], 0.0, acck)

                    def qk(it):
                        kt = kts[it]
                        sp_ = it % 2
                        for mp in range(2):
                            bk = 2 * sp_ + mp
                            mm(bank(bk, nq), KT[sl][64 * mp:64 * (mp + 1), 128 * kt:128 * (kt + 1)],
                               QT[sl][64 * mp:64 * (mp + 1), q0:q0 + nq], True, True,
                               ["KT%d" % sl, "QT%d" % sl], pk(bk))

                    qk(0)
                    for it, kt in enumerate(kts):
                        if it + 1 < len(kts):
                            qk(it + 1)
                        sp_ = it % 2
                        pb_ = npt % NPT
                        npt += 1
                        sview = PS[:, 1024 * sp_:1024 * sp_ + 1024].rearrange("p (b c) -> p b c", b=2)[:, :, 0:nq]
                        act(PT[pb_][:, :, 0:nq], sview, AF.Exp, pk(2 * sp_) + pk(2 * sp_ + 1), ["PT%d" % pb_])
                        for mp in range(2):
                            for qt in range(nqt):
                                mm(bank(4 + qt, 129, 129 * mp), PT[pb_][:, mp, 128 * qt:128 * (qt + 1)],
                                   V1[sl][:, kt, :], False, False, ["PT%d" % pb_] + vkeys + [("acc", qt)],
                                   [("acc", qt)], skip=True)
                    ob = AOB[nblk % 2]
                    okey = "AOB%d" % (nblk % 2)
                    nblk += 1
                    mk.op("dve", lambda e, nqt=nqt: e.reciprocal(out=RC[:, 0:nqt, :], in_=ACC[:, 0:nqt, 128:258:129]),
                          acck, ["RC"])
                    ts("dve", T1[:, 0:nqt, :], RC[:, 0:nqt, 1:2], NLAM[:, 0:1], None, ALU.mult, None, ["RC", "NLAM"], ["T1"])
                    tt("dve", AQ[:, 0:nqt, :], ACC[:, 0:nqt, 0:128], RC[:, 0:nqt, 0:1].to_broadcast([128, nqt, 128]),
                       ALU.mult, acck + ["RC"], ["AQ"])
                    tt("dve", SQ[:, 0:nqt, :], ACC[:, 0:nqt, 129:257], T1[:, 0:nqt, :].to_broadcast([128, nqt, 128]),
                       ALU.mult, acck + ["T1"], ["SQ"])
                    tt("dve", OO[:, 0:nqt, :], AQ[:, 0:nqt, :], SQ[:, 0:nqt, :], ALU.add, ["AQ", "SQ"], ["OO"])
                    tt("pool", SQ[:, 0:nqt, :], OO[:, 0:nqt, :], OO[:, 0:nqt, :], ALU.mult, ["OO"], ["SQ"])
                    mk.op("dve", lambda e, nqt=nqt: e.reduce_sum(out=SS[:, 0:nqt, :], in_=SQ[:, 0:nqt, :],
                                                                 axis=mybir.AxisListType.X), ["SQ"], ["SS"])
                    act(SS[:, 0:nqt, :], SS[:, 0:nqt, :], AF.Ln, ["SS", "eps"], ["SS"], bias=epsT[:, 0:1], scale=1.0 / 128)
                    act(SS[:, 0:nqt, :], SS[:, 0:nqt, :], AF.Exp, ["SS"], ["SS"], scale=-0.5)
                    tt("pool", OO[:, 0:nqt, :], OO[:, 0:nqt, :], SS[:, 0:nqt, :].to_broadcast([128, nqt, 128]),
                       ALU.mult, ["OO", "SS"], ["OO"])
                    tt("pool", ob[:, 0:nqt, :], OO[:, 0:nqt, :], G3[:].to_broadcast([128, nqt, 128]), ALU.mult,
                       ["OO", "G3"], [okey])
                    ld(aos[b][q0:q0 + nq, 128 * h:128 * (h + 1)].rearrange("(q p) c -> p q c", p=128),
                       ob[:, 0:nqt, :], [okey], [], "sa%d" % (nblk % 2))
        mk.release(m)

    def phase_nbr():
        m = mk.mark()
        QB2 = mk.sb([128, 4, TK], BF16, "QB2")
        KB2 = mk.sb([128, 4, TK], BF16, "KB2")
        VBe = mk.sb([128, 34, 8, 65], BF16, "VBe")
        VBo = mk.sb([128, 31, 8, 65], BF16, "VBo")
        BALL = mk.sb([128, 14, 8, 64], F32, "BALL")
        MK2 = mk.sb([128, 1, 64], F32, "MK2")
        SBF = [mk.sb([128, 512], F32, "SBF") for _ in range(2)]
        PT = [mk.sb([128, 6, 512], BF16, "PTn") for _ in range(2)]
        RI = mk.sb([64, 8, 1], F32, "RI")
        OB = [mk.sb([64, 512], F32, "OB") for _ in range(2)]
        memset("pool", VBe[:, :, :, 64:65], 1.0, [("VBe", "one")])
        memset("pool", VBo[:, :, :, 64:65], 1.0, [("VBo", "one")])
        ld(MK2[:, 0, :], mask_d, [], ["MK2"], "nm")

        for base_ in range(14):
            for i2 in range(2):
                ld(BALL[64 * i2:64 * (i2 + 1), base_, :, :], rpbt_d[base_ + i2], [], [("BALL", base_, i2)], "nb%d" % i2)
            tt("dve" if base_ % 2 else "pool", BALL[:, base_, :, :], BALL[:, base_, :, :], MK2[:].to_broadcast([128, 8, 64]), ALU.add,
               [("BALL", base_, 0), ("BALL", base_, 1), "MK2"], [("BALL", base_)])
        nrow = 0
        for b in range(NBB):
            ld(QB2[:], qbt[b].rearrange("h p t -> p h t"), [], ["QB2"], "nq")
            ld(KB2[:], kbt[b].rearrange("h p t -> p h t"), [], ["KB2"], "nk")
            vb_v = vbs[b].rearrange("(t p) (h d) -> p t h d", p=128, h=8)
            for t4 in range(0, 34, 6):
                t5 = min(34, t4 + 6)
                mk.dma("sp", [(VBe[:, t_, :, 0:64], vb_v[:, t_]) for t_ in range(t4, t5)], [], [("VBe", t4)], "nv")
            vbo_v = vbs[b][64:64 + 31 * 128, :].rearrange("(t p) (h d) -> p t h d", p=128, h=8)
            for t4 in range(0, 31, 6):
                t5 = min(31, t4 + 6)
                mk.dma("sp", [(VBo[:, t_, :, 0:64], vbo_v[:, t_]) for t_ in range(t4, t5)], [], [("VBo", t4)], "nv")
            vek = [("VBe", t4) for t4 in range(0, 34, 6)] + [("VBe", "one")]
            vok = [("VBo", t4) for t4 in range(0, 31, 6)] + [("VBo", "one")]
            rows = [("lat", r) for r in range(64)] + [("ctx", g) for g in range(4)]
            if "nbr_rows" in CFG:
                rows = rows[:CFG["nbr_rows"]]
            nst = CFG.get("nbr_stage", 9)
            for (kind, r) in rows:
                par = nrow % 2
                nrow += 1
                if kind == "lat":
                    rs = min(max(r - 4, 0), 56)
                    delta = rs - r
                    q0 = 64 * r
                    kts = []
                    for j in range(4):
                        ks = 64 * (rs + 2 * j)
                        if rs % 2 == 0:
                            kts.append((ks, VBe, (rs + 2 * j) // 2, vek, j))
                        else:
                            kts.append((ks, VBo, (rs + 2 * j - 1) // 2, vok, j))
                    kts.append((T, VBe, 32, vek, None))
                    kts.append((T + 128, VBe, 33, vek, None))
                else:
                    q0 = T + 64 * r
                    kts = [(T, VBe, 32, vek, None), (T + 128, VBe, 33, vek, None)]
                pt = PT[par]
                ptk = "PTn%d" % par
                for n, (ks, vt, vi, vk, j) in enumerate(kts):
                    pp = n % 2
                    for h in range(8):
                        hp, hq = h % 2, h // 2
                        mm(bank(2 * pp + hp, 64, 64 * hq), KB2[64 * hp:64 * (hp + 1), hq, ks:ks + 128],
                           QB2[64 * hp:64 * (hp + 1), hq, q0:q0 + 64], True, True, ["KB2", "QB2"], pk(2 * pp + hp))
                    s2 = PS[:, 1024 * pp:1024 * pp + 1024].rearrange("p (b c) -> p b c", b=2)[:, :, 0:256]
                    sk2 = pk(2 * pp) + pk(2 * pp + 1)
                    if j is not None:
                        sb_ = SBF[n % 2]
                        bidx = delta + 2 * j + 7
                        tt("dve", sb_[:].rearrange("p (b c) -> p b c", b=2), s2,
                           BALL[:, bidx, :, :].rearrange("p (b h) c -> p b (h c)", b=2), ALU.add,
                           sk2 + [("BALL", bidx)], ["SBF%d" % (n % 2)])
                        act(pt[:, n, :], sb_[:], AF.Exp, ["SBF%d" % (n % 2)], [(ptk, n)])
                    else:
                        act(pt[:, n, :].rearrange("p (b c) -> p b c", b=2), s2, AF.Exp, sk2, [(ptk, n)])
                if nst < 2:
                    continue
                ab = 4 + 2 * par
                for h in range(8):
                    b_ = ab + h // 4
                    for n, (ks, vt, vi, vk, j) in enumerate(kts):
                        mm(PS[0:64, 512 * b_ + 65 * (h % 4):512 * b_ + 65 * (h % 4) + 65],
                           pt[:, n, 64 * ((h % 2) * 4 + h // 2):64 * ((h % 2) * 4 + h // 2 + 1)], vt[:, vi, h, :],
                           n == 0, n == len(kts) - 1,
                           [(ptk, n)] + vk, pk(b_))
                if nst < 3:
                    continue
                ob = OB[par]
                obk = "OB%d" % par
                for g in range(2):
                    b_ = ab + g
                    accv = PS[0:64, 512 * b_:512 * b_ + 260].rearrange("p (h c) -> p h c", h=4)
                    mk.op("dve", lambda e, accv=accv, g=g: e.reciprocal(out=RI[:, 4 * g:4 * g + 4, :], in_=accv[:, :, 64:65]),
                          pk(b_), [("RI", g)])
                    tt("dve", ob[:, 256 * g:256 * (g + 1)].rearrange("p (h d) -> p h d", h=4), accv[:, :, 0:64],
                       RI[:, 4 * g:4 * g + 4, :].to_broadcast([64, 4, 64]), ALU.mult, pk(b_) + [("RI", g)], [(obk, g)])
                ld(aos[b][q0:q0 + 64, 512:1024], ob[:], [(obk, 0), (obk, 1)], [], "no%d" % par)
        mk.release(m)

    def phase_oproj(l):
        m = mk.mark()
        WO = mk.sb([128, 8, D], BF16, "WO")
        wsrc = awout_d[0] if l == 0 else rwout_d[0]
        ldc([(WO[:], wsrc.rearrange("(k p) n -> p k n", p=128))], [], ["WO"], "w0")
        GB = mk.sb([128, D], F32, "GB")
        Gt = mk.sb([128, D], F32, "Gt")
        Bt = mk.sb([128, D], F32, "Bt")
        E = epi_alloc()
        bkeys = ["GB", "Gt", "Bt"]
        site = 1 if l == 0 else 3
        seqs = SEQ if l == 0 else [s for s in SEQ if s[0] == "lat"]
        if l == 0:
            AO = [mk.sb([128, D], F32, "AO") for _ in range(3)]
            AT = [mk.sb([128, 8, 128], BF16, "AT") for _ in range(2)]
        else:
            MB = [mk.sb([128, 8, 512], BF16, "MB") for _ in range(2)]
        X3 = [mk.sb([128, D], F32, "X3") for _ in range(3)]
        n0 = 0
        for s in seqs:
            r = seq_var(s)
            load_bc(GB, Gt, Bt, l, 2 * D, r, "ln1_g", "ln1_b")
            b = s[1]
            L = seq_len(s)
            NT = L // 128
            base = 0 if s[0] == "lat" else T
            xsrc = seq_src(s) if l == 0 else xs[s]
            hdst_all = hts[(s, 0)].rearrange("k p t -> p k t")

            def stageL(i):
                n = n0 + i
                s3 = n % 3
                ld(X3[s3][:], xsrc[128 * i:128 * (i + 1), :], [], ["X3%d" % s3], "ex%d" % s3)
                if l == 0:
                    ld(AO[s3][:], aos[b][base + 128 * i:base + 128 * (i + 1), :], [], ["AO%d" % s3], "ol%d" % s3)
                elif i % 4 == 0:
                    ms = (i // 4) % 2
                    ld(MB[ms][:], mts[b].rearrange("k p t -> p k t")[:, :, 128 * i:128 * i + 512], [],
                       ["MB%d" % ms], "om%d" % ms)

            def stageT(i):
                n = n0 + i
                sl = n % 2
                s3 = n % 3
                if l == 0:
                    for k in range(8):
                        tr(bank(4 + k // 4, 128, 128 * (k % 4)), AO[s3][:, 128 * k:128 * (k + 1)], ident[:],
                           ["AO%d" % s3, "ident"], pk(4 + k // 4))
                    cp("act", AT[sl][:, 0:4, :].rearrange("p k t -> p (k t)"), bank(4), pk(4), [("AT%d" % sl, 0)])
                    cp("dve", AT[sl][:, 4:8, :].rearrange("p k t -> p (k t)"), bank(5), pk(5), [("AT%d" % sl, 1)])

            def stageM(i):
                n = n0 + i
                sl = n % 2
                pa = 2 * sl
                for k in range(8):
                    if l == 0:
                        lhs, lk = AT[sl][:, k, :], [("AT%d" % sl, 0), ("AT%d" % sl, 1)]
                    else:
                        ms = (i // 4) % 2
                        lhs, lk = MB[ms][:, k, 128 * (i % 4):128 * (i % 4 + 1)], ["MB%d" % ms]
                    for hf in range(2):
                        mm(bank(pa + hf), lhs, WO[:, k, 512 * hf:512 * (hf + 1)], k == 0, k == 7, lk + ["WO"], pk(pa + hf))

            def stageA(i):
                n = n0 + i
                sl = n % 2
                pa = 2 * sl
                Xt, Y = E["Xt"][sl], E["Y"][sl]
                kx, ky = "Xt%d" % sl, "Y%d" % sl
                s3 = n % 3
                tt("dve", Y[:], PS[:, 512 * pa:512 * pa + 1024], GB[:], ALU.mult, pk(pa) + pk(pa + 1) + ["GB"], [ky])
                stt("dve", Y[:], X3[s3][:], ALPHA, Y[:], ALU.mult, ALU.add, ["X3%d" % s3, ky], [ky])
                BS, MV, RS = E["BS"][sl], E["MV"][sl], E["RS"][sl]
                kb = "BS%d" % sl
                mk.op("dve", lambda e: e.bn_stats(out=BS[:, 0, :], in_=Y[:, 0:512]), [ky], [(kb, 0)])
                mk.op("dve", lambda e: e.bn_stats(out=BS[:, 1, :], in_=Y[:, 512:1024]), [ky], [(kb, 1)])
                mk.op("dve", lambda e: e.bn_aggr(out=MV[:], in_=BS[:]), [(kb, 0), (kb, 1)], [(kb, 2)])
                act(RS[:, 0:1], MV[:, 1:2], AF.Ln, [(kb, 2), "eps"], [(kb, 3)], bias=epsT[:, 0:1])
                act(RS[:, 1:2], RS[:, 0:1], AF.Exp, [(kb, 3)], [(kb, 4)], scale=-0.5)
                stt("dve", RS[:, 2:3], MV[:, 0:1], -1.0, RS[:, 1:2], ALU.mult, ALU.mult, [(kb, 2), (kb, 4)], [(kb, 5)])
                act(Y[:], Y[:], AF.Identity, [ky, (kb, 4), (kb, 5)], [ky], bias=RS[:, 2:3], scale=RS[:, 1:2])
                tt("pool", Xt[:], Y[:], Gt[:], ALU.mult, [ky, "Gt"], [kx])
                tt("pool", Xt[:], Xt[:], Bt[:], ALU.add, [kx, "Bt"], [kx])
                mk.dma("pool", [(xs[s][128 * i:128 * (i + 1), :], Xt[:])], [kx], [], "pes%d" % sl)

            def stageB(i):
                n = n0 + i
                sl = n % 2
                make_ht(E["Y"][sl], ["Y%d" % sl], E["HT"][sl], "HTe%d" % sl, site, r, 6)
                mk.dma("pool", [(hdst_all[:, :, 1 + 128 * i:1 + 128 * (i + 1)], E["HT"][sl][:])],
                       [("HTe%d" % sl, k) for k in range(8)], [], "peh%d" % sl)

            stageL(0)
            if NT > 1:
                stageL(1)
            stageT(0)
            for i in range(NT):
                if i + 2 < NT:
                    stageL(i + 2)
                if i + 1 < NT:
                    stageT(i + 1)
                stageM(i)
                stageA(i)
                if i >= 1:
                    stageB(i - 1)
            stageB(NT - 1)
            n0 += NT
        mk.release(m)

    def phase_ffn(l):
        m = mk.mark()
        WU = mk.sb([128, 8, 2 * DFF], BF16, "WU")
        WD = mk.sb([128, 22, D], BF16, "WD")
        wv = wup_d[l].rearrange("(k p) n -> p k n", p=128)
        for k in range(8):
            ldc([(WU[:, k, :], wv[:, k, :])], [], [("WU", k)], "w%d" % (k % 4))
        wdv = wdn_d[l].rearrange("(k p) n -> p k n", p=128)
        wuk = [("WU", k) for k in range(8)]
        WDG = [(k0, min(22, k0 + 6)) for k0 in range(0, 22, 6)]
        wdk = [("WD", k0) for (k0, _) in WDG]
        Gt = mk.sb([128, D], F32, "Gt")
        Bt = mk.sb([128, D], F32, "Bt")
        E = epi_alloc(2, 1)
        HB = mk.sb([128, 8, 514], BF16, "HB")
        UW = mk.sb([128, 44, 8, 2], F32, "UW")
        C = [mk.sb([128, 512], F32, "C") for _ in range(3)]
        m_ac = mk.mark()
        UH = mk.sb([128, 44, 18], F32, "UH")
        mk.sb_off = m_ac
        AC = mk.sb([128, 22, 512], BF16, "AC")
        ACK = [("AC", i) for i in range(22)]
        X3b = mk.sb([128, D], F32, "Xt")
        E["Xt"].append(X3b)
        HS = mk.sb([128, 8, 16], BF16, "HS")
        US = [C[0], C[1]]
        memset("pool", HS[:], 0.0, ["HS"])
        last = l == 1
        site = 2 if l == 0 else None
        seqs = SEQ if l == 0 else [s for s in SEQ if s[0] == "lat"]
        fw = [VMAP[("fcw", l, k)] for k in range(3)]
        fb = VMAP[("fcb", l)]
        MB = [0, 1, 2, 7]
        bkeys = ["Gt", "Bt"]
        n = 0
        uc = 0
        for s in seqs:
            r = seq_var(s)
            L = seq_len(s)
            hsrc = hts[(s, 0)].rearrange("k p t -> p k t")
            hdst_all = hts[(s, 1)].rearrange("k p t -> p k t")
            nb = min(512, L)
            nblk = L // nb
            GBt = E["Y"][0]
            ld(GBt[:], modd[l, r, 5 * D:6 * D].partition_broadcast(128), [], ["Y0"], "bc0")
            ld(Gt[:], ln_d["ln2_g"][l].partition_broadcast(128), [], ["Gt"], "bc1")
            ld(Bt[:], ln_d["ln2_b"][l].partition_broadcast(128), [], ["Bt"], "bc2")
            for gi, (k0, k1) in enumerate(WDG):
                ldc([(WD[:, k0:k1, :], wdv[:, k0:k1, :])], [], [("WD", k0)], "w%d" % (gi % 4))
                for kk in range(k0, k1):
                    tt("pool" if kk % 2 else "dve", WD[:, kk, :], WD[:, kk, :], GBt[:], ALU.mult, [("WD", k0), "Y0"], [("WD", k0)])
            if nblk > 1:
                for bd in range(1, nblk):
                    mk.dma("sp", [(HS[:, :, 2 * bd:2 * bd + 2], hsrc[:, :, 512 * bd:512 * bd + 2])], [], ["HS"], "fz", slow=True)
                for cb in range(11):
                    for k in range(8):
                        mm(PS[0:16, 512 * 3:512 * 3 + 512], HS[:, k, :], WU[:, k, 512 * cb:512 * (cb + 1)], k == 0, k == 7,
                           ["HS"] + wuk, pk(3))
                    cp("act", US[cb % 2][0:16, :], PS[0:16, 512 * 3:512 * 3 + 512], pk(3), ["C%d" % (cb % 2)])
                    for q in range(4):
                        c = 4 * cb + q
                        b_ = 5 + c // 22
                        tr(PS[:, 512 * b_ + 16 * (c % 22):512 * b_ + 16 * (c % 22) + 16], US[cb % 2][0:16, 128 * q:128 * (q + 1)],
                           ident[0:16, 0:16], ["C%d" % (cb % 2), "ident"], pk(b_))
                memset("pool", UH[:, :, 16:18], 0.0, ["UH"] + ACK)
                cp("dve", UH[:, 0:22, 0:16], PS[:, 512 * 5:512 * 5 + 352].rearrange("p (c e) -> p c e", e=16), pk(5), ["UH"] + ACK)
                cp("dve", UH[:, 22:44, 0:16], PS[:, 512 * 6:512 * 6 + 352].rearrange("p (c e) -> p c e", e=16), pk(6), ["UH"] + ACK)
            else:
                memset("pool", UH[:], 0.0, ["UH"] + ACK)

            w0b = VEC[:, fw[0]:fw[0] + 44].rearrange("p (c o) -> p c o", o=1).to_broadcast([128, 44, 8])
            w2b = VEC[:, fw[2]:fw[2] + 44].rearrange("p (c o) -> p c o", o=1).to_broadcast([128, 44, 8])
            tt("dve", UW[:, :, :, 0], UH[:, :, 0:15:2], w0b, ALU.mult, ["UH", "VEC"] + ACK, [("UW", 0)])
            tt("dve", UW[:, :, :, 1], UH[:, :, 3:18:2], w2b, ALU.mult, ["UH", "VEC"] + ACK, [("UW", 1)])
            uwk = [("UW", 0), ("UW", 1)]

            def xload(nn, tok):
                sl = nn % 3
                ld(E["Xt"][sl][:], xs[s][tok:tok + 128, :], [], ["Xt%d" % sl], "ex%d" % sl)

            def epiA(nn, ps2, pskeys, xdst):
                sl = nn % 2
                s3 = nn % 3
                Xt, Y = E["Xt"][s3], E["Y"][sl]
                kx, ky = "Xt%d" % s3, "Y%d" % sl
                stt("dve", Y[:], Xt[:], ALPHA, ps2, ALU.mult, ALU.add, [kx] + pskeys, [ky])
                BS, MV, RS = E["BS"][sl], E["MV"][sl], E["RS"][sl]
                kb = "BS%d" % sl
                mk.op("dve", lambda e: e.bn_stats(out=BS[:, 0, :], in_=Y[:, 0:512]), [ky], [(kb, 0)])
                mk.op("dve", lambda e: e.bn_stats(out=BS[:, 1, :], in_=Y[:, 512:1024]), [ky], [(kb, 1)])
                mk.op("dve", lambda e: e.bn_aggr(out=MV[:], in_=BS[:]), [(kb, 0), (kb, 1)], [(kb, 2)])
                act(RS[:, 0:1], MV[:, 1:2], AF.Ln, [(kb, 2), "eps"], [(kb, 3)], bias=epsT[:, 0:1])
                act(RS[:, 1:2], RS[:, 0:1], AF.Exp, [(kb, 3)], [(kb, 4)], scale=-0.5)
                stt("dve", RS[:, 2:3], MV[:, 0:1], -1.0, RS[:, 1:2], ALU.mult, ALU.mult, [(kb, 2), (kb, 4)], [(kb, 5)])
                act(Y[:], Y[:], AF.Identity, [ky, (kb, 4), (kb, 5)], [ky], bias=RS[:, 2:3], scale=RS[:, 1:2])
                tt("pool", Xt[:], Y[:], Gt[:], ALU.mult, [ky] + bkeys, [kx])
                tt("pool", Xt[:], Xt[:], Bt[:], ALU.add, [kx] + bkeys, [kx])
                ld(xdst, Xt[:], [kx], [], "es%d" % s3)

            def epiB(nn, hdst):
                sl = nn % 2
                make_ht(E["Y"][sl], ["Y%d" % sl], E["HT"][0], "HTe0", site, r, 5)
                ld(hdst, E["HT"][0][:], [("HTe0", k) for k in range(8)], [], "eh0")

            def tail(pend):
                i_, cg, cgk, cv, cvk = pend
                act(cg[:, 0:nb], cg[:, 0:nb], AF.Gelu_apprx_tanh, [cgk], [cgk])
                tt("pool", AC[:, i_, 0:nb], cg[:, 0:nb], cv[:, 0:nb], ALU.mult, [cgk, cvk], [("AC", i_)])

            for j in range(nblk):
                t0 = nb * j
                if j == 0:
                    ld(HB[:, :, 0:nb + 2], hsrc[:, :, t0:t0 + nb + 2], [], ["HB"], "fh")
                xload(n, t0)
                if nb > 128:
                    xload(n + 1, t0 + 128)
                pend = None
                for i in range(22):
                    cur = []
                    for gv in range(2):
                        c = i + 22 * gv
                        u = uc % 4
                        cb_ = uc % 3
                        uc += 1
                        mb = MB[u]
                        for k in range(8):
                            mm(bank(mb, nb), WU[:, k, 128 * c:128 * (c + 1)], HB[:, k, 1:nb + 1], k == 0, k == 7,
                               wuk + ["HB"], pk(mb))
                        Ct = C[cb_]
                        ck = "C%d" % cb_
                        act(Ct[:, 0:nb], bank(mb, nb), AF.Identity, pk(mb) + ["VEC"], [ck],
                            bias=VEC[:, fb + c:fb + c + 1], scale=VEC[:, fw[1] + c:fw[1] + c + 1])
                        tt("pool", Ct[:, 0:nb:nb - 1], Ct[:, 0:nb:nb - 1], UW[:, c, j, :], ALU.add, uwk + [ck], [ck])
                        stt("dve", Ct[:, 1:nb], bank(mb, nb - 1), VEC[:, fw[0] + c:fw[0] + c + 1], Ct[:, 1:nb], ALU.mult, ALU.add,
                            pk(mb) + ["VEC", ck], [ck])
                        stt("dve", Ct[:, 0:nb - 1], bank(mb, nb - 1, 1), VEC[:, fw[2] + c:fw[2] + c + 1], Ct[:, 0:nb - 1], ALU.mult, ALU.add,
                            pk(mb) + ["VEC", ck], [ck])
                        cur += [Ct, ck]
                        if gv == 0 and pend is not None:
                            tail(pend)
                            pend = None
                    pend = (i, cur[0], cur[1], cur[2], cur[3])
                tail(pend)
                if j + 1 < nblk:
                    ld(HB[:, :, 0:nb + 2], hsrc[:, :, t0 + nb:t0 + 2 * nb + 2], [], ["HB"], "fh")
                ack = [("AC", i) for i in range(22)]
                nt_ = nb // 128
                prevB = None
                for tq in range(nt_):
                    tok = t0 + 128 * tq
                    for k in range(22):
                        for hf in range(2):
                            mm(bank(3 + hf), AC[:, k, 128 * tq:128 * (tq + 1)], WD[:, k, 512 * hf:512 * (hf + 1)],
                               k == 0, k == 21, [("AC", k)] + wdk, pk(3 + hf))
                    if tq + 2 < nt_:
                        xload(n + 2, tok + 256)
                    if last:
                        xdst = out_d[s[1], tok:tok + 128, :]
                    else:
                        xdst = xs[s][tok:tok + 128, :]
                    epiA(n, PS[:, 512 * 3:512 * 3 + 1024], pk(3) + pk(4), xdst)
                    if prevB is not None:
                        epiB(*prevB)
                        prevB = None
                    if not last:
                        prevB = (n, hdst_all[:, :, 1 + tok:1 + tok + 128])
                    n += 1
                if prevB is not None:
                    epiB(*prevB)
        mk.release(m)

    def phase_rglru():
        m = mk.mark()
        CX0, LX0, NX = 2, 261, 4358
        XR = mk.sb([128, NX], F32, "XR")
        XC = mk.sb([128, NX], F32, "XC")
        XCb = mk.sb([128, NX], BF16, "XCb")
        A0 = mk.sb([128, NX], F32, "A0")
        A1 = mk.sb([128, NX], F32, "A1")
        BT0 = mk.sb([128, NX], F32, "BT0")
        TM = mk.sb([128, NX], F32, "TM")
        RF = mk.sb([128, NX], F32, "RF")
        RR = mk.sb([128, NX], F32, "RR")
        GY = [mk.sb([128, T], BF16, "GY") for _ in range(2)]
        MT = mk.sb([128, T], BF16, "MT")
        WI = [mk.sb([128, 8, 2, 128], BF16, "WI") for _ in range(2)]
        GW = [mk.sb([128, 4, 128], BF16, "GW") for _ in range(2)]
        ONE = mk.sb([128, 1], F32, "ONE")
        memset("pool", ONE[:], 1.0, ["ONE"])
        NHB = 2
        HB = [mk.sb([128, 8, 512], BF16, "HBr") for _ in range(NHB)]
        memset("dve", XR[:], 0.0, ["XRz"])
        wv = rwin_d[0].rearrange("(k p) n -> p k n", p=128)
        regions = [(CX0, TC), (LX0, T)]
        BLOCKS = [("ctx", 0, 256, CX0)] + [("lat", 512 * j, 512, LX0 + 512 * j) for j in range(8)]
        chunks = [(b, c) for b in range(NBB) for c in range(8)]
        items = [(b, c, bi) for (b, c) in chunks for bi in range(9)]
        issued = [0]
        nh = [0]

        def prefetch(upto):
            while issued[0] <= min(upto, len(items) - 1):
                b_, c_, bi_ = items[issued[0]]
                kind_, t0_, nt_, xo_ = BLOCKS[bi_]
                hb = issued[0] % NHB
                src = hts[((kind_, b_), 1)].rearrange("k p t -> p k t")
                ld(HB[hb][:, :, 0:nt_], src[:, :, 1 + t0_:1 + t0_ + nt_], [], ["HBr%d" % hb], "rh%d" % hb)
                issued[0] += 1

        def inproj(ci):
            b, c = chunks[ci]
            w_ = ci % 2
            ldc([(WI[w_][:, :, 0, :], wv[:, :, 128 * c:128 * (c + 1)]),
                 (WI[w_][:, :, 1, :], wv[:, :, D + 128 * c:D + 128 * (c + 1)])], [], ["WI%d" % w_], "w%d" % w_)
            ldc([(GW[w_][:, 0, :], rga_d[0, 0, c]), (GW[w_][:, 1, :], rgx_d[0, 0, c]),
                 (GW[w_][:, 2, :], rga_d[0, 1, c]), (GW[w_][:, 3, :], rgx_d[0, 1, c])], [], ["GW%d" % w_], "w%d" % (2 + w_))
            gy = GY[ci % 2]
            for (kind, t0, nt, xo) in BLOCKS:
                prefetch(nh[0] + NHB - 1)
                sl = nh[0] % NHB
                bk = 2 * (nh[0] % 2)
                nh[0] += 1
                for k in range(8):
                    mm(bank(bk, nt), WI[w_][:, k, 1, :], HB[sl][:, k, 0:nt], k == 0, k == 7, ["WI%d" % w_, "HBr%d" % sl], pk(bk))
                cp("act", XR[:, xo:xo + nt], bank(bk, nt), pk(bk) + ["XRz"], [("XR", xo)])
                if kind == "lat":
                    for k in range(8):
                        mm(bank(bk + 1, nt), WI[w_][:, k, 0, :], HB[sl][:, k, 0:nt], k == 0, k == 7,
                           ["WI%d" % w_, "HBr%d" % sl], pk(bk + 1))
                    act(gy[:, t0:t0 + nt], bank(bk + 1, nt), AF.Gelu_apprx_tanh, pk(bk + 1), [("GY%d" % (ci % 2), t0)])

        xrk = [("XR", xo) for (_, _, _, xo) in BLOCKS] + ["XRz"]
        xck = [("XC", o) for (o, _) in regions]
        xcbk = [("XCb", o) for (o, _) in regions]
        (oc, ncx), (ol, nl) = regions
        inproj(0)
        for ci, (b, c) in enumerate(chunks):
            w_ = ci % 2
            w = [VMAP[("rcw", k)] + c for k in range(4)]
            cb = VMAP[("rcb",)] + c
            for (o, n_) in regions:
                act(XC[:, o:o + n_], XR[:, o:o + n_], AF.Identity, xrk + ["VEC"], [("XC", o)],
                    bias=VEC[:, cb:cb + 1], scale=VEC[:, w[2]:w[2] + 1])
                for (kk, sh) in ((0, -2), (1, -1), (3, 1)):
                    stt("dve", XC[:, o:o + n_], XR[:, o + sh:o + sh + n_], VEC[:, w[kk]:w[kk] + 1], XC[:, o:o + n_],
                        ALU.mult, ALU.add, xrk + ["VEC", ("XC", o)], [("XC", o)])
                cp("pool", XCb[:, o:o + n_], XC[:, o:o + n_], [("XC", o)], [("XCb", o)])
            for d in range(2):
                Ad = A0 if d == 0 else A1
                Bd = BT0 if d == 0 else RR
                an, bn = "A%d" % d, "B%d" % d
                ba = VMAP[("rba", d)] + c
                bx = VMAP[("rbx", d)] + c
                for gi, (kind, t0, nt, xo) in enumerate(BLOCKS):
                    bk = 4 + 2 * (gi % 2)
                    ro = CX0 if kind == "ctx" else LX0
                    mm(bank(bk, nt), GW[w_][:, 2 * d, :], XCb[:, xo:xo + nt], True, True, ["GW%d" % w_] + xcbk, pk(bk))
                    mm(bank(bk + 1, nt), GW[w_][:, 2 * d + 1, :], XCb[:, xo:xo + nt], True, True, ["GW%d" % w_] + xcbk,
                       pk(bk + 1))
                    act(Ad[:, xo:xo + nt], bank(bk, nt), AF.Sigmoid, pk(bk) + ["VEC"], [(an, xo), (an, "r", ro)],
                        bias=VEC[:, ba:ba + 1])
                    act(Bd[:, xo:xo + nt], bank(bk + 1, nt), AF.Sigmoid, pk(bk + 1) + ["VEC"], [(bn, xo), (bn, "r", ro)],
                        bias=VEC[:, bx:bx + 1])
                ak = [(an, xo) for (_, _, _, xo) in BLOCKS]
                btk = [(bn, xo) for (_, _, _, xo) in BLOCKS]
                for (o, n_) in regions:
                    sl_ = slice(o, o + n_)
                    kA, kB, kT = (an, "r", o), (bn, "r", o), ("TM", d, o)
                    act(Ad[:, sl_], Ad[:, sl_], AF.Exp, ak + ["CST"], [kA], scale=CST[:, 8 * d + c:8 * d + c + 1])
                    act(TM[:, sl_], Ad[:, sl_], AF.Square, [kA], [("TM", o)])
                    act(TM[:, sl_], TM[:, sl_], AF.Sqrt, [("TM", o), "ONE"], [("TM", o)], bias=ONE[:, 0:1], scale=-1.0)
                    tt("dve", Bd[:, sl_], Bd[:, sl_], XC[:, sl_], ALU.mult, btk + xck, [kB])
                    tt("dve", Bd[:, sl_], Bd[:, sl_], TM[:, sl_], ALU.mult, [kB, ("TM", o)], [kB])
                if d == 0:
                    mk.op("dve", lambda e: e.tensor_tensor_scan(out=RF[:, oc:oc + ncx], data0=A0[:, oc:oc + ncx],
                                                                data1=BT0[:, oc:oc + ncx], initial=0.0,
                                                                op0=ALU.mult, op1=ALU.add),
                          [("A0", "r", oc), ("B0", "r", oc)], [("RFc",)])
                    mk.op("dve", lambda e: e.tensor_tensor_scan(out=RF[:, ol:ol + nl], data0=A0[:, ol:ol + nl],
                                                                data1=BT0[:, ol:ol + nl],
                                                                initial=RF[:, oc + ncx - 1:oc + ncx],
                                                                op0=ALU.mult, op1=ALU.add),
                          [("A0", "r", ol), ("B0", "r", ol), ("RFc",)], [("RFl",)])
                else:
                    mk.op("dve", lambda e: e.tensor_tensor_scan(out=RR[:, oc:oc + ncx][:, ::-1],
                                                                data0=A1[:, oc:oc + ncx][:, ::-1],
                                                                data1=RR[:, oc:oc + ncx][:, ::-1], initial=0.0,
                                                                op0=ALU.mult, op1=ALU.add),
                          [("A1", "r", oc), ("B1", "r", oc)], [("B1", "r", oc), ("RRc",)])
                    mk.op("dve", lambda e: e.tensor_tensor_scan(out=RR[:, ol:ol + nl][:, ::-1],
                                                                data0=A1[:, ol:ol + nl][:, ::-1],
                                                                data1=RR[:, ol:ol + nl][:, ::-1],
                                                                initial=RR[:, oc:oc + 1],
                                                                op0=ALU.mult, op1=ALU.add),
                          [("A1", "r", ol), ("B1", "r", ol), ("RRc",)], [("B1", "r", ol), ("RRl",)])
                if d == 1 and ci + 1 < len(chunks):
                    inproj(ci + 1)
            tt("pool", RF[:, LX0:LX0 + T], RF[:, LX0:LX0 + T], RR[:, LX0:LX0 + T], ALU.add, [("RFl",), ("RRl",), ("B1", "r", ol)], [("RFl",)])
            tt("dve", MT[:], RF[:, LX0:LX0 + T], GY[ci % 2][:], ALU.mult,
               [("RFl",)] + [("GY%d" % (ci % 2), 512 * j) for j in range(8)], ["MT"])
            ld(mts[b][c], MT[:], ["MT"], [], "rm")
        mk.release(m)

    order = ["qkv", "diff", "nbr", "op0", "ffn0", "rg", "op1", "ffn1"]
    fns = {"qkv": phase_qkv, "diff": phase_diff, "nbr": phase_nbr, "op0": lambda: phase_oproj(0),
           "ffn0": lambda: phase_ffn(0), "rg": phase_rglru, "op1": lambda: phase_oproj(1), "ffn1": lambda: phase_ffn(1)}
    for ph in order:
        if stop == "p0":
            break
        if "only" in CFG and ph not in CFG["only"]:
            continue
        fns[ph]()
        if stop == ph:
            break
    mk.barrier()
    named = {"modd": modd, "ao0": aos[0], "xs_lat0": xs[("lat", 0)], "xs_ctx0": xs[("ctx", 0)],
             "qat0": qat[0], "kat0": kat[0], "va0": vas[0], "qbt0": qbt[0], "kbt0": kbt[0], "vb0": vbs[0],
             "hts0_lat0": hts[(("lat", 0), 0)], "hts1_lat0": hts[(("lat", 0), 1)], "mt0": mts[0]}
    for nm in dbg:
        src = named[nm]
        dst = nc.dram_tensor("dbg_" + nm, list(src.shape), src.dtype, kind="ExternalOutput").ap()
        mk.dma("sp", [(dst, src)], [], [], "dbg")
    mk.emit()
    return nc, mk


def prep_inputs(inputs, core):
    g = lambda k: np.asarray(inputs[k])
    b0 = 2 * core
    m = {}
    m["x"] = np.ascontiguousarray(g("x")[b0:b0 + 2])
    m["ctx"] = np.ascontiguousarray(g("ctx")[b0:b0 + 2])
    cc = np.stack([g("c")[b0], g("c")[b0 + 1], g("c_ctx")], 0).astype(np.float32)
    m["cct"] = np.ascontiguousarray(cc.reshape(3, 8, 128).transpose(2, 1, 0))
    vec = np.zeros((128, NV), np.float32)
    for l in range(2):
        for nm in ("ln1_g", "ln1_b", "ln2_g", "ln2_b"):
            vec[:, VMAP[(nm, l)]:VMAP[(nm, l)] + 8] = _pl(g(nm)[l])
        for k in range(3):
            vec[:, VMAP[("fcw", l, k)]:VMAP[("fcw", l, k)] + 44] = _pl(g("ffn_conv_w")[l, k])
        vec[:, VMAP[("fcb", l)]:VMAP[("fcb", l)] + 44] = _pl(g("ffn_conv_b")[l])
    for k in range(4):
        vec[:, VMAP[("rcw", k)]:VMAP[("rcw", k)] + 8] = _pl(g("rnn_conv_w")[0, k])
    vec[:, VMAP[("rcb",)]:VMAP[("rcb",)] + 8] = _pl(g("rnn_conv_b")[0])
    for d in range(2):
        vec[:, VMAP[("apar", d)]:VMAP[("apar", d)] + 8] = _pl(g("rg_a_param")[0, d])
        vec[:, VMAP[("rba", d)]:VMAP[("rba", d)] + 8] = _pl(g("rg_ba")[0, d].reshape(-1))
        vec[:, VMAP[("rbx", d)]:VMAP[("rbx", d)] + 8] = _pl(g("rg_bx")[0, d].reshape(-1))
    m["vec"] = vec
    m["rope"] = _ROPE
    rpb = g("na_rpb")[0].astype(np.float32)
    kc = np.arange(64)[:, None]
    cq = np.arange(64)[None, :]
    rel = kc - cq + 15
    ok = (rel >= 0) & (rel <= 30)
    gath = rpb[:, :, np.clip(rel, 0, 30)] * ok[None, None]
    perm = [0, 2, 4, 6, 1, 3, 5, 7]
    m["rpbt"] = np.ascontiguousarray(gath[perm].transpose(1, 2, 0, 3)).astype(np.float32)
    m["mask"] = _MASK
    m["lamv"] = np.stack([g("diff_lq1")[0], g("diff_lk1")[0], g("diff_lq2")[0], g("diff_lk2")[0]], 0).astype(np.float32)
    m["subg"] = np.ascontiguousarray(g("diff_subln_g")[0]).astype(np.float32)
    for k in ("ada_w", "ada_b", "ln1_g", "ln1_b", "ln2_g", "ln2_b", "ffn_w_up", "ffn_w_down", "att_w_in",
              "att_w_out", "rnn_w_in", "rg_wa", "rg_wx", "rnn_w_out"):
        m[k] = np.ascontiguousarray(g(k)).astype(np.float32)
    return m


def _const_tables():
    t = np.arange(T)
    row = (t // 64).astype(np.float64)[:, None]
    col = (t % 64).astype(np.float64)[:, None]
    inv = 1.0 / (10000.0 ** (np.arange(16, dtype=np.float64) / 16))
    inv = inv.astype(np.float32).astype(np.float64)
    ang = np.concatenate([row * inv, row * inv, col * inv, col * inv], -1).astype(np.float32)
    cos = np.cos(ang).astype(np.float32)
    sin = np.sin(ang).astype(np.float32)
    sgn = np.tile(np.concatenate([-np.ones(16), np.ones(16)]), 2).astype(np.float32)
    rope = np.stack([cos, sin * sgn[None]], 1).astype(np.float32)
    c = np.arange(64)
    cs = np.clip(c - 8, 0, 48)
    kc = np.arange(64)[:, None]
    inside = (kc >= cs[None]) & (kc < cs[None] + 16)
    mask = np.where(inside, 0.0, NEG).astype(np.float32)
    return np.ascontiguousarray(rope), np.ascontiguousarray(np.concatenate([mask, mask], 0))


_ROPE, _MASK = _const_tables()
_CACHE = {}
CFG = {}


def kernel(**inputs):
    if "nc" not in _CACHE:
        _CACHE["nc"] = build()[0]
    nc = _CACHE["nc"]
    in_maps = [prep_inputs(inputs, c) for c in range(8)]
    res = run_bass_kernel_spmd(nc, in_maps, core_ids=list(range(8)))
    out = np.concatenate([r["out"] for r in res.results], axis=0)
    return out.astype(np.float32)
```
